# Optimizing a Trainium2 kernel written in Bass

```python
import math
import jax
import jax.numpy as jnp
from jax import lax
import numpy as np

D_MODEL = 1024
BATCH = 16
SEQ = 256
DEPTH = 2
DEC_BATCH = 2
DEC_SEQ = 2048
PAST_LEN = 512

GRID_W = 64
N_MIXERS = 4
GROUP_W = D_MODEL // N_MIXERS
N_DIR = 2
S5_P = 16
S5_G = GROUP_W // S5_P
S5_N = 64
HG_HEADS = 4
HG_DK = GROUP_W // HG_HEADS
HG_DV = GROUP_W // HG_HEADS
HG_CHUNK = 32
FN_HEADS = 4
FN_DH = GROUP_W // FN_HEADS
GM_HEADS = 4
GM_DH = GROUP_W // GM_HEADS
GM_CHUNK = 128
D_FF = ((8 * D_MODEL // 3 + 127) // 128) * 128
CONV_W = 3
N_MOD = 6
N_IN_SLICES = 9
D_IN = N_IN_SLICES * GROUP_W
EPS = 1e-6
LAM_RE_MAX = -1e-4

kernel_name = 'hybrid_flow_prefix_s5_hgrn2_fnet_gmlp'


def rmsnorm(x, g):
    xf = x.astype(jnp.float32)
    y = xf * lax.rsqrt(jnp.mean(xf * xf, axis=-1, keepdims=True) + EPS)
    return (y * g.astype(jnp.float32)).astype(x.dtype)


def group_rmsnorm(x, g, n_groups):
    shp = x.shape
    xg = x.reshape(shp[:-1] + (n_groups, shp[-1] // n_groups))
    return rmsnorm(xg, g.reshape(n_groups, -1)).reshape(shp)


def s5_direction(u, lam_re, lam_im, log_dt, b_re, b_im, c_re, c_im, h0_re, h0_im, reverse):
    lr = jnp.minimum(lam_re.astype(jnp.float32), LAM_RE_MAX)
    li = lam_im.astype(jnp.float32)
    dt = jnp.exp(log_dt.astype(jnp.float32))[:, None]
    mag = jnp.exp(lr * dt)
    ang = li * dt
    ab_re = mag * jnp.cos(ang)
    ab_im = mag * jnp.sin(ang)
    den = lr * lr + li * li
    xr = ab_re - 1.0
    z_re = (xr * lr + ab_im * li) / den
    z_im = (ab_im * lr - xr * li) / den
    b_re = b_re.astype(jnp.float32)
    b_im = b_im.astype(jnp.float32)
    bb_re = z_re[..., None] * b_re - z_im[..., None] * b_im
    bb_im = z_re[..., None] * b_im + z_im[..., None] * b_re
    bu_re = jnp.einsum('btgp,gnp->btgn', u, bb_re)
    bu_im = jnp.einsum('btgp,gnp->btgn', u, bb_im)
    a_re = jnp.broadcast_to(ab_re, bu_re.shape)
    a_im = jnp.broadcast_to(ab_im, bu_im.shape)

    def combine(e1, e2):
        a1r, a1i, b1r, b1i = e1
        a2r, a2i, b2r, b2i = e2
        return (a2r * a1r - a2i * a1i, a2r * a1i + a2i * a1r,
                a2r * b1r - a2i * b1i + b2r, a2r * b1i + a2i * b1r + b2i)

    ac_re, ac_im, hb_re, hb_im = lax.associative_scan(combine, (a_re, a_im, bu_re, bu_im), reverse=reverse, axis=1)
    h_re = ac_re * h0_re[:, None] - ac_im * h0_im[:, None] + hb_re
    h_im = ac_re * h0_im[:, None] + ac_im * h0_re[:, None] + hb_im
    y = (jnp.einsum('gpn,btgn->btgp', c_re.astype(jnp.float32), h_re)
         - jnp.einsum('gpn,btgn->btgp', c_im.astype(jnp.float32), h_im))
    end = 0 if reverse else -1
    return y, h_re[:, end], h_im[:, end]


def hgrn_direction(q, logf, k, v, s0):
    bsz, t, nh = q.shape[0], q.shape[1], q.shape[2]
    nc = t // HG_CHUNK

    def chunked(a):
        return a.reshape((bsz, nc, HG_CHUNK) + a.shape[2:]).transpose(1, 0, 3, 2, 4)

    qc, fc, kc, vc = chunked(q), chunked(logf), chunked(k), chunked(v)
    bcum = jnp.cumsum(fc, axis=3)
    lower = jnp.tril(jnp.ones((HG_CHUNK, HG_CHUNK), dtype=bool))
    diff = bcum[..., :, None, :] - bcum[..., None, :, :]
    decay = jnp.exp(jnp.where(lower[:, :, None], diff, -jnp.inf))
    scores = jnp.einsum('cbhjk,cbhjik,cbhik->cbhji', qc, decay, kc)
    o_intra = jnp.einsum('cbhji,cbhiv->cbhjv', scores, vc)
    b_last = bcum[..., -1:, :]
    ds = jnp.einsum('cbhik,cbhiv->cbhkv', kc * jnp.exp(b_last - bcum), vc)
    g_last = jnp.exp(b_last[..., 0, :])

    def step(s, inp):
        g, d = inp
        return g[..., None] * s + d, s

    s_fin, s_start = lax.scan(step, s0, (g_last, ds))
    o_inter = jnp.einsum('cbhjk,cbhkv->cbhjv', qc * jnp.exp(bcum), s_start)
    o = (o_intra + o_inter).transpose(1, 0, 3, 2, 4).reshape(bsz, t, nh, -1)
    return o, s_fin


def mixer(h, p, lb, s5_h0_re, s5_h0_im, hg_s0):
    dt = h.dtype
    f32 = jnp.float32
    bsz, t = h.shape[0], h.shape[1]
    proj = h @ p['w_in']
    xa, hq, hf_fwd, hf_bwd, hi, hgate, xc, gu, gv = jnp.split(proj, N_IN_SLICES, axis=-1)
    gn = p['grp_norm_g']
    xa32 = xa.astype(f32)
    u = xa32.reshape(bsz, t, S5_G, S5_P)
    ys, fre, fim = [], [], []
    for d in range(N_DIR):
        y_d, r_d, i_d = s5_direction(u, p['s5_lam_re'][d], p['s5_lam_im'][d], p['s5_log_dt'][d],
                                     p['s5_b_re'][d], p['s5_b_im'][d], p['s5_c_re'][d], p['s5_c_im'][d],
                                     s5_h0_re[:, d].astype(f32), s5_h0_im[:, d].astype(f32), d == 1)
        ys.append(y_d)
        fre.append(r_d)
        fim.append(i_d)
    y5 = (ys[0] + ys[1]).reshape(bsz, t, GROUP_W) + p['s5_d'].astype(f32) * xa32
    y5 = jax.nn.gelu(y5).astype(dt)
    out_a = rmsnorm(y5 * jax.nn.sigmoid(y5 @ p['s5_w_glu']), gn[:GROUP_W])
    q = hq.astype(f32).reshape(bsz, t, HG_HEADS, HG_DK)
    v = hi.astype(f32).reshape(bsz, t, HG_HEADS, HG_DV)
    outs, sfin = [], []
    for d, zf in enumerate((hf_fwd, hf_bwd)):
        z = zf.astype(f32).reshape(bsz, t, HG_HEADS, HG_DK)
        lbd = lb[d].astype(f32).reshape(HG_HEADS, HG_DK)
        logf = jnp.logaddexp(jnp.log(lbd), jnp.log1p(-lbd) + jax.nn.log_sigmoid(z))
        k = (1.0 - lbd) * jax.nn.sigmoid(-z)
        s0 = hg_s0[:, d].astype(f32)
        if d == 0:
            o_d, s_d = hgrn_direction(q, logf, k, v, s0)
        else:
            o_d, s_d = hgrn_direction(q[:, ::-1], logf[:, ::-1], k[:, ::-1], v[:, ::-1], s0)
            o_d = o_d[:, ::-1]
        outs.append(o_d)
        sfin.append(s_d)
    o = (outs[0] + outs[1]).reshape(bsz, t, GROUP_W).astype(dt)
    out_b = group_rmsnorm(o, gn[GROUP_W:2 * GROUP_W], HG_HEADS) * jax.nn.silu(hgate)
    xcf = xc.astype(f32).reshape(bsz, t, FN_HEADS, FN_DH)
    xfr = jnp.real(jnp.fft.fft2(xcf, axes=(1, 3), norm='ortho')).reshape(bsz, t, GROUP_W).astype(dt)
    out_c = rmsnorm(xfr @ p['fn_w'], gn[2 * GROUP_W:3 * GROUP_W])
    gu = jax.nn.gelu(gu)
    gv = rmsnorm(jax.nn.gelu(gv), p['gm_norm_g'])
    nc = t // GM_CHUNK
    gvc = gv.reshape(bsz, nc, GM_CHUNK, GM_HEADS, GM_DH)
    sp = jnp.einsum('hij,bcjhd->bcihd', p['gm_ws'], gvc) + p['gm_bs'].T[None, None, :, :, None]
    out_d = rmsnorm(gu * sp.reshape(bsz, t, GROUP_W), gn[3 * GROUP_W:])
    out = jnp.concatenate([out_a, out_b, out_c, out_d], axis=-1) @ p['w_out']
    return out, (jnp.stack(fre, axis=1), jnp.stack(fim, axis=1), jnp.stack(sfin, axis=1))


def conv_ffn(h, p, grid):
    z = h @ p['ffn_w_up']
    bsz, t, ch = z.shape
    if grid:
        zg = z.reshape(bsz, t // GRID_W, GRID_W, ch)
        axis = 2
    else:
        zg = z
        axis = 1
    pad = [(0, 0)] * zg.ndim
    pad[axis] = (CONV_W // 2, CONV_W // 2)
    zp = jnp.pad(zg, pad)
    n = zg.shape[axis]
    w = p['ffn_conv_w']
    zc = p['ffn_conv_b'] + lax.slice_in_dim(zp, 0, n, axis=axis) * w[0]
    for j in range(1, CONV_W):
        zc = zc + lax.slice_in_dim(zp, j, j + n, axis=axis) * w[j]
    a, b = jnp.split(zc.reshape(bsz, t, ch), 2, axis=-1)
    return (jax.nn.gelu(a) * b) @ p['ffn_w_down']


def trunk_layer(x, mod, p, lb, s5_h0_re, s5_h0_im, hg_s0, grid):
    sh1, sc1, g1, sh2, sc2, g2 = jnp.split(mod, N_MOD, axis=-1)
    h = rmsnorm(x, p['norm1_g']) * (1.0 + sc1) + sh1
    m, fin = mixer(h, p, lb, s5_h0_re, s5_h0_im, hg_s0)
    x = x + g1 * m
    h = rmsnorm(x, p['norm2_g']) * (1.0 + sc2) + sh2
    x = x + g2 * conv_ffn(h, p, grid)
    return x, fin


def setup_inputs(seed: int = 0) -> dict:
    key = jax.random.key(seed)
    ks = list(jax.random.split(key, 40))
    nrm = lambda i, shape, s: jax.random.normal(ks[i], shape, jnp.float32) * s
    n_idx = jnp.arange(S5_N, dtype=jnp.float32)
    return {
        'x_prompt': nrm(0, (BATCH, SEQ, D_MODEL), 1.0),
        'x_sample': nrm(1, (DEC_BATCH, DEC_SEQ, D_MODEL), 1.0),
        'state_s5_re': nrm(2, (DEC_BATCH, DEPTH, N_DIR, S5_G, S5_N), 0.5),
        'state_s5_im': nrm(3, (DEC_BATCH, DEPTH, N_DIR, S5_G, S5_N), 0.5),
        'state_hgrn': nrm(4, (DEC_BATCH, DEPTH, N_DIR, HG_HEADS, HG_DK, HG_DV), 0.5),
        'c': nrm(5, (DEC_BATCH, D_MODEL), 1.0),
        'c_ctx': nrm(6, (D_MODEL,), 1.0),
        'w_ada': nrm(7, (DEPTH, D_MODEL, N_MOD * D_MODEL), 0.5 / math.sqrt(D_MODEL)),
        'b_ada': nrm(8, (DEPTH, N_MOD * D_MODEL), 0.02),
        'norm1_g': 1.0 + nrm(9, (DEPTH, D_MODEL), 0.02),
        'norm2_g': 1.0 + nrm(10, (DEPTH, D_MODEL), 0.02),
        'w_in': nrm(11, (DEPTH, D_MODEL, D_IN), 1.0 / math.sqrt(D_MODEL)),
        's5_lam_re': -0.5 + nrm(12, (DEPTH, N_DIR, S5_G, S5_N), 0.02),
        's5_lam_im': math.pi * n_idx + nrm(13, (DEPTH, N_DIR, S5_G, S5_N), 0.02),
        's5_log_dt': jax.random.uniform(ks[14], (DEPTH, N_DIR, S5_G), jnp.float32, math.log(1e-3), math.log(1e-1)),
        's5_b_re': nrm(15, (DEPTH, N_DIR, S5_G, S5_N, S5_P), 1.0 / math.sqrt(2 * S5_P)),
        's5_b_im': nrm(16, (DEPTH, N_DIR, S5_G, S5_N, S5_P), 1.0 / math.sqrt(2 * S5_P)),
        's5_c_re': nrm(17, (DEPTH, N_DIR, S5_G, S5_P, S5_N), 1.0 / math.sqrt(2 * S5_N)),
        's5_c_im': nrm(18, (DEPTH, N_DIR, S5_G, S5_P, S5_N), 1.0 / math.sqrt(2 * S5_N)),
        's5_d': nrm(19, (DEPTH, GROUP_W), 1.0),
        's5_w_glu': nrm(20, (DEPTH, GROUP_W, GROUP_W), 1.0 / math.sqrt(GROUP_W)),
        'hg_lb_logits': nrm(21, (DEPTH, N_DIR, GROUP_W), 0.5),
        'fn_w': nrm(22, (DEPTH, GROUP_W, GROUP_W), 1.0 / math.sqrt(GROUP_W)),
        'gm_norm_g': 1.0 + nrm(23, (DEPTH, GROUP_W), 0.02),
        'gm_ws': nrm(24, (DEPTH, GM_HEADS, GM_CHUNK, GM_CHUNK), 1.0 / math.sqrt(GM_CHUNK)),
        'gm_bs': 1.0 + nrm(25, (DEPTH, GM_HEADS, GM_CHUNK), 0.02),
        'grp_norm_g': 1.0 + nrm(26, (DEPTH, D_MODEL), 0.02),
        'w_out': nrm(27, (DEPTH, D_MODEL, D_MODEL), 1.0 / math.sqrt(D_MODEL)),
        'ffn_w_up': nrm(28, (DEPTH, D_MODEL, 2 * D_FF), 1.0 / math.sqrt(D_MODEL)),
        'ffn_conv_w': nrm(29, (DEPTH, CONV_W, 2 * D_FF), 1.0 / math.sqrt(CONV_W)),
        'ffn_conv_b': nrm(30, (DEPTH, 2 * D_FF), 0.02),
        'ffn_w_down': nrm(31, (DEPTH, D_FF, D_MODEL), 1.0 / math.sqrt(D_FF)),
        'final_norm_g': 1.0 + nrm(32, (D_MODEL,), 0.02),
    }


def reference(x_prompt, x_sample, state_s5_re, state_s5_im, state_hgrn, c, c_ctx,
              w_ada, b_ada, norm1_g, norm2_g, w_in, s5_lam_re, s5_lam_im, s5_log_dt,
              s5_b_re, s5_b_im, s5_c_re, s5_c_im, s5_d, s5_w_glu, hg_lb_logits, fn_w,
              gm_norm_g, gm_ws, gm_bs, grp_norm_g, w_out, ffn_w_up, ffn_conv_w, ffn_conv_b,
              ffn_w_down, final_norm_g):
    lb_p = jax.nn.softmax(hg_lb_logits.astype(jnp.float32), axis=0)
    lbs = jnp.maximum(jnp.cumsum(lb_p, axis=0) - lb_p[0], 0.0)
    bp = x_prompt.shape[0]
    z_s5 = jnp.zeros((bp, N_DIR, S5_G, S5_N), jnp.float32)
    z_hg = jnp.zeros((bp, N_DIR, HG_HEADS, HG_DK, HG_DV), jnp.float32)
    xp, xs = x_prompt, x_sample
    new_re, new_im, new_hg = [], [], []
    for l in range(DEPTH):
        p = {'w_in': w_in[l], 'norm1_g': norm1_g[l], 'norm2_g': norm2_g[l],
             's5_lam_re': s5_lam_re[l], 's5_lam_im': s5_lam_im[l], 's5_log_dt': s5_log_dt[l],
             's5_b_re': s5_b_re[l], 's5_b_im': s5_b_im[l], 's5_c_re': s5_c_re[l], 's5_c_im': s5_c_im[l],
             's5_d': s5_d[l], 's5_w_glu': s5_w_glu[l], 'fn_w': fn_w[l], 'gm_norm_g': gm_norm_g[l],
             'gm_ws': gm_ws[l], 'gm_bs': gm_bs[l], 'grp_norm_g': grp_norm_g[l], 'w_out': w_out[l],
             'ffn_w_up': ffn_w_up[l], 'ffn_conv_w': ffn_conv_w[l], 'ffn_conv_b': ffn_conv_b[l],
             'ffn_w_down': ffn_w_down[l]}
        mod_ctx = (jax.nn.silu(c_ctx) @ w_ada[l] + b_ada[l])[None, None, :]
        xp, (fr, fi, fh) = trunk_layer(xp, mod_ctx, p, lbs[l], z_s5, z_s5, z_hg, False)
        new_re.append(fr)
        new_im.append(fi)
        new_hg.append(fh)
        mod_lat = (jax.nn.silu(c) @ w_ada[l] + b_ada[l])[:, None, :]
        xs, _ = trunk_layer(xs, mod_lat, p, lbs[l], state_s5_re[:, l], state_s5_im[:, l], state_hgrn[:, l], True)
    y_prompt = rmsnorm(xp, final_norm_g)
    y_sample = rmsnorm(xs, final_norm_g)
    new_state_s5_re = jnp.stack(new_re, axis=1)
    new_state_s5_im = jnp.stack(new_im, axis=1)
    new_state_hgrn = jnp.stack(new_hg, axis=1)
    return (y_prompt, y_sample, new_state_s5_re, new_state_s5_im, new_state_hgrn)
```

```python
import numpy as np
import ml_dtypes
from contextlib import ExitStack
import concourse.bass as bass
import concourse.mybir as mybir
from concourse.bass_utils import run_bass_kernel_spmd

F32 = mybir.dt.float32
BF16 = mybir.dt.bfloat16
I32 = mybir.dt.int32
AF = mybir.ActivationFunctionType
ALU = mybir.AluOpType

D = 1024
DEPTH = 2
DIN = 2304
DFF = 2816
EPS = 1e-6
NCORES = 8
STAGE = 5


class Buf:
    __slots__ = ("name", "w", "r", "dsem", "dcnt", "fence", "fdone")
    current_fence = None

    def __init__(self, name=""):
        self.name = name
        self.w = None
        self.r = []
        self.dsem = None
        self.dcnt = 0
        self.fence = Buf.current_fence
        self.fdone = set()


class _Rec:
    def __init__(self):
        self.call = None

    def __getattr__(self, name):
        def f(*a, **k):
            self.call = (name, a, k)
            return self
        return f


def _bind(fn):
    r = _Rec()
    fn(r)
    assert r.call is not None
    return r.call


class Sched:
    NAMES = ["pe", "act", "dve", "pool", "sp"]

    def __init__(self, nc, es):
        self.nc = nc
        self.es = es
        self.q = {e: [] for e in self.NAMES}
        self.cnt = {e: 0 for e in self.NAMES}
        self.esem = {e: es.enter_context(nc.semaphore("sem_" + e)) for e in self.NAMES}
        self.waited = {e: {} for e in self.NAMES}
        self.out_events = []
        self.ndsem = 0
        self.dma_sems = []
        self.free_slots = []
        self.all_slots = []
        self.active = []

    def _wait(self, eng, ev):
        src, sem, val = ev
        k = id(sem)
        if self.waited[eng].get(k, 0) >= val:
            return
        self.waited[eng][k] = val
        self.q[eng].append(("wait", sem, val))

    def _collect(self, eng, reads, writes):
        for b in list(reads) + list(writes):
            if b.fence is not None and eng not in b.fdone:
                b.fdone.add(eng)
                for ev in b.fence:
                    if ev[0] != eng:
                        self._wait(eng, ev)
        for b in reads:
            if b.w is not None:
                if b.w[0] == eng and eng == "pe":
                    continue
                self._wait(eng, b.w)
        for b in writes:
            if b.w is not None:
                if not (b.w[0] == eng):
                    self._wait(eng, b.w)
            for ev in b.r:
                if ev[0] == eng:
                    continue
                self._wait(eng, ev)

    def op(self, eng, fn, reads=(), writes=()):
        self._collect(eng, reads, writes)
        self.cnt[eng] += 1
        ev = (eng, self.esem[eng], self.cnt[eng])
        self.q[eng].append(("op", _bind(fn), self.esem[eng], 1))
        for b in writes:
            b.w = ev
            b.r = []
        for b in reads:
            b.r.append(ev)
        return ev

    def dma(self, fn, reads=(), writes=(), sembuf=None, eng="sp", is_out=False):
        self._collect(eng, reads, writes)
        sb = sembuf if sembuf is not None else (writes[0] if writes else reads[0])
        if sb.dsem is None:
            if self.free_slots:
                slot = self.free_slots.pop()
                if slot[1] > 0:
                    self._wait(eng, ("dma", slot[0], slot[1]))
            else:
                slot = [self.es.enter_context(self.nc.semaphore("dsem%d" % self.ndsem)), 0]
                self.ndsem += 1
                self.all_slots.append(slot)
            sb.dsem = slot
            self.active.append(sb)
        slot = sb.dsem
        slot[1] += 16
        ev = ("dma", slot[0], slot[1])
        self.q[eng].append(("op", _bind(fn), slot[0], 16))
        for b in writes:
            b.w = ev
            b.r = []
        for b in reads:
            b.r.append(ev)
        return ev

    def barrier(self, force=False):
        evs = []
        for o in self.NAMES:
            if self.cnt[o] > 0:
                evs.append((o, self.esem[o], self.cnt[o]))
        for slot in self.all_slots:
            if slot[1] > 0:
                evs.append(("dma", slot[0], slot[1]))
        Buf.current_fence = evs
        if force:
            for e in self.NAMES:
                for ev in evs:
                    if ev[0] != e:
                        self._wait(e, ev)
        for sb in self.active:
            self.free_slots.append(sb.dsem)
            sb.dsem = None
        self.active = []

    def finish(self):
        self.barrier(force=True)

    def emit(self):
        nc = self.nc
        engmap = {"pe": "tensor", "act": "scalar", "dve": "vector", "pool": "gpsimd", "sp": "sync"}
        with nc.Block() as block:
            for name in self.NAMES:
                items = self.q[name]

                def body(e, items=items):
                    for it in items:
                        if it[0] == "wait":
                            e.wait_ge(it[1], it[2])
                        else:
                            nm_, a_, k_ = it[1]
                            ins = getattr(e, nm_)(*a_, **k_)
                            ins.then_inc(it[2], it[3])
                getattr(block, engmap[name])(body)


class KB:
    def __init__(self, nc, es):
        self.nc = nc
        self.es = es
        self.S = Sched(nc, es)
        self.din = {}
        self.dout = {}
        self.nps = 0
        self.psb = []
        for i in range(8):
            t = es.enter_context(nc.psum_tensor("psb%d" % i, [128, 512], F32))
            self.psb.append((t, Buf("ps%d" % i)))
        self.uid = 0

    def inp(self, name, shape, dtype=F32):
        self.din[name] = self.nc.dram_tensor(name, list(shape), dtype, kind="ExternalInput").ap()
        return self.din[name]

    def outp(self, name, shape, dtype=F32):
        self.dout[name] = self.nc.dram_tensor(name, list(shape), dtype, kind="ExternalOutput").ap()
        return self.dout[name]

    def sb(self, es, name, shape, dtype=F32):
        self.uid += 1
        return es.enter_context(self.nc.sbuf_tensor("%s_%d" % (name, self.uid), list(shape), dtype))

    def ps(self):
        t, b = self.psb[self.nps % 8]
        self.nps += 1
        return t, b

    def dbg(self, name, ap, shape, dtype, r=()):
        if not getattr(self, "debug", False):
            return
        d = self.nc.dram_tensor("dbg_" + name, list(shape), dtype, kind="ExternalOutput").ap()
        self.dbgs = getattr(self, "dbgs", []) + ["dbg_" + name]
        bb = Buf("dbg")
        self.S.dma(lambda e: e.dma_start(out=d, in_=ap), list(r), [bb])

    def pe(self, fn, r=(), w=()):
        return self.S.op("pe", fn, r, w)

    def act(self, fn, r=(), w=()):
        return self.S.op("act", fn, r, w)

    def dve(self, fn, r=(), w=()):
        return self.S.op("dve", fn, r, w)

    def pool(self, fn, r=(), w=()):
        return self.S.op("pool", fn, r, w)

    def dma(self, out, in_, r=(), w=(), eng="sp", is_out=False, sembuf=None, slow=False):
        if slow:
            fn = lambda e: e.dma_start(out=out, in_=in_, allow_slow_non_contiguous=True)
        else:
            fn = lambda e: e.dma_start(out=out, in_=in_)
        return self.S.dma(fn, r, w, sembuf=sembuf, eng=eng, is_out=is_out)


def build_program(stage=STAGE):
    nc = bass.Bass("TRN2", target_bir_lowering=False)
    Buf.current_fence = None
    with ExitStack() as es:
        kb = KB(nc, es)
        kb.debug = (stage >= 40)
        _DBG['kb'] = kb
        S = kb.S
        xs_d = kb.inp("xs", [2048, D])
        xp_d = kb.inp("xp", [512, D])
        cv_d = kb.inp("cvec", [2, D])
        w_ada = kb.inp("w_ada", [DEPTH, D, 6 * D])
        b_ada = kb.inp("b_ada", [DEPTH, 6 * D])
        norm1_g = kb.inp("norm1_g", [DEPTH, D])
        norm2_g = kb.inp("norm2_g", [DEPTH, D])
        w_in = kb.inp("w_in", [DEPTH, D, DIN])
        w_out = kb.inp("w_out", [DEPTH, D, D])
        w_up = kb.inp("ffn_w_up", [DEPTH, D, 2 * DFF])
        cw_d = kb.inp("ffn_conv_w", [DEPTH, 3, 2 * DFF])
        cb_d = kb.inp("ffn_conv_b", [DEPTH, 2 * DFF])
        w_dn = kb.inp("ffn_w_down", [DEPTH, DFF, D])
        fng = kb.inp("final_norm_g", [D])
        ident_d = kb.inp("ident", [128, 128])
        grp_g = kb.inp("grp_norm_g", [DEPTH, D])
        gm_ng = kb.inp("gm_norm_g", [DEPTH, 256])
        gm_ws = kb.inp("gm_ws", [DEPTH, 4, 128, 128])
        gm_bs = kb.inp("gm_bs", [DEPTH, 4, 128])
        fn_w = kb.inp("fn_w", [DEPTH, 256, 256])
        bdcs_d = kb.inp("bdcs", [256, 512], BF16)
        hmask_d = kb.inp("hmask", [3, 128, 128])
        lamre_d = kb.inp("s5_lam_re", [DEPTH, 2, 16, 64])
        lamim_d = kb.inp("s5_lam_im", [DEPTH, 2, 16, 64])
        logdt_d = kb.inp("s5_log_dt", [DEPTH, 2, 16])
        s5b_re_d = kb.inp("s5_b_re", [DEPTH, 2, 16, 64, 16])
        s5b_im_d = kb.inp("s5_b_im", [DEPTH, 2, 16, 64, 16])
        s5c_re_d = kb.inp("s5_c_re", [DEPTH, 2, 16, 16, 64])
        s5c_im_d = kb.inp("s5_c_im", [DEPTH, 2, 16, 16, 64])
        s5d_d = kb.inp("s5_d", [DEPTH, 256])
        wglu_d = kb.inp("s5_w_glu", [DEPTH, 256, 256])
        s5mask_d = kb.inp("s5mask", [2, 128, 128])
        s5st_re_d = kb.inp("s5st_re", [DEPTH, 2, 16, 64])
        s5st_im_d = kb.inp("s5st_im", [DEPTH, 2, 16, 64])
        s5o_re = kb.outp("s5o_re", [2, DEPTH, 2, 16, 64])
        s5o_im = kb.outp("s5o_im", [2, DEPTH, 2, 16, 64])
        smask_d = kb.inp("smask", [128, 2048], BF16)
        cmask_d = kb.inp("cmask", [128, 4])
        hglog = kb.inp("hg_lb_logits", [DEPTH, 2, 256])
        hgst_d = kb.inp("hg_state", [DEPTH, 2, 4, 64, 64])
        sthg_o = kb.outp("st_hg", [2, DEPTH, 2, 4, 64, 64])
        dft_s = kb.inp("dft_s", [2, 2048, 2048], BF16)
        dft_p = kb.inp("dft_p", [2, 256, 256], BF16)
        ys_d = kb.outp("ys", [2048, D])
        yp_d = kb.outp("yp", [512, D])

        ident = kb.sb(es, "ident", [128, 128]); b_ident = Buf("ident")
        ones_bf = kb.sb(es, "ones", [128, 128], BF16); b_ones = Buf("ones")
        modT = kb.sb(es, "modT", [128, DEPTH, 2, 48]); b_mod = Buf("modT")
        n1g = kb.sb(es, "n1g", [128, DEPTH, 8]); n2g = kb.sb(es, "n2g", [128, DEPTH, 8])
        b_cv = Buf("chanvecs")
        gnT = kb.sb(es, "gnT", [128, DEPTH, 8])
        lbT = kb.sb(es, "lbT", [128, DEPTH, 2, 2]); omlT = kb.sb(es, "omlT", [128, DEPTH, 2, 2]); nomlT = kb.sb(es, "nomlT", [128, DEPTH, 2, 2])
        b_lb = Buf("lb")
        identb = kb.sb(es, "identb", [128, 128], BF16)
        cwT = kb.sb(es, "cwT", [128, DEPTH, 3, 44]); cbT = kb.sb(es, "cbT", [128, DEPTH, 44])
        Amod = kb.sb(es, "Amod", [128, DEPTH, 2, 2, 8])
        b_A = Buf("Amod")
        NWS = 3
        wbf = [kb.sb(es, "wbf%d" % i, [128, 2816], BF16) for i in range(NWS)]
        b_wbf = [Buf("wbf%d" % i) for i in range(NWS)]
        b_wbf2 = [Buf("wbfb%d" % i) for i in range(NWS)]
        wctr = [0]

        kb.dma(ident[:], ident_d, w=[b_ident])
        kb.dve(lambda e: e.memset(ones_bf[:], 1.0), w=[b_ones])
        kb.dve(lambda e: e.tensor_copy(out=identb[:], in_=ident[:]), r=[b_ident], w=[b_ident])

        def wload(src_ap, kt, ncol):
            s = wctr[0] % NWS
            wctr[0] += 1
            n = kt * ncol
            dv = wbf[s][:, 0:n].rearrange("p (k c) -> p k c", k=kt)
            kb.dma(dv, src_ap, w=[b_wbf[s], b_wbf2[s]], eng="pool")
            return dv, b_wbf[s]

        def load_chanvec(dst_ap, src_rows_ap, nt, tmp_es, extra_r=(), wbuf=None):
            st = kb.sb(tmp_es, "cvst", [nt, 128]); bst = Buf("cvst")
            kb.dma(st[:], src_rows_ap, w=[bst])
            pt, pb = kb.ps()
            kb.pe(lambda e: e.transpose(out=pt[:, 0:nt], in_=st[:], identity=ident[0:nt, 0:nt]), r=[bst, b_ident], w=[pb])
            kb.dve(lambda e: e.tensor_copy(out=dst_ap, in_=pt[:, 0:nt]), r=[pb], w=[wbuf])

        with ExitStack() as pes:
            for l in range(DEPTH):
                load_chanvec(n1g[:, l, :], norm1_g[l].rearrange("(m p) -> m p", p=128), 8, pes, wbuf=b_cv)
                load_chanvec(n2g[:, l, :], norm2_g[l].rearrange("(m p) -> m p", p=128), 8, pes, wbuf=b_cv)
                load_chanvec(gnT[:, l, :], grp_g[l].rearrange("(m p) -> m p", p=128), 8, pes, wbuf=b_cv)
                for j in range(3):
                    load_chanvec(cwT[:, l, j, :], cw_d[l, j].rearrange("(m p) -> m p", p=128), 44, pes, wbuf=b_cv)
                load_chanvec(cbT[:, l, :], cb_d[l].rearrange("(m p) -> m p", p=128), 44, pes, wbuf=b_cv)
            craw = kb.sb(pes, "craw", [128, 2, 8]); b_craw = Buf("craw")
            for cv in range(2):
                load_chanvec(craw[:, cv, :], cv_d[cv].rearrange("(m p) -> m p", p=128), 8, pes, wbuf=b_craw)
            sc = kb.sb(pes, "sc", [128, 8, 2], BF16); b_sc = Buf("sc")
            for cv in range(2):
                kb.act(lambda e, cv=cv: e.activation(out=sc[:, :, cv], in_=craw[:, cv, :], func=AF.Silu), r=[b_craw], w=[b_sc])
            lgT = kb.sb(pes, "lgT", [128, DEPTH, 2, 2]); b_lg = Buf("lgT")
            for l in range(DEPTH):
                for d in range(2):
                    load_chanvec(lgT[:, l, d, :], hglog[l, d].rearrange("(m p) -> m p", p=128), 2, pes, wbuf=b_lg)
            kb.dve(lambda e: e.memset(lbT[:, 0, :, :], 0.0), w=[b_lb])
            kb.dve(lambda e: e.tensor_tensor(out=lbT[:, 1, :, :], in0=lgT[:, 1, :, :], in1=lgT[:, 0, :, :], op=ALU.subtract), r=[b_lg], w=[b_lb])
            kb.act(lambda e: e.activation(out=lbT[:, 1, :, :], in_=lbT[:, 1, :, :], func=AF.Sigmoid), r=[b_lb], w=[b_lb])
            kb.dve(lambda e: e.tensor_scalar(out=omlT[:], in0=lbT[:], scalar1=-1.0, scalar2=1.0, op0=ALU.mult, op1=ALU.add), r=[b_lb], w=[b_lb])
            kb.dve(lambda e: e.tensor_scalar(out=nomlT[:], in0=lbT[:], scalar1=1.0, scalar2=-1.0, op0=ALU.mult, op1=ALU.add), r=[b_lb], w=[b_lb])
            badaT = kb.sb(pes, "badaT", [128, DEPTH, 48]); b_bada = Buf("bada")
            for l in range(DEPTH):
                load_chanvec(badaT[:, l, :], b_ada[l].rearrange("(m p) -> m p", p=128), 48, pes, wbuf=b_bada)
            ast = [kb.sb(pes, "ast%d" % i, [128, 8, 768], BF16) for i in range(2)]
            b_ast = [Buf("ast0"), Buf("ast1")]
            for l in range(DEPTH):
                pt, pb = kb.ps()
                for ch in range(8):
                    s = ch % 2
                    kb.dma(ast[s][:], w_ada[l].rearrange("(k p) c -> p k c", p=128)[:, :, ch * 768:(ch + 1) * 768], w=[b_ast[s]], eng="pool")
                    for m in range(6):
                        mt = ch * 6 + m
                        for k in range(8):
                            kb.pe(lambda e, s=s, m=m, k=k, mt=mt, pt=pt: e.matmul(pt[:, mt * 2:mt * 2 + 2], lhsT=ast[s][:, k, m * 128:(m + 1) * 128],
                                                                  rhs=sc[:, k, :], start=(k == 0), stop=(k == 7)),
                                  r=[b_ast[s], b_sc], w=[pb])
                for cv in range(2):
                    kb.dve(lambda e, l=l, cv=cv, pt=pt: e.tensor_tensor(out=modT[:, l, cv, :], in0=pt[:, 0:96].rearrange("p (m c) -> p m c", c=2)[:, :, cv],
                                                                  in1=badaT[:, l, :], op=ALU.add), r=[pb, b_bada], w=[b_mod])
            for l in range(DEPTH):
                for cv in range(2):
                    for n, (gt, chunk) in enumerate(((n1g, 1), (n2g, 4))):
                        kb.dve(lambda e, l=l, cv=cv, n=n, gt=gt, chunk=chunk: e.scalar_tensor_tensor(
                            out=Amod[:, l, cv, n, :], in0=modT[:, l, cv, chunk * 8:(chunk + 1) * 8], scalar=1.0, in1=gt[:, l, :],
                            op0=ALU.add, op1=ALU.mult), r=[b_mod, b_cv], w=[b_A])
            S.barrier()

        s5cache = {}

        def run_group(gname, x_dram, y_dram, NT, T, cv, grid):
            is_sample = grid
            NB = NT // 512
            with ExitStack() as ges:
                x = kb.sb(ges, "x", [128, 8, NT]); b_x = Buf("x")
                h = kb.sb(ges, "h", [128, 8, NT], BF16); b_h = Buf("h")
                with ExitStack() as tes:
                    xin = [kb.sb(tes, "xin%d" % i, [128, D]) for i in range(2)]
                    b_xin = [Buf("xin0"), Buf("xin1")]
                    for tt in range(NT // 128):
                        s = tt % 2
                        kb.dma(xin[s][:], x_dram[tt * 128:(tt + 1) * 128, :], w=[b_xin[s]])
                        for kq in range(2):
                            pt, pb = kb.ps()
                            for kk in range(4):
                                k = kq * 4 + kk
                                kb.pe(lambda e, s=s, k=k, kk=kk, pt=pt: e.transpose(out=pt[:, kk * 128:(kk + 1) * 128], in_=xin[s][:, k * 128:(k + 1) * 128], identity=ident[:]),
                                      r=[b_xin[s], b_ident], w=[pb])
                            kb.act(lambda e, kq=kq, tt=tt, pt=pt: e.activation(out=x[:, kq * 4:(kq + 1) * 4, tt * 128:(tt + 1) * 128],
                                                                            in_=pt[:].rearrange("p (k t) -> p k t", k=4), func=AF.Copy),
                                   r=[pb], w=[b_x])
                    S.barrier()

                def norm_to_h(l, n):
                    shchunk = 0 if n == 0 else 3
                    with ExitStack() as nes:
                        sq = kb.sb(nes, "sq", [128, 8, 512], BF16); b_sq = Buf("sq")
                        rs = kb.sb(nes, "rs", [128, 512]); b_rs = Buf("rs")
                        tmp = [kb.sb(nes, "ntmp%d" % i, [128, 512]) for i in range(2)]
                        b_tmp = [Buf("nt0"), Buf("nt1")]
                        for blk in range(NB):
                            c0 = blk * 512
                            kb.act(lambda e, c0=c0: e.activation(out=sq[:], in_=x[:, :, c0:c0 + 512], func=AF.Square), r=[b_x], w=[b_sq])
                            pt, pb = kb.ps()
                            for k in range(8):
                                kb.pe(lambda e, k=k, pt=pt: e.matmul(pt[:], lhsT=ones_bf[:], rhs=sq[:, k, :], start=(k == 0), stop=(k == 7)),
                                      r=[b_sq, b_ones], w=[pb])
                            kb.act(lambda e, pt=pt: e.activation(out=rs[:], in_=pt[:], func=AF.Sqrt, scale=1.0 / D, bias=EPS), r=[pb], w=[b_rs])
                            kb.dve(lambda e: e.reciprocal(out=rs[:], in_=rs[:]), r=[b_rs], w=[b_rs])
                            for k in range(8):
                                s = k % 2
                                kb.dve(lambda e, k=k, s=s, c0=c0: e.scalar_tensor_tensor(out=tmp[s][:], in0=x[:, k, c0:c0 + 512], scalar=Amod[:, l, cv, n, k:k + 1],
                                                                                   in1=rs[:], op0=ALU.mult, op1=ALU.mult), r=[b_x, b_rs, b_A], w=[b_tmp[s]])
                                kb.act(lambda e, k=k, s=s, c0=c0: e.activation(out=h[:, k, c0:c0 + 512], in_=tmp[s][:], func=AF.Identity,
                                                                        bias=modT[:, l, cv, shchunk * 8 + k:shchunk * 8 + k + 1], scale=1.0),
                                       r=[b_tmp[s], b_mod], w=[b_h])
                        S.barrier()

                def ffn(l):
                    gchunk = 5
                    TB = min(NT, 1024)
                    with ExitStack() as fes:
                        gated = kb.sb(fes, "gated", [128, 22, TB], BF16); b_gated = Buf("gated")
                        zc = [[kb.sb(fes, "zc%d%d" % (i, j), [128, 512]) for j in range(2)] for i in range(2)]
                        b_zc = [[Buf("zc"), Buf("zc")] for i in range(2)]
                        u = [kb.sb(fes, "u%d" % i, [128, 512]) for i in range(2)]
                        b_u = [Buf("u0"), Buf("u1")]
                        sg = [kb.sb(fes, "sg%d" % i, [128, 512]) for i in range(2)]
                        b_sg = [Buf("sg0"), Buf("sg1")]
                        it = [0]
                        rowlen = 64 if grid else T
                        for tb in range(NT // TB):
                            t0 = tb * TB
                            for m in range(22):
                                s = wctr[0] % NWS
                                wctr[0] += 1
                                wsrc = w_up[l].rearrange("(k p) c -> p k c", p=128)
                                kb.dma(wbf[s][:, 0:1024].rearrange("p (k c) -> p k c", k=8), wsrc[:, :, m * 128:(m + 1) * 128], w=[b_wbf[s]], eng="pool")
                                kb.dma(wbf[s][:, 1024:2048].rearrange("p (k c) -> p k c", k=8), wsrc[:, :, DFF + m * 128:DFF + (m + 1) * 128], w=[b_wbf2[s]], eng="pool")
                                wv = wbf[s][:, 0:2048].rearrange("p (ab k c) -> p ab k c", ab=2, k=8)
                                for blk in range(TB // 512):
                                    c0 = t0 + blk * 512
                                    i2 = it[0] % 2
                                    it[0] += 1
                                    pts = []
                                    for ab in range(2):
                                        pt, pb = kb.ps()
                                        pts.append((pt, pb))
                                        for k in range(8):
                                            kb.pe(lambda e, k=k, ab=ab, pt=pt, wv=wv, c0=c0: e.matmul(pt[:], lhsT=wv[:, ab, k, :], rhs=h[:, k, c0:c0 + 512],
                                                                                            start=(k == 0), stop=(k == 7)), r=[b_wbf[s], b_wbf2[s], b_h], w=[pb])
                                    for ab in range(2):
                                        pt, pb = pts[ab]
                                        ct = ab * 22 + m
                                        z = zc[i2][ab]
                                        bz = b_zc[i2][ab]
                                        kb.act(lambda e, z=z, pt=pt, ct=ct: e.activation(out=z[:], in_=pt[:], func=AF.Identity, scale=cwT[:, l, 1, ct:ct + 1], bias=cbT[:, l, ct:ct + 1]),
                                               r=[pb, b_cv], w=[bz])
                                        zr = z[:].rearrange("p (r c) -> p r c", c=rowlen)
                                        pr = pt[:].rearrange("p (r c) -> p r c", c=rowlen)
                                        kb.dve(lambda e, zr=zr, pr=pr, ct=ct: e.scalar_tensor_tensor(out=zr[:, :, 1:rowlen], in0=pr[:, :, 0:rowlen - 1], scalar=cwT[:, l, 0, ct:ct + 1],
                                                                                             in1=zr[:, :, 1:rowlen], op0=ALU.mult, op1=ALU.add), r=[pb, bz, b_cv], w=[bz])
                                        kb.dve(lambda e, zr=zr, pr=pr, ct=ct: e.scalar_tensor_tensor(out=zr[:, :, 0:rowlen - 1], in0=pr[:, :, 1:rowlen], scalar=cwT[:, l, 2, ct:ct + 1],
                                                                                             in1=zr[:, :, 0:rowlen - 1], op0=ALU.mult, op1=ALU.add), r=[pb, bz, b_cv], w=[bz])
                                    za, zb = zc[i2]
                                    bza, bzb = b_zc[i2]
                                    uu, bu = u[i2], b_u[i2]
                                    ss, bs = sg[i2], b_sg[i2]
                                    kb.act(lambda e, uu=uu, za=za: e.activation(out=uu[:], in_=za[:], func=AF.Square, scale=0.21145921592590945), r=[bza], w=[bu])
                                    kb.dve(lambda e, uu=uu, za=za: e.scalar_tensor_tensor(out=uu[:], in0=uu[:], scalar=1.0, in1=za[:], op0=ALU.add, op1=ALU.mult), r=[bu, bza], w=[bu])
                                    kb.act(lambda e, uu=uu, ss=ss: e.activation(out=ss[:], in_=uu[:], func=AF.Sigmoid, scale=1.5957691216057308), r=[bu], w=[bs])
                                    kb.pool(lambda e, ss=ss, za=za: e.tensor_tensor(out=ss[:], in0=ss[:], in1=za[:], op=ALU.mult), r=[bs, bza], w=[bs])
                                    kb.pool(lambda e, ss=ss, zb=zb, m=m, c0=c0, t0=t0: e.tensor_tensor(out=gated[:, m, c0 - t0:c0 - t0 + 512], in0=ss[:], in1=zb[:], op=ALU.mult),
                                            r=[bs, bzb], w=[b_gated])
                            for dm in range(8):
                                wv, bw = wload(w_dn[l].rearrange("(k p) c -> p k c", p=128)[:, :, dm * 128:(dm + 1) * 128], 22, 128)
                                for blk in range(TB // 512):
                                    c0 = t0 + blk * 512
                                    pt, pb = kb.ps()
                                    for k in range(22):
                                        kb.pe(lambda e, k=k, pt=pt, wv=wv, c0=c0, t0=t0: e.matmul(pt[:], lhsT=wv[:, k, :], rhs=gated[:, k, c0 - t0:c0 - t0 + 512], start=(k == 0), stop=(k == 21)),
                                              r=[bw, b_gated], w=[pb])
                                    kb.dve(lambda e, dm=dm, pt=pt, c0=c0: e.scalar_tensor_tensor(out=x[:, dm, c0:c0 + 512], in0=pt[:], scalar=modT[:, l, cv, gchunk * 8 + dm:gchunk * 8 + dm + 1],
                                                                                        in1=x[:, dm, c0:c0 + 512], op0=ALU.mult, op1=ALU.add), r=[pb, b_mod, b_x], w=[b_x])
                        S.barrier()

                NTT = NT // 128
                nseq = NT // T

                def gelu_psum(src, b_src, out_ap, b_out, tA, b_tA, tB, b_tB):
                    kb.act(lambda e: e.activation(out=tA, in_=src, func=AF.Square, scale=0.21145921592590945), r=[b_src], w=[b_tA])
                    kb.dve(lambda e: e.scalar_tensor_tensor(out=tA, in0=tA, scalar=1.0, in1=src, op0=ALU.add, op1=ALU.mult), r=[b_tA, b_src], w=[b_tA])
                    kb.act(lambda e: e.activation(out=tB, in_=tA, func=AF.Sigmoid, scale=1.5957691216057308), r=[b_tA], w=[b_tB])
                    kb.dve(lambda e: e.tensor_tensor(out=out_ap, in0=tB, in1=src, op=ALU.mult), r=[b_tB, b_src], w=[b_out])

                def proj_fm(l, c0col, ntile, cb):
                    wsrc = w_in[l].rearrange("(k p) c -> p k c", p=128)
                    for g0 in range(0, ntile, 2):
                        nt2 = min(2, ntile - g0)
                        wv, bw = wload(wsrc[:, :, c0col + g0 * 128:c0col + (g0 + nt2) * 128], 8, nt2 * 128)
                        for ti in range(nt2):
                            for blk in range(NB):
                                pt, pb = kb.ps()
                                for k in range(8):
                                    kb.pe(lambda e, k=k, pt=pt, wv=wv, ti=ti, blk=blk: e.matmul(pt[:], lhsT=wv[:, k, ti * 128:(ti + 1) * 128], rhs=h[:, k, blk * 512:(blk + 1) * 512],
                                                                                      start=(k == 0), stop=(k == 7)), r=[bw, b_h], w=[pb])
                                cb(g0 + ti, blk, pt, pb)

                def proj_tok(l, c0col, ncol, cb):
                    wsrc = w_in[l].rearrange("(k p) c -> p k c", p=128)
                    wv, bw = wload(wsrc[:, :, c0col:c0col + ncol], 8, ncol)
                    for tt in range(NTT):
                        pt, pb = kb.ps()
                        for k in range(8):
                            kb.pe(lambda e, k=k, pt=pt, wv=wv, tt=tt: e.matmul(pt[:, 0:ncol], lhsT=h[:, k, tt * 128:(tt + 1) * 128], rhs=wv[:, k, :],
                                                                       start=(k == 0), stop=(k == 7)), r=[bw, b_h], w=[pb])
                        cb(tt, pt, pb)

                def fm_norm(src, b_src, gain, dst, b_dst, nch, ones_t, per_tile, nes):
                    sq = kb.sb(nes, "fsq", [128, 2, 512], BF16); b_sq = Buf("fsq")
                    rs = kb.sb(nes, "frs", [128, 2, 512]); b_rs = Buf("frs")
                    for blk in range(NB):
                        c0 = blk * 512
                        kb.act(lambda e, c0=c0: e.activation(out=sq[:], in_=src[:, :, c0:c0 + 512], func=AF.Square), r=[b_src], w=[b_sq])
                        for ct in range(2 if per_tile else 1):
                            pt, pb = kb.ps()
                            if per_tile:
                                kb.pe(lambda e, pt=pt, ct=ct: e.matmul(pt[:], lhsT=ones_t[:], rhs=sq[:, ct, :], start=True, stop=True), r=[b_sq, b_ones], w=[pb])
                            else:
                                for k in range(2):
                                    kb.pe(lambda e, pt=pt, k=k: e.matmul(pt[:], lhsT=ones_t[:], rhs=sq[:, k, :], start=(k == 0), stop=(k == 1)), r=[b_sq, b_ones], w=[pb])
                            kb.act(lambda e, pt=pt, ct=ct: e.activation(out=rs[:, ct, :], in_=pt[:], func=AF.Sqrt, scale=1.0 / nch, bias=EPS), r=[pb], w=[b_rs])
                            kb.dve(lambda e, ct=ct: e.reciprocal(out=rs[:, ct, :], in_=rs[:, ct, :]), r=[b_rs], w=[b_rs])
                        for ct in range(2):
                            rc = ct if per_tile else 0
                            kb.dve(lambda e, ct=ct, rc=rc, c0=c0: e.scalar_tensor_tensor(out=dst[:, ct, c0:c0 + 512], in0=src[:, ct, c0:c0 + 512], scalar=gain[:, ct:ct + 1],
                                                                                 in1=rs[:, rc, :], op0=ALU.mult, op1=ALU.mult), r=[b_src, b_rs, b_cv], w=[b_dst])

                def wout_add(l, mi, cat, b_cat, xv=None):
                    wv, bw = wload(w_out[l][mi * 256:(mi + 1) * 256, :].rearrange("(k p) c -> p k c", p=128), 2, 1024)
                    for dm in range(8):
                        for blk in range(NB):
                            pt, pb = kb.ps()
                            for k in range(2):
                                kb.pe(lambda e, k=k, pt=pt, wv=wv, dm=dm, blk=blk: e.matmul(pt[:], lhsT=wv[:, k, dm * 128:(dm + 1) * 128], rhs=cat[:, k, blk * 512:(blk + 1) * 512],
                                                                                  start=(k == 0), stop=(k == 1)), r=[bw, b_cat], w=[pb])
                            xa = x[:, dm, blk * 512:(blk + 1) * 512] if xv is None else xv(dm, blk)
                            pin = pt[:] if xv is None else pt[:].rearrange("p (j c) -> p j c", j=xa.shape[1])
                            kb.dve(lambda e, pin=pin, dm=dm, xa=xa: e.scalar_tensor_tensor(out=xa, in0=pin, scalar=modT[:, l, cv, 16 + dm:16 + dm + 1], in1=xa,
                                                                                 op0=ALU.mult, op1=ALU.add), r=[pb, b_mod, b_x], w=[b_x])

                def mixer_gmlp(l):
                    with ExitStack() as mes:
                        gu = kb.sb(mes, "gu", [128, 2, NT]); b_gu = Buf("gu")
                        gvt = kb.sb(mes, "gvt", [128, NTT, 256], BF16); b_gvt = Buf("gvt")
                        cat = kb.sb(mes, "catd", [128, 2, NT], BF16); b_cat = Buf("catd")
                        tA = kb.sb(mes, "tA", [128, 512]); b_tA = Buf("tA")
                        tB = kb.sb(mes, "tB", [128, 512]); b_tB = Buf("tB")
                        tg = kb.sb(mes, "tg", [128, 256]); b_tg = Buf("tg")
                        gmgB = kb.sb(mes, "gmgB", [128, 256]); b_gmgB = Buf("gmgB")
                        wsT = kb.sb(mes, "wsT", [128, 4, 128], BF16); b_wsT = Buf("wsT")
                        wsraw = kb.sb(mes, "wsraw", [128, 4, 128]); b_wsraw = Buf("wsraw")
                        bsB = kb.sb(mes, "bsB", [128, 2, 128]); b_bsB = [Buf("bsB%d" % i) for i in range(4)]
                        ss = kb.sb(mes, "gss", [128, 2]); b_ss = Buf("gss")
                        kb.dma(gmgB[:], gm_ng[l].rearrange("(o n) -> o n", o=1).broadcast_to([128, 256]), w=[b_gmgB])
                        kb.dma(wsraw[:], gm_ws[l].rearrange("h i j -> i h j"), w=[b_wsraw])
                        for hd in range(4):
                            pt, pb = kb.ps()
                            kb.pe(lambda e, hd=hd, pt=pt: e.transpose(out=pt[:, 0:128], in_=wsraw[:, hd, :], identity=ident[:]), r=[b_wsraw, b_ident], w=[pb])
                            kb.act(lambda e, hd=hd, pt=pt: e.activation(out=wsT[:, hd, :], in_=pt[:, 0:128], func=AF.Copy), r=[pb], w=[b_wsT])
                            ct, hh = hd // 2, hd % 2
                            kb.dma(bsB[hh * 64:(hh + 1) * 64, ct, :], gm_bs[l, hd].rearrange("(o n) -> o n", o=1).broadcast_to([64, 128]), w=[b_bsB[hd]])

                        def cb_gu(ct, blk, pt, pb):
                            gelu_psum(pt[:], pb, gu[:, ct, blk * 512:(blk + 1) * 512], b_gu, tA[:], b_tA, tB[:], b_tB)
                        proj_fm(l, 7 * 256, 2, cb_gu)

                        def cb_gv(tt, pt, pb):
                            gelu_psum(pt[:, 0:256], pb, tg[:], b_tg, tA[:, 0:256], b_tA, tB[:, 0:256], b_tB)
                            kb.act(lambda e: e.activation(out=tA[:, 0:256], in_=tg[:], func=AF.Square, accum_out=ss[:, 0:1]), r=[b_tg], w=[b_tA, b_ss])
                            kb.act(lambda e: e.activation(out=ss[:, 0:1], in_=ss[:, 0:1], func=AF.Sqrt, scale=1.0 / 256, bias=EPS), r=[b_ss], w=[b_ss])
                            kb.dve(lambda e: e.reciprocal(out=ss[:, 0:1], in_=ss[:, 0:1]), r=[b_ss], w=[b_ss])
                            kb.dve(lambda e, tt=tt: e.scalar_tensor_tensor(out=gvt[:, tt, :], in0=tg[:], scalar=ss[:, 0:1], in1=gmgB[:], op0=ALU.mult, op1=ALU.mult),
                                   r=[b_tg, b_ss, b_gmgB], w=[b_gvt])
                        proj_tok(l, 8 * 256, 256, cb_gv)

                        for tt in range(NTT):
                            for ct in range(2):
                                pt, pb = kb.ps()
                                for hh in range(2):
                                    hd = ct * 2 + hh
                                    kb.pe(lambda e, pt=pt, hh=hh, hd=hd, tt=tt: e.matmul(pt[hh * 64:(hh + 1) * 64, 0:128], lhsT=gvt[:, tt, hd * 64:(hd + 1) * 64], rhs=wsT[:, hd, :],
                                                                                 start=True, stop=True), r=[b_gvt, b_wsT], w=[pb])
                                kb.dve(lambda e, pt=pt, ct=ct: e.tensor_tensor(out=tA[:, 0:128], in0=pt[:, 0:128], in1=bsB[:, ct, :], op=ALU.add),
                                       r=[pb] + b_bsB, w=[b_tA])
                                kb.dve(lambda e, ct=ct, tt=tt: e.tensor_tensor(out=gu[:, ct, tt * 128:(tt + 1) * 128], in0=tA[:, 0:128], in1=gu[:, ct, tt * 128:(tt + 1) * 128], op=ALU.mult),
                                       r=[b_tA, b_gu], w=[b_gu])
                        fm_norm(gu, b_gu, gnT[:, l, 6:8], cat, b_cat, 256, ones_bf, False, mes)
                        wout_add(l, 3, cat, b_cat)
                        S.barrier()

                def mixer_fnet(l):
                    TT = T // 128
                    TBK = 256
                    dft_d = dft_s if T == 2048 else dft_p
                    with ExitStack() as mes:
                        cat = kb.sb(mes, "catc", [128, 2, NT], BF16); b_cat = Buf("catc")
                        xfr = kb.sb(mes, "xfr", [128, 2, NT], BF16); b_xfr = Buf("xfr")
                        bd = kb.sb(mes, "bd", [128, 2, 512], BF16); b_bd = Buf("bd")
                        kb.dma(bd[:], bdcs_d.rearrange("(k p) c -> p k c", p=128), w=[b_bd])
                        with ExitStack() as m2:
                            xcs = kb.sb(m2, "xcs", [128, NTT, 512], BF16); b_xcs = Buf("xcs")
                            with ExitStack() as m3:
                                xc = kb.sb(m3, "xc", [128, 2, NT], BF16); b_xc = Buf("xc")

                                def cb_xc(ct, blk, pt, pb):
                                    kb.act(lambda e: e.activation(out=xc[:, ct, blk * 512:(blk + 1) * 512], in_=pt[:], func=AF.Copy), r=[pb], w=[b_xc])
                                proj_fm(l, 6 * 256, 2, cb_xc)
                                for tt in range(NTT):
                                    pt, pb = kb.ps()
                                    for k in range(2):
                                        kb.pe(lambda e, pt=pt, k=k, tt=tt: e.matmul(pt[:], lhsT=xc[:, k, tt * 128:(tt + 1) * 128], rhs=bd[:, k, :], start=(k == 0), stop=(k == 1)),
                                              r=[b_xc, b_bd], w=[pb])
                                    kb.act(lambda e, pt=pt, tt=tt: e.activation(out=xcs[:, tt, :], in_=pt[:], func=AF.Copy), r=[pb], w=[b_xcs])
                                S.barrier()
                            with ExitStack() as m3:
                                dCs = [kb.sb(m3, "dC%d" % i, [128, TT, TBK], BF16) for i in range(2)]; b_dCs = [Buf("dC0"), Buf("dC1")]
                                dSs = [kb.sb(m3, "dS%d" % i, [128, TT, TBK], BF16) for i in range(2)]; b_dSs = [Buf("dS0"), Buf("dS1")]
                                dctr = 0
                                for sq_ in range(nseq):
                                    for tb in range(T // TBK):
                                        dC, b_dC, dS, b_dS = dCs[dctr % 2], b_dCs[dctr % 2], dSs[dctr % 2], b_dSs[dctr % 2]
                                        dctr += 1
                                        kb.dma(dC[:], dft_d[0].rearrange("(t p) c -> p t c", p=128)[:, :, tb * TBK:(tb + 1) * TBK], w=[b_dC])
                                        kb.dma(dS[:], dft_d[1].rearrange("(t p) c -> p t c", p=128)[:, :, tb * TBK:(tb + 1) * TBK], w=[b_dS])
                                        for ct in range(2):
                                            pt, pb = kb.ps()
                                            for tt in range(TT):
                                                kb.pe(lambda e, pt=pt, tt=tt, ct=ct, sq_=sq_: e.matmul(pt[:, 0:TBK], lhsT=xcs[:, sq_ * TT + tt, ct * 128:(ct + 1) * 128], rhs=dC[:, tt, :],
                                                                                             start=(tt == 0), stop=False), r=[b_xcs, b_dC], w=[pb])
                                            for tt in range(TT):
                                                kb.pe(lambda e, pt=pt, tt=tt, ct=ct, sq_=sq_: e.matmul(pt[:, 0:TBK], lhsT=xcs[:, sq_ * TT + tt, 256 + ct * 128:256 + (ct + 1) * 128], rhs=dS[:, tt, :],
                                                                                             start=False, stop=(tt == TT - 1)), r=[b_xcs, b_dS], w=[pb])
                                            c0 = sq_ * T + tb * TBK
                                            kb.act(lambda e, pt=pt, ct=ct, c0=c0: e.activation(out=xfr[:, ct, c0:c0 + TBK], in_=pt[:, 0:TBK], func=AF.Copy), r=[pb], w=[b_xfr])
                                S.barrier()
                        with ExitStack() as m2:
                            cpre = kb.sb(m2, "cpre", [128, 2, NT]); b_cpre = Buf("cpre")
                            wv, bw = wload(fn_w[l].rearrange("(k p) c -> p k c", p=128), 2, 256)
                            for ct in range(2):
                                for blk in range(NB):
                                    pt, pb = kb.ps()
                                    for k in range(2):
                                        kb.pe(lambda e, pt=pt, k=k, ct=ct, blk=blk, wv=wv: e.matmul(pt[:], lhsT=wv[:, k, ct * 128:(ct + 1) * 128], rhs=xfr[:, k, blk * 512:(blk + 1) * 512],
                                                                                          start=(k == 0), stop=(k == 1)), r=[bw, b_xfr], w=[pb])
                                    kb.act(lambda e, pt=pt, ct=ct, blk=blk: e.activation(out=cpre[:, ct, blk * 512:(blk + 1) * 512], in_=pt[:], func=AF.Copy), r=[pb], w=[b_cpre])
                            fm_norm(cpre, b_cpre, gnT[:, l, 4:6], cat, b_cat, 256, ones_bf, False, m2)
                            wout_add(l, 2, cat, b_cat)
                            S.barrier()

                def mixer_hgrn(l):
                    TT = T // 128
                    NCH = NT // 32
                    with ExitStack() as mes:
                        cat = kb.sb(mes, "catb", [128, 2, NT], BF16); b_cat = Buf("catb")
                        mk = kb.sb(mes, "hmask", [128, 3, 128]); b_mk = Buf("hmask")
                        onesbd = kb.sb(mes, "onesbd", [128, 128], BF16)
                        smask = kb.sb(mes, "smask", [128, NT], BF16); b_smask = Buf("smask")
                        kb.dma(mk[:], hmask_d.rearrange("m p c -> p m c"), w=[b_mk])
                        kb.dma(smask[:], smask_d[:, 0:NT], w=[b_smask])
                        kb.dve(lambda e: e.tensor_copy(out=onesbd[:], in_=mk[:, 2, :]), r=[b_mk], w=[b_mk])
                        stmp = kb.sb(mes, "stmp", [128, 512]); b_stmp = Buf("stmp")
                        stmp2 = kb.sb(mes, "stmp2", [128, 512]); b_stmp2 = Buf("stmp2")
                        Sst = [kb.sb(mes, "Sst%d" % i, [128, 128]) for i in range(2)]
                        b_S = [Buf("Sst0"), Buf("Sst1")]
                        Sbd = [kb.sb(mes, "Sbd%d" % i, [128, 8, 128], BF16) for i in range(2)]
                        b_Sbd = [[Buf("Sbd") for i in range(8)] for j in range(2)]
                        scT = [[kb.sb(mes, "scT%d%d" % (j, i), [128, 128], BF16) for i in range(2)] for j in range(2)]
                        b_scT = [[Buf("scT") for i in range(2)] for j in range(2)]
                        vblk = [kb.sb(mes, "vblk%d" % i, [128, 4, 128], BF16) for i in range(2)]
                        b_vblk = [Buf("vblk%d" % i) for i in range(2)]
                        cmk = kb.sb(mes, "cmk", [128, 4]); b_cmk = Buf("cmk")
                        kb.dma(cmk[:], cmask_d, w=[b_cmk])
                        vctr = [0]
                        for ct in range(2):
                            with ExitStack() as ces:
                                vtok = kb.sb(ces, "vtok", [128, NTT, 128], BF16); b_vtok = Buf("vtok")
                                oacc = kb.sb(ces, "oacc", [128, NT]); b_oacc = Buf("oacc")
                                Qt = [kb.sb(ces, "Qt%d" % i, [128, NT], BF16) for i in range(2)]; b_Qt = [Buf("Qt0"), Buf("Qt1")]
                                Kt = [kb.sb(ces, "Kt%d" % i, [128, NT], BF16) for i in range(2)]; b_Kt = [Buf("Kt0"), Buf("Kt1")]
                                Khtok = [kb.sb(ces, "Khtok%d" % i, [128, NTT, 128], BF16) for i in range(2)]; b_Khtok = [Buf("Kh0"), Buf("Kh1")]
                                glast = [kb.sb(ces, "glast%d" % i, [128, NCH]) for i in range(2)]; b_gl = [Buf("gl0"), Buf("gl1")]

                                def cb_v(tt, pt, pb):
                                    kb.act(lambda e: e.activation(out=vtok[:, tt, :], in_=pt[:, 0:128], func=AF.Copy), r=[pb], w=[b_vtok])
                                proj_tok(l, 4 * 256 + ct * 128, 128, cb_v)
                                Kh_sh = kb.sb(ces, "Kh", [128, NT], BF16); b_Kh_sh = Buf("Kh")
                                tmpA = [(kb.sb(ces, "bc%d" % i, [128, NT]), Buf("bc"), kb.sb(ces, "kk%d" % i, [128, NT], BF16), Buf("kk"), Kh_sh, b_Kh_sh) for i in range(2)]
                                for d in range(2):
                                    with ExitStack() as des:
                                        bc, b_bc, kk, b_kk, Kh, b_Kh = tmpA[d]
                                        lbA = lbT[:, l, d, ct:ct + 1]
                                        omlA = omlT[:, l, d, ct:ct + 1]
                                        nomlA = nomlT[:, l, d, ct:ct + 1]

                                        def cb_z(ti, blk, pt, pb):
                                            sl = slice(blk * 512, (blk + 1) * 512)
                                            kb.act(lambda e: e.activation(out=stmp[:], in_=pt[:], func=AF.Sigmoid), r=[pb], w=[b_stmp])
                                            kb.act(lambda e: e.activation(out=bc[:, sl], in_=stmp[:], func=AF.Ln, scale=omlA, bias=lbA), r=[b_stmp, b_lb], w=[b_bc])
                                            kb.dve(lambda e: e.tensor_scalar(out=kk[:, sl], in0=stmp[:], scalar1=nomlA, scalar2=omlA, op0=ALU.mult, op1=ALU.add),
                                                   r=[b_stmp, b_lb], w=[b_kk])
                                        proj_fm(l, (2 + d) * 256 + ct * 128, 1, cb_z)
                                        if d == 0:
                                            kb.dve(lambda e: e.tensor_tensor_scan(out=bc[:], data0=smask[:], data1=bc[:], initial=0.0, op0=ALU.mult, op1=ALU.add),
                                                   r=[b_bc, b_smask], w=[b_bc])
                                            kb.act(lambda e: e.activation(out=glast[d][:], in_=bc[:].rearrange("p (c t) -> p c t", t=32)[:, :, 31], func=AF.Exp), r=[b_bc], w=[b_gl[d]])
                                        else:
                                            kb.dve(lambda e: e.tensor_tensor_scan(out=bc[:, ::-1], data0=smask[:], data1=bc[:, ::-1], initial=0.0, op0=ALU.mult, op1=ALU.add),
                                                   r=[b_bc, b_smask], w=[b_bc])
                                            kb.act(lambda e: e.activation(out=glast[d][:], in_=bc[:].rearrange("p (c t) -> p c t", t=32)[:, :, 0], func=AF.Exp), r=[b_bc], w=[b_gl[d]])
                                        for blk in range(NB):
                                            sl = slice(blk * 512, (blk + 1) * 512)
                                            kb.act(lambda e, sl=sl: e.activation(out=stmp[:], in_=bc[:, sl], func=AF.Exp, scale=-1.0), r=[b_bc], w=[b_stmp])
                                            kb.dve(lambda e, sl=sl: e.tensor_tensor(out=Kt[d][:, sl], in0=kk[:, sl], in1=stmp[:], op=ALU.mult), r=[b_kk, b_stmp], w=[b_Kt[d]])
                                        kb.dve(lambda e: e.tensor_tensor(out=Kh[:].rearrange("p (c t) -> p c t", t=32), in0=Kt[d][:].rearrange("p (c t) -> p c t", t=32),
                                                                         in1=glast[d][:].unsqueeze(2).broadcast_to([128, NCH, 32]), op=ALU.mult), r=[b_Kt[d], b_gl[d]], w=[b_Kh])

                                        def cb_q(ti, blk, pt, pb):
                                            sl = slice(blk * 512, (blk + 1) * 512)
                                            kb.act(lambda e: e.activation(out=stmp2[:], in_=bc[:, sl], func=AF.Exp), r=[b_bc], w=[b_stmp2])
                                            kb.dve(lambda e: e.tensor_tensor(out=Qt[d][:, sl], in0=pt[:], in1=stmp2[:], op=ALU.mult), r=[pb, b_stmp2], w=[b_Qt[d]])
                                        proj_fm(l, 1 * 256 + ct * 128, 1, cb_q)
                                        for tt in range(NTT):
                                            ptb, pb = kb.ps()
                                            ptv = ptb.bitcast(BF16)
                                            kb.pe(lambda e, tt=tt, ptv=ptv: e.transpose(out=ptv[:, 0:128], in_=Kh[:, tt * 128:(tt + 1) * 128], identity=identb[:]), r=[b_Kh, b_ident], w=[pb])
                                            kb.act(lambda e, tt=tt, ptv=ptv: e.activation(out=Khtok[d][:, tt, :], in_=ptv[:, 0:128], func=AF.Copy), r=[pb], w=[b_Khtok[d]])
                                for sq_ in range(nseq):
                                    for d in range(2):
                                        kb.dve(lambda e, d=d: e.memset(Sst[d][:], 0.0), w=[b_S[d]])
                                        if is_sample:
                                            for hh in range(2):
                                                kb.dma(Sst[d][hh * 64:(hh + 1) * 64, hh * 64:(hh + 1) * 64], hgst_d[l, d, ct * 2 + hh], w=[b_S[d]])
                                        kb.dve(lambda e, d=d: e.tensor_tensor(out=Sbd[d][:, 0, :], in0=Sst[d][:], in1=mk[:, 2, :], op=ALU.mult), r=[b_S[d], b_mk], w=[b_Sbd[d][0]])
                                    for step in range(TT):
                                        par = step % 2
                                        info = []
                                        for d in range(2):
                                            ttl = step if d == 0 else TT - 1 - step
                                            tt = sq_ * TT + ttl
                                            tsl = slice(tt * 128, (tt + 1) * 128)
                                            for hh in range(2):
                                                ps_, pbs = kb.ps()
                                                kb.pe(lambda e, ps_=ps_, hh=hh, tsl=tsl, d=d: e.matmul(ps_[:, 0:128], lhsT=Kt[d][hh * 64:(hh + 1) * 64, tsl], rhs=Qt[d][hh * 64:(hh + 1) * 64, tsl],
                                                                                               start=True, stop=True), r=[b_Kt[d], b_Qt[d]], w=[pbs])
                                                kb.dve(lambda e, ps_=ps_, hh=hh, d=d: e.tensor_tensor(out=scT[d][hh][:], in0=ps_[:, 0:128], in1=mk[:, d, :], op=ALU.mult),
                                                       r=[pbs, b_mk], w=[b_scT[d][hh]])
                                            vs = vctr[0] % 2
                                            vctr[0] += 1
                                            kb.pool(lambda e, vs=vs, tt=tt: e.tensor_tensor(out=vblk[vs][:], in0=vtok[:, tt, :].unsqueeze(1).broadcast_to([128, 4, 128]),
                                                                                     in1=cmk[:].unsqueeze(2).broadcast_to([128, 4, 128]), op=ALU.mult),
                                                    r=[b_vtok, b_cmk], w=[b_vblk[vs]])
                                            pd, pbd = kb.ps()
                                            kb.pe(lambda e, pd=pd, vs=vs, tt=tt, d=d: e.matmul(pd[:], lhsT=Khtok[d][:, tt, :], rhs=vblk[vs][:].rearrange("p c v -> p (c v)"), start=True, stop=True),
                                                  r=[b_Khtok[d], b_vblk[vs]], w=[pbd])
                                            po, pbo = kb.ps()
                                            for hh in range(2):
                                                kb.pe(lambda e, po=po, hh=hh, tt=tt, d=d: e.matmul(po[hh * 64:(hh + 1) * 64, 0:128], lhsT=vtok[:, tt, hh * 64:(hh + 1) * 64], rhs=scT[d][hh][:],
                                                                                           start=True, stop=False), r=[b_vtok, b_scT[d][hh]], w=[pbo])
                                            info.append((tt, tsl, pd, pbd, po, pbo))
                                        for ci in range(4):
                                            for d in range(2):
                                                tt, tsl, pd, pbd, po, pbo = info[d]
                                                cl = ci if d == 0 else 3 - ci
                                                cg = tt * 4 + cl
                                                csl = slice(tt * 128 + cl * 32, tt * 128 + (cl + 1) * 32)
                                                scur = par * 4 + ci
                                                snxt = par * 4 + ci + 1 if ci < 3 else (1 - par) * 4
                                                kb.pe(lambda e, po=po, cl=cl, csl=csl, ci=ci, scur=scur, d=d: e.matmul(po[:, cl * 32:(cl + 1) * 32], lhsT=Sbd[d][:, scur, :], rhs=Qt[d][:, csl], start=False, stop=(ci == 3)),
                                                      r=[b_Sbd[d][scur], b_Qt[d]], w=[pbo])
                                                kb.dve(lambda e, pd=pd, cg=cg, cl=cl, d=d: e.scalar_tensor_tensor(out=Sst[d][:], in0=Sst[d][:], scalar=glast[d][:, cg:cg + 1], in1=pd[:, cl * 128:(cl + 1) * 128],
                                                                                                  op0=ALU.mult, op1=ALU.add), r=[b_S[d], b_gl[d], pbd], w=[b_S[d]])
                                                (kb.pool if d == 0 else kb.dve)(lambda e, snxt=snxt, d=d: e.tensor_tensor(out=Sbd[d][:, snxt, :], in0=Sst[d][:], in1=mk[:, 2, :], op=ALU.mult), r=[b_S[d], b_mk], w=[b_Sbd[d][snxt]])
                                        for d in range(2):
                                            tt, tsl, pd, pbd, po, pbo = info[d]
                                            ttl = tt - sq_ * TT
                                            first = (ttl < TT // 2) if d == 0 else (ttl >= TT // 2)
                                            if first:
                                                kb.act(lambda e, po=po, tsl=tsl: e.activation(out=oacc[:, tsl], in_=po[:, 0:128], func=AF.Copy), r=[pbo], w=[b_oacc])
                                            else:
                                                kb.dve(lambda e, po=po, tsl=tsl: e.tensor_tensor(out=oacc[:, tsl], in0=po[:, 0:128], in1=oacc[:, tsl], op=ALU.add), r=[pbo, b_oacc], w=[b_oacc])
                                    if not is_sample:
                                        for d in range(2):
                                            for hh in range(2):
                                                kb.dma(sthg_o[sq_, l, d, ct * 2 + hh], Sst[d][hh * 64:(hh + 1) * 64, hh * 64:(hh + 1) * 64], r=[b_S[d]], sembuf=b_S[d])
                                S.barrier()
                                with ExitStack() as des:
                                    sqb = kb.sb(des, "hsq", [128, 512], BF16); b_sqb = Buf("hsq")
                                    rs = stmp2; b_rs = b_stmp2

                                    def cb_g(ti, blk, pt, pb):
                                        sl = slice(blk * 512, (blk + 1) * 512)
                                        kb.act(lambda e: e.activation(out=sqb[:], in_=oacc[:, sl], func=AF.Square), r=[b_oacc], w=[b_sqb])
                                        p2, pb2 = kb.ps()
                                        kb.pe(lambda e: e.matmul(p2[:], lhsT=onesbd[:], rhs=sqb[:], start=True, stop=True), r=[b_sqb, b_mk], w=[pb2])
                                        kb.act(lambda e: e.activation(out=rs[:], in_=p2[:], func=AF.Sqrt, scale=1.0 / 64, bias=EPS), r=[pb2], w=[b_rs])
                                        kb.dve(lambda e: e.reciprocal(out=rs[:], in_=rs[:]), r=[b_rs], w=[b_rs])
                                        kb.dve(lambda e: e.scalar_tensor_tensor(out=rs[:], in0=oacc[:, sl], scalar=gnT[:, l, 2 + ct:3 + ct], in1=rs[:], op0=ALU.mult, op1=ALU.mult),
                                               r=[b_oacc, b_rs, b_cv], w=[b_rs])
                                        kb.act(lambda e: e.activation(out=stmp[:], in_=pt[:], func=AF.Silu), r=[pb], w=[b_stmp])
                                        kb.dve(lambda e: e.tensor_tensor(out=cat[:, ct, sl], in0=rs[:], in1=stmp[:], op=ALU.mult), r=[b_rs, b_stmp], w=[b_cat])
                                    proj_fm(l, 5 * 256 + ct * 128, 1, cb_g)
                                    S.barrier()
                        wout_add(l, 1, cat, b_cat)
                        S.barrier()

                def mixer_s5(l):
                    NCHT = NT // 8
                    CW = min(128, NCHT)
                    NCT = NCHT // CW
                    NST = T // 8
                    TWO_PI = 6.283185307179586
                    with ExitStack() as mes:
                        PWr = kb.sb(mes, "PWr", [128, 16, 9]); PWi = kb.sb(mes, "PWi", [128, 16, 9])
                        NPr = kb.sb(mes, "NPr", [128, 16, 8]); NPi = kb.sb(mes, "NPi", [128, 16, 8])
                        bbr = kb.sb(mes, "bbr", [128, 16, 16]); bbi = kb.sb(mes, "bbi", [128, 16, 16])
                        Cr = kb.sb(mes, "Cr", [128, 16, 16]); Ci = kb.sb(mes, "Ci", [128, 16, 16])
                        Dg = kb.sb(mes, "Dg", [128, 16])
                        smk = kb.sb(mes, "s5mk", [128, 2, 128])
                        b_P = Buf("s5params")
                        U = kb.sb(mes, "U", [128, 16, NCHT], BF16); b_U = Buf("U")
                        Yx = kb.sb(mes, "Yx", [128, NCT, 8, 256], BF16); b_Yx = Buf("Yx")
                        cnt = [0]

                        def vop(fn, r=(), w=()):
                            cnt[0] += 1
                            return kb.dve(fn, r, w)

                        for pes in ([ExitStack()] if is_sample else []):
                            def t16(nm):
                                return kb.sb(pes, nm, [128, 16])
                            stg = kb.sb(pes, "lamst", [16, 2, 128]); b_stg = [Buf("st0"), Buf("st1"), Buf("st2"), Buf("st3")]
                            lre, lim, dtv = t16("lre"), t16("lim"), t16("dtv")
                            for ri, src in enumerate((lamre_d, lamim_d)):
                                for d in range(2):
                                    kb.dma(stg[:, ri, d * 64:(d + 1) * 64], src[l, d], w=[b_stg[ri * 2 + d]])
                                pt, pb = kb.ps()
                                kb.pe(lambda e, pt=pt, ri=ri: e.transpose(out=pt[:, 0:16], in_=stg[:, ri, :], identity=ident[0:16, 0:16]), r=b_stg + [b_ident], w=[pb])
                                dst = lre if ri == 0 else lim
                                kb.dve(lambda e, pt=pt, dst=dst: e.tensor_copy(out=dst[:], in_=pt[:, 0:16]), r=[pb], w=[b_P])
                            b_dt = [Buf("dt0"), Buf("dt1")]
                            for d in range(2):
                                kb.dma(dtv[d * 64:(d + 1) * 64, :], logdt_d[l, d].rearrange("(o n) -> o n", o=1).broadcast_to([64, 16]), w=[b_dt[d]])
                            kb.act(lambda e: e.activation(out=dtv[:], in_=dtv[:], func=AF.Exp), r=b_dt, w=[b_P])
                            lr, mag, ang, kf, r1, r2, s1, c1 = [t16("p%d" % i) for i in range(8)]
                            ki = kb.sb(pes, "ki", [128, 16], I32)
                            abre, abim, den, xr, zre, zim, inre, inim, ta, tb2 = [t16("q%d" % i) for i in range(10)]
                            P = [b_P]
                            vop(lambda e: e.tensor_scalar(out=lr[:], in0=lre[:], scalar1=-1e-4, scalar2=None, op0=ALU.min), P, P)
                            vop(lambda e: e.tensor_tensor(out=mag[:], in0=lr[:], in1=dtv[:], op=ALU.mult), P, P)
                            kb.act(lambda e: e.activation(out=mag[:], in_=mag[:], func=AF.Exp), P, P)
                            vop(lambda e: e.tensor_tensor(out=ang[:], in0=lim[:], in1=dtv[:], op=ALU.mult), P, P)
                            vop(lambda e: e.tensor_scalar(out=kf[:], in0=ang[:], scalar1=1.0 / TWO_PI, scalar2=None, op0=ALU.mult), P, P)
                            vop(lambda e: e.tensor_copy(out=ki[:], in_=kf[:]), P, P)
                            vop(lambda e: e.tensor_copy(out=kf[:], in_=ki[:]), P, P)
                            vop(lambda e: e.scalar_tensor_tensor(out=r1[:], in0=kf[:], scalar=-6.28125, in1=ang[:], op0=ALU.mult, op1=ALU.add), P, P)
                            vop(lambda e: e.scalar_tensor_tensor(out=r1[:], in0=kf[:], scalar=-0.0019353071795864769, in1=r1[:], op0=ALU.mult, op1=ALU.add), P, P)
                            vop(lambda e: e.tensor_scalar(out=r1[:], in0=r1[:], scalar1=-3.1415925, scalar2=3.1415925, op0=ALU.max, op1=ALU.min), P, P)
                            kb.act(lambda e: e.activation(out=s1[:], in_=r1[:], func=AF.Sin), P, P)
                            vop(lambda e: e.tensor_scalar(out=r2[:], in0=r1[:], scalar1=1.5707963267948966, scalar2=None, op0=ALU.add), P, P)
                            vop(lambda e: e.tensor_scalar(out=ta[:], in0=r2[:], scalar1=3.141592653589793, scalar2=-TWO_PI, op0=ALU.is_gt, op1=ALU.mult), P, P)
                            vop(lambda e: e.tensor_tensor(out=r2[:], in0=r2[:], in1=ta[:], op=ALU.add), P, P)
                            vop(lambda e: e.tensor_scalar(out=r2[:], in0=r2[:], scalar1=-3.1415925, scalar2=3.1415925, op0=ALU.max, op1=ALU.min), P, P)
                            kb.act(lambda e: e.activation(out=c1[:], in_=r2[:], func=AF.Sin), P, P)
                            vop(lambda e: e.tensor_tensor(out=abre[:], in0=mag[:], in1=c1[:], op=ALU.mult), P, P)
                            vop(lambda e: e.tensor_tensor(out=abim[:], in0=mag[:], in1=s1[:], op=ALU.mult), P, P)
                            vop(lambda e: e.tensor_tensor(out=den[:], in0=lr[:], in1=lr[:], op=ALU.mult), P, P)
                            vop(lambda e: e.tensor_tensor(out=ta[:], in0=lim[:], in1=lim[:], op=ALU.mult), P, P)
                            vop(lambda e: e.tensor_tensor(out=den[:], in0=den[:], in1=ta[:], op=ALU.add), P, P)
                            vop(lambda e: e.reciprocal(out=den[:], in_=den[:]), P, P)
                            vop(lambda e: e.tensor_scalar(out=xr[:], in0=abre[:], scalar1=-1.0, scalar2=None, op0=ALU.add), P, P)
                            vop(lambda e: e.tensor_tensor(out=ta[:], in0=xr[:], in1=lr[:], op=ALU.mult), P, P)
                            vop(lambda e: e.tensor_tensor(out=tb2[:], in0=abim[:], in1=lim[:], op=ALU.mult), P, P)
                            vop(lambda e: e.tensor_tensor(out=ta[:], in0=ta[:], in1=tb2[:], op=ALU.add), P, P)
                            vop(lambda e: e.tensor_tensor(out=zre[:], in0=ta[:], in1=den[:], op=ALU.mult), P, P)
                            vop(lambda e: e.tensor_tensor(out=ta[:], in0=abim[:], in1=lr[:], op=ALU.mult), P, P)
                            vop(lambda e: e.tensor_tensor(out=tb2[:], in0=xr[:], in1=lim[:], op=ALU.mult), P, P)
                            vop(lambda e: e.tensor_tensor(out=ta[:], in0=ta[:], in1=tb2[:], op=ALU.subtract), P, P)
                            vop(lambda e: e.tensor_tensor(out=zim[:], in0=ta[:], in1=den[:], op=ALU.mult), P, P)
                            vop(lambda e: e.tensor_tensor(out=ta[:], in0=mag[:], in1=mag[:], op=ALU.mult), P, P)
                            vop(lambda e: e.reciprocal(out=ta[:], in_=ta[:]), P, P)
                            vop(lambda e: e.tensor_tensor(out=inre[:], in0=abre[:], in1=ta[:], op=ALU.mult), P, P)
                            vop(lambda e: e.scalar_tensor_tensor(out=inim[:], in0=abim[:], scalar=-1.0, in1=ta[:], op0=ALU.mult, op1=ALU.mult), P, P)
                            for (Pr, Pi, br_, bi_, nmax) in ((PWr, PWi, abre, abim, 9), (NPr, NPi, inre, inim, 8)):
                                vop(lambda e, Pr=Pr: e.memset(Pr[:, :, 0], 1.0), (), P)
                                vop(lambda e, Pi=Pi: e.memset(Pi[:, :, 0], 0.0), (), P)
                                for m in range(1, nmax):
                                    vop(lambda e, Pr=Pr, br_=br_, m=m: e.tensor_tensor(out=ta[:], in0=Pr[:, :, m - 1], in1=br_[:], op=ALU.mult), P, P)
                                    vop(lambda e, Pi=Pi, bi_=bi_, m=m: e.tensor_tensor(out=tb2[:], in0=Pi[:, :, m - 1], in1=bi_[:], op=ALU.mult), P, P)
                                    vop(lambda e, Pr=Pr, m=m: e.tensor_tensor(out=Pr[:, :, m], in0=ta[:], in1=tb2[:], op=ALU.subtract), P, P)
                                    vop(lambda e, Pr=Pr, bi_=bi_, m=m: e.tensor_tensor(out=ta[:], in0=Pr[:, :, m - 1], in1=bi_[:], op=ALU.mult), P, P)
                                    vop(lambda e, Pi=Pi, br_=br_, m=m: e.tensor_tensor(out=tb2[:], in0=Pi[:, :, m - 1], in1=br_[:], op=ALU.mult), P, P)
                                    vop(lambda e, Pi=Pi, m=m: e.tensor_tensor(out=Pi[:, :, m], in0=ta[:], in1=tb2[:], op=ALU.add), P, P)
                            braw = [kb.sb(pes, "braw%d" % i, [128, 16, 16]) for i in range(2)]
                            b_br = [Buf("br%d" % i) for i in range(4)]
                            for ri, src in enumerate((s5b_re_d, s5b_im_d)):
                                for d in range(2):
                                    kb.dma(braw[ri][d * 64:(d + 1) * 64, :, :], src[l, d].rearrange("g n p -> n g p"), w=[b_br[ri * 2 + d]])
                            t3a = kb.sb(pes, "t3a", [128, 16, 16]); t3b = kb.sb(pes, "t3b", [128, 16, 16])
                            zrb = zre[:].unsqueeze(2).broadcast_to([128, 16, 16])
                            zib = zim[:].unsqueeze(2).broadcast_to([128, 16, 16])
                            vop(lambda e: e.tensor_tensor(out=t3a[:], in0=braw[0][:], in1=zrb, op=ALU.mult), P + b_br, P)
                            vop(lambda e: e.tensor_tensor(out=t3b[:], in0=braw[1][:], in1=zib, op=ALU.mult), P + b_br, P)
                            vop(lambda e: e.tensor_tensor(out=bbr[:], in0=t3a[:], in1=t3b[:], op=ALU.subtract), P, P)
                            vop(lambda e: e.tensor_tensor(out=t3a[:], in0=braw[1][:], in1=zrb, op=ALU.mult), P, P)
                            vop(lambda e: e.tensor_tensor(out=t3b[:], in0=braw[0][:], in1=zib, op=ALU.mult), P, P)
                            vop(lambda e: e.tensor_tensor(out=bbi[:], in0=t3a[:], in1=t3b[:], op=ALU.add), P, P)
                            cst = kb.sb(pes, "cst", [128, 128]); b_cst = [Buf("cst0"), Buf("cst1")]
                            for ri, src in enumerate((s5c_re_d, s5c_im_d)):
                                dstC = Cr if ri == 0 else Ci
                                for half in range(2):
                                    for d in range(2):
                                        kb.dma(cst[:, d * 64:(d + 1) * 64], src[l, d, half * 8:(half + 1) * 8].rearrange("g p n -> (g p) n"), w=[b_cst[d]])
                                    pt, pb = kb.ps()
                                    kb.pe(lambda e, pt=pt: e.transpose(out=pt[:, 0:128], in_=cst[:], identity=ident[:]), r=b_cst + [b_ident], w=[pb])
                                    kb.dve(lambda e, pt=pt, dstC=dstC, half=half: e.tensor_copy(out=dstC[:, half * 8:(half + 1) * 8, :], in_=pt[:, 0:128].rearrange("p (g q) -> p g q", q=16)),
                                           r=[pb], w=[b_P])
                            b_dg = [Buf("dg%d" % i) for i in range(8)]
                            for i in range(8):
                                kb.dma(Dg[i * 16:(i + 1) * 16, :], s5d_d[l].rearrange("(g p) -> p g", p=16), w=[b_dg[i]], slow=True)
                            b_smk = Buf("smk")
                            kb.dma(smk[:], s5mask_d.rearrange("m p c -> p m c"), w=[b_smk])
                            vop(lambda e: e.tensor_copy(out=Dg[:], in_=Dg[:]), b_dg + [b_smk], P)
                            S.barrier()
                            pes.close()
                        if stage == 51:
                            return

                        with ExitStack() as xes:
                            Xx = kb.sb(xes, "Xx", [128, NCT, 16, 8, 16]); b_Xx = Buf("Xx")
                            wv, bw = wload(w_in[l].rearrange("(k p) c -> p k c", p=128)[:, :, 0:256], 8, 256)
                            for ctile in range(NCT):
                                for i in range(8):
                                    pt, pb = kb.ps()
                                    t0_ = ctile * CW * 8 + i
                                    for k in range(8):
                                        kb.pe(lambda e, pt=pt, k=k, t0_=t0_, wv=wv: e.matmul(pt[0:CW, 0:256], lhsT=h[:, k, t0_:t0_ + (CW - 1) * 8 + 1:8], rhs=wv[:, k, :], start=(k == 0), stop=(k == 7)),
                                              r=[bw, b_h], w=[pb])
                                    kb.act(lambda e, pt=pt, ctile=ctile, i=i: e.activation(out=Xx[0:CW, ctile, :, i, :], in_=pt[0:CW, 0:256].rearrange("p (g q) -> p g q", q=16), func=AF.Copy), r=[pb], w=[b_Xx])
                            for ctile in range(NCT):
                                for g4 in range(4):
                                    pt, pb = kb.ps()
                                    for gg in range(4):
                                        g = g4 * 4 + gg
                                        kb.pe(lambda e, pt=pt, gg=gg, g=g, ctile=ctile: e.transpose(out=pt[:, gg * CW:(gg + 1) * CW], in_=Xx[0:CW, ctile, g, :, :], identity=ident[0:CW, 0:CW]),
                                              r=[b_Xx, b_ident], w=[pb])
                                    kb.act(lambda e, pt=pt, g4=g4, ctile=ctile: e.activation(out=U[:, g4 * 4:(g4 + 1) * 4, ctile * CW:(ctile + 1) * CW],
                                                                                     in_=pt[:, 0:4 * CW].rearrange("p (g c) -> p g c", g=4), func=AF.Copy), r=[pb], w=[b_U])
                            S.barrier()
                        if stage == 52:
                            return

                        for hf in range(2):
                            G0 = hf * 8
                            GS = slice(G0, G0 + 8)
                            with ExitStack() as hes:
                                WD = kb.sb(hes, "WD", [128, 8, 2, 128], BF16); b_WD = Buf("WD")
                                KT = kb.sb(hes, "KT", [128, 8, 128], BF16); b_KT = Buf("KT")
                                Vr = kb.sb(hes, "Vr", [128, 8, 8, 16]); Vi = kb.sb(hes, "Vi", [128, 8, 8, 16]); b_V = Buf("V")
                                AA = kb.sb(hes, "AA", [128, 2, 8]); BB = kb.sb(hes, "BB", [128, 2, 8]); b_AB = Buf("AB")
                                P = [b_P]
                                if is_sample:
                                    for ri in range(2):
                                        vop(lambda e, ri=ri: e.tensor_copy(out=AA[:, ri, :], in_=PWr[:, GS, 8]), P, [b_AB])
                                    vop(lambda e: e.tensor_scalar(out=BB[:, 0, :], in0=PWi[:, GS, 8], scalar1=-1.0, scalar2=None, op0=ALU.mult), P, [b_AB])
                                    vop(lambda e: e.tensor_copy(out=BB[:, 1, :], in_=PWi[:, GS, 8]), P, [b_AB])
                                for wes in ([ExitStack()] if is_sample else []):
                                    W = [kb.sb(wes, "wk%d" % i, [128, 8, 8, 16]) for i in range(4)]
                                    b_W = [Buf("wk%d" % i) for i in range(4)]
                                    t4a = kb.sb(wes, "t4a", [128, 8, 8, 16]); t4b = kb.sb(wes, "t4b", [128, 8, 8, 16]); b_t4 = Buf("t4")

                                    def cmul(outr, outi, b_out, ar, ai, br_, bi_, rows, neg_im=False):
                                        A_r = ar.unsqueeze(3).broadcast_to([64, 8, 8, 16]); A_i = ai.unsqueeze(3).broadcast_to([64, 8, 8, 16])
                                        B_r = br_.unsqueeze(2).broadcast_to([64, 8, 8, 16]); B_i = bi_.unsqueeze(2).broadcast_to([64, 8, 8, 16])
                                        ta_, tb_ = t4a[rows], t4b[rows]
                                        vop(lambda e: e.tensor_tensor(out=ta_, in0=A_r, in1=B_r, op=ALU.mult), P, [b_t4])
                                        vop(lambda e: e.tensor_tensor(out=tb_, in0=A_i, in1=B_i, op=ALU.mult), P, [b_t4])
                                        vop(lambda e: e.tensor_tensor(out=outr[rows], in0=ta_, in1=tb_, op=ALU.subtract), [b_t4], [b_out])
                                        vop(lambda e: e.tensor_tensor(out=ta_, in0=A_r, in1=B_i, op=ALU.mult), P, [b_t4])
                                        vop(lambda e: e.tensor_tensor(out=tb_, in0=A_i, in1=B_r, op=ALU.mult), P, [b_t4])
                                        if neg_im:
                                            vop(lambda e: e.scalar_tensor_tensor(out=outi[rows], in0=ta_, scalar=-1.0, in1=tb_, op0=ALU.mult, op1=ALU.subtract), [b_t4], [b_out])
                                        else:
                                            vop(lambda e: e.tensor_tensor(out=outi[rows], in0=ta_, in1=tb_, op=ALU.add), [b_t4], [b_out])
                                    F_, B_ = slice(0, 64), slice(64, 128)
                                    cmul(W[0], W[1], b_W[0], PWr[F_, GS, 7::-1], PWi[F_, GS, 7::-1], bbr[F_, GS, :], bbi[F_, GS, :], F_)
                                    cmul(W[0], W[1], b_W[0], PWr[B_, GS, 0:8], PWi[B_, GS, 0:8], bbr[B_, GS, :], bbi[B_, GS, :], B_)
                                    for g in range(8):
                                        pt, pb = kb.ps()
                                        for ri in range(2):
                                            kb.pe(lambda e, pt=pt, g=g, ri=ri: e.transpose(out=pt[:, ri * 128:(ri + 1) * 128], in_=W[ri][:, g, :, :], identity=ident[:]), r=[b_W[0], b_ident], w=[pb])
                                        kb.act(lambda e, pt=pt, g=g: e.activation(out=WD[:, g, :, :], in_=pt[:, 0:256].rearrange("p (r c) -> p r c", r=2), func=AF.Copy), r=[pb], w=[b_WD])
                                    cmul(W[2], W[3], b_W[2], NPr[F_, GS, 0:8], NPi[F_, GS, 0:8], bbr[F_, GS, :], bbi[F_, GS, :], F_)
                                    cmul(W[2], W[3], b_W[2], PWr[B_, GS, 0:8], PWi[B_, GS, 0:8], bbr[B_, GS, :], bbi[B_, GS, :], B_)
                                    cmul(W[0], W[1], b_W[0], PWr[F_, GS, 0:8], PWi[F_, GS, 0:8], Cr[F_, GS, :], Ci[F_, GS, :], F_, neg_im=True)
                                    cmul(W[0], W[1], b_W[0], NPr[B_, GS, 0:8], NPi[B_, GS, 0:8], Cr[B_, GS, :], Ci[B_, GS, :], B_, neg_im=True)
                                    ktmp = kb.sb(wes, "ktmp", [128, 2, 128]); b_ktmp = Buf("ktmp")
                                    for g in range(8):
                                        pts = []
                                        for d in range(2):
                                            rows = slice(d * 64, (d + 1) * 64)
                                            pt, pb = kb.ps()
                                            pts.append((pt, pb))
                                            kb.pe(lambda e, pt=pt, g=g, rows=rows: e.matmul(pt[:, 0:128], lhsT=W[2][rows, g, :, :], rhs=W[0][rows, g, :, :], start=True, stop=False),
                                                  r=[b_W[2], b_W[0]], w=[pb])
                                            kb.pe(lambda e, pt=pt, g=g, rows=rows: e.matmul(pt[:, 0:128], lhsT=W[3][rows, g, :, :], rhs=W[1][rows, g, :, :], start=False, stop=True),
                                                  r=[b_W[2], b_W[0]], w=[pb])
                                            kb.dve(lambda e, pt=pt, d=d: e.tensor_tensor(out=ktmp[:, d, :], in0=pt[:, 0:128], in1=smk[:, d, :], op=ALU.mult), r=[pb, b_P], w=[b_ktmp])
                                        kb.pool(lambda e: e.tensor_tensor(out=ktmp[:, 0, :], in0=ktmp[:, 0, :], in1=ktmp[:, 1, :], op=ALU.add), r=[b_ktmp], w=[b_ktmp])
                                        kb.dve(lambda e, g=g: e.scalar_tensor_tensor(out=KT[:, g, :], in0=ident[:], scalar=Dg[:, G0 + g:G0 + g + 1], in1=ktmp[:, 0, :], op0=ALU.mult, op1=ALU.add),
                                               r=[b_ktmp, b_P, b_ident], w=[b_KT])
                                    cmul(Vr, Vi, b_V, PWr[F_, GS, 1:9], PWi[F_, GS, 1:9], Cr[F_, GS, :], Ci[F_, GS, :], F_, neg_im=True)
                                    cmul(Vr, Vi, b_V, PWr[B_, GS, 8:0:-1], PWi[B_, GS, 8:0:-1], Cr[B_, GS, :], Ci[B_, GS, :], B_, neg_im=True)
                                    S.barrier()
                                    wes.close()
                                ck = (l, hf)
                                if ck not in s5cache:
                                    s5cache[ck] = {
                                        "KT": (nc.dram_tensor("s5c_KT_%d_%d" % ck, [128, 8, 128], BF16, kind="Internal").ap(), Buf("cKT")),
                                        "WD": (nc.dram_tensor("s5c_WD_%d_%d" % ck, [128, 8, 2, 128], BF16, kind="Internal").ap(), Buf("cWD")),
                                        "Vr": (nc.dram_tensor("s5c_Vr_%d_%d" % ck, [128, 8, 8, 16], F32, kind="Internal").ap(), Buf("cVr")),
                                        "Vi": (nc.dram_tensor("s5c_Vi_%d_%d" % ck, [128, 8, 8, 16], F32, kind="Internal").ap(), Buf("cVi")),
                                        "AA": (nc.dram_tensor("s5c_AA_%d_%d" % ck, [128, 2, 8], F32, kind="Internal").ap(), Buf("cAA")),
                                        "BB": (nc.dram_tensor("s5c_BB_%d_%d" % ck, [128, 2, 8], F32, kind="Internal").ap(), Buf("cBB")),
                                    }
                                cc_ = s5cache[ck]
                                if is_sample:
                                    for nm_, (t_, tb_) in (("KT", (KT, b_KT)), ("WD", (WD, b_WD)), ("Vr", (Vr, b_V)), ("Vi", (Vi, b_V)), ("AA", (AA, b_AB)), ("BB", (BB, b_AB))):
                                        kb.dma(cc_[nm_][0], t_[:], r=[tb_], w=[cc_[nm_][1]])
                                else:
                                    b_Vi2 = Buf("Vi2"); b_BB2 = Buf("BB2")
                                    kb.dma(KT[:], cc_["KT"][0], r=[cc_["KT"][1]], w=[b_KT])
                                    kb.dma(WD[:], cc_["WD"][0], r=[cc_["WD"][1]], w=[b_WD])
                                    kb.dma(Vr[:], cc_["Vr"][0], r=[cc_["Vr"][1]], w=[b_V])
                                    kb.dma(Vi[:], cc_["Vi"][0], r=[cc_["Vi"][1]], w=[b_Vi2])
                                    kb.dma(AA[:], cc_["AA"][0], r=[cc_["AA"][1]], w=[b_AB])
                                    kb.dma(BB[:], cc_["BB"][0], r=[cc_["BB"][1]], w=[b_BB2])
                                    vop(lambda e: e.tensor_copy(out=AA[:, 0, 0:1], in_=AA[:, 0, 0:1]), [b_AB, b_BB2, b_Vi2, b_V], [b_AB, b_V])
                                if stage == 53:
                                    return
                                SS = kb.sb(hes, "SS", [128, NST + 1 + (16 if NST >= 64 else 0), nseq, 2, 8]); b_SS = Buf("SS")
                                vop(lambda e: e.memset(SS[:, 0, :, :, :], 0.0), (), [b_SS])
                                if is_sample:
                                    b_si = [Buf("si%d" % i) for i in range(4)]
                                    for ri, src in enumerate((s5st_re_d, s5st_im_d)):
                                        for d in range(2):
                                            kb.dma(SS[d * 64:(d + 1) * 64, 0, 0, ri, :], src[l, d, G0:G0 + 8].rearrange("g n -> n g"), r=[b_SS], w=[b_si[ri * 2 + d]], slow=True)
                                    vop(lambda e: e.tensor_copy(out=SS[:, 0, 0, 0, 0:1], in_=SS[:, 0, 0, 0, 0:1]), b_si, [b_SS])
                                for g in range(8):
                                    for ri in range(2):
                                        pt, pb = kb.ps()
                                        kb.pe(lambda e, pt=pt, g=g, ri=ri: e.matmul(pt[:, 0:NCHT], lhsT=WD[:, g, ri, :], rhs=U[:, G0 + g, :], start=True, stop=True), r=[b_WD, b_U], w=[pb])
                                        for sq_ in range(nseq):
                                            kb.act(lambda e, pt=pt, g=g, ri=ri, sq_=sq_: e.activation(out=SS[0:64, 1:NST + 1, sq_, ri, g], in_=pt[0:64, sq_ * NST:(sq_ + 1) * NST], func=AF.Copy),
                                                   r=[pb], w=[b_SS])
                                            kb.dve(lambda e, pt=pt, g=g, ri=ri, sq_=sq_: e.tensor_copy(out=SS[64:128, NST:0:-1, sq_, ri, g], in_=pt[64:128, sq_ * NST:(sq_ + 1) * NST]),
                                                   r=[pb], w=[b_SS])
                                if stage == 54:
                                    S.barrier()
                                    return
                                if NST < 64:
                                    ts_ = kb.sb(hes, "ts_", [128, nseq, 2, 8]); tu_ = kb.sb(hes, "tu_", [128, nseq, 2, 8]); b_ts = Buf("ts")
                                    AAb = AA[:].unsqueeze(1).broadcast_to([128, nseq, 2, 8])
                                    BBb = BB[:].unsqueeze(1).broadcast_to([128, nseq, 2, 8])
                                    for s_ in range(NST):
                                        vop(lambda e, s_=s_: e.tensor_tensor(out=ts_[:], in0=SS[:, s_, :, :, :], in1=AAb, op=ALU.mult), [b_SS, b_AB], [b_ts])
                                        vop(lambda e, s_=s_: e.tensor_tensor(out=tu_[:], in0=SS[:, s_, :, ::-1, :], in1=BBb, op=ALU.mult), [b_SS, b_AB], [b_ts])
                                        vop(lambda e: e.tensor_tensor(out=ts_[:], in0=ts_[:], in1=tu_[:], op=ALU.add), [b_ts], [b_ts])
                                        vop(lambda e, s_=s_: e.tensor_tensor(out=SS[:, s_ + 1, :, :, :], in0=SS[:, s_ + 1, :, :, :], in1=ts_[:], op=ALU.add), [b_ts, b_SS], [b_SS])
                                else:
                                    R_ = 16
                                    NBk = NST // R_
                                    SSb = SS[:, 1:1 + (NBk + 1) * R_, 0, :, :].rearrange("p (b r) i g -> p b r i g", r=R_)
                                    tl = kb.sb(hes, "tl", [128, NBk + 1, 2, 8]); tl2 = kb.sb(hes, "tl2", [128, NBk + 1, 2, 8]); b_tl = Buf("tl")
                                    CC = kb.sb(hes, "CC", [128, NBk + 1, 2, 8]); b_CC = Buf("CC")
                                    PA = kb.sb(hes, "PA", [128, R_, 2, 8]); PB = kb.sb(hes, "PB", [128, R_, 2, 8]); b_PAB = Buf("PAB")
                                    tf = [kb.sb(hes, "tf%d" % i, [128, R_, 2, 8]) for i in range(4)]
                                    b_tf = [Buf("tf0"), Buf("tf1")]
                                    AAl = AA[:].unsqueeze(1).broadcast_to([128, NBk + 1, 2, 8])
                                    BBl = BB[:].unsqueeze(1).broadcast_to([128, NBk + 1, 2, 8])
                                    vop(lambda e: e.memset(SSb[:, NBk, :, :, :], 0.0), [b_SS], [b_SS])
                                    vop(lambda e: e.tensor_copy(out=SSb[:, NBk, 0, 0, :], in_=AA[:, 0, :]), [b_AB, b_SS], [b_SS])
                                    vop(lambda e: e.tensor_copy(out=SSb[:, NBk, 0, 1, :], in_=BB[:, 1, :]), [b_AB, b_SS], [b_SS])
                                    for r in range(1, R_):
                                        vop(lambda e, r=r: e.tensor_tensor(out=tl[:], in0=SSb[:, :, r - 1, :, :], in1=AAl, op=ALU.mult), [b_SS, b_AB], [b_tl])
                                        vop(lambda e, r=r: e.tensor_tensor(out=tl2[:], in0=SSb[:, :, r - 1, ::-1, :], in1=BBl, op=ALU.mult), [b_SS, b_AB], [b_tl])
                                        vop(lambda e: e.tensor_tensor(out=tl[:], in0=tl[:], in1=tl2[:], op=ALU.add), [b_tl], [b_tl])
                                        vop(lambda e, r=r: e.tensor_tensor(out=SSb[:, :, r, :, :], in0=SSb[:, :, r, :, :], in1=tl[:], op=ALU.add), [b_tl, b_SS], [b_SS])
                                    vop(lambda e: e.tensor_copy(out=PA[:, :, 0, :], in_=SSb[:, NBk, :, 0, :]), [b_SS], [b_PAB])
                                    vop(lambda e: e.tensor_copy(out=PA[:, :, 1, :], in_=SSb[:, NBk, :, 0, :]), [b_SS], [b_PAB])
                                    vop(lambda e: e.tensor_scalar(out=PB[:, :, 0, :], in0=SSb[:, NBk, :, 1, :], scalar1=-1.0, scalar2=None, op0=ALU.mult), [b_SS], [b_PAB])
                                    vop(lambda e: e.tensor_copy(out=PB[:, :, 1, :], in_=SSb[:, NBk, :, 1, :]), [b_SS], [b_PAB])
                                    vop(lambda e: e.tensor_copy(out=CC[:, 0, :, :], in_=SS[:, 0, 0, :, :]), [b_SS], [b_CC])
                                    for b_ in range(NBk):
                                        vop(lambda e, b_=b_: e.tensor_tensor(out=tf[0][:, 0, :, :], in0=CC[:, b_, :, :], in1=PA[:, R_ - 1, :, :], op=ALU.mult), [b_CC, b_PAB], [b_tf[0]])
                                        vop(lambda e, b_=b_: e.tensor_tensor(out=tf[1][:, 0, :, :], in0=CC[:, b_, ::-1, :], in1=PB[:, R_ - 1, :, :], op=ALU.mult), [b_CC, b_PAB], [b_tf[0]])
                                        vop(lambda e: e.tensor_tensor(out=tf[0][:, 0, :, :], in0=tf[0][:, 0, :, :], in1=tf[1][:, 0, :, :], op=ALU.add), [b_tf[0]], [b_tf[0]])
                                        vop(lambda e, b_=b_: e.tensor_tensor(out=CC[:, b_ + 1, :, :], in0=SSb[:, b_, R_ - 1, :, :], in1=tf[0][:, 0, :, :], op=ALU.add), [b_tf[0], b_SS], [b_CC])
                                    for b_ in range(NBk):
                                        k2 = b_ % 2
                                        ta_, tb_ = tf[2 * k2], tf[2 * k2 + 1]
                                        Cb = CC[:, b_, :, :].unsqueeze(1).broadcast_to([128, R_, 2, 8])
                                        Cs = CC[:, b_, ::-1, :].unsqueeze(1).broadcast_to([128, R_, 2, 8])
                                        vop(lambda e, ta_=ta_, Cb=Cb: e.tensor_tensor(out=ta_[:], in0=PA[:], in1=Cb, op=ALU.mult), [b_CC, b_PAB], [b_tf[k2]])
                                        vop(lambda e, tb_=tb_, Cs=Cs: e.tensor_tensor(out=tb_[:], in0=PB[:], in1=Cs, op=ALU.mult), [b_CC, b_PAB], [b_tf[k2]])
                                        vop(lambda e, ta_=ta_, tb_=tb_: e.tensor_tensor(out=ta_[:], in0=ta_[:], in1=tb_[:], op=ALU.add), [b_tf[k2]], [b_tf[k2]])
                                        vop(lambda e, ta_=ta_, b_=b_: e.tensor_tensor(out=SSb[:, b_, :, :, :], in0=SSb[:, b_, :, :, :], in1=ta_[:], op=ALU.add), [b_tf[k2], b_SS], [b_SS])
                                if not is_sample:
                                    for sq_ in range(nseq):
                                        for ri, dst in enumerate((s5o_re, s5o_im)):
                                            for d in range(2):
                                                kb.dma(dst[sq_, l, d, G0:G0 + 8].rearrange("g n -> n g"), SS[d * 64:(d + 1) * 64, NST, sq_, ri, :], r=[b_SS], is_out=True, sembuf=b_SS, slow=True)
                                if stage == 55:
                                    S.barrier()
                                    return
                                yin = [kb.sb(hes, "yin%d" % i, [128, NCHT]) for i in range(2)]
                                b_yin = [Buf("yin0"), Buf("yin1")]
                                gA = kb.sb(hes, "gA", [128, 512]); b_gA = Buf("gA")
                                gB = kb.sb(hes, "gB", [128, 512]); b_gB = Buf("gB")
                                yg4 = kb.sb(hes, "yg4", [128, 4, NCHT]); b_yg4 = Buf("yg4")
                                SSc = kb.sb(hes, "SSc", [128, nseq, 2, 4, NST]); b_SSc = Buf("SSc")
                                for g4 in range(2):
                                    for sq_ in range(nseq):
                                        kb.act(lambda e, g4=g4, sq_=sq_: e.activation(out=SSc[0:64, sq_, :, :, :], in_=SS[0:64, 0:NST, sq_, :, g4 * 4:(g4 + 1) * 4].rearrange("p s r g -> p r g s"),
                                                                                func=AF.Copy), r=[b_SS], w=[b_SSc])
                                        kb.dve(lambda e, g4=g4, sq_=sq_: e.tensor_copy(out=SSc[64:128, sq_, :, :, :], in_=SS[64:128, NST - 1::-1, sq_, :, g4 * 4:(g4 + 1) * 4].rearrange("p s r g -> p r g s")),
                                               r=[b_SS], w=[b_SSc])
                                    for gg in range(4):
                                        g = g4 * 4 + gg
                                        yi = g % 2
                                        p1, pb1 = kb.ps()
                                        kb.pe(lambda e, p1=p1, g=g: e.matmul(p1[:, 0:NCHT], lhsT=KT[:, g, :], rhs=U[:, G0 + g, :], start=True, stop=True), r=[b_KT, b_U], w=[pb1])
                                        p2, pb2 = kb.ps()
                                        p3, pb3 = kb.ps()
                                        for sq_ in range(nseq):
                                            for d in range(2):
                                                rows = slice(d * 64, (d + 1) * 64)
                                                pp, ppb = (p2, pb2) if d == 0 else (p3, pb3)
                                                for ri, Vt in enumerate((Vr, Vi)):
                                                    rhs = SSc[rows, sq_, ri, gg, :]
                                                    kb.pe(lambda e, pp=pp, g=g, rows=rows, Vt=Vt, rhs=rhs, sq_=sq_, ri=ri: e.matmul(pp[:, sq_ * NST:(sq_ + 1) * NST], lhsT=Vt[rows, g, :, :], rhs=rhs,
                                                                                                                   start=(ri == 0), stop=(ri == 1)), r=[b_V, b_SSc], w=[ppb])
                                        kb.act(lambda e, p2=p2, yi=yi: e.activation(out=yin[yi][:], in_=p2[:, 0:NCHT], func=AF.Copy), r=[pb2], w=[b_yin[yi]])
                                        kb.dve(lambda e, p3=p3, yi=yi: e.tensor_tensor(out=yin[yi][:], in0=p3[:, 0:NCHT], in1=yin[yi][:], op=ALU.add), r=[pb3, b_yin[yi]], w=[b_yin[yi]])
                                        kb.dve(lambda e, p1=p1, yi=yi, gg=gg: e.tensor_tensor(out=yg4[:, gg, :], in0=p1[:, 0:NCHT], in1=yin[yi][:], op=ALU.add), r=[pb1, b_yin[yi]], w=[b_yg4])
                                    if stage == 56:
                                        continue
                                    for ctile in range(NCT):
                                        pt, pb = kb.ps()
                                        for gg in range(4):
                                            kb.pe(lambda e, pt=pt, gg=gg, ctile=ctile: e.transpose(out=pt[0:CW, gg * 128:(gg + 1) * 128], in_=yg4[:, gg, ctile * CW:(ctile + 1) * CW], identity=ident[:]),
                                                  r=[b_yg4, b_ident], w=[pb])
                                        gcol = (G0 + g4 * 4) * 16
                                        src = pt[0:CW, :].rearrange("p (g j q) -> p g j q", g=4, j=8)
                                        dsty = Yx[0:CW, ctile, :, gcol:gcol + 64].rearrange("p j (g q) -> p g j q", g=4)
                                        gelu_psum(src, pb, dsty, b_Yx, gA[0:CW, :].rearrange("p (g j q) -> p g j q", g=4, j=8), b_gA,
                                                  gB[0:CW, :].rearrange("p (g j q) -> p g j q", g=4, j=8), b_gB)
                                S.barrier()
                        if stage in (56, 57):
                            return

                        with ExitStack() as qes:
                            y5 = kb.sb(qes, "y5", [128, 2, NT], BF16); b_y5 = Buf("y5")
                            prod = kb.sb(qes, "prod", [128, 2, NT]); b_prod = Buf("prod")
                            cat = kb.sb(qes, "cata", [128, 2, NT], BF16); b_cat = Buf("cata")
                            sgt = kb.sb(qes, "sgt", [128, 512]); b_sgt = Buf("sgt")
                            for ctile in range(NCT):
                                for ct in range(2):
                                    ptb, pb = kb.ps()
                                    ptv = ptb.bitcast(BF16)
                                    for j in range(8):
                                        kb.pe(lambda e, ptv=ptv, j=j, ct=ct, ctile=ctile: e.transpose(out=ptv[:, j * CW:(j + 1) * CW], in_=Yx[0:CW, ctile, j, ct * 128:(ct + 1) * 128], identity=identb[0:CW, 0:CW]),
                                              r=[b_Yx, b_ident], w=[pb])
                                    kb.act(lambda e, ptv=ptv, ct=ct, ctile=ctile: e.activation(out=y5[:, ct, ctile * 8 * CW:(ctile + 1) * 8 * CW], in_=ptv[:, 0:8 * CW], func=AF.Copy), r=[pb], w=[b_y5])
                            wv, bw = wload(wglu_d[l].rearrange("(k p) c -> p k c", p=128), 2, 256)
                            for ct in range(2):
                                for blk in range(NB):
                                    sl = slice(blk * 512, (blk + 1) * 512)
                                    pt, pb = kb.ps()
                                    for k in range(2):
                                        kb.pe(lambda e, pt=pt, k=k, ct=ct, sl=sl, wv=wv: e.matmul(pt[:], lhsT=wv[:, k, ct * 128:(ct + 1) * 128], rhs=y5[:, k, sl], start=(k == 0), stop=(k == 1)),
                                              r=[bw, b_y5], w=[pb])
                                    kb.act(lambda e, pt=pt: e.activation(out=sgt[:], in_=pt[:], func=AF.Sigmoid), r=[pb], w=[b_sgt])
                                    kb.dve(lambda e, ct=ct, sl=sl: e.tensor_tensor(out=prod[:, ct, sl], in0=y5[:, ct, sl], in1=sgt[:], op=ALU.mult), r=[b_y5, b_sgt], w=[b_prod])
                            fm_norm(prod, b_prod, gnT[:, l, 0:2], cat, b_cat, 256, ones_bf, False, qes)

                            def xv(dm, blk):
                                if NCT == 2:
                                    ctile, jb = blk // 2, blk % 2
                                    v = x[:, dm, ctile * 1024:(ctile + 1) * 1024].rearrange("p (c j) -> p j c", j=8)
                                    return v[:, jb * 4:(jb + 1) * 4, :]
                                return x[:, dm, :].rearrange("p (c j) -> p j c", j=8)
                            wout_add(l, 0, cat, b_cat, xv=xv)
                            S.barrier()

                for l in range(DEPTH if stage < 40 else 1):
                    norm_to_h(l, 0)
                    if stage == 40:
                        mixer_hgrn(l)
                        continue
                    if stage >= 50:
                        mixer_s5(l)
                        continue
                    if stage >= 2:
                        mixer_gmlp(l)
                    if stage >= 3:
                        mixer_fnet(l)
                    if stage >= 4:
                        mixer_hgrn(l)
                    if stage >= 5:
                        mixer_s5(l)
                    norm_to_h(l, 1)
                    ffn(l)

                with ExitStack() as oes:
                    gB = kb.sb(oes, "gB", [128, D]); b_gB = Buf("gB")
                    kb.dma(gB[:], fng.rearrange("(o n) -> o n", o=1).broadcast_to([128, D]), w=[b_gB])
                    yo = [kb.sb(oes, "yo%d" % i, [128, D]) for i in range(2)]
                    b_yo = [Buf("yo0"), Buf("yo1")]
                    junk = kb.sb(oes, "junk", [128, 512]); b_junk = Buf("junk")
                    ssq = [kb.sb(oes, "ssq%d" % i, [128, 2]) for i in range(2)]
                    b_ssq = [Buf("ssq0"), Buf("ssq1")]
                    for tt in range(NT // 128):
                        s = tt % 2
                        pts = []
                        for kq in range(2):
                            pt, pb = kb.ps()
                            pts.append((pt, pb))
                            for kk in range(4):
                                k = kq * 4 + kk
                                kb.pe(lambda e, k=k, kk=kk, pt=pt, tt=tt: e.transpose(out=pt[:, kk * 128:(kk + 1) * 128], in_=x[:, k, tt * 128:(tt + 1) * 128], identity=ident[:]),
                                      r=[b_x, b_ident], w=[pb])
                            kb.act(lambda e, pt=pt, s=s, kq=kq: e.activation(out=junk[:], in_=pt[:], func=AF.Square, accum_out=ssq[s][:, kq:kq + 1]), r=[pb], w=[b_junk, b_ssq[s]])
                        kb.dve(lambda e, s=s: e.tensor_tensor(out=ssq[s][:, 0:1], in0=ssq[s][:, 0:1], in1=ssq[s][:, 1:2], op=ALU.add), r=[b_ssq[s]], w=[b_ssq[s]])
                        kb.act(lambda e, s=s: e.activation(out=ssq[s][:, 0:1], in_=ssq[s][:, 0:1], func=AF.Sqrt, scale=1.0 / D, bias=EPS), r=[b_ssq[s]], w=[b_ssq[s]])
                        kb.dve(lambda e, s=s: e.reciprocal(out=ssq[s][:, 0:1], in_=ssq[s][:, 0:1]), r=[b_ssq[s]], w=[b_ssq[s]])
                        for kq in range(2):
                            pt, pb = pts[kq]
                            kb.dve(lambda e, pt=pt, s=s, kq=kq: e.scalar_tensor_tensor(out=yo[s][:, kq * 512:(kq + 1) * 512], in0=pt[:], scalar=ssq[s][:, 0:1],
                                                                                in1=gB[:, kq * 512:(kq + 1) * 512], op0=ALU.mult, op1=ALU.mult),
                                   r=[pb, b_ssq[s], b_gB], w=[b_yo[s]])
                        kb.dma(y_dram[tt * 128:(tt + 1) * 128, :], yo[s][:], r=[b_yo[s]], is_out=True, sembuf=b_yo[s])
                    S.barrier()

        run_group("s", xs_d, ys_d, 2048, 2048, 1, True)
        if stage < 40:
            run_group("p", xp_d, yp_d, 512, 256, 0, False)
        S.finish()
        S.emit()
    return nc


def make_consts():
    c = {"ident": np.eye(128, dtype=np.float32)}
    n = np.arange(64)
    ang = 2 * np.pi * np.outer(n, n) / 64
    C64 = np.cos(ang) / 8.0
    S64 = np.sin(ang) / 8.0
    bd = np.zeros((256, 512), np.float64)
    for hd in range(4):
        bd[hd * 64:(hd + 1) * 64, hd * 64:(hd + 1) * 64] = C64
        bd[hd * 64:(hd + 1) * 64, 256 + hd * 64:256 + (hd + 1) * 64] = S64
    c["bdcs"] = bd.astype(ml_dtypes.bfloat16)
    i = np.arange(128)
    same = (i[:, None] // 32) == (i[None, :] // 32)
    mF = (same & (i[:, None] <= i[None, :])).astype(np.float32)
    mB = (same & (i[:, None] >= i[None, :])).astype(np.float32)
    bdm = ((i[:, None] // 64) == (i[None, :] // 64)).astype(np.float32)
    c["hmask"] = np.stack([mF, mB, bdm], axis=0)
    jj = i // 16
    c["s5mask"] = np.stack([(jj[None, :] >= jj[:, None]), (jj[:, None] >= jj[None, :])], axis=0).astype(np.float32)
    c["cmask"] = ((i[:, None] // 32) == np.arange(4)[None, :]).astype(np.float32)
    sm = np.ones((128, 2048), np.float32)
    sm[:, ::32] = 0.0
    c["smask"] = sm.astype(ml_dtypes.bfloat16)
    for nm, T in (("dft_s", 2048), ("dft_p", 256)):
        t = np.arange(T)
        a = 2 * np.pi * (np.outer(t, t) % T) / T
        m = np.stack([np.cos(a), -np.sin(a)], axis=0) / np.sqrt(T)
        c[nm] = m.astype(ml_dtypes.bfloat16)
    return c


_CACHE = {}
_DBG = {}


def kernel(**inputs):
    inp = {k: np.ascontiguousarray(np.asarray(v)) for k, v in inputs.items()}
    if "nc" not in _CACHE:
        _CACHE["nc"] = build_program()
    nc = _CACHE["nc"]
    consts = make_consts()
    shared = {k: inp[k] for k in ["w_ada", "b_ada", "norm1_g", "norm2_g", "w_in", "w_out", "ffn_w_up", "ffn_conv_w", "ffn_conv_b",
                                  "ffn_w_down", "final_norm_g", "grp_norm_g", "gm_norm_g", "gm_ws", "gm_bs", "fn_w", "hg_lb_logits", "s5_lam_re", "s5_lam_im", "s5_log_dt", "s5_b_re", "s5_b_im", "s5_c_re", "s5_c_im", "s5_d", "s5_w_glu"]}
    in_maps = []
    for c in range(NCORES):
        b = c // 4
        m = dict(shared)
        m.update(consts)
        m["xs"] = inp["x_sample"][b]
        m["xp"] = inp["x_prompt"][2 * c:2 * c + 2].reshape(512, D)
        m["cvec"] = np.stack([inp["c_ctx"], inp["c"][b]], axis=0)
        m["hg_state"] = inp["state_hgrn"][b]
        m["s5st_re"] = inp["state_s5_re"][b]
        m["s5st_im"] = inp["state_s5_im"][b]
        in_maps.append(m)
    res = run_bass_kernel_spmd(nc, in_maps, core_ids=list(range(NCORES)))
    r = res.results
    y_prompt = np.concatenate([r[c]["yp"].reshape(2, 256, D) for c in range(NCORES)], axis=0)
    y_sample = np.stack([r[0]["ys"], r[4]["ys"]], axis=0)
    st_re = np.concatenate([r[c]["s5o_re"] for c in range(NCORES)], axis=0)
    st_im = np.concatenate([r[c]["s5o_im"] for c in range(NCORES)], axis=0)
    st_hg = np.concatenate([r[c]["st_hg"] for c in range(NCORES)], axis=0)
    return y_prompt, y_sample, st_re, st_im, st_hg
```

```python
import numpy as np
import ml_dtypes
from contextlib import ExitStack
import concourse.bass as bass
import concourse.mybir as mybir
from concourse.bass_utils import run_bass_kernel_spmd

F32 = mybir.dt.float32
BF16 = mybir.dt.bfloat16
I32 = mybir.dt.int32
AF = mybir.ActivationFunctionType
ALU = mybir.AluOpType

D = 1024
DEPTH = 2
DIN = 2304
DFF = 2816
EPS = 1e-6
NCORES = 8
STAGE = 5


class Buf:
    __slots__ = ("name", "w", "r", "dsem", "dcnt", "fence", "fdone")
    current_fence = None

    def __init__(self, name=""):
        self.name = name
        self.w = None
        self.r = []
        self.dsem = None
        self.dcnt = 0
        self.fence = Buf.current_fence
        self.fdone = set()


class _Rec:
    def __init__(self):
        self.call = None

    def __getattr__(self, name):
        def f(*a, **k):
            self.call = (name, a, k)
            return self
        return f


def _bind(fn):
    r = _Rec()
    fn(r)
    assert r.call is not None
    return r.call


class Sched:
    NAMES = ["pe", "act", "dve", "pool", "sp"]

    def __init__(self, nc, es):
        self.nc = nc
        self.es = es
        self.q = {e: [] for e in self.NAMES}
        self.cnt = {e: 0 for e in self.NAMES}
        self.esem = {e: es.enter_context(nc.semaphore("sem_" + e)) for e in self.NAMES}
        self.waited = {e: {} for e in self.NAMES}
        self.out_events = []
        self.ndsem = 0
        self.dma_sems = []
        self.free_slots = []
        self.all_slots = []
        self.active = []

    def _wait(self, eng, ev):
        src, sem, val = ev
        k = id(sem)
        if self.waited[eng].get(k, 0) >= val:
            return
        self.waited[eng][k] = val
        self.q[eng].append(("wait", sem, val))

    def _collect(self, eng, reads, writes):
        for b in list(reads) + list(writes):
            if b.fence is not None and eng not in b.fdone:
                b.fdone.add(eng)
                for ev in b.fence:
                    if ev[0] != eng:
                        self._wait(eng, ev)
        for b in reads:
            if b.w is not None:
                if b.w[0] == eng and eng == "pe":
                    continue
                self._wait(eng, b.w)
        for b in writes:
            if b.w is not None:
                if not (b.w[0] == eng):
                    self._wait(eng, b.w)
            for ev in b.r:
                if ev[0] == eng:
                    continue
                self._wait(eng, ev)

    def op(self, eng, fn, reads=(), writes=()):
        self._collect(eng, reads, writes)
        self.cnt[eng] += 1
        ev = (eng, self.esem[eng], self.cnt[eng])
        self.q[eng].append(("op", _bind(fn), self.esem[eng], 1))
        for b in writes:
            b.w = ev
            b.r = []
        for b in reads:
            b.r.append(ev)
        return ev

    def dma(self, fn, reads=(), writes=(), sembuf=None, eng="sp", is_out=False):
        self._collect(eng, reads, writes)
        sb = sembuf if sembuf is not None else (writes[0] if writes else reads[0])
        if sb.dsem is None:
            if self.free_slots:
                slot = self.free_slots.pop()
                if slot[1] > 0:
                    self._wait(eng, ("dma", slot[0], slot[1]))
            else:
                slot = [self.es.enter_context(self.nc.semaphore("dsem%d" % self.ndsem)), 0]
                self.ndsem += 1
                self.all_slots.append(slot)
            sb.dsem = slot
            self.active.append(sb)
        slot = sb.dsem
        slot[1] += 16
        ev = ("dma", slot[0], slot[1])
        self.q[eng].append(("op", _bind(fn), slot[0], 16))
        for b in writes:
            b.w = ev
            b.r = []
        for b in reads:
            b.r.append(ev)
        return ev

    def barrier(self, force=False):
        evs = []
        for o in self.NAMES:
            if self.cnt[o] > 0:
                evs.append((o, self.esem[o], self.cnt[o]))
        for slot in self.all_slots:
            if slot[1] > 0:
                evs.append(("dma", slot[0], slot[1]))
        Buf.current_fence = evs
        if force:
            for e in self.NAMES:
                for ev in evs:
                    if ev[0] != e:
                        self._wait(e, ev)
        for sb in self.active:
            self.free_slots.append(sb.dsem)
            sb.dsem = None
        self.active = []

    def finish(self):
        self.barrier(force=True)

    def emit(self):
        nc = self.nc
        engmap = {"pe": "tensor", "act": "scalar", "dve": "vector", "pool": "gpsimd", "sp": "sync"}
        with nc.Block() as block:
            for name in self.NAMES:
                items = self.q[name]

                def body(e, items=items):
                    for it in items:
                        if it[0] == "wait":
                            e.wait_ge(it[1], it[2])
                        else:
                            nm_, a_, k_ = it[1]
                            ins = getattr(e, nm_)(*a_, **k_)
                            ins.then_inc(it[2], it[3])
                getattr(block, engmap[name])(body)


class KB:
    def __init__(self, nc, es):
        self.nc = nc
        self.es = es
        self.S = Sched(nc, es)
        self.din = {}
        self.dout = {}
        self.nps = 0
        self.psb = []
        for i in range(8):
            t = es.enter_context(nc.psum_tensor("psb%d" % i, [128, 512], F32))
            self.psb.append((t, Buf("ps%d" % i)))
        self.uid = 0

    def inp(self, name, shape, dtype=F32):
        self.din[name] = self.nc.dram_tensor(name, list(shape), dtype, kind="ExternalInput").ap()
        return self.din[name]

    def outp(self, name, shape, dtype=F32):
        self.dout[name] = self.nc.dram_tensor(name, list(shape), dtype, kind="ExternalOutput").ap()
        return self.dout[name]

    def sb(self, es, name, shape, dtype=F32):
        self.uid += 1
        return es.enter_context(self.nc.sbuf_tensor("%s_%d" % (name, self.uid), list(shape), dtype))

    def ps(self):
        t, b = self.psb[self.nps % 8]
        self.nps += 1
        return t, b

    def dbg(self, name, ap, shape, dtype, r=()):
        if not getattr(self, "debug", False):
            return
        d = self.nc.dram_tensor("dbg_" + name, list(shape), dtype, kind="ExternalOutput").ap()
        self.dbgs = getattr(self, "dbgs", []) + ["dbg_" + name]
        bb = Buf("dbg")
        self.S.dma(lambda e: e.dma_start(out=d, in_=ap), list(r), [bb])

    def pe(self, fn, r=(), w=()):
        return self.S.op("pe", fn, r, w)

    def act(self, fn, r=(), w=()):
        return self.S.op("act", fn, r, w)

    def dve(self, fn, r=(), w=()):
        return self.S.op("dve", fn, r, w)

    def pool(self, fn, r=(), w=()):
        return self.S.op("pool", fn, r, w)

    def dma(self, out, in_, r=(), w=(), eng="sp", is_out=False, sembuf=None, slow=False):
        if slow:
            fn = lambda e: e.dma_start(out=out, in_=in_, allow_slow_non_contiguous=True)
        else:
            fn = lambda e: e.dma_start(out=out, in_=in_)
        return self.S.dma(fn, r, w, sembuf=sembuf, eng=eng, is_out=is_out)


def build_program(stage=STAGE):
    nc = bass.Bass("TRN2", target_bir_lowering=False)
    Buf.current_fence = None
    with ExitStack() as es:
        kb = KB(nc, es)
        kb.debug = (stage >= 40)
        _DBG['kb'] = kb
        S = kb.S
        xs_d = kb.inp("xs", [2048, D])
        xp_d = kb.inp("xp", [512, D])
        cv_d = kb.inp("cvec", [2, D])
        w_ada = kb.inp("w_ada", [DEPTH, D, 6 * D])
        b_ada = kb.inp("b_ada", [DEPTH, 6 * D])
        norm1_g = kb.inp("norm1_g", [DEPTH, D])
        norm2_g = kb.inp("norm2_g", [DEPTH, D])
        w_in = kb.inp("w_in", [DEPTH, D, DIN])
        w_out = kb.inp("w_out", [DEPTH, D, D])
        w_up = kb.inp("ffn_w_up", [DEPTH, D, 2 * DFF])
        cw_d = kb.inp("ffn_conv_w", [DEPTH, 3, 2 * DFF])
        cb_d = kb.inp("ffn_conv_b", [DEPTH, 2 * DFF])
        w_dn = kb.inp("ffn_w_down", [DEPTH, DFF, D])
        fng = kb.inp("final_norm_g", [D])
        ident_d = kb.inp("ident", [128, 128])
        grp_g = kb.inp("grp_norm_g", [DEPTH, D])
        gm_ng = kb.inp("gm_norm_g", [DEPTH, 256])
        gm_ws = kb.inp("gm_ws", [DEPTH, 4, 128, 128])
        gm_bs = kb.inp("gm_bs", [DEPTH, 4, 128])
        fn_w = kb.inp("fn_w", [DEPTH, 256, 256])
        bdcs_d = kb.inp("bdcs", [256, 512], BF16)
        hmask_d = kb.inp("hmask", [3, 128, 128])
        lamre_d = kb.inp("s5_lam_re", [DEPTH, 2, 16, 64])
        lamim_d = kb.inp("s5_lam_im", [DEPTH, 2, 16, 64])
        logdt_d = kb.inp("s5_log_dt", [DEPTH, 2, 16])
        s5b_re_d = kb.inp("s5_b_re", [DEPTH, 2, 16, 64, 16])
        s5b_im_d = kb.inp("s5_b_im", [DEPTH, 2, 16, 64, 16])
        s5c_re_d = kb.inp("s5_c_re", [DEPTH, 2, 16, 16, 64])
        s5c_im_d = kb.inp("s5_c_im", [DEPTH, 2, 16, 16, 64])
        s5d_d = kb.inp("s5_d", [DEPTH, 256])
        wglu_d = kb.inp("s5_w_glu", [DEPTH, 256, 256])
        s5mask_d = kb.inp("s5mask", [2, 128, 128])
        s5st_re_d = kb.inp("s5st_re", [DEPTH, 2, 16, 64])
        s5st_im_d = kb.inp("s5st_im", [DEPTH, 2, 16, 64])
        s5o_re = kb.outp("s5o_re", [2, DEPTH, 2, 16, 64])
        s5o_im = kb.outp("s5o_im", [2, DEPTH, 2, 16, 64])
        smask_d = kb.inp("smask", [128, 2048], BF16)
        cmask_d = kb.inp("cmask", [128, 4])
        hglog = kb.inp("hg_lb_logits", [DEPTH, 2, 256])
        hgst_d = kb.inp("hg_state", [DEPTH, 2, 4, 64, 64])
        sthg_o = kb.outp("st_hg", [2, DEPTH, 2, 4, 64, 64])
        dft_s = kb.inp("dft_s", [2, 2048, 2048], BF16)
        dft_p = kb.inp("dft_p", [2, 256, 256], BF16)
        ys_d = kb.outp("ys", [2048, D])
        yp_d = kb.outp("yp", [512, D])

        ident = kb.sb(es, "ident", [128, 128]); b_ident = Buf("ident")
        ones_bf = kb.sb(es, "ones", [128, 128], BF16); b_ones = Buf("ones")
        modT = kb.sb(es, "modT", [128, DEPTH, 2, 48]); b_mod = Buf("modT")
        n1g = kb.sb(es, "n1g", [128, DEPTH, 8]); n2g = kb.sb(es, "n2g", [128, DEPTH, 8])
        b_cv = Buf("chanvecs")
        gnT = kb.sb(es, "gnT", [128, DEPTH, 8])
        lbT = kb.sb(es, "lbT", [128, DEPTH, 2, 2]); omlT = kb.sb(es, "omlT", [128, DEPTH, 2, 2]); nomlT = kb.sb(es, "nomlT", [128, DEPTH, 2, 2])
        b_lb = Buf("lb")
        identb = kb.sb(es, "identb", [128, 128], BF16)
        cwT = kb.sb(es, "cwT", [128, DEPTH, 3, 44]); cbT = kb.sb(es, "cbT", [128, DEPTH, 44])
        Amod = kb.sb(es, "Amod", [128, DEPTH, 2, 2, 8])
        b_A = Buf("Amod")
        NWS = 3
        wbf = [kb.sb(es, "wbf%d" % i, [128, 2816], BF16) for i in range(NWS)]
        b_wbf = [Buf("wbf%d" % i) for i in range(NWS)]
        b_wbf2 = [Buf("wbfb%d" % i) for i in range(NWS)]
        wctr = [0]

        kb.dma(ident[:], ident_d, w=[b_ident])
        kb.dve(lambda e: e.memset(ones_bf[:], 1.0), w=[b_ones])
        kb.dve(lambda e: e.tensor_copy(out=identb[:], in_=ident[:]), r=[b_ident], w=[b_ident])

        def wload(src_ap, kt, ncol):
            s = wctr[0] % NWS
            wctr[0] += 1
            n = kt * ncol
            dv = wbf[s][:, 0:n].rearrange("p (k c) -> p k c", k=kt)
            kb.dma(dv, src_ap, w=[b_wbf[s], b_wbf2[s]], eng="pool")
            return dv, b_wbf[s]

        def load_chanvec(dst_ap, src_rows_ap, nt, tmp_es, extra_r=(), wbuf=None):
            st = kb.sb(tmp_es, "cvst", [nt, 128]); bst = Buf("cvst")
            kb.dma(st[:], src_rows_ap, w=[bst])
            pt, pb = kb.ps()
            kb.pe(lambda e: e.transpose(out=pt[:, 0:nt], in_=st[:], identity=ident[0:nt, 0:nt]), r=[bst, b_ident], w=[pb])
            kb.dve(lambda e: e.tensor_copy(out=dst_ap, in_=pt[:, 0:nt]), r=[pb], w=[wbuf])

        with ExitStack() as pes:
            for l in range(DEPTH):
                load_chanvec(n1g[:, l, :], norm1_g[l].rearrange("(m p) -> m p", p=128), 8, pes, wbuf=b_cv)
                load_chanvec(n2g[:, l, :], norm2_g[l].rearrange("(m p) -> m p", p=128), 8, pes, wbuf=b_cv)
                load_chanvec(gnT[:, l, :], grp_g[l].rearrange("(m p) -> m p", p=128), 8, pes, wbuf=b_cv)
                for j in range(3):
                    load_chanvec(cwT[:, l, j, :], cw_d[l, j].rearrange("(m p) -> m p", p=128), 44, pes, wbuf=b_cv)
                load_chanvec(cbT[:, l, :], cb_d[l].rearrange("(m p) -> m p", p=128), 44, pes, wbuf=b_cv)
            craw = kb.sb(pes, "craw", [128, 2, 8]); b_craw = Buf("craw")
            for cv in range(2):
                load_chanvec(craw[:, cv, :], cv_d[cv].rearrange("(m p) -> m p", p=128), 8, pes, wbuf=b_craw)
            sc = kb.sb(pes, "sc", [128, 8, 2], BF16); b_sc = Buf("sc")
            for cv in range(2):
                kb.act(lambda e, cv=cv: e.activation(out=sc[:, :, cv], in_=craw[:, cv, :], func=AF.Silu), r=[b_craw], w=[b_sc])
            lgT = kb.sb(pes, "lgT", [128, DEPTH, 2, 2]); b_lg = Buf("lgT")
            for l in range(DEPTH):
                for d in range(2):
                    load_chanvec(lgT[:, l, d, :], hglog[l, d].rearrange("(m p) -> m p", p=128), 2, pes, wbuf=b_lg)
            kb.dve(lambda e: e.memset(lbT[:, 0, :, :], 0.0), w=[b_lb])
            kb.dve(lambda e: e.tensor_tensor(out=lbT[:, 1, :, :], in0=lgT[:, 1, :, :], in1=lgT[:, 0, :, :], op=ALU.subtract), r=[b_lg], w=[b_lb])
            kb.act(lambda e: e.activation(out=lbT[:, 1, :, :], in_=lbT[:, 1, :, :], func=AF.Sigmoid), r=[b_lb], w=[b_lb])
            kb.dve(lambda e: e.tensor_scalar(out=omlT[:], in0=lbT[:], scalar1=-1.0, scalar2=1.0, op0=ALU.mult, op1=ALU.add), r=[b_lb], w=[b_lb])
            kb.dve(lambda e: e.tensor_scalar(out=nomlT[:], in0=lbT[:], scalar1=1.0, scalar2=-1.0, op0=ALU.mult, op1=ALU.add), r=[b_lb], w=[b_lb])
            badaT = kb.sb(pes, "badaT", [128, DEPTH, 48]); b_bada = Buf("bada")
            for l in range(DEPTH):
                load_chanvec(badaT[:, l, :], b_ada[l].rearrange("(m p) -> m p", p=128), 48, pes, wbuf=b_bada)
            ast = [kb.sb(pes, "ast%d" % i, [128, 8, 768], BF16) for i in range(2)]
            b_ast = [Buf("ast0"), Buf("ast1")]
            for l in range(DEPTH):
                pt, pb = kb.ps()
                for ch in range(8):
                    s = ch % 2
                    kb.dma(ast[s][:], w_ada[l].rearrange("(k p) c -> p k c", p=128)[:, :, ch * 768:(ch + 1) * 768], w=[b_ast[s]], eng="pool")
                    for m in range(6):
                        mt = ch * 6 + m
                        for k in range(8):
                            kb.pe(lambda e, s=s, m=m, k=k, mt=mt, pt=pt: e.matmul(pt[:, mt * 2:mt * 2 + 2], lhsT=ast[s][:, k, m * 128:(m + 1) * 128],
                                                                  rhs=sc[:, k, :], start=(k == 0), stop=(k == 7)),
                                  r=[b_ast[s], b_sc], w=[pb])
                for cv in range(2):
                    kb.dve(lambda e, l=l, cv=cv, pt=pt: e.tensor_tensor(out=modT[:, l, cv, :], in0=pt[:, 0:96].rearrange("p (m c) -> p m c", c=2)[:, :, cv],
                                                                  in1=badaT[:, l, :], op=ALU.add), r=[pb, b_bada], w=[b_mod])
            for l in range(DEPTH):
                for cv in range(2):
                    for n, (gt, chunk) in enumerate(((n1g, 1), (n2g, 4))):
                        kb.dve(lambda e, l=l, cv=cv, n=n, gt=gt, chunk=chunk: e.scalar_tensor_tensor(
                            out=Amod[:, l, cv, n, :], in0=modT[:, l, cv, chunk * 8:(chunk + 1) * 8], scalar=1.0, in1=gt[:, l, :],
                            op0=ALU.add, op1=ALU.mult), r=[b_mod, b_cv], w=[b_A])
            S.barrier()

        s5cache = {}

        def run_group(gname, x_dram, y_dram, NT, T, cv, grid):
            is_sample = grid
            NB = NT // 512
            with ExitStack() as ges:
                x = kb.sb(ges, "x", [128, 8, NT]); b_x = Buf("x")
                h = kb.sb(ges, "h", [128, 8, NT], BF16); b_h = Buf("h")
                with ExitStack() as tes:
                    xin = [kb.sb(tes, "xin%d" % i, [128, D]) for i in range(2)]
                    b_xin = [Buf("xin0"), Buf("xin1")]
                    for tt in range(NT // 128):
                        s = tt % 2
                        kb.dma(xin[s][:], x_dram[tt * 128:(tt + 1) * 128, :], w=[b_xin[s]])
                        for kq in range(2):
                            pt, pb = kb.ps()
                            for kk in range(4):
                                k = kq * 4 + kk
                                kb.pe(lambda e, s=s, k=k, kk=kk, pt=pt: e.transpose(out=pt[:, kk * 128:(kk + 1) * 128], in_=xin[s][:, k * 128:(k + 1) * 128], identity=ident[:]),
                                      r=[b_xin[s], b_ident], w=[pb])
                            kb.act(lambda e, kq=kq, tt=tt, pt=pt: e.activation(out=x[:, kq * 4:(kq + 1) * 4, tt * 128:(tt + 1) * 128],
                                                                            in_=pt[:].rearrange("p (k t) -> p k t", k=4), func=AF.Copy),
                                   r=[pb], w=[b_x])
                    S.barrier()

                def norm_to_h(l, n):
                    shchunk = 0 if n == 0 else 3
                    with ExitStack() as nes:
                        sq = [kb.sb(nes, "sq%d" % i, [128, 8, 512], BF16) for i in range(2)]; b_sq = [Buf("sq0"), Buf("sq1")]
                        rs = [kb.sb(nes, "rs%d" % i, [128, 512]) for i in range(2)]; b_rs = [Buf("rs0"), Buf("rs1")]
                        tmp = [kb.sb(nes, "ntmp%d" % i, [128, 512]) for i in range(4)]
                        b_tmp = [Buf("nt%d" % i) for i in range(4)]
                        pts = {}

                        def stage1(blk):
                            c0 = blk * 512
                            j = blk % 2
                            kb.act(lambda e: e.activation(out=sq[j][:], in_=x[:, :, c0:c0 + 512], func=AF.Square), r=[b_x], w=[b_sq[j]])
                            pt, pb = kb.ps()
                            pts[blk] = (pt, pb)
                            for k in range(8):
                                kb.pe(lambda e, k=k: e.matmul(pt[:], lhsT=ones_bf[:], rhs=sq[j][:, k, :], start=(k == 0), stop=(k == 7)),
                                      r=[b_sq[j], b_ones], w=[pb])
                        stage1(0)
                        for blk in range(NB):
                            c0 = blk * 512
                            j = blk % 2
                            if blk + 1 < NB:
                                stage1(blk + 1)
                            pt, pb = pts[blk]
                            kb.act(lambda e, pt=pt, j=j: e.activation(out=rs[j][:], in_=pt[:], func=AF.Sqrt, scale=1.0 / D, bias=EPS), r=[pb], w=[b_rs[j]])
                            kb.dve(lambda e, j=j: e.reciprocal(out=rs[j][:], in_=rs[j][:]), r=[b_rs[j]], w=[b_rs[j]])
                            for k in range(8):
                                s = k % 4
                                kb.dve(lambda e, k=k, s=s, c0=c0, j=j: e.scalar_tensor_tensor(out=tmp[s][:], in0=x[:, k, c0:c0 + 512], scalar=Amod[:, l, cv, n, k:k + 1],
                                                                                        in1=rs[j][:], op0=ALU.mult, op1=ALU.mult), r=[b_x, b_rs[j], b_A], w=[b_tmp[s]])
                                kb.act(lambda e, k=k, s=s, c0=c0: e.activation(out=h[:, k, c0:c0 + 512], in_=tmp[s][:], func=AF.Identity,
                                                                        bias=modT[:, l, cv, shchunk * 8 + k:shchunk * 8 + k + 1], scale=1.0),
                                       r=[b_tmp[s], b_mod], w=[b_h])
                        S.barrier()

                def ffn(l):
                    gchunk = 5
                    TB = min(NT, 1024)
                    with ExitStack() as fes:
                        gated = kb.sb(fes, "gated", [128, 22, TB], BF16); b_gated = Buf("gated")
                        zc = [[kb.sb(fes, "zc%d%d" % (i, j), [128, 512]) for j in range(2)] for i in range(2)]
                        b_zc = [[Buf("zc"), Buf("zc")] for i in range(2)]
                        u = [kb.sb(fes, "u%d" % i, [128, 512]) for i in range(2)]
                        b_u = [Buf("u0"), Buf("u1")]
                        sg = [kb.sb(fes, "sg%d" % i, [128, 512]) for i in range(2)]
                        b_sg = [Buf("sg0"), Buf("sg1")]
                        it = [0]
                        rowlen = 64 if grid else T
                        for tb in range(NT // TB):
                            t0 = tb * TB
                            for m in range(22):
                                s = wctr[0] % NWS
                                wctr[0] += 1
                                wsrc = w_up[l].rearrange("(k p) c -> p k c", p=128)
                                kb.dma(wbf[s][:, 0:1024].rearrange("p (k c) -> p k c", k=8), wsrc[:, :, m * 128:(m + 1) * 128], w=[b_wbf[s]], eng="pool")
                                kb.dma(wbf[s][:, 1024:2048].rearrange("p (k c) -> p k c", k=8), wsrc[:, :, DFF + m * 128:DFF + (m + 1) * 128], w=[b_wbf2[s]], eng="pool")
                                wv = wbf[s][:, 0:2048].rearrange("p (ab k c) -> p ab k c", ab=2, k=8)
                                for blk in range(TB // 512):
                                    c0 = t0 + blk * 512
                                    i2 = it[0] % 2
                                    it[0] += 1
                                    pts = []
                                    for ab in range(2):
                                        pt, pb = kb.ps()
                                        pts.append((pt, pb))
                                        for k in range(8):
                                            kb.pe(lambda e, k=k, ab=ab, pt=pt, wv=wv, c0=c0: e.matmul(pt[:], lhsT=wv[:, ab, k, :], rhs=h[:, k, c0:c0 + 512],
                                                                                            start=(k == 0), stop=(k == 7)), r=[b_wbf[s], b_wbf2[s], b_h], w=[pb])
                                    for ab in range(2):
                                        pt, pb = pts[ab]
                                        ct = ab * 22 + m
                                        z = zc[i2][ab]
                                        bz = b_zc[i2][ab]
                                        kb.act(lambda e, z=z, pt=pt, ct=ct: e.activation(out=z[:], in_=pt[:], func=AF.Identity, scale=cwT[:, l, 1, ct:ct + 1], bias=cbT[:, l, ct:ct + 1]),
                                               r=[pb, b_cv], w=[bz])
                                        zr = z[:].rearrange("p (r c) -> p r c", c=rowlen)
                                        pr = pt[:].rearrange("p (r c) -> p r c", c=rowlen)
                                        kb.dve(lambda e, zr=zr, pr=pr, ct=ct: e.scalar_tensor_tensor(out=zr[:, :, 1:rowlen], in0=pr[:, :, 0:rowlen - 1], scalar=cwT[:, l, 0, ct:ct + 1],
                                                                                             in1=zr[:, :, 1:rowlen], op0=ALU.mult, op1=ALU.add), r=[pb, bz, b_cv], w=[bz])
                                        kb.dve(lambda e, zr=zr, pr=pr, ct=ct: e.scalar_tensor_tensor(out=zr[:, :, 0:rowlen - 1], in0=pr[:, :, 1:rowlen], scalar=cwT[:, l, 2, ct:ct + 1],
                                                                                             in1=zr[:, :, 0:rowlen - 1], op0=ALU.mult, op1=ALU.add), r=[pb, bz, b_cv], w=[bz])
                                    za, zb = zc[i2]
                                    bza, bzb = b_zc[i2]
                                    uu, bu = u[i2], b_u[i2]
                                    ss, bs = sg[i2], b_sg[i2]
                                    kb.act(lambda e, uu=uu, za=za: e.activation(out=uu[:], in_=za[:], func=AF.Square, scale=0.21145921592590945), r=[bza], w=[bu])
                                    kb.dve(lambda e, uu=uu, za=za: e.scalar_tensor_tensor(out=uu[:], in0=uu[:], scalar=1.0, in1=za[:], op0=ALU.add, op1=ALU.mult), r=[bu, bza], w=[bu])
                                    kb.act(lambda e, uu=uu, ss=ss: e.activation(out=ss[:], in_=uu[:], func=AF.Sigmoid, scale=1.5957691216057308), r=[bu], w=[bs])
                                    kb.dve(lambda e, ss=ss, za=za: e.tensor_tensor(out=ss[:], in0=ss[:], in1=za[:], op=ALU.mult), r=[bs, bza], w=[bs])
                                    kb.dve(lambda e, ss=ss, zb=zb, m=m, c0=c0, t0=t0: e.tensor_tensor(out=gated[:, m, c0 - t0:c0 - t0 + 512], in0=ss[:], in1=zb[:], op=ALU.mult),
                                           r=[bs, bzb], w=[b_gated])
                            for dm in range(8):
                                wv, bw = wload(w_dn[l].rearrange("(k p) c -> p k c", p=128)[:, :, dm * 128:(dm + 1) * 128], 22, 128)
                                for blk in range(TB // 512):
                                    c0 = t0 + blk * 512
                                    pt, pb = kb.ps()
                                    for k in range(22):
                                        kb.pe(lambda e, k=k, pt=pt, wv=wv, c0=c0, t0=t0: e.matmul(pt[:], lhsT=wv[:, k, :], rhs=gated[:, k, c0 - t0:c0 - t0 + 512], start=(k == 0), stop=(k == 21)),
                                              r=[bw, b_gated], w=[pb])
                                    kb.dve(lambda e, dm=dm, pt=pt, c0=c0: e.scalar_tensor_tensor(out=x[:, dm, c0:c0 + 512], in0=pt[:], scalar=modT[:, l, cv, gchunk * 8 + dm:gchunk * 8 + dm + 1],
                                                                                        in1=x[:, dm, c0:c0 + 512], op0=ALU.mult, op1=ALU.add), r=[pb, b_mod, b_x], w=[b_x])
                        S.barrier()

                NTT = NT // 128
                nseq = NT // T

                def gelu_psum(src, b_src, out_ap, b_out, tA, b_tA, tB, b_tB):
                    kb.act(lambda e: e.activation(out=tA, in_=src, func=AF.Square, scale=0.21145921592590945), r=[b_src], w=[b_tA])
                    kb.dve(lambda e: e.scalar_tensor_tensor(out=tA, in0=tA, scalar=1.0, in1=src, op0=ALU.add, op1=ALU.mult), r=[b_tA, b_src], w=[b_tA])
                    kb.act(lambda e: e.activation(out=tB, in_=tA, func=AF.Sigmoid, scale=1.5957691216057308), r=[b_tA], w=[b_tB])
                    kb.dve(lambda e: e.tensor_tensor(out=out_ap, in0=tB, in1=src, op=ALU.mult), r=[b_tB, b_src], w=[b_out])

                def proj_fm(l, c0col, ntile, cb):
                    wsrc = w_in[l].rearrange("(k p) c -> p k c", p=128)
                    for g0 in range(0, ntile, 2):
                        nt2 = min(2, ntile - g0)
                        wv, bw = wload(wsrc[:, :, c0col + g0 * 128:c0col + (g0 + nt2) * 128], 8, nt2 * 128)
                        for ti in range(nt2):
                            for blk in range(NB):
                                pt, pb = kb.ps()
                                for k in range(8):
                                    kb.pe(lambda e, k=k, pt=pt, wv=wv, ti=ti, blk=blk: e.matmul(pt[:], lhsT=wv[:, k, ti * 128:(ti + 1) * 128], rhs=h[:, k, blk * 512:(blk + 1) * 512],
                                                                                      start=(k == 0), stop=(k == 7)), r=[bw, b_h], w=[pb])
                                cb(g0 + ti, blk, pt, pb)

                def proj_tok(l, c0col, ncol, cb):
                    wsrc = w_in[l].rearrange("(k p) c -> p k c", p=128)
                    wv, bw = wload(wsrc[:, :, c0col:c0col + ncol], 8, ncol)
                    for tt in range(NTT):
                        pt, pb = kb.ps()
                        for k in range(8):
                            kb.pe(lambda e, k=k, pt=pt, wv=wv, tt=tt: e.matmul(pt[:, 0:ncol], lhsT=h[:, k, tt * 128:(tt + 1) * 128], rhs=wv[:, k, :],
                                                                       start=(k == 0), stop=(k == 7)), r=[bw, b_h], w=[pb])
                        cb(tt, pt, pb)

                def fm_norm(src, b_src, gain, dst, b_dst, nch, ones_t, per_tile, nes):
                    sq = kb.sb(nes, "fsq", [128, 2, 512], BF16); b_sq = Buf("fsq")
                    rs = kb.sb(nes, "frs", [128, 2, 512]); b_rs = Buf("frs")
                    for blk in range(NB):
                        c0 = blk * 512
                        kb.act(lambda e, c0=c0: e.activation(out=sq[:], in_=src[:, :, c0:c0 + 512], func=AF.Square), r=[b_src], w=[b_sq])
                        for ct in range(2 if per_tile else 1):
                            pt, pb = kb.ps()
                            if per_tile:
                                kb.pe(lambda e, pt=pt, ct=ct: e.matmul(pt[:], lhsT=ones_t[:], rhs=sq[:, ct, :], start=True, stop=True), r=[b_sq, b_ones], w=[pb])
                            else:
                                for k in range(2):
                                    kb.pe(lambda e, pt=pt, k=k: e.matmul(pt[:], lhsT=ones_t[:], rhs=sq[:, k, :], start=(k == 0), stop=(k == 1)), r=[b_sq, b_ones], w=[pb])
                            kb.act(lambda e, pt=pt, ct=ct: e.activation(out=rs[:, ct, :], in_=pt[:], func=AF.Sqrt, scale=1.0 / nch, bias=EPS), r=[pb], w=[b_rs])
                            kb.dve(lambda e, ct=ct: e.reciprocal(out=rs[:, ct, :], in_=rs[:, ct, :]), r=[b_rs], w=[b_rs])
                        for ct in range(2):
                            rc = ct if per_tile else 0
                            kb.dve(lambda e, ct=ct, rc=rc, c0=c0: e.scalar_tensor_tensor(out=dst[:, ct, c0:c0 + 512], in0=src[:, ct, c0:c0 + 512], scalar=gain[:, ct:ct + 1],
                                                                                 in1=rs[:, rc, :], op0=ALU.mult, op1=ALU.mult), r=[b_src, b_rs, b_cv], w=[b_dst])

                def wout_add(l, mi, cat, b_cat, xv=None):
                    wv, bw = wload(w_out[l][mi * 256:(mi + 1) * 256, :].rearrange("(k p) c -> p k c", p=128), 2, 1024)
                    for dm in range(8):
                        for blk in range(NB):
                            pt, pb = kb.ps()
                            for k in range(2):
                                kb.pe(lambda e, k=k, pt=pt, wv=wv, dm=dm, blk=blk: e.matmul(pt[:], lhsT=wv[:, k, dm * 128:(dm + 1) * 128], rhs=cat[:, k, blk * 512:(blk + 1) * 512],
                                                                                  start=(k == 0), stop=(k == 1)), r=[bw, b_cat], w=[pb])
                            xa = x[:, dm, blk * 512:(blk + 1) * 512] if xv is None else xv(dm, blk)
                            pin = pt[:] if xv is None else pt[:].rearrange("p (j c) -> p j c", j=xa.shape[1])
                            kb.dve(lambda e, pin=pin, dm=dm, xa=xa: e.scalar_tensor_tensor(out=xa, in0=pin, scalar=modT[:, l, cv, 16 + dm:16 + dm + 1], in1=xa,
                                                                                 op0=ALU.mult, op1=ALU.add), r=[pb, b_mod, b_x], w=[b_x])

                def mixer_gmlp(l):
                    with ExitStack() as mes:
                        gu = kb.sb(mes, "gu", [128, 2, NT]); b_gu = Buf("gu")
                        gvt = kb.sb(mes, "gvt", [128, NTT, 256], BF16); b_gvt = Buf("gvt")
                        cat = kb.sb(mes, "catd", [128, 2, NT], BF16); b_cat = Buf("catd")
                        tA = kb.sb(mes, "tA", [128, 512]); b_tA = Buf("tA")
                        tB = kb.sb(mes, "tB", [128, 512]); b_tB = Buf("tB")
                        tg = kb.sb(mes, "tg", [128, 256]); b_tg = Buf("tg")
                        gmgB = kb.sb(mes, "gmgB", [128, 256]); b_gmgB = Buf("gmgB")
                        wsT = kb.sb(mes, "wsT", [128, 4, 128], BF16); b_wsT = Buf("wsT")
                        wsraw = kb.sb(mes, "wsraw", [128, 4, 128]); b_wsraw = Buf("wsraw")
                        bsB = kb.sb(mes, "bsB", [128, 2, 128]); b_bsB = [Buf("bsB%d" % i) for i in range(4)]
                        ss = kb.sb(mes, "gss", [128, 2]); b_ss = Buf("gss")
                        kb.dma(gmgB[:], gm_ng[l].rearrange("(o n) -> o n", o=1).broadcast_to([128, 256]), w=[b_gmgB])
                        kb.dma(wsraw[:], gm_ws[l].rearrange("h i j -> i h j"), w=[b_wsraw])
                        for hd in range(4):
                            pt, pb = kb.ps()
                            kb.pe(lambda e, hd=hd, pt=pt: e.transpose(out=pt[:, 0:128], in_=wsraw[:, hd, :], identity=ident[:]), r=[b_wsraw, b_ident], w=[pb])
                            kb.act(lambda e, hd=hd, pt=pt: e.activation(out=wsT[:, hd, :], in_=pt[:, 0:128], func=AF.Copy), r=[pb], w=[b_wsT])
                            ct, hh = hd // 2, hd % 2
                            kb.dma(bsB[hh * 64:(hh + 1) * 64, ct, :], gm_bs[l, hd].rearrange("(o n) -> o n", o=1).broadcast_to([64, 128]), w=[b_bsB[hd]])

                        def cb_gu(ct, blk, pt, pb):
                            gelu_psum(pt[:], pb, gu[:, ct, blk * 512:(blk + 1) * 512], b_gu, tA[:], b_tA, tB[:], b_tB)
                        proj_fm(l, 7 * 256, 2, cb_gu)

                        def cb_gv(tt, pt, pb):
                            gelu_psum(pt[:, 0:256], pb, tg[:], b_tg, tA[:, 0:256], b_tA, tB[:, 0:256], b_tB)
                            kb.act(lambda e: e.activation(out=tA[:, 0:256], in_=tg[:], func=AF.Square, accum_out=ss[:, 0:1]), r=[b_tg], w=[b_tA, b_ss])
                            kb.act(lambda e: e.activation(out=ss[:, 0:1], in_=ss[:, 0:1], func=AF.Sqrt, scale=1.0 / 256, bias=EPS), r=[b_ss], w=[b_ss])
                            kb.dve(lambda e: e.reciprocal(out=ss[:, 0:1], in_=ss[:, 0:1]), r=[b_ss], w=[b_ss])
                            kb.dve(lambda e, tt=tt: e.scalar_tensor_tensor(out=gvt[:, tt, :], in0=tg[:], scalar=ss[:, 0:1], in1=gmgB[:], op0=ALU.mult, op1=ALU.mult),
                                   r=[b_tg, b_ss, b_gmgB], w=[b_gvt])
                        proj_tok(l, 8 * 256, 256, cb_gv)

                        for tt in range(NTT):
                            for ct in range(2):
                                pt, pb = kb.ps()
                                for hh in range(2):
                                    hd = ct * 2 + hh
                                    kb.pe(lambda e, pt=pt, hh=hh, hd=hd, tt=tt: e.matmul(pt[hh * 64:(hh + 1) * 64, 0:128], lhsT=gvt[:, tt, hd * 64:(hd + 1) * 64], rhs=wsT[:, hd, :],
                                                                                 start=True, stop=True), r=[b_gvt, b_wsT], w=[pb])
                                kb.dve(lambda e, pt=pt, ct=ct: e.tensor_tensor(out=tA[:, 0:128], in0=pt[:, 0:128], in1=bsB[:, ct, :], op=ALU.add),
                                       r=[pb] + b_bsB, w=[b_tA])
                                kb.dve(lambda e, ct=ct, tt=tt: e.tensor_tensor(out=gu[:, ct, tt * 128:(tt + 1) * 128], in0=tA[:, 0:128], in1=gu[:, ct, tt * 128:(tt + 1) * 128], op=ALU.mult),
                                       r=[b_tA, b_gu], w=[b_gu])
                        fm_norm(gu, b_gu, gnT[:, l, 6:8], cat, b_cat, 256, ones_bf, False, mes)
                        wout_add(l, 3, cat, b_cat)
                        S.barrier()

                def mixer_fnet(l):
                    TT = T // 128
                    TBK = 256
                    dft_d = dft_s if T == 2048 else dft_p
                    with ExitStack() as mes:
                        cat = kb.sb(mes, "catc", [128, 2, NT], BF16); b_cat = Buf("catc")
                        xfr = kb.sb(mes, "xfr", [128, 2, NT], BF16); b_xfr = Buf("xfr")
                        bd = kb.sb(mes, "bd", [128, 2, 512], BF16); b_bd = Buf("bd")
                        kb.dma(bd[:], bdcs_d.rearrange("(k p) c -> p k c", p=128), w=[b_bd])
                        with ExitStack() as m2:
                            xcs = kb.sb(m2, "xcs", [128, NTT, 512], BF16); b_xcs = Buf("xcs")
                            with ExitStack() as m3:
                                xc = kb.sb(m3, "xc", [128, 2, NT], BF16); b_xc = Buf("xc")

                                def cb_xc(ct, blk, pt, pb):
                                    kb.act(lambda e: e.activation(out=xc[:, ct, blk * 512:(blk + 1) * 512], in_=pt[:], func=AF.Copy), r=[pb], w=[b_xc])
                                proj_fm(l, 6 * 256, 2, cb_xc)
                                for tt in range(NTT):
                                    pt, pb = kb.ps()
                                    for k in range(2):
                                        kb.pe(lambda e, pt=pt, k=k, tt=tt: e.matmul(pt[:], lhsT=xc[:, k, tt * 128:(tt + 1) * 128], rhs=bd[:, k, :], start=(k == 0), stop=(k == 1)),
                                              r=[b_xc, b_bd], w=[pb])
                                    kb.act(lambda e, pt=pt, tt=tt: e.activation(out=xcs[:, tt, :], in_=pt[:], func=AF.Copy), r=[pb], w=[b_xcs])
                                S.barrier()
                            with ExitStack() as m3:
                                dCs = [kb.sb(m3, "dC%d" % i, [128, TT, TBK], BF16) for i in range(2)]; b_dCs = [Buf("dC0"), Buf("dC1")]
                                dSs = [kb.sb(m3, "dS%d" % i, [128, TT, TBK], BF16) for i in range(2)]; b_dSs = [Buf("dS0"), Buf("dS1")]
                                dctr = 0
                                for sq_ in range(nseq):
                                    for tb in range(T // TBK):
                                        dC, b_dC, dS, b_dS = dCs[dctr % 2], b_dCs[dctr % 2], dSs[dctr % 2], b_dSs[dctr % 2]
                                        dctr += 1
                                        kb.dma(dC[:], dft_d[0].rearrange("(t p) c -> p t c", p=128)[:, :, tb * TBK:(tb + 1) * TBK], w=[b_dC])
                                        kb.dma(dS[:], dft_d[1].rearrange("(t p) c -> p t c", p=128)[:, :, tb * TBK:(tb + 1) * TBK], w=[b_dS])
                                        for ct in range(2):
                                            pt, pb = kb.ps()
                                            for tt in range(TT):
                                                kb.pe(lambda e, pt=pt, tt=tt, ct=ct, sq_=sq_: e.matmul(pt[:, 0:TBK], lhsT=xcs[:, sq_ * TT + tt, ct * 128:(ct + 1) * 128], rhs=dC[:, tt, :],
                                                                                             start=(tt == 0), stop=False), r=[b_xcs, b_dC], w=[pb])
                                            for tt in range(TT):
                                                kb.pe(lambda e, pt=pt, tt=tt, ct=ct, sq_=sq_: e.matmul(pt[:, 0:TBK], lhsT=xcs[:, sq_ * TT + tt, 256 + ct * 128:256 + (ct + 1) * 128], rhs=dS[:, tt, :],
                                                                                             start=False, stop=(tt == TT - 1)), r=[b_xcs, b_dS], w=[pb])
                                            c0 = sq_ * T + tb * TBK
                                            kb.act(lambda e, pt=pt, ct=ct, c0=c0: e.activation(out=xfr[:, ct, c0:c0 + TBK], in_=pt[:, 0:TBK], func=AF.Copy), r=[pb], w=[b_xfr])
                                S.barrier()
                        with ExitStack() as m2:
                            cpre = kb.sb(m2, "cpre", [128, 2, NT]); b_cpre = Buf("cpre")
                            wv, bw = wload(fn_w[l].rearrange("(k p) c -> p k c", p=128), 2, 256)
                            for ct in range(2):
                                for blk in range(NB):
                                    pt, pb = kb.ps()
                                    for k in range(2):
                                        kb.pe(lambda e, pt=pt, k=k, ct=ct, blk=blk, wv=wv: e.matmul(pt[:], lhsT=wv[:, k, ct * 128:(ct + 1) * 128], rhs=xfr[:, k, blk * 512:(blk + 1) * 512],
                                                                                          start=(k == 0), stop=(k == 1)), r=[bw, b_xfr], w=[pb])
                                    kb.act(lambda e, pt=pt, ct=ct, blk=blk: e.activation(out=cpre[:, ct, blk * 512:(blk + 1) * 512], in_=pt[:], func=AF.Copy), r=[pb], w=[b_cpre])
                            fm_norm(cpre, b_cpre, gnT[:, l, 4:6], cat, b_cat, 256, ones_bf, False, m2)
                            wout_add(l, 2, cat, b_cat)
                            S.barrier()

                def mixer_hgrn(l):
                    TT = T // 128
                    NCH = NT // 32
                    with ExitStack() as mes:
                        cat = kb.sb(mes, "catb", [128, 2, NT], BF16); b_cat = Buf("catb")
                        mk = kb.sb(mes, "hmask", [128, 3, 128]); b_mk = Buf("hmask")
                        onesbd = kb.sb(mes, "onesbd", [128, 128], BF16)
                        smask = kb.sb(mes, "smask", [128, NT], BF16); b_smask = Buf("smask")
                        kb.dma(mk[:], hmask_d.rearrange("m p c -> p m c"), w=[b_mk])
                        kb.dma(smask[:], smask_d[:, 0:NT], w=[b_smask])
                        kb.dve(lambda e: e.tensor_copy(out=onesbd[:], in_=mk[:, 2, :]), r=[b_mk], w=[b_mk])
                        stmp = kb.sb(mes, "stmp", [128, 512]); b_stmp = Buf("stmp")
                        stmp2 = kb.sb(mes, "stmp2", [128, 512]); b_stmp2 = Buf("stmp2")
                        Sst = [kb.sb(mes, "Sst%d" % i, [128, 128]) for i in range(2)]
                        b_S = [Buf("Sst0"), Buf("Sst1")]
                        Sbd = [kb.sb(mes, "Sbd%d" % i, [128, 8, 128], BF16) for i in range(2)]
                        b_Sbd = [[Buf("Sbd") for i in range(8)] for j in range(2)]
                        scT = [[kb.sb(mes, "scT%d%d" % (j, i), [128, 128], BF16) for i in range(2)] for j in range(2)]
                        b_scT = [[Buf("scT") for i in range(2)] for j in range(2)]
                        vblk = [kb.sb(mes, "vblk%d" % i, [128, 4, 128], BF16) for i in range(2)]
                        b_vblk = [Buf("vblk%d" % i) for i in range(2)]
                        cmk = kb.sb(mes, "cmk", [128, 4]); b_cmk = Buf("cmk")
                        kb.dma(cmk[:], cmask_d, w=[b_cmk])
                        vctr = [0]
                        for ct in range(2):
                            with ExitStack() as ces:
                                vtok = kb.sb(ces, "vtok", [128, NTT, 128], BF16); b_vtok = Buf("vtok")
                                oacc = kb.sb(ces, "oacc", [128, NT]); b_oacc = Buf("oacc")
                                Qt = [kb.sb(ces, "Qt%d" % i, [128, NT], BF16) for i in range(2)]; b_Qt = [Buf("Qt0"), Buf("Qt1")]
                                Kt = [kb.sb(ces, "Kt%d" % i, [128, NT], BF16) for i in range(2)]; b_Kt = [Buf("Kt0"), Buf("Kt1")]
                                Khtok = [kb.sb(ces, "Khtok%d" % i, [128, NTT, 128], BF16) for i in range(2)]; b_Khtok = [Buf("Kh0"), Buf("Kh1")]
                                glast = [kb.sb(ces, "glast%d" % i, [128, NCH]) for i in range(2)]; b_gl = [Buf("gl0"), Buf("gl1")]

                                def cb_v(tt, pt, pb):
                                    kb.act(lambda e: e.activation(out=vtok[:, tt, :], in_=pt[:, 0:128], func=AF.Copy), r=[pb], w=[b_vtok])
                                proj_tok(l, 4 * 256 + ct * 128, 128, cb_v)
                                Kh_sh = kb.sb(ces, "Kh", [128, NT], BF16); b_Kh_sh = Buf("Kh")
                                tmpA = [(kb.sb(ces, "bc%d" % i, [128, NT]), Buf("bc"), kb.sb(ces, "kk%d" % i, [128, NT], BF16), Buf("kk"), Kh_sh, b_Kh_sh) for i in range(2)]
                                for d in range(2):
                                    with ExitStack() as des:
                                        bc, b_bc, kk, b_kk, Kh, b_Kh = tmpA[d]
                                        lbA = lbT[:, l, d, ct:ct + 1]
                                        omlA = omlT[:, l, d, ct:ct + 1]
                                        nomlA = nomlT[:, l, d, ct:ct + 1]

                                        def cb_z(ti, blk, pt, pb):
                                            sl = slice(blk * 512, (blk + 1) * 512)
                                            kb.act(lambda e: e.activation(out=stmp[:], in_=pt[:], func=AF.Sigmoid), r=[pb], w=[b_stmp])
                                            kb.act(lambda e: e.activation(out=bc[:, sl], in_=stmp[:], func=AF.Ln, scale=omlA, bias=lbA), r=[b_stmp, b_lb], w=[b_bc])
                                            kb.dve(lambda e: e.tensor_scalar(out=kk[:, sl], in0=stmp[:], scalar1=nomlA, scalar2=omlA, op0=ALU.mult, op1=ALU.add),
                                                   r=[b_stmp, b_lb], w=[b_kk])
                                        proj_fm(l, (2 + d) * 256 + ct * 128, 1, cb_z)
                                        if d == 0:
                                            kb.dve(lambda e: e.tensor_tensor_scan(out=bc[:], data0=smask[:], data1=bc[:], initial=0.0, op0=ALU.mult, op1=ALU.add),
                                                   r=[b_bc, b_smask], w=[b_bc])
                                            kb.act(lambda e: e.activation(out=glast[d][:], in_=bc[:].rearrange("p (c t) -> p c t", t=32)[:, :, 31], func=AF.Exp), r=[b_bc], w=[b_gl[d]])
                                        else:
                                            kb.dve(lambda e: e.tensor_tensor_scan(out=bc[:, ::-1], data0=smask[:], data1=bc[:, ::-1], initial=0.0, op0=ALU.mult, op1=ALU.add),
                                                   r=[b_bc, b_smask], w=[b_bc])
                                            kb.act(lambda e: e.activation(out=glast[d][:], in_=bc[:].rearrange("p (c t) -> p c t", t=32)[:, :, 0], func=AF.Exp), r=[b_bc], w=[b_gl[d]])
                                        for blk in range(NB):
                                            sl = slice(blk * 512, (blk + 1) * 512)
                                            kb.act(lambda e, sl=sl: e.activation(out=stmp[:], in_=bc[:, sl], func=AF.Exp, scale=-1.0), r=[b_bc], w=[b_stmp])
                                            kb.dve(lambda e, sl=sl: e.tensor_tensor(out=Kt[d][:, sl], in0=kk[:, sl], in1=stmp[:], op=ALU.mult), r=[b_kk, b_stmp], w=[b_Kt[d]])
                                        kb.dve(lambda e: e.tensor_tensor(out=Kh[:].rearrange("p (c t) -> p c t", t=32), in0=Kt[d][:].rearrange("p (c t) -> p c t", t=32),
                                                                         in1=glast[d][:].unsqueeze(2).broadcast_to([128, NCH, 32]), op=ALU.mult), r=[b_Kt[d], b_gl[d]], w=[b_Kh])

                                        def cb_q(ti, blk, pt, pb):
                                            sl = slice(blk * 512, (blk + 1) * 512)
                                            kb.act(lambda e: e.activation(out=stmp2[:], in_=bc[:, sl], func=AF.Exp), r=[b_bc], w=[b_stmp2])
                                            kb.dve(lambda e: e.tensor_tensor(out=Qt[d][:, sl], in0=pt[:], in1=stmp2[:], op=ALU.mult), r=[pb, b_stmp2], w=[b_Qt[d]])
                                        proj_fm(l, 1 * 256 + ct * 128, 1, cb_q)
                                        for tt in range(NTT):
                                            ptb, pb = kb.ps()
                                            ptv = ptb.bitcast(BF16)
                                            kb.pe(lambda e, tt=tt, ptv=ptv: e.transpose(out=ptv[:, 0:128], in_=Kh[:, tt * 128:(tt + 1) * 128], identity=identb[:]), r=[b_Kh, b_ident], w=[pb])
                                            kb.act(lambda e, tt=tt, ptv=ptv: e.activation(out=Khtok[d][:, tt, :], in_=ptv[:, 0:128], func=AF.Copy), r=[pb], w=[b_Khtok[d]])
                                for sq_ in range(nseq):
                                    for d in range(2):
                                        kb.dve(lambda e, d=d: e.memset(Sst[d][:], 0.0), w=[b_S[d]])
                                        if is_sample:
                                            for hh in range(2):
                                                kb.dma(Sst[d][hh * 64:(hh + 1) * 64, hh * 64:(hh + 1) * 64], hgst_d[l, d, ct * 2 + hh], w=[b_S[d]])
                                        kb.dve(lambda e, d=d: e.tensor_tensor(out=Sbd[d][:, 0, :], in0=Sst[d][:], in1=mk[:, 2, :], op=ALU.mult), r=[b_S[d], b_mk], w=[b_Sbd[d][0]])
                                    for step in range(TT):
                                        par = step % 2
                                        info = []
                                        for d in range(2):
                                            ttl = step if d == 0 else TT - 1 - step
                                            tt = sq_ * TT + ttl
                                            tsl = slice(tt * 128, (tt + 1) * 128)
                                            for hh in range(2):
                                                ps_, pbs = kb.ps()
                                                kb.pe(lambda e, ps_=ps_, hh=hh, tsl=tsl, d=d: e.matmul(ps_[:, 0:128], lhsT=Kt[d][hh * 64:(hh + 1) * 64, tsl], rhs=Qt[d][hh * 64:(hh + 1) * 64, tsl],
                                                                                               start=True, stop=True), r=[b_Kt[d], b_Qt[d]], w=[pbs])
                                                kb.dve(lambda e, ps_=ps_, hh=hh, d=d: e.tensor_tensor(out=scT[d][hh][:], in0=ps_[:, 0:128], in1=mk[:, d, :], op=ALU.mult),
                                                       r=[pbs, b_mk], w=[b_scT[d][hh]])
                                            vs = vctr[0] % 2
                                            vctr[0] += 1
                                            kb.pool(lambda e, vs=vs, tt=tt: e.tensor_tensor(out=vblk[vs][:], in0=vtok[:, tt, :].unsqueeze(1).broadcast_to([128, 4, 128]),
                                                                                     in1=cmk[:].unsqueeze(2).broadcast_to([128, 4, 128]), op=ALU.mult),
                                                    r=[b_vtok, b_cmk], w=[b_vblk[vs]])
                                            pd, pbd = kb.ps()
                                            kb.pe(lambda e, pd=pd, vs=vs, tt=tt, d=d: e.matmul(pd[:], lhsT=Khtok[d][:, tt, :], rhs=vblk[vs][:].rearrange("p c v -> p (c v)"), start=True, stop=True),
                                                  r=[b_Khtok[d], b_vblk[vs]], w=[pbd])
                                            po, pbo = kb.ps()
                                            for hh in range(2):
                                                kb.pe(lambda e, po=po, hh=hh, tt=tt, d=d: e.matmul(po[hh * 64:(hh + 1) * 64, 0:128], lhsT=vtok[:, tt, hh * 64:(hh + 1) * 64], rhs=scT[d][hh][:],
                                                                                           start=True, stop=False), r=[b_vtok, b_scT[d][hh]], w=[pbo])
                                            info.append((tt, tsl, pd, pbd, po, pbo))
                                        for ci in range(4):
                                            for d in range(2):
                                                tt, tsl, pd, pbd, po, pbo = info[d]
                                                cl = ci if d == 0 else 3 - ci
                                                cg = tt * 4 + cl
                                                csl = slice(tt * 128 + cl * 32, tt * 128 + (cl + 1) * 32)
                                                scur = par * 4 + ci
                                                snxt = par * 4 + ci + 1 if ci < 3 else (1 - par) * 4
                                                kb.pe(lambda e, po=po, cl=cl, csl=csl, ci=ci, scur=scur, d=d: e.matmul(po[:, cl * 32:(cl + 1) * 32], lhsT=Sbd[d][:, scur, :], rhs=Qt[d][:, csl], start=False, stop=(ci == 3)),
                                                      r=[b_Sbd[d][scur], b_Qt[d]], w=[pbo])
                                                kb.dve(lambda e, pd=pd, cg=cg, cl=cl, d=d: e.scalar_tensor_tensor(out=Sst[d][:], in0=Sst[d][:], scalar=glast[d][:, cg:cg + 1], in1=pd[:, cl * 128:(cl + 1) * 128],
                                                                                                  op0=ALU.mult, op1=ALU.add), r=[b_S[d], b_gl[d], pbd], w=[b_S[d]])
                                                (kb.pool if d == 0 else kb.dve)(lambda e, snxt=snxt, d=d: e.tensor_tensor(out=Sbd[d][:, snxt, :], in0=Sst[d][:], in1=mk[:, 2, :], op=ALU.mult), r=[b_S[d], b_mk], w=[b_Sbd[d][snxt]])
                                        for d in range(2):
                                            tt, tsl, pd, pbd, po, pbo = info[d]
                                            ttl = tt - sq_ * TT
                                            first = (ttl < TT // 2) if d == 0 else (ttl >= TT // 2)
                                            if first:
                                                kb.act(lambda e, po=po, tsl=tsl: e.activation(out=oacc[:, tsl], in_=po[:, 0:128], func=AF.Copy), r=[pbo], w=[b_oacc])
                                            else:
                                                kb.dve(lambda e, po=po, tsl=tsl: e.tensor_tensor(out=oacc[:, tsl], in0=po[:, 0:128], in1=oacc[:, tsl], op=ALU.add), r=[pbo, b_oacc], w=[b_oacc])
                                    if not is_sample:
                                        for d in range(2):
                                            for hh in range(2):
                                                kb.dma(sthg_o[sq_, l, d, ct * 2 + hh], Sst[d][hh * 64:(hh + 1) * 64, hh * 64:(hh + 1) * 64], r=[b_S[d]], sembuf=b_S[d])
                                S.barrier()
                                with ExitStack() as des:
                                    sqb = kb.sb(des, "hsq", [128, 512], BF16); b_sqb = Buf("hsq")
                                    rs = stmp2; b_rs = b_stmp2

                                    def cb_g(ti, blk, pt, pb):
                                        sl = slice(blk * 512, (blk + 1) * 512)
                                        kb.act(lambda e: e.activation(out=sqb[:], in_=oacc[:, sl], func=AF.Square), r=[b_oacc], w=[b_sqb])
                                        p2, pb2 = kb.ps()
                                        kb.pe(lambda e: e.matmul(p2[:], lhsT=onesbd[:], rhs=sqb[:], start=True, stop=True), r=[b_sqb, b_mk], w=[pb2])
                                        kb.act(lambda e: e.activation(out=rs[:], in_=p2[:], func=AF.Sqrt, scale=1.0 / 64, bias=EPS), r=[pb2], w=[b_rs])
                                        kb.dve(lambda e: e.reciprocal(out=rs[:], in_=rs[:]), r=[b_rs], w=[b_rs])
                                        kb.dve(lambda e: e.scalar_tensor_tensor(out=rs[:], in0=oacc[:, sl], scalar=gnT[:, l, 2 + ct:3 + ct], in1=rs[:], op0=ALU.mult, op1=ALU.mult),
                                               r=[b_oacc, b_rs, b_cv], w=[b_rs])
                                        kb.act(lambda e: e.activation(out=stmp[:], in_=pt[:], func=AF.Silu), r=[pb], w=[b_stmp])
                                        kb.dve(lambda e: e.tensor_tensor(out=cat[:, ct, sl], in0=rs[:], in1=stmp[:], op=ALU.mult), r=[b_rs, b_stmp], w=[b_cat])
                                    proj_fm(l, 5 * 256 + ct * 128, 1, cb_g)
                                    S.barrier()
                        wout_add(l, 1, cat, b_cat)
                        S.barrier()

                def mixer_s5(l):
                    NCHT = NT // 8
                    CW = min(128, NCHT)
                    NCT = NCHT // CW
                    NST = T // 8
                    TWO_PI = 6.283185307179586
                    with ExitStack() as mes:
                        PWr = kb.sb(mes, "PWr", [128, 16, 9]); PWi = kb.sb(mes, "PWi", [128, 16, 9])
                        NPr = kb.sb(mes, "NPr", [128, 16, 8]); NPi = kb.sb(mes, "NPi", [128, 16, 8])
                        bbr = kb.sb(mes, "bbr", [128, 16, 16]); bbi = kb.sb(mes, "bbi", [128, 16, 16])
                        Cr = kb.sb(mes, "Cr", [128, 16, 16]); Ci = kb.sb(mes, "Ci", [128, 16, 16])
                        Dg = kb.sb(mes, "Dg", [128, 16])
                        smk = kb.sb(mes, "s5mk", [128, 2, 128])
                        b_P = Buf("s5params")
                        U = kb.sb(mes, "U", [128, 16, NCHT], BF16); b_U = Buf("U")
                        Yx = kb.sb(mes, "Yx", [128, NCT, 8, 256], BF16); b_Yx = Buf("Yx")
                        cnt = [0]

                        def vop(fn, r=(), w=()):
                            cnt[0] += 1
                            return kb.dve(fn, r, w)

                        for pes in ([ExitStack()] if is_sample else []):
                            def t16(nm):
                                return kb.sb(pes, nm, [128, 16])
                            stg = kb.sb(pes, "lamst", [16, 2, 128]); b_stg = [Buf("st0"), Buf("st1"), Buf("st2"), Buf("st3")]
                            lre, lim, dtv = t16("lre"), t16("lim"), t16("dtv")
                            for ri, src in enumerate((lamre_d, lamim_d)):
                                for d in range(2):
                                    kb.dma(stg[:, ri, d * 64:(d + 1) * 64], src[l, d], w=[b_stg[ri * 2 + d]])
                                pt, pb = kb.ps()
                                kb.pe(lambda e, pt=pt, ri=ri: e.transpose(out=pt[:, 0:16], in_=stg[:, ri, :], identity=ident[0:16, 0:16]), r=b_stg + [b_ident], w=[pb])
                                dst = lre if ri == 0 else lim
                                kb.dve(lambda e, pt=pt, dst=dst: e.tensor_copy(out=dst[:], in_=pt[:, 0:16]), r=[pb], w=[b_P])
                            b_dt = [Buf("dt0"), Buf("dt1")]
                            for d in range(2):
                                kb.dma(dtv[d * 64:(d + 1) * 64, :], logdt_d[l, d].rearrange("(o n) -> o n", o=1).broadcast_to([64, 16]), w=[b_dt[d]])
                            kb.act(lambda e: e.activation(out=dtv[:], in_=dtv[:], func=AF.Exp), r=b_dt, w=[b_P])
                            lr, mag, ang, kf, r1, r2, s1, c1 = [t16("p%d" % i) for i in range(8)]
                            ki = kb.sb(pes, "ki", [128, 16], I32)
                            abre, abim, den, xr, zre, zim, inre, inim, ta, tb2 = [t16("q%d" % i) for i in range(10)]
                            P = [b_P]
                            vop(lambda e: e.tensor_scalar(out=lr[:], in0=lre[:], scalar1=-1e-4, scalar2=None, op0=ALU.min), P, P)
                            vop(lambda e: e.tensor_tensor(out=mag[:], in0=lr[:], in1=dtv[:], op=ALU.mult), P, P)
                            kb.act(lambda e: e.activation(out=mag[:], in_=mag[:], func=AF.Exp), P, P)
                            vop(lambda e: e.tensor_tensor(out=ang[:], in0=lim[:], in1=dtv[:], op=ALU.mult), P, P)
                            vop(lambda e: e.tensor_scalar(out=kf[:], in0=ang[:], scalar1=1.0 / TWO_PI, scalar2=None, op0=ALU.mult), P, P)
                            vop(lambda e: e.tensor_copy(out=ki[:], in_=kf[:]), P, P)
                            vop(lambda e: e.tensor_copy(out=kf[:], in_=ki[:]), P, P)
                            vop(lambda e: e.scalar_tensor_tensor(out=r1[:], in0=kf[:], scalar=-6.28125, in1=ang[:], op0=ALU.mult, op1=ALU.add), P, P)
                            vop(lambda e: e.scalar_tensor_tensor(out=r1[:], in0=kf[:], scalar=-0.0019353071795864769, in1=r1[:], op0=ALU.mult, op1=ALU.add), P, P)
                            vop(lambda e: e.tensor_scalar(out=r1[:], in0=r1[:], scalar1=-3.1415925, scalar2=3.1415925, op0=ALU.max, op1=ALU.min), P, P)
                            kb.act(lambda e: e.activation(out=s1[:], in_=r1[:], func=AF.Sin), P, P)
                            vop(lambda e: e.tensor_scalar(out=r2[:], in0=r1[:], scalar1=1.5707963267948966, scalar2=None, op0=ALU.add), P, P)
                            vop(lambda e: e.tensor_scalar(out=ta[:], in0=r2[:], scalar1=3.141592653589793, scalar2=-TWO_PI, op0=ALU.is_gt, op1=ALU.mult), P, P)
                            vop(lambda e: e.tensor_tensor(out=r2[:], in0=r2[:], in1=ta[:], op=ALU.add), P, P)
                            vop(lambda e: e.tensor_scalar(out=r2[:], in0=r2[:], scalar1=-3.1415925, scalar2=3.1415925, op0=ALU.max, op1=ALU.min), P, P)
                            kb.act(lambda e: e.activation(out=c1[:], in_=r2[:], func=AF.Sin), P, P)
                            vop(lambda e: e.tensor_tensor(out=abre[:], in0=mag[:], in1=c1[:], op=ALU.mult), P, P)
                            vop(lambda e: e.tensor_tensor(out=abim[:], in0=mag[:], in1=s1[:], op=ALU.mult), P, P)
                            vop(lambda e: e.tensor_tensor(out=den[:], in0=lr[:], in1=lr[:], op=ALU.mult), P, P)
                            vop(lambda e: e.tensor_tensor(out=ta[:], in0=lim[:], in1=lim[:], op=ALU.mult), P, P)
                            vop(lambda e: e.tensor_tensor(out=den[:], in0=den[:], in1=ta[:], op=ALU.add), P, P)
                            vop(lambda e: e.reciprocal(out=den[:], in_=den[:]), P, P)
                            vop(lambda e: e.tensor_scalar(out=xr[:], in0=abre[:], scalar1=-1.0, scalar2=None, op0=ALU.add), P, P)
                            vop(lambda e: e.tensor_tensor(out=ta[:], in0=xr[:], in1=lr[:], op=ALU.mult), P, P)
                            vop(lambda e: e.tensor_tensor(out=tb2[:], in0=abim[:], in1=lim[:], op=ALU.mult), P, P)
                            vop(lambda e: e.tensor_tensor(out=ta[:], in0=ta[:], in1=tb2[:], op=ALU.add), P, P)
                            vop(lambda e: e.tensor_tensor(out=zre[:], in0=ta[:], in1=den[:], op=ALU.mult), P, P)
                            vop(lambda e: e.tensor_tensor(out=ta[:], in0=abim[:], in1=lr[:], op=ALU.mult), P, P)
                            vop(lambda e: e.tensor_tensor(out=tb2[:], in0=xr[:], in1=lim[:], op=ALU.mult), P, P)
                            vop(lambda e: e.tensor_tensor(out=ta[:], in0=ta[:], in1=tb2[:], op=ALU.subtract), P, P)
                            vop(lambda e: e.tensor_tensor(out=zim[:], in0=ta[:], in1=den[:], op=ALU.mult), P, P)
                            vop(lambda e: e.tensor_tensor(out=ta[:], in0=mag[:], in1=mag[:], op=ALU.mult), P, P)
                            vop(lambda e: e.reciprocal(out=ta[:], in_=ta[:]), P, P)
                            vop(lambda e: e.tensor_tensor(out=inre[:], in0=abre[:], in1=ta[:], op=ALU.mult), P, P)
                            vop(lambda e: e.scalar_tensor_tensor(out=inim[:], in0=abim[:], scalar=-1.0, in1=ta[:], op0=ALU.mult, op1=ALU.mult), P, P)
                            for (Pr, Pi, br_, bi_, nmax) in ((PWr, PWi, abre, abim, 9), (NPr, NPi, inre, inim, 8)):
                                vop(lambda e, Pr=Pr: e.memset(Pr[:, :, 0], 1.0), (), P)
                                vop(lambda e, Pi=Pi: e.memset(Pi[:, :, 0], 0.0), (), P)
                                for m in range(1, nmax):
                                    vop(lambda e, Pr=Pr, br_=br_, m=m: e.tensor_tensor(out=ta[:], in0=Pr[:, :, m - 1], in1=br_[:], op=ALU.mult), P, P)
                                    vop(lambda e, Pi=Pi, bi_=bi_, m=m: e.tensor_tensor(out=tb2[:], in0=Pi[:, :, m - 1], in1=bi_[:], op=ALU.mult), P, P)
                                    vop(lambda e, Pr=Pr, m=m: e.tensor_tensor(out=Pr[:, :, m], in0=ta[:], in1=tb2[:], op=ALU.subtract), P, P)
                                    vop(lambda e, Pr=Pr, bi_=bi_, m=m: e.tensor_tensor(out=ta[:], in0=Pr[:, :, m - 1], in1=bi_[:], op=ALU.mult), P, P)
                                    vop(lambda e, Pi=Pi, br_=br_, m=m: e.tensor_tensor(out=tb2[:], in0=Pi[:, :, m - 1], in1=br_[:], op=ALU.mult), P, P)
                                    vop(lambda e, Pi=Pi, m=m: e.tensor_tensor(out=Pi[:, :, m], in0=ta[:], in1=tb2[:], op=ALU.add), P, P)
                            braw = [kb.sb(pes, "braw%d" % i, [128, 16, 16]) for i in range(2)]
                            b_br = [Buf("br%d" % i) for i in range(4)]
                            for ri, src in enumerate((s5b_re_d, s5b_im_d)):
                                for d in range(2):
                                    kb.dma(braw[ri][d * 64:(d + 1) * 64, :, :], src[l, d].rearrange("g n p -> n g p"), w=[b_br[ri * 2 + d]])
                            t3a = kb.sb(pes, "t3a", [128, 16, 16]); t3b = kb.sb(pes, "t3b", [128, 16, 16])
                            zrb = zre[:].unsqueeze(2).broadcast_to([128, 16, 16])
                            zib = zim[:].unsqueeze(2).broadcast_to([128, 16, 16])
                            vop(lambda e: e.tensor_tensor(out=t3a[:], in0=braw[0][:], in1=zrb, op=ALU.mult), P + b_br, P)
                            vop(lambda e: e.tensor_tensor(out=t3b[:], in0=braw[1][:], in1=zib, op=ALU.mult), P + b_br, P)
                            vop(lambda e: e.tensor_tensor(out=bbr[:], in0=t3a[:], in1=t3b[:], op=ALU.subtract), P, P)
                            vop(lambda e: e.tensor_tensor(out=t3a[:], in0=braw[1][:], in1=zrb, op=ALU.mult), P, P)
                            vop(lambda e: e.tensor_tensor(out=t3b[:], in0=braw[0][:], in1=zib, op=ALU.mult), P, P)
                            vop(lambda e: e.tensor_tensor(out=bbi[:], in0=t3a[:], in1=t3b[:], op=ALU.add), P, P)
                            cst = kb.sb(pes, "cst", [128, 128]); b_cst = [Buf("cst0"), Buf("cst1")]
                            for ri, src in enumerate((s5c_re_d, s5c_im_d)):
                                dstC = Cr if ri == 0 else Ci
                                for half in range(2):
                                    for d in range(2):
                                        kb.dma(cst[:, d * 64:(d + 1) * 64], src[l, d, half * 8:(half + 1) * 8].rearrange("g p n -> (g p) n"), w=[b_cst[d]])
                                    pt, pb = kb.ps()
                                    kb.pe(lambda e, pt=pt: e.transpose(out=pt[:, 0:128], in_=cst[:], identity=ident[:]), r=b_cst + [b_ident], w=[pb])
                                    kb.dve(lambda e, pt=pt, dstC=dstC, half=half: e.tensor_copy(out=dstC[:, half * 8:(half + 1) * 8, :], in_=pt[:, 0:128].rearrange("p (g q) -> p g q", q=16)),
                                           r=[pb], w=[b_P])
                            b_dg = [Buf("dg%d" % i) for i in range(8)]
                            for i in range(8):
                                kb.dma(Dg[i * 16:(i + 1) * 16, :], s5d_d[l].rearrange("(g p) -> p g", p=16), w=[b_dg[i]], slow=True)
                            b_smk = Buf("smk")
                            kb.dma(smk[:], s5mask_d.rearrange("m p c -> p m c"), w=[b_smk])
                            vop(lambda e: e.tensor_copy(out=Dg[:], in_=Dg[:]), b_dg + [b_smk], P)
                            S.barrier()
                            pes.close()
                        if stage == 51:
                            return

                        with ExitStack() as xes:
                            Xx = kb.sb(xes, "Xx", [128, NCT, 16, 8, 16]); b_Xx = Buf("Xx")
                            wv, bw = wload(w_in[l].rearrange("(k p) c -> p k c", p=128)[:, :, 0:256], 8, 256)
                            for ctile in range(NCT):
                                for i in range(8):
                                    pt, pb = kb.ps()
                                    t0_ = ctile * CW * 8 + i
                                    for k in range(8):
                                        kb.pe(lambda e, pt=pt, k=k, t0_=t0_, wv=wv: e.matmul(pt[0:CW, 0:256], lhsT=h[:, k, t0_:t0_ + (CW - 1) * 8 + 1:8], rhs=wv[:, k, :], start=(k == 0), stop=(k == 7)),
                                              r=[bw, b_h], w=[pb])
                                    kb.act(lambda e, pt=pt, ctile=ctile, i=i: e.activation(out=Xx[0:CW, ctile, :, i, :], in_=pt[0:CW, 0:256].rearrange("p (g q) -> p g q", q=16), func=AF.Copy), r=[pb], w=[b_Xx])
                            for ctile in range(NCT):
                                for g4 in range(4):
                                    pt, pb = kb.ps()
                                    for gg in range(4):
                                        g = g4 * 4 + gg
                                        kb.pe(lambda e, pt=pt, gg=gg, g=g, ctile=ctile: e.transpose(out=pt[:, gg * CW:(gg + 1) * CW], in_=Xx[0:CW, ctile, g, :, :], identity=ident[0:CW, 0:CW]),
                                              r=[b_Xx, b_ident], w=[pb])
                                    kb.act(lambda e, pt=pt, g4=g4, ctile=ctile: e.activation(out=U[:, g4 * 4:(g4 + 1) * 4, ctile * CW:(ctile + 1) * CW],
                                                                                     in_=pt[:, 0:4 * CW].rearrange("p (g c) -> p g c", g=4), func=AF.Copy), r=[pb], w=[b_U])
                            S.barrier()
                        if stage == 52:
                            return

                        for hf in range(2):
                            G0 = hf * 8
                            GS = slice(G0, G0 + 8)
                            with ExitStack() as hes:
                                WD = kb.sb(hes, "WD", [128, 8, 2, 128], BF16); b_WD = Buf("WD")
                                KT = kb.sb(hes, "KT", [128, 8, 128], BF16); b_KT = Buf("KT")
                                Vr = kb.sb(hes, "Vr", [128, 8, 8, 16]); Vi = kb.sb(hes, "Vi", [128, 8, 8, 16]); b_V = Buf("V")
                                AA = kb.sb(hes, "AA", [128, 2, 8]); BB = kb.sb(hes, "BB", [128, 2, 8]); b_AB = Buf("AB")
                                P = [b_P]
                                if is_sample:
                                    for ri in range(2):
                                        vop(lambda e, ri=ri: e.tensor_copy(out=AA[:, ri, :], in_=PWr[:, GS, 8]), P, [b_AB])
                                    vop(lambda e: e.tensor_scalar(out=BB[:, 0, :], in0=PWi[:, GS, 8], scalar1=-1.0, scalar2=None, op0=ALU.mult), P, [b_AB])
                                    vop(lambda e: e.tensor_copy(out=BB[:, 1, :], in_=PWi[:, GS, 8]), P, [b_AB])
                                for wes in ([ExitStack()] if is_sample else []):
                                    W = [kb.sb(wes, "wk%d" % i, [128, 8, 8, 16]) for i in range(4)]
                                    b_W = [Buf("wk%d" % i) for i in range(4)]
                                    t4a = kb.sb(wes, "t4a", [128, 8, 8, 16]); t4b = kb.sb(wes, "t4b", [128, 8, 8, 16]); b_t4 = Buf("t4")

                                    def cmul(outr, outi, b_out, ar, ai, br_, bi_, rows, neg_im=False):
                                        A_r = ar.unsqueeze(3).broadcast_to([64, 8, 8, 16]); A_i = ai.unsqueeze(3).broadcast_to([64, 8, 8, 16])
                                        B_r = br_.unsqueeze(2).broadcast_to([64, 8, 8, 16]); B_i = bi_.unsqueeze(2).broadcast_to([64, 8, 8, 16])
                                        ta_, tb_ = t4a[rows], t4b[rows]
                                        vop(lambda e: e.tensor_tensor(out=ta_, in0=A_r, in1=B_r, op=ALU.mult), P, [b_t4])
                                        vop(lambda e: e.tensor_tensor(out=tb_, in0=A_i, in1=B_i, op=ALU.mult), P, [b_t4])
                                        vop(lambda e: e.tensor_tensor(out=outr[rows], in0=ta_, in1=tb_, op=ALU.subtract), [b_t4], [b_out])
                                        vop(lambda e: e.tensor_tensor(out=ta_, in0=A_r, in1=B_i, op=ALU.mult), P, [b_t4])
                                        vop(lambda e: e.tensor_tensor(out=tb_, in0=A_i, in1=B_r, op=ALU.mult), P, [b_t4])
                                        if neg_im:
                                            vop(lambda e: e.scalar_tensor_tensor(out=outi[rows], in0=ta_, scalar=-1.0, in1=tb_, op0=ALU.mult, op1=ALU.subtract), [b_t4], [b_out])
                                        else:
                                            vop(lambda e: e.tensor_tensor(out=outi[rows], in0=ta_, in1=tb_, op=ALU.add), [b_t4], [b_out])
                                    F_, B_ = slice(0, 64), slice(64, 128)
                                    cmul(W[0], W[1], b_W[0], PWr[F_, GS, 7::-1], PWi[F_, GS, 7::-1], bbr[F_, GS, :], bbi[F_, GS, :], F_)
                                    cmul(W[0], W[1], b_W[0], PWr[B_, GS, 0:8], PWi[B_, GS, 0:8], bbr[B_, GS, :], bbi[B_, GS, :], B_)
                                    for g in range(8):
                                        pt, pb = kb.ps()
                                        for ri in range(2):
                                            kb.pe(lambda e, pt=pt, g=g, ri=ri: e.transpose(out=pt[:, ri * 128:(ri + 1) * 128], in_=W[ri][:, g, :, :], identity=ident[:]), r=[b_W[0], b_ident], w=[pb])
                                        kb.act(lambda e, pt=pt, g=g: e.activation(out=WD[:, g, :, :], in_=pt[:, 0:256].rearrange("p (r c) -> p r c", r=2), func=AF.Copy), r=[pb], w=[b_WD])
                                    cmul(W[2], W[3], b_W[2], NPr[F_, GS, 0:8], NPi[F_, GS, 0:8], bbr[F_, GS, :], bbi[F_, GS, :], F_)
                                    cmul(W[2], W[3], b_W[2], PWr[B_, GS, 0:8], PWi[B_, GS, 0:8], bbr[B_, GS, :], bbi[B_, GS, :], B_)
                                    cmul(W[0], W[1], b_W[0], PWr[F_, GS, 0:8], PWi[F_, GS, 0:8], Cr[F_, GS, :], Ci[F_, GS, :], F_, neg_im=True)
                                    cmul(W[0], W[1], b_W[0], NPr[B_, GS, 0:8], NPi[B_, GS, 0:8], Cr[B_, GS, :], Ci[B_, GS, :], B_, neg_im=True)
                                    ktmp = kb.sb(wes, "ktmp", [128, 2, 128]); b_ktmp = Buf("ktmp")
                                    for g in range(8):
                                        pts = []
                                        for d in range(2):
                                            rows = slice(d * 64, (d + 1) * 64)
                                            pt, pb = kb.ps()
                                            pts.append((pt, pb))
                                            kb.pe(lambda e, pt=pt, g=g, rows=rows: e.matmul(pt[:, 0:128], lhsT=W[2][rows, g, :, :], rhs=W[0][rows, g, :, :], start=True, stop=False),
                                                  r=[b_W[2], b_W[0]], w=[pb])
                                            kb.pe(lambda e, pt=pt, g=g, rows=rows: e.matmul(pt[:, 0:128], lhsT=W[3][rows, g, :, :], rhs=W[1][rows, g, :, :], start=False, stop=True),
                                                  r=[b_W[2], b_W[0]], w=[pb])
                                            kb.dve(lambda e, pt=pt, d=d: e.tensor_tensor(out=ktmp[:, d, :], in0=pt[:, 0:128], in1=smk[:, d, :], op=ALU.mult), r=[pb, b_P], w=[b_ktmp])
                                        kb.pool(lambda e: e.tensor_tensor(out=ktmp[:, 0, :], in0=ktmp[:, 0, :], in1=ktmp[:, 1, :], op=ALU.add), r=[b_ktmp], w=[b_ktmp])
                                        kb.dve(lambda e, g=g: e.scalar_tensor_tensor(out=KT[:, g, :], in0=ident[:], scalar=Dg[:, G0 + g:G0 + g + 1], in1=ktmp[:, 0, :], op0=ALU.mult, op1=ALU.add),
                                               r=[b_ktmp, b_P, b_ident], w=[b_KT])
                                    cmul(Vr, Vi, b_V, PWr[F_, GS, 1:9], PWi[F_, GS, 1:9], Cr[F_, GS, :], Ci[F_, GS, :], F_, neg_im=True)
                                    cmul(Vr, Vi, b_V, PWr[B_, GS, 8:0:-1], PWi[B_, GS, 8:0:-1], Cr[B_, GS, :], Ci[B_, GS, :], B_, neg_im=True)
                                    S.barrier()
                                    wes.close()
                                ck = (l, hf)
                                if ck not in s5cache:
                                    s5cache[ck] = {
                                        "KT": (nc.dram_tensor("s5c_KT_%d_%d" % ck, [128, 8, 128], BF16, kind="Internal").ap(), Buf("cKT")),
                                        "WD": (nc.dram_tensor("s5c_WD_%d_%d" % ck, [128, 8, 2, 128], BF16, kind="Internal").ap(), Buf("cWD")),
                                        "Vr": (nc.dram_tensor("s5c_Vr_%d_%d" % ck, [128, 8, 8, 16], F32, kind="Internal").ap(), Buf("cVr")),
                                        "Vi": (nc.dram_tensor("s5c_Vi_%d_%d" % ck, [128, 8, 8, 16], F32, kind="Internal").ap(), Buf("cVi")),
                                        "AA": (nc.dram_tensor("s5c_AA_%d_%d" % ck, [128, 2, 8], F32, kind="Internal").ap(), Buf("cAA")),
                                        "BB": (nc.dram_tensor("s5c_BB_%d_%d" % ck, [128, 2, 8], F32, kind="Internal").ap(), Buf("cBB")),
                                    }
                                cc_ = s5cache[ck]
                                if is_sample:
                                    for nm_, (t_, tb_) in (("KT", (KT, b_KT)), ("WD", (WD, b_WD)), ("Vr", (Vr, b_V)), ("Vi", (Vi, b_V)), ("AA", (AA, b_AB)), ("BB", (BB, b_AB))):
                                        kb.dma(cc_[nm_][0], t_[:], r=[tb_], w=[cc_[nm_][1]])
                                else:
                                    b_Vi2 = Buf("Vi2"); b_BB2 = Buf("BB2")
                                    kb.dma(KT[:], cc_["KT"][0], r=[cc_["KT"][1]], w=[b_KT])
                                    kb.dma(WD[:], cc_["WD"][0], r=[cc_["WD"][1]], w=[b_WD])
                                    kb.dma(Vr[:], cc_["Vr"][0], r=[cc_["Vr"][1]], w=[b_V])
                                    kb.dma(Vi[:], cc_["Vi"][0], r=[cc_["Vi"][1]], w=[b_Vi2])
                                    kb.dma(AA[:], cc_["AA"][0], r=[cc_["AA"][1]], w=[b_AB])
                                    kb.dma(BB[:], cc_["BB"][0], r=[cc_["BB"][1]], w=[b_BB2])
                                    vop(lambda e: e.tensor_copy(out=AA[:, 0, 0:1], in_=AA[:, 0, 0:1]), [b_AB, b_BB2, b_Vi2, b_V], [b_AB, b_V])
                                if stage == 53:
                                    return
                                SS = kb.sb(hes, "SS", [128, NST + 1 + (16 if NST >= 64 else 0), nseq, 2, 8]); b_SS = Buf("SS")
                                vop(lambda e: e.memset(SS[:, 0, :, :, :], 0.0), (), [b_SS])
                                if is_sample:
                                    b_si = [Buf("si%d" % i) for i in range(4)]
                                    for ri, src in enumerate((s5st_re_d, s5st_im_d)):
                                        for d in range(2):
                                            kb.dma(SS[d * 64:(d + 1) * 64, 0, 0, ri, :], src[l, d, G0:G0 + 8].rearrange("g n -> n g"), r=[b_SS], w=[b_si[ri * 2 + d]], slow=True)
                                    vop(lambda e: e.tensor_copy(out=SS[:, 0, 0, 0, 0:1], in_=SS[:, 0, 0, 0, 0:1]), b_si, [b_SS])
                                for g in range(8):
                                    for ri in range(2):
                                        pt, pb = kb.ps()
                                        kb.pe(lambda e, pt=pt, g=g, ri=ri: e.matmul(pt[:, 0:NCHT], lhsT=WD[:, g, ri, :], rhs=U[:, G0 + g, :], start=True, stop=True), r=[b_WD, b_U], w=[pb])
                                        for sq_ in range(nseq):
                                            kb.act(lambda e, pt=pt, g=g, ri=ri, sq_=sq_: e.activation(out=SS[0:64, 1:NST + 1, sq_, ri, g], in_=pt[0:64, sq_ * NST:(sq_ + 1) * NST], func=AF.Copy),
                                                   r=[pb], w=[b_SS])
                                            kb.dve(lambda e, pt=pt, g=g, ri=ri, sq_=sq_: e.tensor_copy(out=SS[64:128, NST:0:-1, sq_, ri, g], in_=pt[64:128, sq_ * NST:(sq_ + 1) * NST]),
                                                   r=[pb], w=[b_SS])
                                if stage == 54:
                                    S.barrier()
                                    return
                                if NST < 64:
                                    ts_ = kb.sb(hes, "ts_", [128, nseq, 2, 8]); tu_ = kb.sb(hes, "tu_", [128, nseq, 2, 8]); b_ts = Buf("ts")
                                    AAb = AA[:].unsqueeze(1).broadcast_to([128, nseq, 2, 8])
                                    BBb = BB[:].unsqueeze(1).broadcast_to([128, nseq, 2, 8])
                                    for s_ in range(NST):
                                        vop(lambda e, s_=s_: e.tensor_tensor(out=ts_[:], in0=SS[:, s_, :, :, :], in1=AAb, op=ALU.mult), [b_SS, b_AB], [b_ts])
                                        vop(lambda e, s_=s_: e.tensor_tensor(out=tu_[:], in0=SS[:, s_, :, ::-1, :], in1=BBb, op=ALU.mult), [b_SS, b_AB], [b_ts])
                                        vop(lambda e: e.tensor_tensor(out=ts_[:], in0=ts_[:], in1=tu_[:], op=ALU.add), [b_ts], [b_ts])
                                        vop(lambda e, s_=s_: e.tensor_tensor(out=SS[:, s_ + 1, :, :, :], in0=SS[:, s_ + 1, :, :, :], in1=ts_[:], op=ALU.add), [b_ts, b_SS], [b_SS])
                                else:
                                    R_ = 16
                                    NBk = NST // R_
                                    SSb = SS[:, 1:1 + (NBk + 1) * R_, 0, :, :].rearrange("p (b r) i g -> p b r i g", r=R_)
                                    tl = kb.sb(hes, "tl", [128, NBk + 1, 2, 8]); tl2 = kb.sb(hes, "tl2", [128, NBk + 1, 2, 8]); b_tl = Buf("tl")
                                    CC = kb.sb(hes, "CC", [128, NBk + 1, 2, 8]); b_CC = Buf("CC")
                                    PA = kb.sb(hes, "PA", [128, R_, 2, 8]); PB = kb.sb(hes, "PB", [128, R_, 2, 8]); b_PAB = Buf("PAB")
                                    tf = [kb.sb(hes, "tf%d" % i, [128, R_, 2, 8]) for i in range(4)]
                                    b_tf = [Buf("tf0"), Buf("tf1")]
                                    AAl = AA[:].unsqueeze(1).broadcast_to([128, NBk + 1, 2, 8])
                                    BBl = BB[:].unsqueeze(1).broadcast_to([128, NBk + 1, 2, 8])
                                    vop(lambda e: e.memset(SSb[:, NBk, :, :, :], 0.0), [b_SS], [b_SS])
                                    vop(lambda e: e.tensor_copy(out=SSb[:, NBk, 0, 0, :], in_=AA[:, 0, :]), [b_AB, b_SS], [b_SS])
                                    vop(lambda e: e.tensor_copy(out=SSb[:, NBk, 0, 1, :], in_=BB[:, 1, :]), [b_AB, b_SS], [b_SS])
                                    for r in range(1, R_):
                                        vop(lambda e, r=r: e.tensor_tensor(out=tl[:], in0=SSb[:, :, r - 1, :, :], in1=AAl, op=ALU.mult), [b_SS, b_AB], [b_tl])
                                        vop(lambda e, r=r: e.tensor_tensor(out=tl2[:], in0=SSb[:, :, r - 1, ::-1, :], in1=BBl, op=ALU.mult), [b_SS, b_AB], [b_tl])
                                        vop(lambda e: e.tensor_tensor(out=tl[:], in0=tl[:], in1=tl2[:], op=ALU.add), [b_tl], [b_tl])
                                        vop(lambda e, r=r: e.tensor_tensor(out=SSb[:, :, r, :, :], in0=SSb[:, :, r, :, :], in1=tl[:], op=ALU.add), [b_tl, b_SS], [b_SS])
                                    vop(lambda e: e.tensor_copy(out=PA[:, :, 0, :], in_=SSb[:, NBk, :, 0, :]), [b_SS], [b_PAB])
                                    vop(lambda e: e.tensor_copy(out=PA[:, :, 1, :], in_=SSb[:, NBk, :, 0, :]), [b_SS], [b_PAB])
                                    vop(lambda e: e.tensor_scalar(out=PB[:, :, 0, :], in0=SSb[:, NBk, :, 1, :], scalar1=-1.0, scalar2=None, op0=ALU.mult), [b_SS], [b_PAB])
                                    vop(lambda e: e.tensor_copy(out=PB[:, :, 1, :], in_=SSb[:, NBk, :, 1, :]), [b_SS], [b_PAB])
                                    vop(lambda e: e.tensor_copy(out=CC[:, 0, :, :], in_=SS[:, 0, 0, :, :]), [b_SS], [b_CC])
                                    for b_ in range(NBk):
                                        vop(lambda e, b_=b_: e.tensor_tensor(out=tf[0][:, 0, :, :], in0=CC[:, b_, :, :], in1=PA[:, R_ - 1, :, :], op=ALU.mult), [b_CC, b_PAB], [b_tf[0]])
                                        vop(lambda e, b_=b_: e.tensor_tensor(out=tf[1][:, 0, :, :], in0=CC[:, b_, ::-1, :], in1=PB[:, R_ - 1, :, :], op=ALU.mult), [b_CC, b_PAB], [b_tf[0]])
                                        vop(lambda e: e.tensor_tensor(out=tf[0][:, 0, :, :], in0=tf[0][:, 0, :, :], in1=tf[1][:, 0, :, :], op=ALU.add), [b_tf[0]], [b_tf[0]])
                                        vop(lambda e, b_=b_: e.tensor_tensor(out=CC[:, b_ + 1, :, :], in0=SSb[:, b_, R_ - 1, :, :], in1=tf[0][:, 0, :, :], op=ALU.add), [b_tf[0], b_SS], [b_CC])
                                    for b_ in range(NBk):
                                        k2 = b_ % 2
                                        ta_, tb_ = tf[2 * k2], tf[2 * k2 + 1]
                                        Cb = CC[:, b_, :, :].unsqueeze(1).broadcast_to([128, R_, 2, 8])
                                        Cs = CC[:, b_, ::-1, :].unsqueeze(1).broadcast_to([128, R_, 2, 8])
                                        vop(lambda e, ta_=ta_, Cb=Cb: e.tensor_tensor(out=ta_[:], in0=PA[:], in1=Cb, op=ALU.mult), [b_CC, b_PAB], [b_tf[k2]])
                                        vop(lambda e, tb_=tb_, Cs=Cs: e.tensor_tensor(out=tb_[:], in0=PB[:], in1=Cs, op=ALU.mult), [b_CC, b_PAB], [b_tf[k2]])
                                        vop(lambda e, ta_=ta_, tb_=tb_: e.tensor_tensor(out=ta_[:], in0=ta_[:], in1=tb_[:], op=ALU.add), [b_tf[k2]], [b_tf[k2]])
                                        vop(lambda e, ta_=ta_, b_=b_: e.tensor_tensor(out=SSb[:, b_, :, :, :], in0=SSb[:, b_, :, :, :], in1=ta_[:], op=ALU.add), [b_tf[k2], b_SS], [b_SS])
                                if not is_sample:
                                    for sq_ in range(nseq):
                                        for ri, dst in enumerate((s5o_re, s5o_im)):
                                            for d in range(2):
                                                kb.dma(dst[sq_, l, d, G0:G0 + 8].rearrange("g n -> n g"), SS[d * 64:(d + 1) * 64, NST, sq_, ri, :], r=[b_SS], is_out=True, sembuf=b_SS, slow=True)
                                if stage == 55:
                                    S.barrier()
                                    return
                                yin = [kb.sb(hes, "yin%d" % i, [128, NCHT]) for i in range(2)]
                                b_yin = [Buf("yin0"), Buf("yin1")]
                                gA = kb.sb(hes, "gA", [128, 512]); b_gA = Buf("gA")
                                gB = kb.sb(hes, "gB", [128, 512]); b_gB = Buf("gB")
                                yg4 = kb.sb(hes, "yg4", [128, 4, NCHT]); b_yg4 = Buf("yg4")
                                SSc = kb.sb(hes, "SSc", [128, nseq, 2, 4, NST]); b_SSc = Buf("SSc")
                                for g4 in range(2):
                                    for sq_ in range(nseq):
                                        kb.act(lambda e, g4=g4, sq_=sq_: e.activation(out=SSc[0:64, sq_, :, :, :], in_=SS[0:64, 0:NST, sq_, :, g4 * 4:(g4 + 1) * 4].rearrange("p s r g -> p r g s"),
                                                                                func=AF.Copy), r=[b_SS], w=[b_SSc])
                                        kb.dve(lambda e, g4=g4, sq_=sq_: e.tensor_copy(out=SSc[64:128, sq_, :, :, :], in_=SS[64:128, NST - 1::-1, sq_, :, g4 * 4:(g4 + 1) * 4].rearrange("p s r g -> p r g s")),
                                               r=[b_SS], w=[b_SSc])
                                    for gg in range(4):
                                        g = g4 * 4 + gg
                                        yi = g % 2
                                        p1, pb1 = kb.ps()
                                        kb.pe(lambda e, p1=p1, g=g: e.matmul(p1[:, 0:NCHT], lhsT=KT[:, g, :], rhs=U[:, G0 + g, :], start=True, stop=True), r=[b_KT, b_U], w=[pb1])
                                        p2, pb2 = kb.ps()
                                        p3, pb3 = kb.ps()
                                        for sq_ in range(nseq):
                                            for d in range(2):
                                                rows = slice(d * 64, (d + 1) * 64)
                                                pp, ppb = (p2, pb2) if d == 0 else (p3, pb3)
                                                for ri, Vt in enumerate((Vr, Vi)):
                                                    rhs = SSc[rows, sq_, ri, gg, :]
                                                    kb.pe(lambda e, pp=pp, g=g, rows=rows, Vt=Vt, rhs=rhs, sq_=sq_, ri=ri: e.matmul(pp[:, sq_ * NST:(sq_ + 1) * NST], lhsT=Vt[rows, g, :, :], rhs=rhs,
                                                                                                                   start=(ri == 0), stop=(ri == 1)), r=[b_V, b_SSc], w=[ppb])
                                        kb.act(lambda e, p2=p2, yi=yi: e.activation(out=yin[yi][:], in_=p2[:, 0:NCHT], func=AF.Copy), r=[pb2], w=[b_yin[yi]])
                                        kb.dve(lambda e, p3=p3, yi=yi: e.tensor_tensor(out=yin[yi][:], in0=p3[:, 0:NCHT], in1=yin[yi][:], op=ALU.add), r=[pb3, b_yin[yi]], w=[b_yin[yi]])
                                        kb.dve(lambda e, p1=p1, yi=yi, gg=gg: e.tensor_tensor(out=yg4[:, gg, :], in0=p1[:, 0:NCHT], in1=yin[yi][:], op=ALU.add), r=[pb1, b_yin[yi]], w=[b_yg4])
                                    if stage == 56:
                                        continue
                                    for ctile in range(NCT):
                                        pt, pb = kb.ps()
                                        for gg in range(4):
                                            kb.pe(lambda e, pt=pt, gg=gg, ctile=ctile: e.transpose(out=pt[0:CW, gg * 128:(gg + 1) * 128], in_=yg4[:, gg, ctile * CW:(ctile + 1) * CW], identity=ident[:]),
                                                  r=[b_yg4, b_ident], w=[pb])
                                        gcol = (G0 + g4 * 4) * 16
                                        src = pt[0:CW, :].rearrange("p (g j q) -> p g j q", g=4, j=8)
                                        dsty = Yx[0:CW, ctile, :, gcol:gcol + 64].rearrange("p j (g q) -> p g j q", g=4)
                                        gelu_psum(src, pb, dsty, b_Yx, gA[0:CW, :].rearrange("p (g j q) -> p g j q", g=4, j=8), b_gA,
                                                  gB[0:CW, :].rearrange("p (g j q) -> p g j q", g=4, j=8), b_gB)
                                S.barrier()
                        if stage in (56, 57):
                            return

                        with ExitStack() as qes:
                            y5 = kb.sb(qes, "y5", [128, 2, NT], BF16); b_y5 = Buf("y5")
                            prod = kb.sb(qes, "prod", [128, 2, NT]); b_prod = Buf("prod")
                            cat = kb.sb(qes, "cata", [128, 2, NT], BF16); b_cat = Buf("cata")
                            sgt = kb.sb(qes, "sgt", [128, 512]); b_sgt = Buf("sgt")
                            for ctile in range(NCT):
                                for ct in range(2):
                                    ptb, pb = kb.ps()
                                    ptv = ptb.bitcast(BF16)
                                    for j in range(8):
                                        kb.pe(lambda e, ptv=ptv, j=j, ct=ct, ctile=ctile: e.transpose(out=ptv[:, j * CW:(j + 1) * CW], in_=Yx[0:CW, ctile, j, ct * 128:(ct + 1) * 128], identity=identb[0:CW, 0:CW]),
                                              r=[b_Yx, b_ident], w=[pb])
                                    kb.act(lambda e, ptv=ptv, ct=ct, ctile=ctile: e.activation(out=y5[:, ct, ctile * 8 * CW:(ctile + 1) * 8 * CW], in_=ptv[:, 0:8 * CW], func=AF.Copy), r=[pb], w=[b_y5])
                            wv, bw = wload(wglu_d[l].rearrange("(k p) c -> p k c", p=128), 2, 256)
                            for ct in range(2):
                                for blk in range(NB):
                                    sl = slice(blk * 512, (blk + 1) * 512)
                                    pt, pb = kb.ps()
                                    for k in range(2):
                                        kb.pe(lambda e, pt=pt, k=k, ct=ct, sl=sl, wv=wv: e.matmul(pt[:], lhsT=wv[:, k, ct * 128:(ct + 1) * 128], rhs=y5[:, k, sl], start=(k == 0), stop=(k == 1)),
                                              r=[bw, b_y5], w=[pb])
                                    kb.act(lambda e, pt=pt: e.activation(out=sgt[:], in_=pt[:], func=AF.Sigmoid), r=[pb], w=[b_sgt])
                                    kb.dve(lambda e, ct=ct, sl=sl: e.tensor_tensor(out=prod[:, ct, sl], in0=y5[:, ct, sl], in1=sgt[:], op=ALU.mult), r=[b_y5, b_sgt], w=[b_prod])
                            fm_norm(prod, b_prod, gnT[:, l, 0:2], cat, b_cat, 256, ones_bf, False, qes)

                            def xv(dm, blk):
                                if NCT == 2:
                                    ctile, jb = blk // 2, blk % 2
                                    v = x[:, dm, ctile * 1024:(ctile + 1) * 1024].rearrange("p (c j) -> p j c", j=8)
                                    return v[:, jb * 4:(jb + 1) * 4, :]
                                return x[:, dm, :].rearrange("p (c j) -> p j c", j=8)
                            wout_add(l, 0, cat, b_cat, xv=xv)
                            S.barrier()

                for l in range(DEPTH if stage < 40 else 1):
                    norm_to_h(l, 0)
                    if stage == 40:
                        mixer_hgrn(l)
                        continue
                    if stage >= 50:
                        mixer_s5(l)
                        continue
                    if stage >= 2:
                        mixer_gmlp(l)
                    if stage >= 3:
                        mixer_fnet(l)
                    if stage >= 4:
                        mixer_hgrn(l)
                    if stage >= 5:
                        mixer_s5(l)
                    norm_to_h(l, 1)
                    ffn(l)

                with ExitStack() as oes:
                    gB = kb.sb(oes, "gB", [128, D]); b_gB = Buf("gB")
                    kb.dma(gB[:], fng.rearrange("(o n) -> o n", o=1).broadcast_to([128, D]), w=[b_gB])
                    yo = [kb.sb(oes, "yo%d" % i, [128, D]) for i in range(2)]
                    b_yo = [Buf("yo0"), Buf("yo1")]
                    junk = kb.sb(oes, "junk", [128, 512]); b_junk = Buf("junk")
                    ssq = [kb.sb(oes, "ssq%d" % i, [128, 2]) for i in range(2)]
                    b_ssq = [Buf("ssq0"), Buf("ssq1")]
                    for tt in range(NT // 128):
                        s = tt % 2
                        pts = []
                        for kq in range(2):
                            pt, pb = kb.ps()
                            pts.append((pt, pb))
                            for kk in range(4):
                                k = kq * 4 + kk
                                kb.pe(lambda e, k=k, kk=kk, pt=pt, tt=tt: e.transpose(out=pt[:, kk * 128:(kk + 1) * 128], in_=x[:, k, tt * 128:(tt + 1) * 128], identity=ident[:]),
                                      r=[b_x, b_ident], w=[pb])
                            kb.act(lambda e, pt=pt, s=s, kq=kq: e.activation(out=junk[:], in_=pt[:], func=AF.Square, accum_out=ssq[s][:, kq:kq + 1]), r=[pb], w=[b_junk, b_ssq[s]])
                        kb.dve(lambda e, s=s: e.tensor_tensor(out=ssq[s][:, 0:1], in0=ssq[s][:, 0:1], in1=ssq[s][:, 1:2], op=ALU.add), r=[b_ssq[s]], w=[b_ssq[s]])
                        kb.act(lambda e, s=s: e.activation(out=ssq[s][:, 0:1], in_=ssq[s][:, 0:1], func=AF.Sqrt, scale=1.0 / D, bias=EPS), r=[b_ssq[s]], w=[b_ssq[s]])
                        kb.dve(lambda e, s=s: e.reciprocal(out=ssq[s][:, 0:1], in_=ssq[s][:, 0:1]), r=[b_ssq[s]], w=[b_ssq[s]])
                        for kq in range(2):
                            pt, pb = pts[kq]
                            kb.dve(lambda e, pt=pt, s=s, kq=kq: e.scalar_tensor_tensor(out=yo[s][:, kq * 512:(kq + 1) * 512], in0=pt[:], scalar=ssq[s][:, 0:1],
                                                                                in1=gB[:, kq * 512:(kq + 1) * 512], op0=ALU.mult, op1=ALU.mult),
                                   r=[pb, b_ssq[s], b_gB], w=[b_yo[s]])
                        kb.dma(y_dram[tt * 128:(tt + 1) * 128, :], yo[s][:], r=[b_yo[s]], is_out=True, sembuf=b_yo[s])
                    S.barrier()

        run_group("s", xs_d, ys_d, 2048, 2048, 1, True)
        if stage < 40:
            run_group("p", xp_d, yp_d, 512, 256, 0, False)
        S.finish()
        S.emit()
    return nc


def make_consts():
    c = {"ident": np.eye(128, dtype=np.float32)}
    n = np.arange(64)
    ang = 2 * np.pi * np.outer(n, n) / 64
    C64 = np.cos(ang) / 8.0
    S64 = np.sin(ang) / 8.0
    bd = np.zeros((256, 512), np.float64)
    for hd in range(4):
        bd[hd * 64:(hd + 1) * 64, hd * 64:(hd + 1) * 64] = C64
        bd[hd * 64:(hd + 1) * 64, 256 + hd * 64:256 + (hd + 1) * 64] = S64
    c["bdcs"] = bd.astype(ml_dtypes.bfloat16)
    i = np.arange(128)
    same = (i[:, None] // 32) == (i[None, :] // 32)
    mF = (same & (i[:, None] <= i[None, :])).astype(np.float32)
    mB = (same & (i[:, None] >= i[None, :])).astype(np.float32)
    bdm = ((i[:, None] // 64) == (i[None, :] // 64)).astype(np.float32)
    c["hmask"] = np.stack([mF, mB, bdm], axis=0)
    jj = i // 16
    c["s5mask"] = np.stack([(jj[None, :] >= jj[:, None]), (jj[:, None] >= jj[None, :])], axis=0).astype(np.float32)
    c["cmask"] = ((i[:, None] // 32) == np.arange(4)[None, :]).astype(np.float32)
    sm = np.ones((128, 2048), np.float32)
    sm[:, ::32] = 0.0
    c["smask"] = sm.astype(ml_dtypes.bfloat16)
    for nm, T in (("dft_s", 2048), ("dft_p", 256)):
        t = np.arange(T)
        a = 2 * np.pi * (np.outer(t, t) % T) / T
        m = np.stack([np.cos(a), -np.sin(a)], axis=0) / np.sqrt(T)
        c[nm] = m.astype(ml_dtypes.bfloat16)
    return c


_CACHE = {}
_DBG = {}


def kernel(**inputs):
    inp = {k: np.ascontiguousarray(np.asarray(v)) for k, v in inputs.items()}
    if "nc" not in _CACHE:
        _CACHE["nc"] = build_program()
    nc = _CACHE["nc"]
    consts = make_consts()
    shared = {k: inp[k] for k in ["w_ada", "b_ada", "norm1_g", "norm2_g", "w_in", "w_out", "ffn_w_up", "ffn_conv_w", "ffn_conv_b",
                                  "ffn_w_down", "final_norm_g", "grp_norm_g", "gm_norm_g", "gm_ws", "gm_bs", "fn_w", "hg_lb_logits", "s5_lam_re", "s5_lam_im", "s5_log_dt", "s5_b_re", "s5_b_im", "s5_c_re", "s5_c_im", "s5_d", "s5_w_glu"]}
    in_maps = []
    for c in range(NCORES):
        b = c // 4
        m = dict(shared)
        m.update(consts)
        m["xs"] = inp["x_sample"][b]
        m["xp"] = inp["x_prompt"][2 * c:2 * c + 2].reshape(512, D)
        m["cvec"] = np.stack([inp["c_ctx"], inp["c"][b]], axis=0)
        m["hg_state"] = inp["state_hgrn"][b]
        m["s5st_re"] = inp["state_s5_re"][b]
        m["s5st_im"] = inp["state_s5_im"][b]
        in_maps.append(m)
    res = run_bass_kernel_spmd(nc, in_maps, core_ids=list(range(NCORES)))
    r = res.results
    y_prompt = np.concatenate([r[c]["yp"].reshape(2, 256, D) for c in range(NCORES)], axis=0)
    y_sample = np.stack([r[0]["ys"], r[4]["ys"]], axis=0)
    st_re = np.concatenate([r[c]["s5o_re"] for c in range(NCORES)], axis=0)
    st_im = np.concatenate([r[c]["s5o_im"] for c in range(NCORES)], axis=0)
    st_hg = np.concatenate([r[c]["st_hg"] for c in range(NCORES)], axis=0)
    return y_prompt, y_sample, st_re, st_im, st_hg
```

```python
import numpy as np
import ml_dtypes
from contextlib import ExitStack
import concourse.bass as bass
import concourse.mybir as mybir
from concourse.bass_utils import run_bass_kernel_spmd

F32 = mybir.dt.float32
BF16 = mybir.dt.bfloat16
I32 = mybir.dt.int32
AF = mybir.ActivationFunctionType
ALU = mybir.AluOpType

D = 1024
DEPTH = 2
DIN = 2304
DFF = 2816
EPS = 1e-6
NCORES = 8
STAGE = 5


class Buf:
    __slots__ = ("name", "w", "r", "dsem", "dcnt", "fence", "fdone")
    current_fence = None

    def __init__(self, name=""):
        self.name = name
        self.w = None
        self.r = []
        self.dsem = None
        self.dcnt = 0
        self.fence = Buf.current_fence
        self.fdone = set()


class _Rec:
    def __init__(self):
        self.call = None

    def __getattr__(self, name):
        def f(*a, **k):
            self.call = (name, a, k)
            return self
        return f


def _bind(fn):
    r = _Rec()
    fn(r)
    assert r.call is not None
    return r.call


class Sched:
    NAMES = ["pe", "act", "dve", "pool", "sp"]

    def __init__(self, nc, es):
        self.nc = nc
        self.es = es
        self.q = {e: [] for e in self.NAMES}
        self.cnt = {e: 0 for e in self.NAMES}
        self.esem = {e: es.enter_context(nc.semaphore("sem_" + e)) for e in self.NAMES}
        self.waited = {e: {} for e in self.NAMES}
        self.out_events = []
        self.ndsem = 0
        self.dma_sems = []
        self.free_slots = []
        self.all_slots = []
        self.active = []

    def _wait(self, eng, ev):
        src, sem, val = ev
        k = id(sem)
        if self.waited[eng].get(k, 0) >= val:
            return
        self.waited[eng][k] = val
        self.q[eng].append(("wait", sem, val))

    def _collect(self, eng, reads, writes):
        for b in list(reads) + list(writes):
            if b.fence is not None and eng not in b.fdone:
                b.fdone.add(eng)
                for ev in b.fence:
                    if ev[0] != eng:
                        self._wait(eng, ev)
        for b in reads:
            if b.w is not None:
                if b.w[0] == eng and eng == "pe":
                    continue
                self._wait(eng, b.w)
        for b in writes:
            if b.w is not None:
                if not (b.w[0] == eng):
                    self._wait(eng, b.w)
            for ev in b.r:
                if ev[0] == eng:
                    continue
                self._wait(eng, ev)

    def op(self, eng, fn, reads=(), writes=()):
        self._collect(eng, reads, writes)
        self.cnt[eng] += 1
        ev = (eng, self.esem[eng], self.cnt[eng])
        self.q[eng].append(("op", _bind(fn), self.esem[eng], 1))
        for b in writes:
            b.w = ev
            b.r = []
        for b in reads:
            b.r.append(ev)
        return ev

    def dma(self, fn, reads=(), writes=(), sembuf=None, eng="sp", is_out=False):
        self._collect(eng, reads, writes)
        sb = sembuf if sembuf is not None else (writes[0] if writes else reads[0])
        if sb.dsem is None:
            if self.free_slots:
                slot = self.free_slots.pop()
                if slot[1] > 0:
                    self._wait(eng, ("dma", slot[0], slot[1]))
            else:
                slot = [self.es.enter_context(self.nc.semaphore("dsem%d" % self.ndsem)), 0]
                self.ndsem += 1
                self.all_slots.append(slot)
            sb.dsem = slot
            self.active.append(sb)
        slot = sb.dsem
        slot[1] += 16
        ev = ("dma", slot[0], slot[1])
        self.q[eng].append(("op", _bind(fn), slot[0], 16))
        for b in writes:
            b.w = ev
            b.r = []
        for b in reads:
            b.r.append(ev)
        return ev

    def barrier(self, force=False):
        evs = []
        for o in self.NAMES:
            if self.cnt[o] > 0:
                evs.append((o, self.esem[o], self.cnt[o]))
        for slot in self.all_slots:
            if slot[1] > 0:
                evs.append(("dma", slot[0], slot[1]))
        Buf.current_fence = evs
        if force:
            for e in self.NAMES:
                for ev in evs:
                    if ev[0] != e:
                        self._wait(e, ev)
        for sb in self.active:
            self.free_slots.append(sb.dsem)
            sb.dsem = None
        self.active = []

    def finish(self):
        self.barrier(force=True)

    def emit(self):
        nc = self.nc
        engmap = {"pe": "tensor", "act": "scalar", "dve": "vector", "pool": "gpsimd", "sp": "sync"}
        with nc.Block() as block:
            for name in self.NAMES:
                items = self.q[name]

                def body(e, items=items):
                    for it in items:
                        if it[0] == "wait":
                            e.wait_ge(it[1], it[2])
                        else:
                            nm_, a_, k_ = it[1]
                            ins = getattr(e, nm_)(*a_, **k_)
                            ins.then_inc(it[2], it[3])
                getattr(block, engmap[name])(body)


class KB:
    def __init__(self, nc, es):
        self.nc = nc
        self.es = es
        self.S = Sched(nc, es)
        self.din = {}
        self.dout = {}
        self.nps = 0
        self.psb = []
        for i in range(8):
            t = es.enter_context(nc.psum_tensor("psb%d" % i, [128, 512], F32))
            self.psb.append((t, Buf("ps%d" % i)))
        self.uid = 0

    def inp(self, name, shape, dtype=F32):
        self.din[name] = self.nc.dram_tensor(name, list(shape), dtype, kind="ExternalInput").ap()
        return self.din[name]

    def outp(self, name, shape, dtype=F32):
        self.dout[name] = self.nc.dram_tensor(name, list(shape), dtype, kind="ExternalOutput").ap()
        return self.dout[name]

    def sb(self, es, name, shape, dtype=F32):
        self.uid += 1
        return es.enter_context(self.nc.sbuf_tensor("%s_%d" % (name, self.uid), list(shape), dtype))

    def ps(self):
        t, b = self.psb[self.nps % 8]
        self.nps += 1
        return t, b

    def dbg(self, name, ap, shape, dtype, r=()):
        if not getattr(self, "debug", False):
            return
        d = self.nc.dram_tensor("dbg_" + name, list(shape), dtype, kind="ExternalOutput").ap()
        self.dbgs = getattr(self, "dbgs", []) + ["dbg_" + name]
        bb = Buf("dbg")
        self.S.dma(lambda e: e.dma_start(out=d, in_=ap), list(r), [bb])

    def pe(self, fn, r=(), w=()):
        return self.S.op("pe", fn, r, w)

    def act(self, fn, r=(), w=()):
        return self.S.op("act", fn, r, w)

    def dve(self, fn, r=(), w=()):
        return self.S.op("dve", fn, r, w)

    def pool(self, fn, r=(), w=()):
        return self.S.op("pool", fn, r, w)

    def dma(self, out, in_, r=(), w=(), eng="sp", is_out=False, sembuf=None, slow=False):
        if slow:
            fn = lambda e: e.dma_start(out=out, in_=in_, allow_slow_non_contiguous=True)
        else:
            fn = lambda e: e.dma_start(out=out, in_=in_)
        return self.S.dma(fn, r, w, sembuf=sembuf, eng=eng, is_out=is_out)


def build_program(stage=STAGE):
    nc = bass.Bass("TRN2", target_bir_lowering=False)
    Buf.current_fence = None
    with ExitStack() as es:
        kb = KB(nc, es)
        kb.debug = (stage >= 40)
        _DBG['kb'] = kb
        S = kb.S
        xs_d = kb.inp("xs", [2048, D])
        xp_d = kb.inp("xp", [512, D])
        cv_d = kb.inp("cvec", [2, D])
        w_ada = kb.inp("w_ada", [DEPTH, D, 6 * D])
        b_ada = kb.inp("b_ada", [DEPTH, 6 * D])
        norm1_g = kb.inp("norm1_g", [DEPTH, D])
        norm2_g = kb.inp("norm2_g", [DEPTH, D])
        w_in = kb.inp("w_in", [DEPTH, D, DIN])
        w_out = kb.inp("w_out", [DEPTH, D, D])
        w_up = kb.inp("ffn_w_up", [DEPTH, D, 2 * DFF])
        cw_d = kb.inp("ffn_conv_w", [DEPTH, 3, 2 * DFF])
        cb_d = kb.inp("ffn_conv_b", [DEPTH, 2 * DFF])
        w_dn = kb.inp("ffn_w_down", [DEPTH, DFF, D])
        fng = kb.inp("final_norm_g", [D])
        ident_d = kb.inp("ident", [128, 128])
        grp_g = kb.inp("grp_norm_g", [DEPTH, D])
        gm_ng = kb.inp("gm_norm_g", [DEPTH, 256])
        gm_ws = kb.inp("gm_ws", [DEPTH, 4, 128, 128])
        gm_bs = kb.inp("gm_bs", [DEPTH, 4, 128])
        fn_w = kb.inp("fn_w", [DEPTH, 256, 256])
        bdcs_d = kb.inp("bdcs", [256, 512], BF16)
        hmask_d = kb.inp("hmask", [3, 128, 128])
        lamre_d = kb.inp("s5_lam_re", [DEPTH, 2, 16, 64])
        lamim_d = kb.inp("s5_lam_im", [DEPTH, 2, 16, 64])
        logdt_d = kb.inp("s5_log_dt", [DEPTH, 2, 16])
        s5b_re_d = kb.inp("s5_b_re", [DEPTH, 2, 16, 64, 16])
        s5b_im_d = kb.inp("s5_b_im", [DEPTH, 2, 16, 64, 16])
        s5c_re_d = kb.inp("s5_c_re", [DEPTH, 2, 16, 16, 64])
        s5c_im_d = kb.inp("s5_c_im", [DEPTH, 2, 16, 16, 64])
        s5d_d = kb.inp("s5_d", [DEPTH, 256])
        wglu_d = kb.inp("s5_w_glu", [DEPTH, 256, 256])
        s5mask_d = kb.inp("s5mask", [2, 128, 128])
        s5st_re_d = kb.inp("s5st_re", [DEPTH, 2, 16, 64])
        s5st_im_d = kb.inp("s5st_im", [DEPTH, 2, 16, 64])
        s5o_re = kb.outp("s5o_re", [2, DEPTH, 2, 16, 64])
        s5o_im = kb.outp("s5o_im", [2, DEPTH, 2, 16, 64])
        smask_d = kb.inp("smask", [128, 2048], BF16)
        cmask_d = kb.inp("cmask", [128, 4])
        hglog = kb.inp("hg_lb_logits", [DEPTH, 2, 256])
        hgst_d = kb.inp("hg_state", [DEPTH, 2, 4, 64, 64])
        sthg_o = kb.outp("st_hg", [2, DEPTH, 2, 4, 64, 64])
        dft_s = kb.inp("dft_s", [2, 2048, 2048], BF16)
        dft_p = kb.inp("dft_p", [2, 256, 256], BF16)
        ys_d = kb.outp("ys", [2048, D])
        yp_d = kb.outp("yp", [512, D])

        ident = kb.sb(es, "ident", [128, 128]); b_ident = Buf("ident")
        ones_bf = kb.sb(es, "ones", [128, 128], BF16); b_ones = Buf("ones")
        modT = kb.sb(es, "modT", [128, DEPTH, 2, 48]); b_mod = Buf("modT")
        n1g = kb.sb(es, "n1g", [128, DEPTH, 8]); n2g = kb.sb(es, "n2g", [128, DEPTH, 8])
        b_cv = Buf("chanvecs")
        gnT = kb.sb(es, "gnT", [128, DEPTH, 8])
        lbT = kb.sb(es, "lbT", [128, DEPTH, 2, 2]); omlT = kb.sb(es, "omlT", [128, DEPTH, 2, 2]); nomlT = kb.sb(es, "nomlT", [128, DEPTH, 2, 2])
        b_lb = Buf("lb")
        identb = kb.sb(es, "identb", [128, 128], BF16)
        cwT = kb.sb(es, "cwT", [128, DEPTH, 3, 44]); cbT = kb.sb(es, "cbT", [128, DEPTH, 44])
        Amod = kb.sb(es, "Amod", [128, DEPTH, 2, 2, 8])
        b_A = Buf("Amod")
        NWS = 3
        wbf = [kb.sb(es, "wbf%d" % i, [128, 2816], BF16) for i in range(NWS)]
        b_wbf = [Buf("wbf%d" % i) for i in range(NWS)]
        b_wbf2 = [Buf("wbfb%d" % i) for i in range(NWS)]
        wctr = [0]

        kb.dma(ident[:], ident_d, w=[b_ident])
        kb.dve(lambda e: e.memset(ones_bf[:], 1.0), w=[b_ones])
        kb.dve(lambda e: e.tensor_copy(out=identb[:], in_=ident[:]), r=[b_ident], w=[b_ident])

        def wload(src_ap, kt, ncol):
            s = wctr[0] % NWS
            wctr[0] += 1
            n = kt * ncol
            dv = wbf[s][:, 0:n].rearrange("p (k c) -> p k c", k=kt)
            kb.dma(dv, src_ap, w=[b_wbf[s], b_wbf2[s]], eng="pool")
            return dv, b_wbf[s]

        def load_chanvec(dst_ap, src_rows_ap, nt, tmp_es, extra_r=(), wbuf=None):
            st = kb.sb(tmp_es, "cvst", [nt, 128]); bst = Buf("cvst")
            kb.dma(st[:], src_rows_ap, w=[bst])
            pt, pb = kb.ps()
            kb.pe(lambda e: e.transpose(out=pt[:, 0:nt], in_=st[:], identity=ident[0:nt, 0:nt]), r=[bst, b_ident], w=[pb])
            kb.dve(lambda e: e.tensor_copy(out=dst_ap, in_=pt[:, 0:nt]), r=[pb], w=[wbuf])

        with ExitStack() as pes:
            for l in range(DEPTH):
                load_chanvec(n1g[:, l, :], norm1_g[l].rearrange("(m p) -> m p", p=128), 8, pes, wbuf=b_cv)
                load_chanvec(n2g[:, l, :], norm2_g[l].rearrange("(m p) -> m p", p=128), 8, pes, wbuf=b_cv)
                load_chanvec(gnT[:, l, :], grp_g[l].rearrange("(m p) -> m p", p=128), 8, pes, wbuf=b_cv)
                for j in range(3):
                    load_chanvec(cwT[:, l, j, :], cw_d[l, j].rearrange("(m p) -> m p", p=128), 44, pes, wbuf=b_cv)
                load_chanvec(cbT[:, l, :], cb_d[l].rearrange("(m p) -> m p", p=128), 44, pes, wbuf=b_cv)
            craw = kb.sb(pes, "craw", [128, 2, 8]); b_craw = Buf("craw")
            for cv in range(2):
                load_chanvec(craw[:, cv, :], cv_d[cv].rearrange("(m p) -> m p", p=128), 8, pes, wbuf=b_craw)
            sc = kb.sb(pes, "sc", [128, 8, 2], BF16); b_sc = Buf("sc")
            for cv in range(2):
                kb.act(lambda e, cv=cv: e.activation(out=sc[:, :, cv], in_=craw[:, cv, :], func=AF.Silu), r=[b_craw], w=[b_sc])
            lgT = kb.sb(pes, "lgT", [128, DEPTH, 2, 2]); b_lg = Buf("lgT")
            for l in range(DEPTH):
                for d in range(2):
                    load_chanvec(lgT[:, l, d, :], hglog[l, d].rearrange("(m p) -> m p", p=128), 2, pes, wbuf=b_lg)
            kb.dve(lambda e: e.memset(lbT[:, 0, :, :], 0.0), w=[b_lb])
            kb.dve(lambda e: e.tensor_tensor(out=lbT[:, 1, :, :], in0=lgT[:, 1, :, :], in1=lgT[:, 0, :, :], op=ALU.subtract), r=[b_lg], w=[b_lb])
            kb.act(lambda e: e.activation(out=lbT[:, 1, :, :], in_=lbT[:, 1, :, :], func=AF.Sigmoid), r=[b_lb], w=[b_lb])
            kb.dve(lambda e: e.tensor_scalar(out=omlT[:], in0=lbT[:], scalar1=-1.0, scalar2=1.0, op0=ALU.mult, op1=ALU.add), r=[b_lb], w=[b_lb])
            kb.dve(lambda e: e.tensor_scalar(out=nomlT[:], in0=lbT[:], scalar1=1.0, scalar2=-1.0, op0=ALU.mult, op1=ALU.add), r=[b_lb], w=[b_lb])
            badaT = kb.sb(pes, "badaT", [128, DEPTH, 48]); b_bada = Buf("bada")
            for l in range(DEPTH):
                load_chanvec(badaT[:, l, :], b_ada[l].rearrange("(m p) -> m p", p=128), 48, pes, wbuf=b_bada)
            ast = [kb.sb(pes, "ast%d" % i, [128, 8, 768], BF16) for i in range(2)]
            b_ast = [Buf("ast0"), Buf("ast1")]
            for l in range(DEPTH):
                pt, pb = kb.ps()
                for ch in range(8):
                    s = ch % 2
                    kb.dma(ast[s][:], w_ada[l].rearrange("(k p) c -> p k c", p=128)[:, :, ch * 768:(ch + 1) * 768], w=[b_ast[s]], eng="pool")
                    for m in range(6):
                        mt = ch * 6 + m
                        for k in range(8):
                            kb.pe(lambda e, s=s, m=m, k=k, mt=mt, pt=pt: e.matmul(pt[:, mt * 2:mt * 2 + 2], lhsT=ast[s][:, k, m * 128:(m + 1) * 128],
                                                                  rhs=sc[:, k, :], start=(k == 0), stop=(k == 7)),
                                  r=[b_ast[s], b_sc], w=[pb])
                for cv in range(2):
                    kb.dve(lambda e, l=l, cv=cv, pt=pt: e.tensor_tensor(out=modT[:, l, cv, :], in0=pt[:, 0:96].rearrange("p (m c) -> p m c", c=2)[:, :, cv],
                                                                  in1=badaT[:, l, :], op=ALU.add), r=[pb, b_bada], w=[b_mod])
            for l in range(DEPTH):
                for cv in range(2):
                    for n, (gt, chunk) in enumerate(((n1g, 1), (n2g, 4))):
                        kb.dve(lambda e, l=l, cv=cv, n=n, gt=gt, chunk=chunk: e.scalar_tensor_tensor(
                            out=Amod[:, l, cv, n, :], in0=modT[:, l, cv, chunk * 8:(chunk + 1) * 8], scalar=1.0, in1=gt[:, l, :],
                            op0=ALU.add, op1=ALU.mult), r=[b_mod, b_cv], w=[b_A])
            S.barrier()

        s5cache = {}

        def run_group(gname, x_dram, y_dram, NT, T, cv, grid):
            is_sample = grid
            NB = NT // 512
            with ExitStack() as ges:
                x = kb.sb(ges, "x", [128, 8, NT]); b_x = Buf("x")
                h = kb.sb(ges, "h", [128, 8, NT], BF16); b_h = Buf("h")
                with ExitStack() as tes:
                    xin = [kb.sb(tes, "xin%d" % i, [128, D]) for i in range(2)]
                    b_xin = [Buf("xin0"), Buf("xin1")]
                    for tt in range(NT // 128):
                        s = tt % 2
                        kb.dma(xin[s][:], x_dram[tt * 128:(tt + 1) * 128, :], w=[b_xin[s]])
                        for kq in range(2):
                            pt, pb = kb.ps()
                            for kk in range(4):
                                k = kq * 4 + kk
                                kb.pe(lambda e, s=s, k=k, kk=kk, pt=pt: e.transpose(out=pt[:, kk * 128:(kk + 1) * 128], in_=xin[s][:, k * 128:(k + 1) * 128], identity=ident[:]),
                                      r=[b_xin[s], b_ident], w=[pb])
                            kb.act(lambda e, kq=kq, tt=tt, pt=pt: e.activation(out=x[:, kq * 4:(kq + 1) * 4, tt * 128:(tt + 1) * 128],
                                                                            in_=pt[:].rearrange("p (k t) -> p k t", k=4), func=AF.Copy),
                                   r=[pb], w=[b_x])
                    S.barrier()

                def norm_to_h(l, n):
                    shchunk = 0 if n == 0 else 3
                    with ExitStack() as nes:
                        sq = [kb.sb(nes, "sq%d" % i, [128, 8, 512], BF16) for i in range(2)]; b_sq = [Buf("sq0"), Buf("sq1")]
                        rs = [kb.sb(nes, "rs%d" % i, [128, 512]) for i in range(2)]; b_rs = [Buf("rs0"), Buf("rs1")]
                        tmp = [kb.sb(nes, "ntmp%d" % i, [128, 512]) for i in range(4)]
                        b_tmp = [Buf("nt%d" % i) for i in range(4)]
                        pts = {}

                        def stage1(blk):
                            c0 = blk * 512
                            j = blk % 2
                            kb.act(lambda e: e.activation(out=sq[j][:], in_=x[:, :, c0:c0 + 512], func=AF.Square), r=[b_x], w=[b_sq[j]])
                            pt, pb = kb.ps()
                            pts[blk] = (pt, pb)
                            for k in range(8):
                                kb.pe(lambda e, k=k: e.matmul(pt[:], lhsT=ones_bf[:], rhs=sq[j][:, k, :], start=(k == 0), stop=(k == 7)),
                                      r=[b_sq[j], b_ones], w=[pb])
                        stage1(0)
                        for blk in range(NB):
                            c0 = blk * 512
                            j = blk % 2
                            if blk + 1 < NB:
                                stage1(blk + 1)
                            pt, pb = pts[blk]
                            kb.act(lambda e, pt=pt, j=j: e.activation(out=rs[j][:], in_=pt[:], func=AF.Sqrt, scale=1.0 / D, bias=EPS), r=[pb], w=[b_rs[j]])
                            kb.dve(lambda e, j=j: e.reciprocal(out=rs[j][:], in_=rs[j][:]), r=[b_rs[j]], w=[b_rs[j]])
                            for k in range(8):
                                s = k % 4
                                kb.dve(lambda e, k=k, s=s, c0=c0, j=j: e.scalar_tensor_tensor(out=tmp[s][:], in0=x[:, k, c0:c0 + 512], scalar=Amod[:, l, cv, n, k:k + 1],
                                                                                        in1=rs[j][:], op0=ALU.mult, op1=ALU.mult), r=[b_x, b_rs[j], b_A], w=[b_tmp[s]])
                                kb.act(lambda e, k=k, s=s, c0=c0: e.activation(out=h[:, k, c0:c0 + 512], in_=tmp[s][:], func=AF.Identity,
                                                                        bias=modT[:, l, cv, shchunk * 8 + k:shchunk * 8 + k + 1], scale=1.0),
                                       r=[b_tmp[s], b_mod], w=[b_h])
                        S.barrier()

                def ffn(l):
                    gchunk = 5
                    TB = min(NT, 1024)
                    with ExitStack() as fes:
                        gated = kb.sb(fes, "gated", [128, 22, TB], BF16); b_gated = Buf("gated")
                        zc = [[kb.sb(fes, "zc%d%d" % (i, j), [128, 512]) for j in range(2)] for i in range(2)]
                        b_zc = [[Buf("zc"), Buf("zc")] for i in range(2)]
                        u = [kb.sb(fes, "u%d" % i, [128, 512]) for i in range(2)]
                        b_u = [Buf("u0"), Buf("u1")]
                        sg = [kb.sb(fes, "sg%d" % i, [128, 512]) for i in range(2)]
                        b_sg = [Buf("sg0"), Buf("sg1")]
                        it = [0]
                        rowlen = 64 if grid else T
                        for tb in range(NT // TB):
                            t0 = tb * TB
                            for m in range(22):
                                s = wctr[0] % NWS
                                wctr[0] += 1
                                wsrc = w_up[l].rearrange("(k p) c -> p k c", p=128)
                                kb.dma(wbf[s][:, 0:1024].rearrange("p (k c) -> p k c", k=8), wsrc[:, :, m * 128:(m + 1) * 128], w=[b_wbf[s]], eng="pool")
                                kb.dma(wbf[s][:, 1024:2048].rearrange("p (k c) -> p k c", k=8), wsrc[:, :, DFF + m * 128:DFF + (m + 1) * 128], w=[b_wbf2[s]], eng="pool")
                                wv = wbf[s][:, 0:2048].rearrange("p (ab k c) -> p ab k c", ab=2, k=8)
                                for blk in range(TB // 512):
                                    c0 = t0 + blk * 512
                                    i2 = it[0] % 2
                                    it[0] += 1
                                    pts = []
                                    for ab in range(2):
                                        pt, pb = kb.ps()
                                        pts.append((pt, pb))
                                        for k in range(8):
                                            kb.pe(lambda e, k=k, ab=ab, pt=pt, wv=wv, c0=c0: e.matmul(pt[:], lhsT=wv[:, ab, k, :], rhs=h[:, k, c0:c0 + 512],
                                                                                            start=(k == 0), stop=(k == 7)), r=[b_wbf[s], b_wbf2[s], b_h], w=[pb])
                                    for ab in range(2):
                                        pt, pb = pts[ab]
                                        ct = ab * 22 + m
                                        z = zc[i2][ab]
                                        bz = b_zc[i2][ab]
                                        kb.act(lambda e, z=z, pt=pt, ct=ct: e.activation(out=z[:], in_=pt[:], func=AF.Identity, scale=cwT[:, l, 1, ct:ct + 1], bias=cbT[:, l, ct:ct + 1]),
                                               r=[pb, b_cv], w=[bz])
                                        zr = z[:].rearrange("p (r c) -> p r c", c=rowlen)
                                        pr = pt[:].rearrange("p (r c) -> p r c", c=rowlen)
                                        kb.dve(lambda e, zr=zr, pr=pr, ct=ct: e.scalar_tensor_tensor(out=zr[:, :, 1:rowlen], in0=pr[:, :, 0:rowlen - 1], scalar=cwT[:, l, 0, ct:ct + 1],
                                                                                             in1=zr[:, :, 1:rowlen], op0=ALU.mult, op1=ALU.add), r=[pb, bz, b_cv], w=[bz])
                                        kb.dve(lambda e, zr=zr, pr=pr, ct=ct: e.scalar_tensor_tensor(out=zr[:, :, 0:rowlen - 1], in0=pr[:, :, 1:rowlen], scalar=cwT[:, l, 2, ct:ct + 1],
                                                                                             in1=zr[:, :, 0:rowlen - 1], op0=ALU.mult, op1=ALU.add), r=[pb, bz, b_cv], w=[bz])
                                    za, zb = zc[i2]
                                    bza, bzb = b_zc[i2]
                                    uu, bu = u[i2], b_u[i2]
                                    ss, bs = sg[i2], b_sg[i2]
                                    kb.act(lambda e, uu=uu, za=za: e.activation(out=uu[:], in_=za[:], func=AF.Square, scale=0.21145921592590945), r=[bza], w=[bu])
                                    kb.dve(lambda e, uu=uu, za=za: e.scalar_tensor_tensor(out=uu[:], in0=uu[:], scalar=1.0, in1=za[:], op0=ALU.add, op1=ALU.mult), r=[bu, bza], w=[bu])
                                    kb.act(lambda e, uu=uu, ss=ss: e.activation(out=ss[:], in_=uu[:], func=AF.Sigmoid, scale=1.5957691216057308), r=[bu], w=[bs])
                                    kb.dve(lambda e, ss=ss, za=za: e.tensor_tensor(out=ss[:], in0=ss[:], in1=za[:], op=ALU.mult), r=[bs, bza], w=[bs])
                                    kb.dve(lambda e, ss=ss, zb=zb, m=m, c0=c0, t0=t0: e.tensor_tensor(out=gated[:, m, c0 - t0:c0 - t0 + 512], in0=ss[:], in1=zb[:], op=ALU.mult),
                                           r=[bs, bzb], w=[b_gated])
                            for dm in range(8):
                                wv, bw = wload(w_dn[l].rearrange("(k p) c -> p k c", p=128)[:, :, dm * 128:(dm + 1) * 128], 22, 128)
                                for blk in range(TB // 512):
                                    c0 = t0 + blk * 512
                                    pt, pb = kb.ps()
                                    for k in range(22):
                                        kb.pe(lambda e, k=k, pt=pt, wv=wv, c0=c0, t0=t0: e.matmul(pt[:], lhsT=wv[:, k, :], rhs=gated[:, k, c0 - t0:c0 - t0 + 512], start=(k == 0), stop=(k == 21)),
                                              r=[bw, b_gated], w=[pb])
                                    kb.dve(lambda e, dm=dm, pt=pt, c0=c0: e.scalar_tensor_tensor(out=x[:, dm, c0:c0 + 512], in0=pt[:], scalar=modT[:, l, cv, gchunk * 8 + dm:gchunk * 8 + dm + 1],
                                                                                        in1=x[:, dm, c0:c0 + 512], op0=ALU.mult, op1=ALU.add), r=[pb, b_mod, b_x], w=[b_x])
                        S.barrier()

                NTT = NT // 128
                nseq = NT // T

                def gelu_psum(src, b_src, out_ap, b_out, tA, b_tA, tB, b_tB):
                    kb.act(lambda e: e.activation(out=tA, in_=src, func=AF.Square, scale=0.21145921592590945), r=[b_src], w=[b_tA])
                    kb.dve(lambda e: e.scalar_tensor_tensor(out=tA, in0=tA, scalar=1.0, in1=src, op0=ALU.add, op1=ALU.mult), r=[b_tA, b_src], w=[b_tA])
                    kb.act(lambda e: e.activation(out=tB, in_=tA, func=AF.Sigmoid, scale=1.5957691216057308), r=[b_tA], w=[b_tB])
                    kb.dve(lambda e: e.tensor_tensor(out=out_ap, in0=tB, in1=src, op=ALU.mult), r=[b_tB, b_src], w=[b_out])

                def proj_fm(l, c0col, ntile, cb):
                    wsrc = w_in[l].rearrange("(k p) c -> p k c", p=128)
                    for g0 in range(0, ntile, 2):
                        nt2 = min(2, ntile - g0)
                        wv, bw = wload(wsrc[:, :, c0col + g0 * 128:c0col + (g0 + nt2) * 128], 8, nt2 * 128)
                        for ti in range(nt2):
                            for blk in range(NB):
                                pt, pb = kb.ps()
                                for k in range(8):
                                    kb.pe(lambda e, k=k, pt=pt, wv=wv, ti=ti, blk=blk: e.matmul(pt[:], lhsT=wv[:, k, ti * 128:(ti + 1) * 128], rhs=h[:, k, blk * 512:(blk + 1) * 512],
                                                                                      start=(k == 0), stop=(k == 7)), r=[bw, b_h], w=[pb])
                                cb(g0 + ti, blk, pt, pb)

                def proj_tok(l, c0col, ncol, cb):
                    wsrc = w_in[l].rearrange("(k p) c -> p k c", p=128)
                    wv, bw = wload(wsrc[:, :, c0col:c0col + ncol], 8, ncol)
                    for tt in range(NTT):
                        pt, pb = kb.ps()
                        for k in range(8):
                            kb.pe(lambda e, k=k, pt=pt, wv=wv, tt=tt: e.matmul(pt[:, 0:ncol], lhsT=h[:, k, tt * 128:(tt + 1) * 128], rhs=wv[:, k, :],
                                                                       start=(k == 0), stop=(k == 7)), r=[bw, b_h], w=[pb])
                        cb(tt, pt, pb)

                def fm_norm(src, b_src, gain, dst, b_dst, nch, ones_t, per_tile, nes):
                    sq = [kb.sb(nes, "fsq%d" % i, [128, 2, 512], BF16) for i in range(2)]; b_sq = [Buf("fsq0"), Buf("fsq1")]
                    rs = [kb.sb(nes, "frs%d" % i, [128, 2, 512]) for i in range(2)]; b_rs = [Buf("frs0"), Buf("frs1")]
                    pts = {}
                    nrt = 2 if per_tile else 1

                    def stage1(blk):
                        c0 = blk * 512
                        j = blk % 2
                        kb.act(lambda e: e.activation(out=sq[j][:], in_=src[:, :, c0:c0 + 512], func=AF.Square), r=[b_src], w=[b_sq[j]])
                        lst = []
                        for ct in range(nrt):
                            pt, pb = kb.ps()
                            lst.append((pt, pb))
                            if per_tile:
                                kb.pe(lambda e, pt=pt, ct=ct: e.matmul(pt[:], lhsT=ones_t[:], rhs=sq[j][:, ct, :], start=True, stop=True), r=[b_sq[j], b_ones], w=[pb])
                            else:
                                for k in range(2):
                                    kb.pe(lambda e, pt=pt, k=k: e.matmul(pt[:], lhsT=ones_t[:], rhs=sq[j][:, k, :], start=(k == 0), stop=(k == 1)), r=[b_sq[j], b_ones], w=[pb])
                        pts[blk] = lst
                    stage1(0)
                    for blk in range(NB):
                        c0 = blk * 512
                        j = blk % 2
                        if blk + 1 < NB:
                            stage1(blk + 1)
                        for ct in range(nrt):
                            pt, pb = pts[blk][ct]
                            kb.act(lambda e, pt=pt, ct=ct, j=j: e.activation(out=rs[j][:, ct, :], in_=pt[:], func=AF.Sqrt, scale=1.0 / nch, bias=EPS), r=[pb], w=[b_rs[j]])
                            kb.dve(lambda e, ct=ct, j=j: e.reciprocal(out=rs[j][:, ct, :], in_=rs[j][:, ct, :]), r=[b_rs[j]], w=[b_rs[j]])
                        for ct in range(2):
                            rc = ct if per_tile else 0
                            kb.dve(lambda e, ct=ct, rc=rc, c0=c0, j=j: e.scalar_tensor_tensor(out=dst[:, ct, c0:c0 + 512], in0=src[:, ct, c0:c0 + 512], scalar=gain[:, ct:ct + 1],
                                                                                      in1=rs[j][:, rc, :], op0=ALU.mult, op1=ALU.mult), r=[b_src, b_rs[j], b_cv], w=[b_dst])

                def wout_add(l, mi, cat, b_cat, xv=None):
                    wv, bw = wload(w_out[l][mi * 256:(mi + 1) * 256, :].rearrange("(k p) c -> p k c", p=128), 2, 1024)
                    for dm in range(8):
                        for blk in range(NB):
                            pt, pb = kb.ps()
                            for k in range(2):
                                kb.pe(lambda e, k=k, pt=pt, wv=wv, dm=dm, blk=blk: e.matmul(pt[:], lhsT=wv[:, k, dm * 128:(dm + 1) * 128], rhs=cat[:, k, blk * 512:(blk + 1) * 512],
                                                                                  start=(k == 0), stop=(k == 1)), r=[bw, b_cat], w=[pb])
                            xa = x[:, dm, blk * 512:(blk + 1) * 512] if xv is None else xv(dm, blk)
                            pin = pt[:] if xv is None else pt[:].rearrange("p (j c) -> p j c", j=xa.shape[1])
                            kb.dve(lambda e, pin=pin, dm=dm, xa=xa: e.scalar_tensor_tensor(out=xa, in0=pin, scalar=modT[:, l, cv, 16 + dm:16 + dm + 1], in1=xa,
                                                                                 op0=ALU.mult, op1=ALU.add), r=[pb, b_mod, b_x], w=[b_x])

                def mixer_gmlp(l):
                    with ExitStack() as mes:
                        gu = kb.sb(mes, "gu", [128, 2, NT]); b_gu = Buf("gu")
                        gvt = kb.sb(mes, "gvt", [128, NTT, 256], BF16); b_gvt = Buf("gvt")
                        cat = kb.sb(mes, "catd", [128, 2, NT], BF16); b_cat = Buf("catd")
                        tA = kb.sb(mes, "tA", [128, 512]); b_tA = Buf("tA")
                        tB = kb.sb(mes, "tB", [128, 512]); b_tB = Buf("tB")
                        tg = kb.sb(mes, "tg", [128, 256]); b_tg = Buf("tg")
                        gmgB = kb.sb(mes, "gmgB", [128, 256]); b_gmgB = Buf("gmgB")
                        wsT = kb.sb(mes, "wsT", [128, 4, 128], BF16); b_wsT = Buf("wsT")
                        wsraw = kb.sb(mes, "wsraw", [128, 4, 128]); b_wsraw = Buf("wsraw")
                        bsB = kb.sb(mes, "bsB", [128, 2, 128]); b_bsB = [Buf("bsB%d" % i) for i in range(4)]
                        ss = kb.sb(mes, "gss", [128, 2]); b_ss = Buf("gss")
                        kb.dma(gmgB[:], gm_ng[l].rearrange("(o n) -> o n", o=1).broadcast_to([128, 256]), w=[b_gmgB])
                        kb.dma(wsraw[:], gm_ws[l].rearrange("h i j -> i h j"), w=[b_wsraw])
                        for hd in range(4):
                            pt, pb = kb.ps()
                            kb.pe(lambda e, hd=hd, pt=pt: e.transpose(out=pt[:, 0:128], in_=wsraw[:, hd, :], identity=ident[:]), r=[b_wsraw, b_ident], w=[pb])
                            kb.act(lambda e, hd=hd, pt=pt: e.activation(out=wsT[:, hd, :], in_=pt[:, 0:128], func=AF.Copy), r=[pb], w=[b_wsT])
                            ct, hh = hd // 2, hd % 2
                            kb.dma(bsB[hh * 64:(hh + 1) * 64, ct, :], gm_bs[l, hd].rearrange("(o n) -> o n", o=1).broadcast_to([64, 128]), w=[b_bsB[hd]])

                        def cb_gu(ct, blk, pt, pb):
                            gelu_psum(pt[:], pb, gu[:, ct, blk * 512:(blk + 1) * 512], b_gu, tA[:], b_tA, tB[:], b_tB)
                        proj_fm(l, 7 * 256, 2, cb_gu)

                        def cb_gv(tt, pt, pb):
                            gelu_psum(pt[:, 0:256], pb, tg[:], b_tg, tA[:, 0:256], b_tA, tB[:, 0:256], b_tB)
                            kb.act(lambda e: e.activation(out=tA[:, 0:256], in_=tg[:], func=AF.Square, accum_out=ss[:, 0:1]), r=[b_tg], w=[b_tA, b_ss])
                            kb.act(lambda e: e.activation(out=ss[:, 0:1], in_=ss[:, 0:1], func=AF.Sqrt, scale=1.0 / 256, bias=EPS), r=[b_ss], w=[b_ss])
                            kb.dve(lambda e: e.reciprocal(out=ss[:, 0:1], in_=ss[:, 0:1]), r=[b_ss], w=[b_ss])
                            kb.dve(lambda e, tt=tt: e.scalar_tensor_tensor(out=gvt[:, tt, :], in0=tg[:], scalar=ss[:, 0:1], in1=gmgB[:], op0=ALU.mult, op1=ALU.mult),
                                   r=[b_tg, b_ss, b_gmgB], w=[b_gvt])
                        proj_tok(l, 8 * 256, 256, cb_gv)

                        for tt in range(NTT):
                            for ct in range(2):
                                pt, pb = kb.ps()
                                for hh in range(2):
                                    hd = ct * 2 + hh
                                    kb.pe(lambda e, pt=pt, hh=hh, hd=hd, tt=tt: e.matmul(pt[hh * 64:(hh + 1) * 64, 0:128], lhsT=gvt[:, tt, hd * 64:(hd + 1) * 64], rhs=wsT[:, hd, :],
                                                                                 start=True, stop=True), r=[b_gvt, b_wsT], w=[pb])
                                kb.dve(lambda e, pt=pt, ct=ct: e.tensor_tensor(out=tA[:, 0:128], in0=pt[:, 0:128], in1=bsB[:, ct, :], op=ALU.add),
                                       r=[pb] + b_bsB, w=[b_tA])
                                kb.dve(lambda e, ct=ct, tt=tt: e.tensor_tensor(out=gu[:, ct, tt * 128:(tt + 1) * 128], in0=tA[:, 0:128], in1=gu[:, ct, tt * 128:(tt + 1) * 128], op=ALU.mult),
                                       r=[b_tA, b_gu], w=[b_gu])
                        fm_norm(gu, b_gu, gnT[:, l, 6:8], cat, b_cat, 256, ones_bf, False, mes)
                        wout_add(l, 3, cat, b_cat)
                        S.barrier()

                def mixer_fnet(l):
                    TT = T // 128
                    TBK = 256
                    dft_d = dft_s if T == 2048 else dft_p
                    with ExitStack() as mes:
                        cat = kb.sb(mes, "catc", [128, 2, NT], BF16); b_cat = Buf("catc")
                        xfr = kb.sb(mes, "xfr", [128, 2, NT], BF16); b_xfr = Buf("xfr")
                        bd = kb.sb(mes, "bd", [128, 2, 512], BF16); b_bd = Buf("bd")
                        kb.dma(bd[:], bdcs_d.rearrange("(k p) c -> p k c", p=128), w=[b_bd])
                        with ExitStack() as m2:
                            xcs = kb.sb(m2, "xcs", [128, NTT, 512], BF16); b_xcs = Buf("xcs")
                            with ExitStack() as m3:
                                xc = kb.sb(m3, "xc", [128, 2, NT], BF16); b_xc = Buf("xc")

                                def cb_xc(ct, blk, pt, pb):
                                    kb.act(lambda e: e.activation(out=xc[:, ct, blk * 512:(blk + 1) * 512], in_=pt[:], func=AF.Copy), r=[pb], w=[b_xc])
                                proj_fm(l, 6 * 256, 2, cb_xc)
                                for tt in range(NTT):
                                    pt, pb = kb.ps()
                                    for k in range(2):
                                        kb.pe(lambda e, pt=pt, k=k, tt=tt: e.matmul(pt[:], lhsT=xc[:, k, tt * 128:(tt + 1) * 128], rhs=bd[:, k, :], start=(k == 0), stop=(k == 1)),
                                              r=[b_xc, b_bd], w=[pb])
                                    kb.act(lambda e, pt=pt, tt=tt: e.activation(out=xcs[:, tt, :], in_=pt[:], func=AF.Copy), r=[pb], w=[b_xcs])
                                S.barrier()
                            with ExitStack() as m3:
                                dCs = [kb.sb(m3, "dC%d" % i, [128, TT, TBK], BF16) for i in range(2)]; b_dCs = [Buf("dC0"), Buf("dC1")]
                                dSs = [kb.sb(m3, "dS%d" % i, [128, TT, TBK], BF16) for i in range(2)]; b_dSs = [Buf("dS0"), Buf("dS1")]
                                dctr = 0
                                for sq_ in range(nseq):
                                    for tb in range(T // TBK):
                                        dC, b_dC, dS, b_dS = dCs[dctr % 2], b_dCs[dctr % 2], dSs[dctr % 2], b_dSs[dctr % 2]
                                        dctr += 1
                                        kb.dma(dC[:], dft_d[0].rearrange("(t p) c -> p t c", p=128)[:, :, tb * TBK:(tb + 1) * TBK], w=[b_dC])
                                        kb.dma(dS[:], dft_d[1].rearrange("(t p) c -> p t c", p=128)[:, :, tb * TBK:(tb + 1) * TBK], w=[b_dS])
                                        for ct in range(2):
                                            pt, pb = kb.ps()
                                            for tt in range(TT):
                                                kb.pe(lambda e, pt=pt, tt=tt, ct=ct, sq_=sq_: e.matmul(pt[:, 0:TBK], lhsT=xcs[:, sq_ * TT + tt, ct * 128:(ct + 1) * 128], rhs=dC[:, tt, :],
                                                                                             start=(tt == 0), stop=False), r=[b_xcs, b_dC], w=[pb])
                                            for tt in range(TT):
                                                kb.pe(lambda e, pt=pt, tt=tt, ct=ct, sq_=sq_: e.matmul(pt[:, 0:TBK], lhsT=xcs[:, sq_ * TT + tt, 256 + ct * 128:256 + (ct + 1) * 128], rhs=dS[:, tt, :],
                                                                                             start=False, stop=(tt == TT - 1)), r=[b_xcs, b_dS], w=[pb])
                                            c0 = sq_ * T + tb * TBK
                                            kb.act(lambda e, pt=pt, ct=ct, c0=c0: e.activation(out=xfr[:, ct, c0:c0 + TBK], in_=pt[:, 0:TBK], func=AF.Copy), r=[pb], w=[b_xfr])
                                S.barrier()
                        with ExitStack() as m2:
                            cpre = kb.sb(m2, "cpre", [128, 2, NT]); b_cpre = Buf("cpre")
                            wv, bw = wload(fn_w[l].rearrange("(k p) c -> p k c", p=128), 2, 256)
                            for ct in range(2):
                                for blk in range(NB):
                                    pt, pb = kb.ps()
                                    for k in range(2):
                                        kb.pe(lambda e, pt=pt, k=k, ct=ct, blk=blk, wv=wv: e.matmul(pt[:], lhsT=wv[:, k, ct * 128:(ct + 1) * 128], rhs=xfr[:, k, blk * 512:(blk + 1) * 512],
                                                                                          start=(k == 0), stop=(k == 1)), r=[bw, b_xfr], w=[pb])
                                    kb.act(lambda e, pt=pt, ct=ct, blk=blk: e.activation(out=cpre[:, ct, blk * 512:(blk + 1) * 512], in_=pt[:], func=AF.Copy), r=[pb], w=[b_cpre])
                            fm_norm(cpre, b_cpre, gnT[:, l, 4:6], cat, b_cat, 256, ones_bf, False, m2)
                            wout_add(l, 2, cat, b_cat)
                            S.barrier()

                def mixer_hgrn(l):
                    TT = T // 128
                    NCH = NT // 32
                    with ExitStack() as mes:
                        cat = kb.sb(mes, "catb", [128, 2, NT], BF16); b_cat = Buf("catb")
                        mk = kb.sb(mes, "hmask", [128, 3, 128]); b_mk = Buf("hmask")
                        onesbd = kb.sb(mes, "onesbd", [128, 128], BF16)
                        smask = kb.sb(mes, "smask", [128, NT], BF16); b_smask = Buf("smask")
                        kb.dma(mk[:], hmask_d.rearrange("m p c -> p m c"), w=[b_mk])
                        kb.dma(smask[:], smask_d[:, 0:NT], w=[b_smask])
                        kb.dve(lambda e: e.tensor_copy(out=onesbd[:], in_=mk[:, 2, :]), r=[b_mk], w=[b_mk])
                        stmp = kb.sb(mes, "stmp", [128, 512]); b_stmp = Buf("stmp")
                        stmp2 = kb.sb(mes, "stmp2", [128, 512]); b_stmp2 = Buf("stmp2")
                        Sst = [kb.sb(mes, "Sst%d" % i, [128, 128]) for i in range(2)]
                        b_S = [Buf("Sst0"), Buf("Sst1")]
                        Sbd = [kb.sb(mes, "Sbd%d" % i, [128, 8, 128], BF16) for i in range(2)]
                        b_Sbd = [[Buf("Sbd") for i in range(8)] for j in range(2)]
                        scT = [[kb.sb(mes, "scT%d%d" % (j, i), [128, 128], BF16) for i in range(2)] for j in range(2)]
                        b_scT = [[Buf("scT") for i in range(2)] for j in range(2)]
                        vblk = [kb.sb(mes, "vblk%d" % i, [128, 4, 128], BF16) for i in range(2)]
                        b_vblk = [Buf("vblk%d" % i) for i in range(2)]
                        cmk = kb.sb(mes, "cmk", [128, 4]); b_cmk = Buf("cmk")
                        kb.dma(cmk[:], cmask_d, w=[b_cmk])
                        vctr = [0]
                        for ct in range(2):
                            with ExitStack() as ces:
                                vtok = kb.sb(ces, "vtok", [128, NTT, 128], BF16); b_vtok = Buf("vtok")
                                oacc = kb.sb(ces, "oacc", [128, NT]); b_oacc = Buf("oacc")
                                Qt = [kb.sb(ces, "Qt%d" % i, [128, NT], BF16) for i in range(2)]; b_Qt = [Buf("Qt0"), Buf("Qt1")]
                                Kt = [kb.sb(ces, "Kt%d" % i, [128, NT], BF16) for i in range(2)]; b_Kt = [Buf("Kt0"), Buf("Kt1")]
                                Khtok = [kb.sb(ces, "Khtok%d" % i, [128, NTT, 128], BF16) for i in range(2)]; b_Khtok = [Buf("Kh0"), Buf("Kh1")]
                                glast = [kb.sb(ces, "glast%d" % i, [128, NCH]) for i in range(2)]; b_gl = [Buf("gl0"), Buf("gl1")]

                                def cb_v(tt, pt, pb):
                                    kb.act(lambda e: e.activation(out=vtok[:, tt, :], in_=pt[:, 0:128], func=AF.Copy), r=[pb], w=[b_vtok])
                                proj_tok(l, 4 * 256 + ct * 128, 128, cb_v)
                                Kh_sh = kb.sb(ces, "Kh", [128, NT], BF16); b_Kh_sh = Buf("Kh")
                                tmpA = [(kb.sb(ces, "bc%d" % i, [128, NT]), Buf("bc"), kb.sb(ces, "kk%d" % i, [128, NT], BF16), Buf("kk"), Kh_sh, b_Kh_sh) for i in range(2)]
                                for d in range(2):
                                    with ExitStack() as des:
                                        bc, b_bc, kk, b_kk, Kh, b_Kh = tmpA[d]
                                        lbA = lbT[:, l, d, ct:ct + 1]
                                        omlA = omlT[:, l, d, ct:ct + 1]
                                        nomlA = nomlT[:, l, d, ct:ct + 1]

                                        def cb_z(ti, blk, pt, pb):
                                            sl = slice(blk * 512, (blk + 1) * 512)
                                            kb.act(lambda e: e.activation(out=stmp[:], in_=pt[:], func=AF.Sigmoid), r=[pb], w=[b_stmp])
                                            kb.act(lambda e: e.activation(out=bc[:, sl], in_=stmp[:], func=AF.Ln, scale=omlA, bias=lbA), r=[b_stmp, b_lb], w=[b_bc])
                                            kb.dve(lambda e: e.tensor_scalar(out=kk[:, sl], in0=stmp[:], scalar1=nomlA, scalar2=omlA, op0=ALU.mult, op1=ALU.add),
                                                   r=[b_stmp, b_lb], w=[b_kk])
                                        proj_fm(l, (2 + d) * 256 + ct * 128, 1, cb_z)
                                        if d == 0:
                                            kb.dve(lambda e: e.tensor_tensor_scan(out=bc[:], data0=smask[:], data1=bc[:], initial=0.0, op0=ALU.mult, op1=ALU.add),
                                                   r=[b_bc, b_smask], w=[b_bc])
                                            kb.act(lambda e: e.activation(out=glast[d][:], in_=bc[:].rearrange("p (c t) -> p c t", t=32)[:, :, 31], func=AF.Exp), r=[b_bc], w=[b_gl[d]])
                                        else:
                                            kb.dve(lambda e: e.tensor_tensor_scan(out=bc[:, ::-1], data0=smask[:], data1=bc[:, ::-1], initial=0.0, op0=ALU.mult, op1=ALU.add),
                                                   r=[b_bc, b_smask], w=[b_bc])
                                            kb.act(lambda e: e.activation(out=glast[d][:], in_=bc[:].rearrange("p (c t) -> p c t", t=32)[:, :, 0], func=AF.Exp), r=[b_bc], w=[b_gl[d]])
                                        for blk in range(NB):
                                            sl = slice(blk * 512, (blk + 1) * 512)
                                            kb.act(lambda e, sl=sl: e.activation(out=stmp[:], in_=bc[:, sl], func=AF.Exp, scale=-1.0), r=[b_bc], w=[b_stmp])
                                            kb.dve(lambda e, sl=sl: e.tensor_tensor(out=Kt[d][:, sl], in0=kk[:, sl], in1=stmp[:], op=ALU.mult), r=[b_kk, b_stmp], w=[b_Kt[d]])
                                        kb.dve(lambda e: e.tensor_tensor(out=Kh[:].rearrange("p (c t) -> p c t", t=32), in0=Kt[d][:].rearrange("p (c t) -> p c t", t=32),
                                                                         in1=glast[d][:].unsqueeze(2).broadcast_to([128, NCH, 32]), op=ALU.mult), r=[b_Kt[d], b_gl[d]], w=[b_Kh])

                                        def cb_q(ti, blk, pt, pb):
                                            sl = slice(blk * 512, (blk + 1) * 512)
                                            kb.act(lambda e: e.activation(out=stmp2[:], in_=bc[:, sl], func=AF.Exp), r=[b_bc], w=[b_stmp2])
                                            kb.dve(lambda e: e.tensor_tensor(out=Qt[d][:, sl], in0=pt[:], in1=stmp2[:], op=ALU.mult), r=[pb, b_stmp2], w=[b_Qt[d]])
                                        proj_fm(l, 1 * 256 + ct * 128, 1, cb_q)
                                        for tt in range(NTT):
                                            ptb, pb = kb.ps()
                                            ptv = ptb.bitcast(BF16)
                                            kb.pe(lambda e, tt=tt, ptv=ptv: e.transpose(out=ptv[:, 0:128], in_=Kh[:, tt * 128:(tt + 1) * 128], identity=identb[:]), r=[b_Kh, b_ident], w=[pb])
                                            kb.act(lambda e, tt=tt, ptv=ptv: e.activation(out=Khtok[d][:, tt, :], in_=ptv[:, 0:128], func=AF.Copy), r=[pb], w=[b_Khtok[d]])
                                for sq_ in range(nseq):
                                    for d in range(2):
                                        kb.dve(lambda e, d=d: e.memset(Sst[d][:], 0.0), w=[b_S[d]])
                                        if is_sample:
                                            for hh in range(2):
                                                kb.dma(Sst[d][hh * 64:(hh + 1) * 64, hh * 64:(hh + 1) * 64], hgst_d[l, d, ct * 2 + hh], w=[b_S[d]])
                                        kb.dve(lambda e, d=d: e.tensor_tensor(out=Sbd[d][:, 0, :], in0=Sst[d][:], in1=mk[:, 2, :], op=ALU.mult), r=[b_S[d], b_mk], w=[b_Sbd[d][0]])
                                    for step in range(TT):
                                        par = step % 2
                                        info = []
                                        for d in range(2):
                                            ttl = step if d == 0 else TT - 1 - step
                                            tt = sq_ * TT + ttl
                                            tsl = slice(tt * 128, (tt + 1) * 128)
                                            for hh in range(2):
                                                ps_, pbs = kb.ps()
                                                kb.pe(lambda e, ps_=ps_, hh=hh, tsl=tsl, d=d: e.matmul(ps_[:, 0:128], lhsT=Kt[d][hh * 64:(hh + 1) * 64, tsl], rhs=Qt[d][hh * 64:(hh + 1) * 64, tsl],
                                                                                               start=True, stop=True), r=[b_Kt[d], b_Qt[d]], w=[pbs])
                                                kb.dve(lambda e, ps_=ps_, hh=hh, d=d: e.tensor_tensor(out=scT[d][hh][:], in0=ps_[:, 0:128], in1=mk[:, d, :], op=ALU.mult),
                                                       r=[pbs, b_mk], w=[b_scT[d][hh]])
                                            vs = vctr[0] % 2
                                            vctr[0] += 1
                                            kb.pool(lambda e, vs=vs, tt=tt: e.tensor_tensor(out=vblk[vs][:], in0=vtok[:, tt, :].unsqueeze(1).broadcast_to([128, 4, 128]),
                                                                                     in1=cmk[:].unsqueeze(2).broadcast_to([128, 4, 128]), op=ALU.mult),
                                                    r=[b_vtok, b_cmk], w=[b_vblk[vs]])
                                            pd, pbd = kb.ps()
                                            kb.pe(lambda e, pd=pd, vs=vs, tt=tt, d=d: e.matmul(pd[:], lhsT=Khtok[d][:, tt, :], rhs=vblk[vs][:].rearrange("p c v -> p (c v)"), start=True, stop=True),
                                                  r=[b_Khtok[d], b_vblk[vs]], w=[pbd])
                                            po, pbo = kb.ps()
                                            for hh in range(2):
                                                kb.pe(lambda e, po=po, hh=hh, tt=tt, d=d: e.matmul(po[hh * 64:(hh + 1) * 64, 0:128], lhsT=vtok[:, tt, hh * 64:(hh + 1) * 64], rhs=scT[d][hh][:],
                                                                                           start=True, stop=False), r=[b_vtok, b_scT[d][hh]], w=[pbo])
                                            info.append((tt, tsl, pd, pbd, po, pbo))
                                        for ci in range(4):
                                            for d in range(2):
                                                tt, tsl, pd, pbd, po, pbo = info[d]
                                                cl = ci if d == 0 else 3 - ci
                                                cg = tt * 4 + cl
                                                csl = slice(tt * 128 + cl * 32, tt * 128 + (cl + 1) * 32)
                                                scur = par * 4 + ci
                                                snxt = par * 4 + ci + 1 if ci < 3 else (1 - par) * 4
                                                kb.pe(lambda e, po=po, cl=cl, csl=csl, ci=ci, scur=scur, d=d: e.matmul(po[:, cl * 32:(cl + 1) * 32], lhsT=Sbd[d][:, scur, :], rhs=Qt[d][:, csl], start=False, stop=(ci == 3)),
                                                      r=[b_Sbd[d][scur], b_Qt[d]], w=[pbo])
                                                kb.dve(lambda e, pd=pd, cg=cg, cl=cl, d=d: e.scalar_tensor_tensor(out=Sst[d][:], in0=Sst[d][:], scalar=glast[d][:, cg:cg + 1], in1=pd[:, cl * 128:(cl + 1) * 128],
                                                                                                  op0=ALU.mult, op1=ALU.add), r=[b_S[d], b_gl[d], pbd], w=[b_S[d]])
                                                (kb.pool if d == 0 else kb.dve)(lambda e, snxt=snxt, d=d: e.tensor_tensor(out=Sbd[d][:, snxt, :], in0=Sst[d][:], in1=mk[:, 2, :], op=ALU.mult), r=[b_S[d], b_mk], w=[b_Sbd[d][snxt]])
                                        for d in range(2):
                                            tt, tsl, pd, pbd, po, pbo = info[d]
                                            ttl = tt - sq_ * TT
                                            first = (ttl < TT // 2) if d == 0 else (ttl >= TT // 2)
                                            if first:
                                                kb.act(lambda e, po=po, tsl=tsl: e.activation(out=oacc[:, tsl], in_=po[:, 0:128], func=AF.Copy), r=[pbo], w=[b_oacc])
                                            else:
                                                kb.dve(lambda e, po=po, tsl=tsl: e.tensor_tensor(out=oacc[:, tsl], in0=po[:, 0:128], in1=oacc[:, tsl], op=ALU.add), r=[pbo, b_oacc], w=[b_oacc])
                                    if not is_sample:
                                        for d in range(2):
                                            for hh in range(2):
                                                kb.dma(sthg_o[sq_, l, d, ct * 2 + hh], Sst[d][hh * 64:(hh + 1) * 64, hh * 64:(hh + 1) * 64], r=[b_S[d]], sembuf=b_S[d])
                                S.barrier()
                                with ExitStack() as des:
                                    sqb = kb.sb(des, "hsq", [128, 512], BF16); b_sqb = Buf("hsq")
                                    rs = stmp2; b_rs = b_stmp2

                                    def cb_g(ti, blk, pt, pb):
                                        sl = slice(blk * 512, (blk + 1) * 512)
                                        kb.act(lambda e: e.activation(out=sqb[:], in_=oacc[:, sl], func=AF.Square), r=[b_oacc], w=[b_sqb])
                                        p2, pb2 = kb.ps()
                                        kb.pe(lambda e: e.matmul(p2[:], lhsT=onesbd[:], rhs=sqb[:], start=True, stop=True), r=[b_sqb, b_mk], w=[pb2])
                                        kb.act(lambda e: e.activation(out=rs[:], in_=p2[:], func=AF.Sqrt, scale=1.0 / 64, bias=EPS), r=[pb2], w=[b_rs])
                                        kb.dve(lambda e: e.reciprocal(out=rs[:], in_=rs[:]), r=[b_rs], w=[b_rs])
                                        kb.dve(lambda e: e.scalar_tensor_tensor(out=rs[:], in0=oacc[:, sl], scalar=gnT[:, l, 2 + ct:3 + ct], in1=rs[:], op0=ALU.mult, op1=ALU.mult),
                                               r=[b_oacc, b_rs, b_cv], w=[b_rs])
                                        kb.act(lambda e: e.activation(out=stmp[:], in_=pt[:], func=AF.Silu), r=[pb], w=[b_stmp])
                                        kb.dve(lambda e: e.tensor_tensor(out=cat[:, ct, sl], in0=rs[:], in1=stmp[:], op=ALU.mult), r=[b_rs, b_stmp], w=[b_cat])
                                    proj_fm(l, 5 * 256 + ct * 128, 1, cb_g)
                                    S.barrier()
                        wout_add(l, 1, cat, b_cat)
                        S.barrier()

                def mixer_s5(l):
                    NCHT = NT // 8
                    CW = min(128, NCHT)
                    NCT = NCHT // CW
                    NST = T // 8
                    TWO_PI = 6.283185307179586
                    with ExitStack() as mes:
                        PWr = kb.sb(mes, "PWr", [128, 16, 9]); PWi = kb.sb(mes, "PWi", [128, 16, 9])
                        NPr = kb.sb(mes, "NPr", [128, 16, 8]); NPi = kb.sb(mes, "NPi", [128, 16, 8])
                        bbr = kb.sb(mes, "bbr", [128, 16, 16]); bbi = kb.sb(mes, "bbi", [128, 16, 16])
                        Cr = kb.sb(mes, "Cr", [128, 16, 16]); Ci = kb.sb(mes, "Ci", [128, 16, 16])
                        Dg = kb.sb(mes, "Dg", [128, 16])
                        smk = kb.sb(mes, "s5mk", [128, 2, 128])
                        b_P = Buf("s5params")
                        U = kb.sb(mes, "U", [128, 16, NCHT], BF16); b_U = Buf("U")
                        Yx = kb.sb(mes, "Yx", [128, NCT, 8, 256], BF16); b_Yx = Buf("Yx")
                        cnt = [0]

                        def vop(fn, r=(), w=()):
                            cnt[0] += 1
                            return kb.dve(fn, r, w)

                        for pes in ([ExitStack()] if is_sample else []):
                            def t16(nm):
                                return kb.sb(pes, nm, [128, 16])
                            stg = kb.sb(pes, "lamst", [16, 2, 128]); b_stg = [Buf("st0"), Buf("st1"), Buf("st2"), Buf("st3")]
                            lre, lim, dtv = t16("lre"), t16("lim"), t16("dtv")
                            for ri, src in enumerate((lamre_d, lamim_d)):
                                for d in range(2):
                                    kb.dma(stg[:, ri, d * 64:(d + 1) * 64], src[l, d], w=[b_stg[ri * 2 + d]])
                                pt, pb = kb.ps()
                                kb.pe(lambda e, pt=pt, ri=ri: e.transpose(out=pt[:, 0:16], in_=stg[:, ri, :], identity=ident[0:16, 0:16]), r=b_stg + [b_ident], w=[pb])
                                dst = lre if ri == 0 else lim
                                kb.dve(lambda e, pt=pt, dst=dst: e.tensor_copy(out=dst[:], in_=pt[:, 0:16]), r=[pb], w=[b_P])
                            b_dt = [Buf("dt0"), Buf("dt1")]
                            for d in range(2):
                                kb.dma(dtv[d * 64:(d + 1) * 64, :], logdt_d[l, d].rearrange("(o n) -> o n", o=1).broadcast_to([64, 16]), w=[b_dt[d]])
                            kb.act(lambda e: e.activation(out=dtv[:], in_=dtv[:], func=AF.Exp), r=b_dt, w=[b_P])
                            lr, mag, ang, kf, r1, r2, s1, c1 = [t16("p%d" % i) for i in range(8)]
                            ki = kb.sb(pes, "ki", [128, 16], I32)
                            abre, abim, den, xr, zre, zim, inre, inim, ta, tb2 = [t16("q%d" % i) for i in range(10)]
                            P = [b_P]
                            vop(lambda e: e.tensor_scalar(out=lr[:], in0=lre[:], scalar1=-1e-4, scalar2=None, op0=ALU.min), P, P)
                            vop(lambda e: e.tensor_tensor(out=mag[:], in0=lr[:], in1=dtv[:], op=ALU.mult), P, P)
                            kb.act(lambda e: e.activation(out=mag[:], in_=mag[:], func=AF.Exp), P, P)
                            vop(lambda e: e.tensor_tensor(out=ang[:], in0=lim[:], in1=dtv[:], op=ALU.mult), P, P)
                            vop(lambda e: e.tensor_scalar(out=kf[:], in0=ang[:], scalar1=1.0 / TWO_PI, scalar2=None, op0=ALU.mult), P, P)
                            vop(lambda e: e.tensor_copy(out=ki[:], in_=kf[:]), P, P)
                            vop(lambda e: e.tensor_copy(out=kf[:], in_=ki[:]), P, P)
                            vop(lambda e: e.scalar_tensor_tensor(out=r1[:], in0=kf[:], scalar=-6.28125, in1=ang[:], op0=ALU.mult, op1=ALU.add), P, P)
                            vop(lambda e: e.scalar_tensor_tensor(out=r1[:], in0=kf[:], scalar=-0.0019353071795864769, in1=r1[:], op0=ALU.mult, op1=ALU.add), P, P)
                            vop(lambda e: e.tensor_scalar(out=r1[:], in0=r1[:], scalar1=-3.1415925, scalar2=3.1415925, op0=ALU.max, op1=ALU.min), P, P)
                            kb.act(lambda e: e.activation(out=s1[:], in_=r1[:], func=AF.Sin), P, P)
                            vop(lambda e: e.tensor_scalar(out=r2[:], in0=r1[:], scalar1=1.5707963267948966, scalar2=None, op0=ALU.add), P, P)
                            vop(lambda e: e.tensor_scalar(out=ta[:], in0=r2[:], scalar1=3.141592653589793, scalar2=-TWO_PI, op0=ALU.is_gt, op1=ALU.mult), P, P)
                            vop(lambda e: e.tensor_tensor(out=r2[:], in0=r2[:], in1=ta[:], op=ALU.add), P, P)
                            vop(lambda e: e.tensor_scalar(out=r2[:], in0=r2[:], scalar1=-3.1415925, scalar2=3.1415925, op0=ALU.max, op1=ALU.min), P, P)
                            kb.act(lambda e: e.activation(out=c1[:], in_=r2[:], func=AF.Sin), P, P)
                            vop(lambda e: e.tensor_tensor(out=abre[:], in0=mag[:], in1=c1[:], op=ALU.mult), P, P)
                            vop(lambda e: e.tensor_tensor(out=abim[:], in0=mag[:], in1=s1[:], op=ALU.mult), P, P)
                            vop(lambda e: e.tensor_tensor(out=den[:], in0=lr[:], in1=lr[:], op=ALU.mult), P, P)
                            vop(lambda e: e.tensor_tensor(out=ta[:], in0=lim[:], in1=lim[:], op=ALU.mult), P, P)
                            vop(lambda e: e.tensor_tensor(out=den[:], in0=den[:], in1=ta[:], op=ALU.add), P, P)
                            vop(lambda e: e.reciprocal(out=den[:], in_=den[:]), P, P)
                            vop(lambda e: e.tensor_scalar(out=xr[:], in0=abre[:], scalar1=-1.0, scalar2=None, op0=ALU.add), P, P)
                            vop(lambda e: e.tensor_tensor(out=ta[:], in0=xr[:], in1=lr[:], op=ALU.mult), P, P)
                            vop(lambda e: e.tensor_tensor(out=tb2[:], in0=abim[:], in1=lim[:], op=ALU.mult), P, P)
                            vop(lambda e: e.tensor_tensor(out=ta[:], in0=ta[:], in1=tb2[:], op=ALU.add), P, P)
                            vop(lambda e: e.tensor_tensor(out=zre[:], in0=ta[:], in1=den[:], op=ALU.mult), P, P)
                            vop(lambda e: e.tensor_tensor(out=ta[:], in0=abim[:], in1=lr[:], op=ALU.mult), P, P)
                            vop(lambda e: e.tensor_tensor(out=tb2[:], in0=xr[:], in1=lim[:], op=ALU.mult), P, P)
                            vop(lambda e: e.tensor_tensor(out=ta[:], in0=ta[:], in1=tb2[:], op=ALU.subtract), P, P)
                            vop(lambda e: e.tensor_tensor(out=zim[:], in0=ta[:], in1=den[:], op=ALU.mult), P, P)
                            vop(lambda e: e.tensor_tensor(out=ta[:], in0=mag[:], in1=mag[:], op=ALU.mult), P, P)
                            vop(lambda e: e.reciprocal(out=ta[:], in_=ta[:]), P, P)
                            vop(lambda e: e.tensor_tensor(out=inre[:], in0=abre[:], in1=ta[:], op=ALU.mult), P, P)
                            vop(lambda e: e.scalar_tensor_tensor(out=inim[:], in0=abim[:], scalar=-1.0, in1=ta[:], op0=ALU.mult, op1=ALU.mult), P, P)
                            for (Pr, Pi, br_, bi_, nmax) in ((PWr, PWi, abre, abim, 9), (NPr, NPi, inre, inim, 8)):
                                vop(lambda e, Pr=Pr: e.memset(Pr[:, :, 0], 1.0), (), P)
                                vop(lambda e, Pi=Pi: e.memset(Pi[:, :, 0], 0.0), (), P)
                                for m in range(1, nmax):
                                    vop(lambda e, Pr=Pr, br_=br_, m=m: e.tensor_tensor(out=ta[:], in0=Pr[:, :, m - 1], in1=br_[:], op=ALU.mult), P, P)
                                    vop(lambda e, Pi=Pi, bi_=bi_, m=m: e.tensor_tensor(out=tb2[:], in0=Pi[:, :, m - 1], in1=bi_[:], op=ALU.mult), P, P)
                                    vop(lambda e, Pr=Pr, m=m: e.tensor_tensor(out=Pr[:, :, m], in0=ta[:], in1=tb2[:], op=ALU.subtract), P, P)
                                    vop(lambda e, Pr=Pr, bi_=bi_, m=m: e.tensor_tensor(out=ta[:], in0=Pr[:, :, m - 1], in1=bi_[:], op=ALU.mult), P, P)
                                    vop(lambda e, Pi=Pi, br_=br_, m=m: e.tensor_tensor(out=tb2[:], in0=Pi[:, :, m - 1], in1=br_[:], op=ALU.mult), P, P)
                                    vop(lambda e, Pi=Pi, m=m: e.tensor_tensor(out=Pi[:, :, m], in0=ta[:], in1=tb2[:], op=ALU.add), P, P)
                            braw = [kb.sb(pes, "braw%d" % i, [128, 16, 16]) for i in range(2)]
                            b_br = [Buf("br%d" % i) for i in range(4)]
                            for ri, src in enumerate((s5b_re_d, s5b_im_d)):
                                for d in range(2):
                                    kb.dma(braw[ri][d * 64:(d + 1) * 64, :, :], src[l, d].rearrange("g n p -> n g p"), w=[b_br[ri * 2 + d]])
                            t3a = kb.sb(pes, "t3a", [128, 16, 16]); t3b = kb.sb(pes, "t3b", [128, 16, 16])
                            zrb = zre[:].unsqueeze(2).broadcast_to([128, 16, 16])
                            zib = zim[:].unsqueeze(2).broadcast_to([128, 16, 16])
                            vop(lambda e: e.tensor_tensor(out=t3a[:], in0=braw[0][:], in1=zrb, op=ALU.mult), P + b_br, P)
                            vop(lambda e: e.tensor_tensor(out=t3b[:], in0=braw[1][:], in1=zib, op=ALU.mult), P + b_br, P)
                            vop(lambda e: e.tensor_tensor(out=bbr[:], in0=t3a[:], in1=t3b[:], op=ALU.subtract), P, P)
                            vop(lambda e: e.tensor_tensor(out=t3a[:], in0=braw[1][:], in1=zrb, op=ALU.mult), P, P)
                            vop(lambda e: e.tensor_tensor(out=t3b[:], in0=braw[0][:], in1=zib, op=ALU.mult), P, P)
                            vop(lambda e: e.tensor_tensor(out=bbi[:], in0=t3a[:], in1=t3b[:], op=ALU.add), P, P)
                            cst = kb.sb(pes, "cst", [128, 128]); b_cst = [Buf("cst0"), Buf("cst1")]
                            for ri, src in enumerate((s5c_re_d, s5c_im_d)):
                                dstC = Cr if ri == 0 else Ci
                                for half in range(2):
                                    for d in range(2):
                                        kb.dma(cst[:, d * 64:(d + 1) * 64], src[l, d, half * 8:(half + 1) * 8].rearrange("g p n -> (g p) n"), w=[b_cst[d]])
                                    pt, pb = kb.ps()
                                    kb.pe(lambda e, pt=pt: e.transpose(out=pt[:, 0:128], in_=cst[:], identity=ident[:]), r=b_cst + [b_ident], w=[pb])
                                    kb.dve(lambda e, pt=pt, dstC=dstC, half=half: e.tensor_copy(out=dstC[:, half * 8:(half + 1) * 8, :], in_=pt[:, 0:128].rearrange("p (g q) -> p g q", q=16)),
                                           r=[pb], w=[b_P])
                            b_dg = [Buf("dg%d" % i) for i in range(8)]
                            for i in range(8):
                                kb.dma(Dg[i * 16:(i + 1) * 16, :], s5d_d[l].rearrange("(g p) -> p g", p=16), w=[b_dg[i]], slow=True)
                            b_smk = Buf("smk")
                            kb.dma(smk[:], s5mask_d.rearrange("m p c -> p m c"), w=[b_smk])
                            vop(lambda e: e.tensor_copy(out=Dg[:], in_=Dg[:]), b_dg + [b_smk], P)
                            S.barrier()
                            pes.close()
                        if stage == 51:
                            return

                        with ExitStack() as xes:
                            Xx = kb.sb(xes, "Xx", [128, NCT, 16, 8, 16]); b_Xx = Buf("Xx")
                            wv, bw = wload(w_in[l].rearrange("(k p) c -> p k c", p=128)[:, :, 0:256], 8, 256)
                            for ctile in range(NCT):
                                for i in range(8):
                                    pt, pb = kb.ps()
                                    t0_ = ctile * CW * 8 + i
                                    for k in range(8):
                                        kb.pe(lambda e, pt=pt, k=k, t0_=t0_, wv=wv: e.matmul(pt[0:CW, 0:256], lhsT=h[:, k, t0_:t0_ + (CW - 1) * 8 + 1:8], rhs=wv[:, k, :], start=(k == 0), stop=(k == 7)),
                                              r=[bw, b_h], w=[pb])
                                    kb.act(lambda e, pt=pt, ctile=ctile, i=i: e.activation(out=Xx[0:CW, ctile, :, i, :], in_=pt[0:CW, 0:256].rearrange("p (g q) -> p g q", q=16), func=AF.Copy), r=[pb], w=[b_Xx])
                            for ctile in range(NCT):
                                for g4 in range(4):
                                    pt, pb = kb.ps()
                                    for gg in range(4):
                                        g = g4 * 4 + gg
                                        kb.pe(lambda e, pt=pt, gg=gg, g=g, ctile=ctile: e.transpose(out=pt[:, gg * CW:(gg + 1) * CW], in_=Xx[0:CW, ctile, g, :, :], identity=ident[0:CW, 0:CW]),
                                              r=[b_Xx, b_ident], w=[pb])
                                    kb.act(lambda e, pt=pt, g4=g4, ctile=ctile: e.activation(out=U[:, g4 * 4:(g4 + 1) * 4, ctile * CW:(ctile + 1) * CW],
                                                                                     in_=pt[:, 0:4 * CW].rearrange("p (g c) -> p g c", g=4), func=AF.Copy), r=[pb], w=[b_U])
                            S.barrier()
                        if stage == 52:
                            return

                        for hf in range(2):
                            G0 = hf * 8
                            GS = slice(G0, G0 + 8)
                            with ExitStack() as hes:
                                WD = kb.sb(hes, "WD", [128, 8, 2, 128], BF16); b_WD = Buf("WD")
                                KT = kb.sb(hes, "KT", [128, 8, 128], BF16); b_KT = Buf("KT")
                                Vr = kb.sb(hes, "Vr", [128, 8, 8, 16]); Vi = kb.sb(hes, "Vi", [128, 8, 8, 16]); b_V = Buf("V")
                                AA = kb.sb(hes, "AA", [128, 2, 8]); BB = kb.sb(hes, "BB", [128, 2, 8]); b_AB = Buf("AB")
                                P = [b_P]
                                if is_sample:
                                    for ri in range(2):
                                        vop(lambda e, ri=ri: e.tensor_copy(out=AA[:, ri, :], in_=PWr[:, GS, 8]), P, [b_AB])
                                    vop(lambda e: e.tensor_scalar(out=BB[:, 0, :], in0=PWi[:, GS, 8], scalar1=-1.0, scalar2=None, op0=ALU.mult), P, [b_AB])
                                    vop(lambda e: e.tensor_copy(out=BB[:, 1, :], in_=PWi[:, GS, 8]), P, [b_AB])
                                for wes in ([ExitStack()] if is_sample else []):
                                    W = [kb.sb(wes, "wk%d" % i, [128, 8, 8, 16]) for i in range(4)]
                                    b_W = [Buf("wk%d" % i) for i in range(4)]
                                    t4a = kb.sb(wes, "t4a", [128, 8, 8, 16]); t4b = kb.sb(wes, "t4b", [128, 8, 8, 16]); b_t4 = Buf("t4")

                                    def cmul(outr, outi, b_out, ar, ai, br_, bi_, rows, neg_im=False):
                                        A_r = ar.unsqueeze(3).broadcast_to([64, 8, 8, 16]); A_i = ai.unsqueeze(3).broadcast_to([64, 8, 8, 16])
                                        B_r = br_.unsqueeze(2).broadcast_to([64, 8, 8, 16]); B_i = bi_.unsqueeze(2).broadcast_to([64, 8, 8, 16])
                                        ta_, tb_ = t4a[rows], t4b[rows]
                                        vop(lambda e: e.tensor_tensor(out=ta_, in0=A_r, in1=B_r, op=ALU.mult), P, [b_t4])
                                        vop(lambda e: e.tensor_tensor(out=tb_, in0=A_i, in1=B_i, op=ALU.mult), P, [b_t4])
                                        vop(lambda e: e.tensor_tensor(out=outr[rows], in0=ta_, in1=tb_, op=ALU.subtract), [b_t4], [b_out])
                                        vop(lambda e: e.tensor_tensor(out=ta_, in0=A_r, in1=B_i, op=ALU.mult), P, [b_t4])
                                        vop(lambda e: e.tensor_tensor(out=tb_, in0=A_i, in1=B_r, op=ALU.mult), P, [b_t4])
                                        if neg_im:
                                            vop(lambda e: e.scalar_tensor_tensor(out=outi[rows], in0=ta_, scalar=-1.0, in1=tb_, op0=ALU.mult, op1=ALU.subtract), [b_t4], [b_out])
                                        else:
                                            vop(lambda e: e.tensor_tensor(out=outi[rows], in0=ta_, in1=tb_, op=ALU.add), [b_t4], [b_out])
                                    F_, B_ = slice(0, 64), slice(64, 128)
                                    cmul(W[0], W[1], b_W[0], PWr[F_, GS, 7::-1], PWi[F_, GS, 7::-1], bbr[F_, GS, :], bbi[F_, GS, :], F_)
                                    cmul(W[0], W[1], b_W[0], PWr[B_, GS, 0:8], PWi[B_, GS, 0:8], bbr[B_, GS, :], bbi[B_, GS, :], B_)
                                    for g in range(8):
                                        pt, pb = kb.ps()
                                        for ri in range(2):
                                            kb.pe(lambda e, pt=pt, g=g, ri=ri: e.transpose(out=pt[:, ri * 128:(ri + 1) * 128], in_=W[ri][:, g, :, :], identity=ident[:]), r=[b_W[0], b_ident], w=[pb])
                                        kb.act(lambda e, pt=pt, g=g: e.activation(out=WD[:, g, :, :], in_=pt[:, 0:256].rearrange("p (r c) -> p r c", r=2), func=AF.Copy), r=[pb], w=[b_WD])
                                    cmul(W[2], W[3], b_W[2], NPr[F_, GS, 0:8], NPi[F_, GS, 0:8], bbr[F_, GS, :], bbi[F_, GS, :], F_)
                                    cmul(W[2], W[3], b_W[2], PWr[B_, GS, 0:8], PWi[B_, GS, 0:8], bbr[B_, GS, :], bbi[B_, GS, :], B_)
                                    cmul(W[0], W[1], b_W[0], PWr[F_, GS, 0:8], PWi[F_, GS, 0:8], Cr[F_, GS, :], Ci[F_, GS, :], F_, neg_im=True)
                                    cmul(W[0], W[1], b_W[0], NPr[B_, GS, 0:8], NPi[B_, GS, 0:8], Cr[B_, GS, :], Ci[B_, GS, :], B_, neg_im=True)
                                    ktmp = kb.sb(wes, "ktmp", [128, 2, 128]); b_ktmp = Buf("ktmp")
                                    for g in range(8):
                                        pts = []
                                        for d in range(2):
                                            rows = slice(d * 64, (d + 1) * 64)
                                            pt, pb = kb.ps()
                                            pts.append((pt, pb))
                                            kb.pe(lambda e, pt=pt, g=g, rows=rows: e.matmul(pt[:, 0:128], lhsT=W[2][rows, g, :, :], rhs=W[0][rows, g, :, :], start=True, stop=False),
                                                  r=[b_W[2], b_W[0]], w=[pb])
                                            kb.pe(lambda e, pt=pt, g=g, rows=rows: e.matmul(pt[:, 0:128], lhsT=W[3][rows, g, :, :], rhs=W[1][rows, g, :, :], start=False, stop=True),
                                                  r=[b_W[2], b_W[0]], w=[pb])
                                            kb.dve(lambda e, pt=pt, d=d: e.tensor_tensor(out=ktmp[:, d, :], in0=pt[:, 0:128], in1=smk[:, d, :], op=ALU.mult), r=[pb, b_P], w=[b_ktmp])
                                        kb.pool(lambda e: e.tensor_tensor(out=ktmp[:, 0, :], in0=ktmp[:, 0, :], in1=ktmp[:, 1, :], op=ALU.add), r=[b_ktmp], w=[b_ktmp])
                                        kb.dve(lambda e, g=g: e.scalar_tensor_tensor(out=KT[:, g, :], in0=ident[:], scalar=Dg[:, G0 + g:G0 + g + 1], in1=ktmp[:, 0, :], op0=ALU.mult, op1=ALU.add),
                                               r=[b_ktmp, b_P, b_ident], w=[b_KT])
                                    cmul(Vr, Vi, b_V, PWr[F_, GS, 1:9], PWi[F_, GS, 1:9], Cr[F_, GS, :], Ci[F_, GS, :], F_, neg_im=True)
                                    cmul(Vr, Vi, b_V, PWr[B_, GS, 8:0:-1], PWi[B_, GS, 8:0:-1], Cr[B_, GS, :], Ci[B_, GS, :], B_, neg_im=True)
                                    S.barrier()
                                    wes.close()
                                ck = (l, hf)
                                if ck not in s5cache:
                                    s5cache[ck] = {
                                        "KT": (nc.dram_tensor("s5c_KT_%d_%d" % ck, [128, 8, 128], BF16, kind="Internal").ap(), Buf("cKT")),
                                        "WD": (nc.dram_tensor("s5c_WD_%d_%d" % ck, [128, 8, 2, 128], BF16, kind="Internal").ap(), Buf("cWD")),
                                        "Vr": (nc.dram_tensor("s5c_Vr_%d_%d" % ck, [128, 8, 8, 16], F32, kind="Internal").ap(), Buf("cVr")),
                                        "Vi": (nc.dram_tensor("s5c_Vi_%d_%d" % ck, [128, 8, 8, 16], F32, kind="Internal").ap(), Buf("cVi")),
                                        "AA": (nc.dram_tensor("s5c_AA_%d_%d" % ck, [128, 2, 8], F32, kind="Internal").ap(), Buf("cAA")),
                                        "BB": (nc.dram_tensor("s5c_BB_%d_%d" % ck, [128, 2, 8], F32, kind="Internal").ap(), Buf("cBB")),
                                    }
                                cc_ = s5cache[ck]
                                if is_sample:
                                    for nm_, (t_, tb_) in (("KT", (KT, b_KT)), ("WD", (WD, b_WD)), ("Vr", (Vr, b_V)), ("Vi", (Vi, b_V)), ("AA", (AA, b_AB)), ("BB", (BB, b_AB))):
                                        kb.dma(cc_[nm_][0], t_[:], r=[tb_], w=[cc_[nm_][1]])
                                else:
                                    b_Vi2 = Buf("Vi2"); b_BB2 = Buf("BB2")
                                    kb.dma(KT[:], cc_["KT"][0], r=[cc_["KT"][1]], w=[b_KT])
                                    kb.dma(WD[:], cc_["WD"][0], r=[cc_["WD"][1]], w=[b_WD])
                                    kb.dma(Vr[:], cc_["Vr"][0], r=[cc_["Vr"][1]], w=[b_V])
                                    kb.dma(Vi[:], cc_["Vi"][0], r=[cc_["Vi"][1]], w=[b_Vi2])
                                    kb.dma(AA[:], cc_["AA"][0], r=[cc_["AA"][1]], w=[b_AB])
                                    kb.dma(BB[:], cc_["BB"][0], r=[cc_["BB"][1]], w=[b_BB2])
                                    vop(lambda e: e.tensor_copy(out=AA[:, 0, 0:1], in_=AA[:, 0, 0:1]), [b_AB, b_BB2, b_Vi2, b_V], [b_AB, b_V])
                                if stage == 53:
                                    return
                                SS = kb.sb(hes, "SS", [128, NST + 1 + (16 if NST >= 64 else 0), nseq, 2, 8]); b_SS = Buf("SS")
                                vop(lambda e: e.memset(SS[:, 0, :, :, :], 0.0), (), [b_SS])
                                if is_sample:
                                    b_si = [Buf("si%d" % i) for i in range(4)]
                                    for ri, src in enumerate((s5st_re_d, s5st_im_d)):
                                        for d in range(2):
                                            kb.dma(SS[d * 64:(d + 1) * 64, 0, 0, ri, :], src[l, d, G0:G0 + 8].rearrange("g n -> n g"), r=[b_SS], w=[b_si[ri * 2 + d]], slow=True)
                                    vop(lambda e: e.tensor_copy(out=SS[:, 0, 0, 0, 0:1], in_=SS[:, 0, 0, 0, 0:1]), b_si, [b_SS])
                                for g in range(8):
                                    for ri in range(2):
                                        pt, pb = kb.ps()
                                        kb.pe(lambda e, pt=pt, g=g, ri=ri: e.matmul(pt[:, 0:NCHT], lhsT=WD[:, g, ri, :], rhs=U[:, G0 + g, :], start=True, stop=True), r=[b_WD, b_U], w=[pb])
                                        for sq_ in range(nseq):
                                            kb.act(lambda e, pt=pt, g=g, ri=ri, sq_=sq_: e.activation(out=SS[0:64, 1:NST + 1, sq_, ri, g], in_=pt[0:64, sq_ * NST:(sq_ + 1) * NST], func=AF.Copy),
                                                   r=[pb], w=[b_SS])
                                            kb.dve(lambda e, pt=pt, g=g, ri=ri, sq_=sq_: e.tensor_copy(out=SS[64:128, NST:0:-1, sq_, ri, g], in_=pt[64:128, sq_ * NST:(sq_ + 1) * NST]),
                                                   r=[pb], w=[b_SS])
                                if stage == 54:
                                    S.barrier()
                                    return
                                if NST < 64:
                                    ts_ = kb.sb(hes, "ts_", [128, nseq, 2, 8]); tu_ = kb.sb(hes, "tu_", [128, nseq, 2, 8]); b_ts = Buf("ts")
                                    AAb = AA[:].unsqueeze(1).broadcast_to([128, nseq, 2, 8])
                                    BBb = BB[:].unsqueeze(1).broadcast_to([128, nseq, 2, 8])
                                    for s_ in range(NST):
                                        vop(lambda e, s_=s_: e.tensor_tensor(out=ts_[:], in0=SS[:, s_, :, :, :], in1=AAb, op=ALU.mult), [b_SS, b_AB], [b_ts])
                                        vop(lambda e, s_=s_: e.tensor_tensor(out=tu_[:], in0=SS[:, s_, :, ::-1, :], in1=BBb, op=ALU.mult), [b_SS, b_AB], [b_ts])
                                        vop(lambda e: e.tensor_tensor(out=ts_[:], in0=ts_[:], in1=tu_[:], op=ALU.add), [b_ts], [b_ts])
                                        vop(lambda e, s_=s_: e.tensor_tensor(out=SS[:, s_ + 1, :, :, :], in0=SS[:, s_ + 1, :, :, :], in1=ts_[:], op=ALU.add), [b_ts, b_SS], [b_SS])
                                else:
                                    R_ = 16
                                    NBk = NST // R_
                                    SSb = SS[:, 1:1 + (NBk + 1) * R_, 0, :, :].rearrange("p (b r) i g -> p b r i g", r=R_)
                                    tl = kb.sb(hes, "tl", [128, NBk + 1, 2, 8]); tl2 = kb.sb(hes, "tl2", [128, NBk + 1, 2, 8]); b_tl = Buf("tl")
                                    CC = kb.sb(hes, "CC", [128, NBk + 1, 2, 8]); b_CC = Buf("CC")
                                    PA = kb.sb(hes, "PA", [128, R_, 2, 8]); PB = kb.sb(hes, "PB", [128, R_, 2, 8]); b_PAB = Buf("PAB")
                                    tf = [kb.sb(hes, "tf%d" % i, [128, R_, 2, 8]) for i in range(4)]
                                    b_tf = [Buf("tf0"), Buf("tf1")]
                                    AAl = AA[:].unsqueeze(1).broadcast_to([128, NBk + 1, 2, 8])
                                    BBl = BB[:].unsqueeze(1).broadcast_to([128, NBk + 1, 2, 8])
                                    vop(lambda e: e.memset(SSb[:, NBk, :, :, :], 0.0), [b_SS], [b_SS])
                                    vop(lambda e: e.tensor_copy(out=SSb[:, NBk, 0, 0, :], in_=AA[:, 0, :]), [b_AB, b_SS], [b_SS])
                                    vop(lambda e: e.tensor_copy(out=SSb[:, NBk, 0, 1, :], in_=BB[:, 1, :]), [b_AB, b_SS], [b_SS])
                                    for r in range(1, R_):
                                        vop(lambda e, r=r: e.tensor_tensor(out=tl[:], in0=SSb[:, :, r - 1, :, :], in1=AAl, op=ALU.mult), [b_SS, b_AB], [b_tl])
                                        vop(lambda e, r=r: e.tensor_tensor(out=tl2[:], in0=SSb[:, :, r - 1, ::-1, :], in1=BBl, op=ALU.mult), [b_SS, b_AB], [b_tl])
                                        vop(lambda e: e.tensor_tensor(out=tl[:], in0=tl[:], in1=tl2[:], op=ALU.add), [b_tl], [b_tl])
                                        vop(lambda e, r=r: e.tensor_tensor(out=SSb[:, :, r, :, :], in0=SSb[:, :, r, :, :], in1=tl[:], op=ALU.add), [b_tl, b_SS], [b_SS])
                                    vop(lambda e: e.tensor_copy(out=PA[:, :, 0, :], in_=SSb[:, NBk, :, 0, :]), [b_SS], [b_PAB])
                                    vop(lambda e: e.tensor_copy(out=PA[:, :, 1, :], in_=SSb[:, NBk, :, 0, :]), [b_SS], [b_PAB])
                                    vop(lambda e: e.tensor_scalar(out=PB[:, :, 0, :], in0=SSb[:, NBk, :, 1, :], scalar1=-1.0, scalar2=None, op0=ALU.mult), [b_SS], [b_PAB])
                                    vop(lambda e: e.tensor_copy(out=PB[:, :, 1, :], in_=SSb[:, NBk, :, 1, :]), [b_SS], [b_PAB])
                                    vop(lambda e: e.tensor_copy(out=CC[:, 0, :, :], in_=SS[:, 0, 0, :, :]), [b_SS], [b_CC])
                                    for b_ in range(NBk):
                                        vop(lambda e, b_=b_: e.tensor_tensor(out=tf[0][:, 0, :, :], in0=CC[:, b_, :, :], in1=PA[:, R_ - 1, :, :], op=ALU.mult), [b_CC, b_PAB], [b_tf[0]])
                                        vop(lambda e, b_=b_: e.tensor_tensor(out=tf[1][:, 0, :, :], in0=CC[:, b_, ::-1, :], in1=PB[:, R_ - 1, :, :], op=ALU.mult), [b_CC, b_PAB], [b_tf[0]])
                                        vop(lambda e: e.tensor_tensor(out=tf[0][:, 0, :, :], in0=tf[0][:, 0, :, :], in1=tf[1][:, 0, :, :], op=ALU.add), [b_tf[0]], [b_tf[0]])
                                        vop(lambda e, b_=b_: e.tensor_tensor(out=CC[:, b_ + 1, :, :], in0=SSb[:, b_, R_ - 1, :, :], in1=tf[0][:, 0, :, :], op=ALU.add), [b_tf[0], b_SS], [b_CC])
                                    for b_ in range(NBk):
                                        k2 = b_ % 2
                                        ta_, tb_ = tf[2 * k2], tf[2 * k2 + 1]
                                        Cb = CC[:, b_, :, :].unsqueeze(1).broadcast_to([128, R_, 2, 8])
                                        Cs = CC[:, b_, ::-1, :].unsqueeze(1).broadcast_to([128, R_, 2, 8])
                                        vop(lambda e, ta_=ta_, Cb=Cb: e.tensor_tensor(out=ta_[:], in0=PA[:], in1=Cb, op=ALU.mult), [b_CC, b_PAB], [b_tf[k2]])
                                        vop(lambda e, tb_=tb_, Cs=Cs: e.tensor_tensor(out=tb_[:], in0=PB[:], in1=Cs, op=ALU.mult), [b_CC, b_PAB], [b_tf[k2]])
                                        vop(lambda e, ta_=ta_, tb_=tb_: e.tensor_tensor(out=ta_[:], in0=ta_[:], in1=tb_[:], op=ALU.add), [b_tf[k2]], [b_tf[k2]])
                                        vop(lambda e, ta_=ta_, b_=b_: e.tensor_tensor(out=SSb[:, b_, :, :, :], in0=SSb[:, b_, :, :, :], in1=ta_[:], op=ALU.add), [b_tf[k2], b_SS], [b_SS])
                                if not is_sample:
                                    for sq_ in range(nseq):
                                        for ri, dst in enumerate((s5o_re, s5o_im)):
                                            for d in range(2):
                                                kb.dma(dst[sq_, l, d, G0:G0 + 8].rearrange("g n -> n g"), SS[d * 64:(d + 1) * 64, NST, sq_, ri, :], r=[b_SS], is_out=True, sembuf=b_SS, slow=True)
                                if stage == 55:
                                    S.barrier()
                                    return
                                yin = [kb.sb(hes, "yin%d" % i, [128, NCHT]) for i in range(2)]
                                b_yin = [Buf("yin0"), Buf("yin1")]
                                gA = kb.sb(hes, "gA", [128, 512]); b_gA = Buf("gA")
                                gB = kb.sb(hes, "gB", [128, 512]); b_gB = Buf("gB")
                                yg4 = kb.sb(hes, "yg4", [128, 4, NCHT]); b_yg4 = Buf("yg4")
                                SSc = kb.sb(hes, "SSc", [128, nseq, 2, 4, NST]); b_SSc = Buf("SSc")
                                for g4 in range(2):
                                    for sq_ in range(nseq):
                                        kb.act(lambda e, g4=g4, sq_=sq_: e.activation(out=SSc[0:64, sq_, :, :, :], in_=SS[0:64, 0:NST, sq_, :, g4 * 4:(g4 + 1) * 4].rearrange("p s r g -> p r g s"),
                                                                                func=AF.Copy), r=[b_SS], w=[b_SSc])
                                        kb.dve(lambda e, g4=g4, sq_=sq_: e.tensor_copy(out=SSc[64:128, sq_, :, :, :], in_=SS[64:128, NST - 1::-1, sq_, :, g4 * 4:(g4 + 1) * 4].rearrange("p s r g -> p r g s")),
                                               r=[b_SS], w=[b_SSc])
                                    for gg in range(4):
                                        g = g4 * 4 + gg
                                        yi = g % 2
                                        p1, pb1 = kb.ps()
                                        kb.pe(lambda e, p1=p1, g=g: e.matmul(p1[:, 0:NCHT], lhsT=KT[:, g, :], rhs=U[:, G0 + g, :], start=True, stop=True), r=[b_KT, b_U], w=[pb1])
                                        p2, pb2 = kb.ps()
                                        p3, pb3 = kb.ps()
                                        for sq_ in range(nseq):
                                            for d in range(2):
                                                rows = slice(d * 64, (d + 1) * 64)
                                                pp, ppb = (p2, pb2) if d == 0 else (p3, pb3)
                                                for ri, Vt in enumerate((Vr, Vi)):
                                                    rhs = SSc[rows, sq_, ri, gg, :]
                                                    kb.pe(lambda e, pp=pp, g=g, rows=rows, Vt=Vt, rhs=rhs, sq_=sq_, ri=ri: e.matmul(pp[:, sq_ * NST:(sq_ + 1) * NST], lhsT=Vt[rows, g, :, :], rhs=rhs,
                                                                                                                   start=(ri == 0), stop=(ri == 1)), r=[b_V, b_SSc], w=[ppb])
                                        kb.act(lambda e, p2=p2, yi=yi: e.activation(out=yin[yi][:], in_=p2[:, 0:NCHT], func=AF.Copy), r=[pb2], w=[b_yin[yi]])
                                        kb.dve(lambda e, p3=p3, yi=yi: e.tensor_tensor(out=yin[yi][:], in0=p3[:, 0:NCHT], in1=yin[yi][:], op=ALU.add), r=[pb3, b_yin[yi]], w=[b_yin[yi]])
                                        kb.dve(lambda e, p1=p1, yi=yi, gg=gg: e.tensor_tensor(out=yg4[:, gg, :], in0=p1[:, 0:NCHT], in1=yin[yi][:], op=ALU.add), r=[pb1, b_yin[yi]], w=[b_yg4])
                                    if stage == 56:
                                        continue
                                    for ctile in range(NCT):
                                        pt, pb = kb.ps()
                                        for gg in range(4):
                                            kb.pe(lambda e, pt=pt, gg=gg, ctile=ctile: e.transpose(out=pt[0:CW, gg * 128:(gg + 1) * 128], in_=yg4[:, gg, ctile * CW:(ctile + 1) * CW], identity=ident[:]),
                                                  r=[b_yg4, b_ident], w=[pb])
                                        gcol = (G0 + g4 * 4) * 16
                                        src = pt[0:CW, :].rearrange("p (g j q) -> p g j q", g=4, j=8)
                                        dsty = Yx[0:CW, ctile, :, gcol:gcol + 64].rearrange("p j (g q) -> p g j q", g=4)
                                        gelu_psum(src, pb, dsty, b_Yx, gA[0:CW, :].rearrange("p (g j q) -> p g j q", g=4, j=8), b_gA,
                                                  gB[0:CW, :].rearrange("p (g j q) -> p g j q", g=4, j=8), b_gB)
                                S.barrier()
                        if stage in (56, 57):
                            return

                        with ExitStack() as qes:
                            y5 = kb.sb(qes, "y5", [128, 2, NT], BF16); b_y5 = Buf("y5")
                            prod = kb.sb(qes, "prod", [128, 2, NT]); b_prod = Buf("prod")
                            cat = kb.sb(qes, "cata", [128, 2, NT], BF16); b_cat = Buf("cata")
                            sgt = kb.sb(qes, "sgt", [128, 512]); b_sgt = Buf("sgt")
                            for ctile in range(NCT):
                                for ct in range(2):
                                    ptb, pb = kb.ps()
                                    ptv = ptb.bitcast(BF16)
                                    for j in range(8):
                                        kb.pe(lambda e, ptv=ptv, j=j, ct=ct, ctile=ctile: e.transpose(out=ptv[:, j * CW:(j + 1) * CW], in_=Yx[0:CW, ctile, j, ct * 128:(ct + 1) * 128], identity=identb[0:CW, 0:CW]),
                                              r=[b_Yx, b_ident], w=[pb])
                                    kb.act(lambda e, ptv=ptv, ct=ct, ctile=ctile: e.activation(out=y5[:, ct, ctile * 8 * CW:(ctile + 1) * 8 * CW], in_=ptv[:, 0:8 * CW], func=AF.Copy), r=[pb], w=[b_y5])
                            wv, bw = wload(wglu_d[l].rearrange("(k p) c -> p k c", p=128), 2, 256)
                            for ct in range(2):
                                for blk in range(NB):
                                    sl = slice(blk * 512, (blk + 1) * 512)
                                    pt, pb = kb.ps()
                                    for k in range(2):
                                        kb.pe(lambda e, pt=pt, k=k, ct=ct, sl=sl, wv=wv: e.matmul(pt[:], lhsT=wv[:, k, ct * 128:(ct + 1) * 128], rhs=y5[:, k, sl], start=(k == 0), stop=(k == 1)),
                                              r=[bw, b_y5], w=[pb])
                                    kb.act(lambda e, pt=pt: e.activation(out=sgt[:], in_=pt[:], func=AF.Sigmoid), r=[pb], w=[b_sgt])
                                    kb.dve(lambda e, ct=ct, sl=sl: e.tensor_tensor(out=prod[:, ct, sl], in0=y5[:, ct, sl], in1=sgt[:], op=ALU.mult), r=[b_y5, b_sgt], w=[b_prod])
                            fm_norm(prod, b_prod, gnT[:, l, 0:2], cat, b_cat, 256, ones_bf, False, qes)

                            def xv(dm, blk):
                                if NCT == 2:
                                    ctile, jb = blk // 2, blk % 2
                                    v = x[:, dm, ctile * 1024:(ctile + 1) * 1024].rearrange("p (c j) -> p j c", j=8)
                                    return v[:, jb * 4:(jb + 1) * 4, :]
                                return x[:, dm, :].rearrange("p (c j) -> p j c", j=8)
                            wout_add(l, 0, cat, b_cat, xv=xv)
                            S.barrier()

                for l in range(DEPTH if stage < 40 else 1):
                    norm_to_h(l, 0)
                    if stage == 40:
                        mixer_hgrn(l)
                        continue
                    if stage >= 50:
                        mixer_s5(l)
                        continue
                    if stage >= 2:
                        mixer_gmlp(l)
                    if stage >= 3:
                        mixer_fnet(l)
                    if stage >= 4:
                        mixer_hgrn(l)
                    if stage >= 5:
                        mixer_s5(l)
                    norm_to_h(l, 1)
                    ffn(l)

                with ExitStack() as oes:
                    gB = kb.sb(oes, "gB", [128, D]); b_gB = Buf("gB")
                    kb.dma(gB[:], fng.rearrange("(o n) -> o n", o=1).broadcast_to([128, D]), w=[b_gB])
                    yo = [kb.sb(oes, "yo%d" % i, [128, D]) for i in range(2)]
                    b_yo = [Buf("yo0"), Buf("yo1")]
                    junk = kb.sb(oes, "junk", [128, 512]); b_junk = Buf("junk")
                    ssq = [kb.sb(oes, "ssq%d" % i, [128, 2]) for i in range(2)]
                    b_ssq = [Buf("ssq0"), Buf("ssq1")]
                    for tt in range(NT // 128):
                        s = tt % 2
                        pts = []
                        for kq in range(2):
                            pt, pb = kb.ps()
                            pts.append((pt, pb))
                            for kk in range(4):
                                k = kq * 4 + kk
                                kb.pe(lambda e, k=k, kk=kk, pt=pt, tt=tt: e.transpose(out=pt[:, kk * 128:(kk + 1) * 128], in_=x[:, k, tt * 128:(tt + 1) * 128], identity=ident[:]),
                                      r=[b_x, b_ident], w=[pb])
                            kb.act(lambda e, pt=pt, s=s, kq=kq: e.activation(out=junk[:], in_=pt[:], func=AF.Square, accum_out=ssq[s][:, kq:kq + 1]), r=[pb], w=[b_junk, b_ssq[s]])
                        kb.dve(lambda e, s=s: e.tensor_tensor(out=ssq[s][:, 0:1], in0=ssq[s][:, 0:1], in1=ssq[s][:, 1:2], op=ALU.add), r=[b_ssq[s]], w=[b_ssq[s]])
                        kb.act(lambda e, s=s: e.activation(out=ssq[s][:, 0:1], in_=ssq[s][:, 0:1], func=AF.Sqrt, scale=1.0 / D, bias=EPS), r=[b_ssq[s]], w=[b_ssq[s]])
                        kb.dve(lambda e, s=s: e.reciprocal(out=ssq[s][:, 0:1], in_=ssq[s][:, 0:1]), r=[b_ssq[s]], w=[b_ssq[s]])
                        for kq in range(2):
                            pt, pb = pts[kq]
                            kb.dve(lambda e, pt=pt, s=s, kq=kq: e.scalar_tensor_tensor(out=yo[s][:, kq * 512:(kq + 1) * 512], in0=pt[:], scalar=ssq[s][:, 0:1],
                                                                                in1=gB[:, kq * 512:(kq + 1) * 512], op0=ALU.mult, op1=ALU.mult),
                                   r=[pb, b_ssq[s], b_gB], w=[b_yo[s]])
                        kb.dma(y_dram[tt * 128:(tt + 1) * 128, :], yo[s][:], r=[b_yo[s]], is_out=True, sembuf=b_yo[s])
                    S.barrier()

        run_group("s", xs_d, ys_d, 2048, 2048, 1, True)
        if stage < 40:
            run_group("p", xp_d, yp_d, 512, 256, 0, False)
        S.finish()
        S.emit()
    return nc


def make_consts():
    c = {"ident": np.eye(128, dtype=np.float32)}
    n = np.arange(64)
    ang = 2 * np.pi * np.outer(n, n) / 64
    C64 = np.cos(ang) / 8.0
    S64 = np.sin(ang) / 8.0
    bd = np.zeros((256, 512), np.float64)
    for hd in range(4):
        bd[hd * 64:(hd + 1) * 64, hd * 64:(hd + 1) * 64] = C64
        bd[hd * 64:(hd + 1) * 64, 256 + hd * 64:256 + (hd + 1) * 64] = S64
    c["bdcs"] = bd.astype(ml_dtypes.bfloat16)
    i = np.arange(128)
    same = (i[:, None] // 32) == (i[None, :] // 32)
    mF = (same & (i[:, None] <= i[None, :])).astype(np.float32)
    mB = (same & (i[:, None] >= i[None, :])).astype(np.float32)
    bdm = ((i[:, None] // 64) == (i[None, :] // 64)).astype(np.float32)
    c["hmask"] = np.stack([mF, mB, bdm], axis=0)
    jj = i // 16
    c["s5mask"] = np.stack([(jj[None, :] >= jj[:, None]), (jj[:, None] >= jj[None, :])], axis=0).astype(np.float32)
    c["cmask"] = ((i[:, None] // 32) == np.arange(4)[None, :]).astype(np.float32)
    sm = np.ones((128, 2048), np.float32)
    sm[:, ::32] = 0.0
    c["smask"] = sm.astype(ml_dtypes.bfloat16)
    for nm, T in (("dft_s", 2048), ("dft_p", 256)):
        t = np.arange(T)
        a = 2 * np.pi * (np.outer(t, t) % T) / T
        m = np.stack([np.cos(a), -np.sin(a)], axis=0) / np.sqrt(T)
        c[nm] = m.astype(ml_dtypes.bfloat16)
    return c


_CACHE = {}
_DBG = {}


def kernel(**inputs):
    inp = {k: np.ascontiguousarray(np.asarray(v)) for k, v in inputs.items()}
    if "nc" not in _CACHE:
        _CACHE["nc"] = build_program()
    nc = _CACHE["nc"]
    consts = make_consts()
    shared = {k: inp[k] for k in ["w_ada", "b_ada", "norm1_g", "norm2_g", "w_in", "w_out", "ffn_w_up", "ffn_conv_w", "ffn_conv_b",
                                  "ffn_w_down", "final_norm_g", "grp_norm_g", "gm_norm_g", "gm_ws", "gm_bs", "fn_w", "hg_lb_logits", "s5_lam_re", "s5_lam_im", "s5_log_dt", "s5_b_re", "s5_b_im", "s5_c_re", "s5_c_im", "s5_d", "s5_w_glu"]}
    in_maps = []
    for c in range(NCORES):
        b = c // 4
        m = dict(shared)
        m.update(consts)
        m["xs"] = inp["x_sample"][b]
        m["xp"] = inp["x_prompt"][2 * c:2 * c + 2].reshape(512, D)
        m["cvec"] = np.stack([inp["c_ctx"], inp["c"][b]], axis=0)
        m["hg_state"] = inp["state_hgrn"][b]
        m["s5st_re"] = inp["state_s5_re"][b]
        m["s5st_im"] = inp["state_s5_im"][b]
        in_maps.append(m)
    res = run_bass_kernel_spmd(nc, in_maps, core_ids=list(range(NCORES)))
    r = res.results
    y_prompt = np.concatenate([r[c]["yp"].reshape(2, 256, D) for c in range(NCORES)], axis=0)
    y_sample = np.stack([r[0]["ys"], r[4]["ys"]], axis=0)
    st_re = np.concatenate([r[c]["s5o_re"] for c in range(NCORES)], axis=0)
    st_im = np.concatenate([r[c]["s5o_im"] for c in range(NCORES)], axis=0)
    st_hg = np.concatenate([r[c]["st_hg"] for c in range(NCORES)], axis=0)
    return y_prompt, y_sample, st_re, st_im, st_hg
```

```python
import numpy as np
import ml_dtypes
from contextlib import ExitStack
import concourse.bass as bass
import concourse.mybir as mybir
from concourse.bass_utils import run_bass_kernel_spmd

F32 = mybir.dt.float32
BF16 = mybir.dt.bfloat16
I32 = mybir.dt.int32
AF = mybir.ActivationFunctionType
ALU = mybir.AluOpType

D = 1024
DEPTH = 2
DIN = 2304
DFF = 2816
EPS = 1e-6
NCORES = 8
STAGE = 5


class Buf:
    __slots__ = ("name", "w", "r", "dsem", "dcnt", "fence", "fdone")
    current_fence = None

    def __init__(self, name=""):
        self.name = name
        self.w = None
        self.r = []
        self.dsem = None
        self.dcnt = 0
        self.fence = Buf.current_fence
        self.fdone = set()


class _Rec:
    def __init__(self):
        self.call = None

    def __getattr__(self, name):
        def f(*a, **k):
            self.call = (name, a, k)
            return self
        return f


def _bind(fn):
    r = _Rec()
    fn(r)
    assert r.call is not None
    return r.call


class Sched:
    NAMES = ["pe", "act", "dve", "pool", "sp"]

    def __init__(self, nc, es):
        self.nc = nc
        self.es = es
        self.q = {e: [] for e in self.NAMES}
        self.cnt = {e: 0 for e in self.NAMES}
        self.esem = {e: es.enter_context(nc.semaphore("sem_" + e)) for e in self.NAMES}
        self.waited = {e: {} for e in self.NAMES}
        self.out_events = []
        self.ndsem = 0
        self.dma_sems = []
        self.free_slots = []
        self.all_slots = []
        self.active = []

    def _wait(self, eng, ev):
        src, sem, val = ev
        k = id(sem)
        if self.waited[eng].get(k, 0) >= val:
            return
        self.waited[eng][k] = val
        self.q[eng].append(("wait", sem, val))

    def _collect(self, eng, reads, writes):
        for b in list(reads) + list(writes):
            if b.fence is not None and eng not in b.fdone:
                b.fdone.add(eng)
                for ev in b.fence:
                    if ev[0] != eng:
                        self._wait(eng, ev)
        for b in reads:
            if b.w is not None:
                if b.w[0] == eng and eng == "pe":
                    continue
                self._wait(eng, b.w)
        for b in writes:
            if b.w is not None:
                if not (b.w[0] == eng):
                    self._wait(eng, b.w)
            for ev in b.r:
                if ev[0] == eng:
                    continue
                self._wait(eng, ev)

    def op(self, eng, fn, reads=(), writes=()):
        self._collect(eng, reads, writes)
        self.cnt[eng] += 1
        ev = (eng, self.esem[eng], self.cnt[eng])
        self.q[eng].append(("op", _bind(fn), self.esem[eng], 1))
        for b in writes:
            b.w = ev
            b.r = []
        for b in reads:
            b.r.append(ev)
        return ev

    def dma(self, fn, reads=(), writes=(), sembuf=None, eng="sp", is_out=False):
        self._collect(eng, reads, writes)
        sb = sembuf if sembuf is not None else (writes[0] if writes else reads[0])
        if sb.dsem is None:
            if self.free_slots:
                slot = self.free_slots.pop()
                if slot[1] > 0:
                    self._wait(eng, ("dma", slot[0], slot[1]))
            else:
                slot = [self.es.enter_context(self.nc.semaphore("dsem%d" % self.ndsem)), 0]
                self.ndsem += 1
                self.all_slots.append(slot)
            sb.dsem = slot
            self.active.append(sb)
        slot = sb.dsem
        slot[1] += 16
        ev = ("dma", slot[0], slot[1])
        self.q[eng].append(("op", _bind(fn), slot[0], 16))
        for b in writes:
            b.w = ev
            b.r = []
        for b in reads:
            b.r.append(ev)
        return ev

    def barrier(self, force=False):
        evs = []
        for o in self.NAMES:
            if self.cnt[o] > 0:
                evs.append((o, self.esem[o], self.cnt[o]))
        for slot in self.all_slots:
            if slot[1] > 0:
                evs.append(("dma", slot[0], slot[1]))
        Buf.current_fence = evs
        if force:
            for e in self.NAMES:
                for ev in evs:
                    if ev[0] != e:
                        self._wait(e, ev)
        for sb in self.active:
            self.free_slots.append(sb.dsem)
            sb.dsem = None
        self.active = []

    def finish(self):
        self.barrier(force=True)

    def emit(self):
        nc = self.nc
        engmap = {"pe": "tensor", "act": "scalar", "dve": "vector", "pool": "gpsimd", "sp": "sync"}
        with nc.Block() as block:
            for name in self.NAMES:
                items = self.q[name]

                def body(e, items=items):
                    for it in items:
                        if it[0] == "wait":
                            e.wait_ge(it[1], it[2])
                        else:
                            nm_, a_, k_ = it[1]
                            ins = getattr(e, nm_)(*a_, **k_)
                            ins.then_inc(it[2], it[3])
                getattr(block, engmap[name])(body)


class KB:
    def __init__(self, nc, es):
        self.nc = nc
        self.es = es
        self.S = Sched(nc, es)
        self.din = {}
        self.dout = {}
        self.nps = 0
        self.psb = []
        for i in range(8):
            t = es.enter_context(nc.psum_tensor("psb%d" % i, [128, 512], F32))
            self.psb.append((t, Buf("ps%d" % i)))
        self.uid = 0

    def inp(self, name, shape, dtype=F32):
        self.din[name] = self.nc.dram_tensor(name, list(shape), dtype, kind="ExternalInput").ap()
        return self.din[name]

    def outp(self, name, shape, dtype=F32):
        self.dout[name] = self.nc.dram_tensor(name, list(shape), dtype, kind="ExternalOutput").ap()
        return self.dout[name]

    def sb(self, es, name, shape, dtype=F32):
        self.uid += 1
        return es.enter_context(self.nc.sbuf_tensor("%s_%d" % (name, self.uid), list(shape), dtype))

    def ps(self):
        t, b = self.psb[self.nps % 8]
        self.nps += 1
        return t, b

    def dbg(self, name, ap, shape, dtype, r=()):
        if not getattr(self, "debug", False):
            return
        d = self.nc.dram_tensor("dbg_" + name, list(shape), dtype, kind="ExternalOutput").ap()
        self.dbgs = getattr(self, "dbgs", []) + ["dbg_" + name]
        bb = Buf("dbg")
        self.S.dma(lambda e: e.dma_start(out=d, in_=ap), list(r), [bb])

    def pe(self, fn, r=(), w=()):
        return self.S.op("pe", fn, r, w)

    def act(self, fn, r=(), w=()):
        return self.S.op("act", fn, r, w)

    def dve(self, fn, r=(), w=()):
        return self.S.op("dve", fn, r, w)

    def pool(self, fn, r=(), w=()):
        return self.S.op("pool", fn, r, w)

    def dma(self, out, in_, r=(), w=(), eng="sp", is_out=False, sembuf=None, slow=False):
        if slow:
            fn = lambda e: e.dma_start(out=out, in_=in_, allow_slow_non_contiguous=True)
        else:
            fn = lambda e: e.dma_start(out=out, in_=in_)
        return self.S.dma(fn, r, w, sembuf=sembuf, eng=eng, is_out=is_out)


def build_program(stage=STAGE):
    nc = bass.Bass("TRN2", target_bir_lowering=False)
    Buf.current_fence = None
    with ExitStack() as es:
        kb = KB(nc, es)
        kb.debug = (stage >= 40)
        _DBG['kb'] = kb
        S = kb.S
        xs_d = kb.inp("xs", [2048, D])
        xp_d = kb.inp("xp", [512, D])
        cv_d = kb.inp("cvec", [2, D])
        w_ada = kb.inp("w_ada", [DEPTH, D, 6 * D])
        b_ada = kb.inp("b_ada", [DEPTH, 6 * D])
        norm1_g = kb.inp("norm1_g", [DEPTH, D])
        norm2_g = kb.inp("norm2_g", [DEPTH, D])
        w_in = kb.inp("w_in", [DEPTH, D, DIN])
        w_out = kb.inp("w_out", [DEPTH, D, D])
        w_up = kb.inp("ffn_w_up", [DEPTH, D, 2 * DFF])
        cw_d = kb.inp("ffn_conv_w", [DEPTH, 3, 2 * DFF])
        cb_d = kb.inp("ffn_conv_b", [DEPTH, 2 * DFF])
        w_dn = kb.inp("ffn_w_down", [DEPTH, DFF, D])
        fng = kb.inp("final_norm_g", [D])
        ident_d = kb.inp("ident", [128, 128])
        grp_g = kb.inp("grp_norm_g", [DEPTH, D])
        gm_ng = kb.inp("gm_norm_g", [DEPTH, 256])
        gm_ws = kb.inp("gm_ws", [DEPTH, 4, 128, 128])
        gm_bs = kb.inp("gm_bs", [DEPTH, 4, 128])
        fn_w = kb.inp("fn_w", [DEPTH, 256, 256])
        bdcs_d = kb.inp("bdcs", [256, 512], BF16)
        hmask_d = kb.inp("hmask", [3, 128, 128])
        lamre_d = kb.inp("s5_lam_re", [DEPTH, 2, 16, 64])
        lamim_d = kb.inp("s5_lam_im", [DEPTH, 2, 16, 64])
        logdt_d = kb.inp("s5_log_dt", [DEPTH, 2, 16])
        s5b_re_d = kb.inp("s5_b_re", [DEPTH, 2, 16, 64, 16])
        s5b_im_d = kb.inp("s5_b_im", [DEPTH, 2, 16, 64, 16])
        s5c_re_d = kb.inp("s5_c_re", [DEPTH, 2, 16, 16, 64])
        s5c_im_d = kb.inp("s5_c_im", [DEPTH, 2, 16, 16, 64])
        s5d_d = kb.inp("s5_d", [DEPTH, 256])
        wglu_d = kb.inp("s5_w_glu", [DEPTH, 256, 256])
        s5mask_d = kb.inp("s5mask", [2, 128, 128])
        s5st_re_d = kb.inp("s5st_re", [DEPTH, 2, 16, 64])
        s5st_im_d = kb.inp("s5st_im", [DEPTH, 2, 16, 64])
        s5o_re = kb.outp("s5o_re", [2, DEPTH, 2, 16, 64])
        s5o_im = kb.outp("s5o_im", [2, DEPTH, 2, 16, 64])
        smask_d = kb.inp("smask", [128, 2048], BF16)
        cmask_d = kb.inp("cmask", [128, 4])
        hglog = kb.inp("hg_lb_logits", [DEPTH, 2, 256])
        hgst_d = kb.inp("hg_state", [DEPTH, 2, 4, 64, 64])
        sthg_o = kb.outp("st_hg", [2, DEPTH, 2, 4, 64, 64])
        dft_s = kb.inp("dft_s", [2, 2048, 2048], BF16)
        dft_p = kb.inp("dft_p", [2, 256, 256], BF16)
        ys_d = kb.outp("ys", [2048, D])
        yp_d = kb.outp("yp", [512, D])

        ident = kb.sb(es, "ident", [128, 128]); b_ident = Buf("ident")
        ones_bf = kb.sb(es, "ones", [128, 128], BF16); b_ones = Buf("ones")
        modT = kb.sb(es, "modT", [128, DEPTH, 2, 48]); b_mod = Buf("modT")
        n1g = kb.sb(es, "n1g", [128, DEPTH, 8]); n2g = kb.sb(es, "n2g", [128, DEPTH, 8])
        b_cv = Buf("chanvecs")
        gnT = kb.sb(es, "gnT", [128, DEPTH, 8])
        lbT = kb.sb(es, "lbT", [128, DEPTH, 2, 2]); omlT = kb.sb(es, "omlT", [128, DEPTH, 2, 2]); nomlT = kb.sb(es, "nomlT", [128, DEPTH, 2, 2])
        b_lb = Buf("lb")
        identb = kb.sb(es, "identb", [128, 128], BF16)
        cwT = kb.sb(es, "cwT", [128, DEPTH, 3, 44]); cbT = kb.sb(es, "cbT", [128, DEPTH, 44])
        Amod = kb.sb(es, "Amod", [128, DEPTH, 2, 2, 8])
        b_A = Buf("Amod")
        NWS = 3
        wbf = [kb.sb(es, "wbf%d" % i, [128, 2816], BF16) for i in range(NWS)]
        b_wbf = [Buf("wbf%d" % i) for i in range(NWS)]
        b_wbf2 = [Buf("wbfb%d" % i) for i in range(NWS)]
        wctr = [0]

        kb.dma(ident[:], ident_d, w=[b_ident])
        kb.dve(lambda e: e.memset(ones_bf[:], 1.0), w=[b_ones])
        kb.dve(lambda e: e.tensor_copy(out=identb[:], in_=ident[:]), r=[b_ident], w=[b_ident])

        def wload(src_ap, kt, ncol):
            s = wctr[0] % NWS
            wctr[0] += 1
            n = kt * ncol
            dv = wbf[s][:, 0:n].rearrange("p (k c) -> p k c", k=kt)
            kb.dma(dv, src_ap, w=[b_wbf[s], b_wbf2[s]], eng="pool")
            return dv, b_wbf[s]

        def load_chanvec(dst_ap, src_rows_ap, nt, tmp_es, extra_r=(), wbuf=None):
            st = kb.sb(tmp_es, "cvst", [nt, 128]); bst = Buf("cvst")
            kb.dma(st[:], src_rows_ap, w=[bst])
            pt, pb = kb.ps()
            kb.pe(lambda e: e.transpose(out=pt[:, 0:nt], in_=st[:], identity=ident[0:nt, 0:nt]), r=[bst, b_ident], w=[pb])
            kb.dve(lambda e: e.tensor_copy(out=dst_ap, in_=pt[:, 0:nt]), r=[pb], w=[wbuf])

        with ExitStack() as pes:
            for l in range(DEPTH):
                load_chanvec(n1g[:, l, :], norm1_g[l].rearrange("(m p) -> m p", p=128), 8, pes, wbuf=b_cv)
                load_chanvec(n2g[:, l, :], norm2_g[l].rearrange("(m p) -> m p", p=128), 8, pes, wbuf=b_cv)
                load_chanvec(gnT[:, l, :], grp_g[l].rearrange("(m p) -> m p", p=128), 8, pes, wbuf=b_cv)
                for j in range(3):
                    load_chanvec(cwT[:, l, j, :], cw_d[l, j].rearrange("(m p) -> m p", p=128), 44, pes, wbuf=b_cv)
                load_chanvec(cbT[:, l, :], cb_d[l].rearrange("(m p) -> m p", p=128), 44, pes, wbuf=b_cv)
            craw = kb.sb(pes, "craw", [128, 2, 8]); b_craw = Buf("craw")
            for cv in range(2):
                load_chanvec(craw[:, cv, :], cv_d[cv].rearrange("(m p) -> m p", p=128), 8, pes, wbuf=b_craw)
            sc = kb.sb(pes, "sc", [128, 8, 2], BF16); b_sc = Buf("sc")
            for cv in range(2):
                kb.act(lambda e, cv=cv: e.activation(out=sc[:, :, cv], in_=craw[:, cv, :], func=AF.Silu), r=[b_craw], w=[b_sc])
            lgT = kb.sb(pes, "lgT", [128, DEPTH, 2, 2]); b_lg = Buf("lgT")
            for l in range(DEPTH):
                for d in range(2):
                    load_chanvec(lgT[:, l, d, :], hglog[l, d].rearrange("(m p) -> m p", p=128), 2, pes, wbuf=b_lg)
            kb.dve(lambda e: e.memset(lbT[:, 0, :, :], 0.0), w=[b_lb])
            kb.dve(lambda e: e.tensor_tensor(out=lbT[:, 1, :, :], in0=lgT[:, 1, :, :], in1=lgT[:, 0, :, :], op=ALU.subtract), r=[b_lg], w=[b_lb])
            kb.act(lambda e: e.activation(out=lbT[:, 1, :, :], in_=lbT[:, 1, :, :], func=AF.Sigmoid), r=[b_lb], w=[b_lb])
            kb.dve(lambda e: e.tensor_scalar(out=omlT[:], in0=lbT[:], scalar1=-1.0, scalar2=1.0, op0=ALU.mult, op1=ALU.add), r=[b_lb], w=[b_lb])
            kb.dve(lambda e: e.tensor_scalar(out=nomlT[:], in0=lbT[:], scalar1=1.0, scalar2=-1.0, op0=ALU.mult, op1=ALU.add), r=[b_lb], w=[b_lb])
            badaT = kb.sb(pes, "badaT", [128, DEPTH, 48]); b_bada = Buf("bada")
            for l in range(DEPTH):
                load_chanvec(badaT[:, l, :], b_ada[l].rearrange("(m p) -> m p", p=128), 48, pes, wbuf=b_bada)
            ast = [kb.sb(pes, "ast%d" % i, [128, 8, 768], BF16) for i in range(2)]
            b_ast = [Buf("ast0"), Buf("ast1")]
            for l in range(DEPTH):
                pt, pb = kb.ps()
                for ch in range(8):
                    s = ch % 2
                    kb.dma(ast[s][:], w_ada[l].rearrange("(k p) c -> p k c", p=128)[:, :, ch * 768:(ch + 1) * 768], w=[b_ast[s]], eng="pool")
                    for m in range(6):
                        mt = ch * 6 + m
                        for k in range(8):
                            kb.pe(lambda e, s=s, m=m, k=k, mt=mt, pt=pt: e.matmul(pt[:, mt * 2:mt * 2 + 2], lhsT=ast[s][:, k, m * 128:(m + 1) * 128],
                                                                  rhs=sc[:, k, :], start=(k == 0), stop=(k == 7)),
                                  r=[b_ast[s], b_sc], w=[pb])
                for cv in range(2):
                    kb.dve(lambda e, l=l, cv=cv, pt=pt: e.tensor_tensor(out=modT[:, l, cv, :], in0=pt[:, 0:96].rearrange("p (m c) -> p m c", c=2)[:, :, cv],
                                                                  in1=badaT[:, l, :], op=ALU.add), r=[pb, b_bada], w=[b_mod])
            for l in range(DEPTH):
                for cv in range(2):
                    for n, (gt, chunk) in enumerate(((n1g, 1), (n2g, 4))):
                        kb.dve(lambda e, l=l, cv=cv, n=n, gt=gt, chunk=chunk: e.scalar_tensor_tensor(
                            out=Amod[:, l, cv, n, :], in0=modT[:, l, cv, chunk * 8:(chunk + 1) * 8], scalar=1.0, in1=gt[:, l, :],
                            op0=ALU.add, op1=ALU.mult), r=[b_mod, b_cv], w=[b_A])
            S.barrier()

        s5cache = {}

        def run_group(gname, x_dram, y_dram, NT, T, cv, grid):
            is_sample = grid
            NB = NT // 512
            with ExitStack() as ges:
                x = kb.sb(ges, "x", [128, 8, NT]); b_x = Buf("x")
                h = kb.sb(ges, "h", [128, 8, NT], BF16); b_h = Buf("h")
                with ExitStack() as tes:
                    xin = [kb.sb(tes, "xin%d" % i, [128, D]) for i in range(2)]
                    b_xin = [Buf("xin0"), Buf("xin1")]
                    for tt in range(NT // 128):
                        s = tt % 2
                        kb.dma(xin[s][:], x_dram[tt * 128:(tt + 1) * 128, :], w=[b_xin[s]])
                        for kq in range(2):
                            pt, pb = kb.ps()
                            for kk in range(4):
                                k = kq * 4 + kk
                                kb.pe(lambda e, s=s, k=k, kk=kk, pt=pt: e.transpose(out=pt[:, kk * 128:(kk + 1) * 128], in_=xin[s][:, k * 128:(k + 1) * 128], identity=ident[:]),
                                      r=[b_xin[s], b_ident], w=[pb])
                            kb.act(lambda e, kq=kq, tt=tt, pt=pt: e.activation(out=x[:, kq * 4:(kq + 1) * 4, tt * 128:(tt + 1) * 128],
                                                                            in_=pt[:].rearrange("p (k t) -> p k t", k=4), func=AF.Copy),
                                   r=[pb], w=[b_x])
                    S.barrier()

                def norm_to_h(l, n):
                    shchunk = 0 if n == 0 else 3
                    with ExitStack() as nes:
                        sq = [kb.sb(nes, "sq%d" % i, [128, 8, 512], BF16) for i in range(2)]; b_sq = [Buf("sq0"), Buf("sq1")]
                        rs = [kb.sb(nes, "rs%d" % i, [128, 512]) for i in range(2)]; b_rs = [Buf("rs0"), Buf("rs1")]
                        tmp = [kb.sb(nes, "ntmp%d" % i, [128, 512]) for i in range(4)]
                        b_tmp = [Buf("nt%d" % i) for i in range(4)]
                        pts = {}

                        def stage1(blk):
                            c0 = blk * 512
                            j = blk % 2
                            kb.act(lambda e: e.activation(out=sq[j][:], in_=x[:, :, c0:c0 + 512], func=AF.Square), r=[b_x], w=[b_sq[j]])
                            pt, pb = kb.ps()
                            pts[blk] = (pt, pb)
                            for k in range(8):
                                kb.pe(lambda e, k=k: e.matmul(pt[:], lhsT=ones_bf[:], rhs=sq[j][:, k, :], start=(k == 0), stop=(k == 7)),
                                      r=[b_sq[j], b_ones], w=[pb])
                        stage1(0)
                        for blk in range(NB):
                            c0 = blk * 512
                            j = blk % 2
                            if blk + 1 < NB:
                                stage1(blk + 1)
                            pt, pb = pts[blk]
                            kb.act(lambda e, pt=pt, j=j: e.activation(out=rs[j][:], in_=pt[:], func=AF.Sqrt, scale=1.0 / D, bias=EPS), r=[pb], w=[b_rs[j]])
                            kb.dve(lambda e, j=j: e.reciprocal(out=rs[j][:], in_=rs[j][:]), r=[b_rs[j]], w=[b_rs[j]])
                            for k in range(8):
                                s = k % 4
                                kb.dve(lambda e, k=k, s=s, c0=c0, j=j: e.scalar_tensor_tensor(out=tmp[s][:], in0=x[:, k, c0:c0 + 512], scalar=Amod[:, l, cv, n, k:k + 1],
                                                                                        in1=rs[j][:], op0=ALU.mult, op1=ALU.mult), r=[b_x, b_rs[j], b_A], w=[b_tmp[s]])
                                kb.act(lambda e, k=k, s=s, c0=c0: e.activation(out=h[:, k, c0:c0 + 512], in_=tmp[s][:], func=AF.Identity,
                                                                        bias=modT[:, l, cv, shchunk * 8 + k:shchunk * 8 + k + 1], scale=1.0),
                                       r=[b_tmp[s], b_mod], w=[b_h])
                        S.barrier()

                def ffn(l):
                    gchunk = 5
                    TB = min(NT, 1024)
                    with ExitStack() as fes:
                        gated = kb.sb(fes, "gated", [128, 22, TB], BF16); b_gated = Buf("gated")
                        zc = [[kb.sb(fes, "zc%d%d" % (i, j), [128, 512]) for j in range(2)] for i in range(2)]
                        b_zc = [[Buf("zc"), Buf("zc")] for i in range(2)]
                        t0s = [[kb.sb(fes, "t0%d%d" % (i, j), [128, 512]) for j in range(2)] for i in range(2)]
                        b_t0s = [[Buf("t0"), Buf("t0")] for i in range(2)]
                        u = [kb.sb(fes, "u%d" % i, [128, 512]) for i in range(2)]
                        b_u = [Buf("u0"), Buf("u1")]
                        sg = [kb.sb(fes, "sg%d" % i, [128, 512]) for i in range(2)]
                        b_sg = [Buf("sg0"), Buf("sg1")]
                        it = [0]
                        rowlen = 64 if grid else T
                        for tb in range(NT // TB):
                            t0 = tb * TB
                            for m in range(22):
                                s = wctr[0] % NWS
                                wctr[0] += 1
                                wsrc = w_up[l].rearrange("(k p) c -> p k c", p=128)
                                kb.dma(wbf[s][:, 0:1024].rearrange("p (k c) -> p k c", k=8), wsrc[:, :, m * 128:(m + 1) * 128], w=[b_wbf[s]], eng="pool")
                                kb.dma(wbf[s][:, 1024:2048].rearrange("p (k c) -> p k c", k=8), wsrc[:, :, DFF + m * 128:DFF + (m + 1) * 128], w=[b_wbf2[s]], eng="pool")
                                wv = wbf[s][:, 0:2048].rearrange("p (ab k c) -> p ab k c", ab=2, k=8)
                                for blk in range(TB // 512):
                                    c0 = t0 + blk * 512
                                    i2 = it[0] % 2
                                    it[0] += 1
                                    pts = []
                                    for ab in range(2):
                                        pt, pb = kb.ps()
                                        pts.append((pt, pb))
                                        for k in range(8):
                                            kb.pe(lambda e, k=k, ab=ab, pt=pt, wv=wv, c0=c0: e.matmul(pt[:], lhsT=wv[:, ab, k, :], rhs=h[:, k, c0:c0 + 512],
                                                                                            start=(k == 0), stop=(k == 7)), r=[b_wbf[s], b_wbf2[s], b_h], w=[pb])
                                    for ab in range(2):
                                        pt, pb = pts[ab]
                                        ct = ab * 22 + m
                                        z = zc[i2][ab]
                                        bz = b_zc[i2][ab]
                                        kb.act(lambda e, z=z, pt=pt, ct=ct: e.activation(out=z[:], in_=pt[:], func=AF.Identity, scale=cwT[:, l, 1, ct:ct + 1], bias=cbT[:, l, ct:ct + 1]),
                                               r=[pb, b_cv], w=[bz])
                                        zr = z[:].rearrange("p (r c) -> p r c", c=rowlen)
                                        pr = pt[:].rearrange("p (r c) -> p r c", c=rowlen)
                                        tp0 = t0s[i2][ab]
                                        bt0 = b_t0s[i2][ab]
                                        kb.act(lambda e, tp0=tp0, pt=pt, ct=ct: e.activation(out=tp0[:], in_=pt[:], func=AF.Identity, scale=cwT[:, l, 0, ct:ct + 1], bias=0.0),
                                               r=[pb, b_cv], w=[bt0])
                                        t0r = tp0[:].rearrange("p (r c) -> p r c", c=rowlen)
                                        kb.dve(lambda e, zr=zr, t0r=t0r: e.tensor_tensor(out=zr[:, :, 1:rowlen], in0=t0r[:, :, 0:rowlen - 1], in1=zr[:, :, 1:rowlen], op=ALU.add),
                                               r=[bt0, bz], w=[bz])
                                        kb.dve(lambda e, zr=zr, pr=pr, ct=ct: e.scalar_tensor_tensor(out=zr[:, :, 0:rowlen - 1], in0=pr[:, :, 1:rowlen], scalar=cwT[:, l, 2, ct:ct + 1],
                                                                                             in1=zr[:, :, 0:rowlen - 1], op0=ALU.mult, op1=ALU.add), r=[pb, bz, b_cv], w=[bz])
                                    za, zb = zc[i2]
                                    bza, bzb = b_zc[i2]
                                    uu, bu = u[i2], b_u[i2]
                                    ss, bs = sg[i2], b_sg[i2]
                                    kb.act(lambda e, uu=uu, za=za: e.activation(out=uu[:], in_=za[:], func=AF.Square, scale=0.21145921592590945), r=[bza], w=[bu])
                                    kb.dve(lambda e, uu=uu, za=za: e.scalar_tensor_tensor(out=uu[:], in0=uu[:], scalar=1.0, in1=za[:], op0=ALU.add, op1=ALU.mult), r=[bu, bza], w=[bu])
                                    kb.act(lambda e, uu=uu, ss=ss: e.activation(out=ss[:], in_=uu[:], func=AF.Sigmoid, scale=1.5957691216057308), r=[bu], w=[bs])
                                    kb.dve(lambda e, ss=ss, za=za: e.tensor_tensor(out=ss[:], in0=ss[:], in1=za[:], op=ALU.mult), r=[bs, bza], w=[bs])
                                    kb.dve(lambda e, ss=ss, zb=zb, m=m, c0=c0, t0=t0: e.tensor_tensor(out=gated[:, m, c0 - t0:c0 - t0 + 512], in0=ss[:], in1=zb[:], op=ALU.mult),
                                           r=[bs, bzb], w=[b_gated])
                            for dm in range(8):
                                wv, bw = wload(w_dn[l].rearrange("(k p) c -> p k c", p=128)[:, :, dm * 128:(dm + 1) * 128], 22, 128)
                                for blk in range(TB // 512):
                                    c0 = t0 + blk * 512
                                    pt, pb = kb.ps()
                                    for k in range(22):
                                        kb.pe(lambda e, k=k, pt=pt, wv=wv, c0=c0, t0=t0: e.matmul(pt[:], lhsT=wv[:, k, :], rhs=gated[:, k, c0 - t0:c0 - t0 + 512], start=(k == 0), stop=(k == 21)),
                                              r=[bw, b_gated], w=[pb])
                                    kb.dve(lambda e, dm=dm, pt=pt, c0=c0: e.scalar_tensor_tensor(out=x[:, dm, c0:c0 + 512], in0=pt[:], scalar=modT[:, l, cv, gchunk * 8 + dm:gchunk * 8 + dm + 1],
                                                                                        in1=x[:, dm, c0:c0 + 512], op0=ALU.mult, op1=ALU.add), r=[pb, b_mod, b_x], w=[b_x])
                        S.barrier()

                NTT = NT // 128
                nseq = NT // T

                def gelu_psum(src, b_src, out_ap, b_out, tA, b_tA, tB, b_tB):
                    kb.act(lambda e: e.activation(out=tA, in_=src, func=AF.Square, scale=0.21145921592590945), r=[b_src], w=[b_tA])
                    kb.dve(lambda e: e.scalar_tensor_tensor(out=tA, in0=tA, scalar=1.0, in1=src, op0=ALU.add, op1=ALU.mult), r=[b_tA, b_src], w=[b_tA])
                    kb.act(lambda e: e.activation(out=tB, in_=tA, func=AF.Sigmoid, scale=1.5957691216057308), r=[b_tA], w=[b_tB])
                    kb.dve(lambda e: e.tensor_tensor(out=out_ap, in0=tB, in1=src, op=ALU.mult), r=[b_tB, b_src], w=[b_out])

                def proj_fm(l, c0col, ntile, cb):
                    wsrc = w_in[l].rearrange("(k p) c -> p k c", p=128)
                    for g0 in range(0, ntile, 2):
                        nt2 = min(2, ntile - g0)
                        wv, bw = wload(wsrc[:, :, c0col + g0 * 128:c0col + (g0 + nt2) * 128], 8, nt2 * 128)
                        for ti in range(nt2):
                            for blk in range(NB):
                                pt, pb = kb.ps()
                                for k in range(8):
                                    kb.pe(lambda e, k=k, pt=pt, wv=wv, ti=ti, blk=blk: e.matmul(pt[:], lhsT=wv[:, k, ti * 128:(ti + 1) * 128], rhs=h[:, k, blk * 512:(blk + 1) * 512],
                                                                                      start=(k == 0), stop=(k == 7)), r=[bw, b_h], w=[pb])
                                cb(g0 + ti, blk, pt, pb)

                def proj_tok(l, c0col, ncol, cb):
                    wsrc = w_in[l].rearrange("(k p) c -> p k c", p=128)
                    wv, bw = wload(wsrc[:, :, c0col:c0col + ncol], 8, ncol)
                    for tt in range(NTT):
                        pt, pb = kb.ps()
                        for k in range(8):
                            kb.pe(lambda e, k=k, pt=pt, wv=wv, tt=tt: e.matmul(pt[:, 0:ncol], lhsT=h[:, k, tt * 128:(tt + 1) * 128], rhs=wv[:, k, :],
                                                                       start=(k == 0), stop=(k == 7)), r=[bw, b_h], w=[pb])
                        cb(tt, pt, pb)

                def fm_norm(src, b_src, gain, dst, b_dst, nch, ones_t, per_tile, nes):
                    sq = kb.sb(nes, "fsq", [128, 2, 512], BF16); b_sq = Buf("fsq")
                    rs = kb.sb(nes, "frs", [128, 2, 512]); b_rs = Buf("frs")
                    for blk in range(NB):
                        c0 = blk * 512
                        kb.act(lambda e, c0=c0: e.activation(out=sq[:], in_=src[:, :, c0:c0 + 512], func=AF.Square), r=[b_src], w=[b_sq])
                        for ct in range(2 if per_tile else 1):
                            pt, pb = kb.ps()
                            if per_tile:
                                kb.pe(lambda e, pt=pt, ct=ct: e.matmul(pt[:], lhsT=ones_t[:], rhs=sq[:, ct, :], start=True, stop=True), r=[b_sq, b_ones], w=[pb])
                            else:
                                for k in range(2):
                                    kb.pe(lambda e, pt=pt, k=k: e.matmul(pt[:], lhsT=ones_t[:], rhs=sq[:, k, :], start=(k == 0), stop=(k == 1)), r=[b_sq, b_ones], w=[pb])
                            kb.act(lambda e, pt=pt, ct=ct: e.activation(out=rs[:, ct, :], in_=pt[:], func=AF.Sqrt, scale=1.0 / nch, bias=EPS), r=[pb], w=[b_rs])
                            kb.dve(lambda e, ct=ct: e.reciprocal(out=rs[:, ct, :], in_=rs[:, ct, :]), r=[b_rs], w=[b_rs])
                        for ct in range(2):
                            rc = ct if per_tile else 0
                            kb.dve(lambda e, ct=ct, rc=rc, c0=c0: e.scalar_tensor_tensor(out=dst[:, ct, c0:c0 + 512], in0=src[:, ct, c0:c0 + 512], scalar=gain[:, ct:ct + 1],
                                                                                 in1=rs[:, rc, :], op0=ALU.mult, op1=ALU.mult), r=[b_src, b_rs, b_cv], w=[b_dst])

                def wout_add(l, mi, cat, b_cat, xv=None):
                    wv, bw = wload(w_out[l][mi * 256:(mi + 1) * 256, :].rearrange("(k p) c -> p k c", p=128), 2, 1024)
                    for dm in range(8):
                        for blk in range(NB):
                            pt, pb = kb.ps()
                            for k in range(2):
                                kb.pe(lambda e, k=k, pt=pt, wv=wv, dm=dm, blk=blk: e.matmul(pt[:], lhsT=wv[:, k, dm * 128:(dm + 1) * 128], rhs=cat[:, k, blk * 512:(blk + 1) * 512],
                                                                                  start=(k == 0), stop=(k == 1)), r=[bw, b_cat], w=[pb])
                            xa = x[:, dm, blk * 512:(blk + 1) * 512] if xv is None else xv(dm, blk)
                            pin = pt[:] if xv is None else pt[:].rearrange("p (j c) -> p j c", j=xa.shape[1])
                            kb.dve(lambda e, pin=pin, dm=dm, xa=xa: e.scalar_tensor_tensor(out=xa, in0=pin, scalar=modT[:, l, cv, 16 + dm:16 + dm + 1], in1=xa,
                                                                                 op0=ALU.mult, op1=ALU.add), r=[pb, b_mod, b_x], w=[b_x])

                def mixer_gmlp(l):
                    with ExitStack() as mes:
                        gu = kb.sb(mes, "gu", [128, 2, NT]); b_gu = Buf("gu")
                        gvt = kb.sb(mes, "gvt", [128, NTT, 256], BF16); b_gvt = Buf("gvt")
                        cat = kb.sb(mes, "catd", [128, 2, NT], BF16); b_cat = Buf("catd")
                        tA = kb.sb(mes, "tA", [128, 512]); b_tA = Buf("tA")
                        tB = kb.sb(mes, "tB", [128, 512]); b_tB = Buf("tB")
                        tg = kb.sb(mes, "tg", [128, 256]); b_tg = Buf("tg")
                        gmgB = kb.sb(mes, "gmgB", [128, 256]); b_gmgB = Buf("gmgB")
                        wsT = kb.sb(mes, "wsT", [128, 4, 128], BF16); b_wsT = Buf("wsT")
                        wsraw = kb.sb(mes, "wsraw", [128, 4, 128]); b_wsraw = Buf("wsraw")
                        bsB = kb.sb(mes, "bsB", [128, 2, 128]); b_bsB = [Buf("bsB%d" % i) for i in range(4)]
                        ss = kb.sb(mes, "gss", [128, 2]); b_ss = Buf("gss")
                        kb.dma(gmgB[:], gm_ng[l].rearrange("(o n) -> o n", o=1).broadcast_to([128, 256]), w=[b_gmgB])
                        kb.dma(wsraw[:], gm_ws[l].rearrange("h i j -> i h j"), w=[b_wsraw])
                        for hd in range(4):
                            pt, pb = kb.ps()
                            kb.pe(lambda e, hd=hd, pt=pt: e.transpose(out=pt[:, 0:128], in_=wsraw[:, hd, :], identity=ident[:]), r=[b_wsraw, b_ident], w=[pb])
                            kb.act(lambda e, hd=hd, pt=pt: e.activation(out=wsT[:, hd, :], in_=pt[:, 0:128], func=AF.Copy), r=[pb], w=[b_wsT])
                            ct, hh = hd // 2, hd % 2
                            kb.dma(bsB[hh * 64:(hh + 1) * 64, ct, :], gm_bs[l, hd].rearrange("(o n) -> o n", o=1).broadcast_to([64, 128]), w=[b_bsB[hd]])

                        def cb_gu(ct, blk, pt, pb):
                            gelu_psum(pt[:], pb, gu[:, ct, blk * 512:(blk + 1) * 512], b_gu, tA[:], b_tA, tB[:], b_tB)
                        proj_fm(l, 7 * 256, 2, cb_gu)

                        def cb_gv(tt, pt, pb):
                            gelu_psum(pt[:, 0:256], pb, tg[:], b_tg, tA[:, 0:256], b_tA, tB[:, 0:256], b_tB)
                            kb.act(lambda e: e.activation(out=tA[:, 0:256], in_=tg[:], func=AF.Square, accum_out=ss[:, 0:1]), r=[b_tg], w=[b_tA, b_ss])
                            kb.act(lambda e: e.activation(out=ss[:, 0:1], in_=ss[:, 0:1], func=AF.Sqrt, scale=1.0 / 256, bias=EPS), r=[b_ss], w=[b_ss])
                            kb.dve(lambda e: e.reciprocal(out=ss[:, 0:1], in_=ss[:, 0:1]), r=[b_ss], w=[b_ss])
                            kb.dve(lambda e, tt=tt: e.scalar_tensor_tensor(out=gvt[:, tt, :], in0=tg[:], scalar=ss[:, 0:1], in1=gmgB[:], op0=ALU.mult, op1=ALU.mult),
                                   r=[b_tg, b_ss, b_gmgB], w=[b_gvt])
                        proj_tok(l, 8 * 256, 256, cb_gv)

                        for tt in range(NTT):
                            for ct in range(2):
                                pt, pb = kb.ps()
                                for hh in range(2):
                                    hd = ct * 2 + hh
                                    kb.pe(lambda e, pt=pt, hh=hh, hd=hd, tt=tt: e.matmul(pt[hh * 64:(hh + 1) * 64, 0:128], lhsT=gvt[:, tt, hd * 64:(hd + 1) * 64], rhs=wsT[:, hd, :],
                                                                                 start=True, stop=True), r=[b_gvt, b_wsT], w=[pb])
                                kb.dve(lambda e, pt=pt, ct=ct: e.tensor_tensor(out=tA[:, 0:128], in0=pt[:, 0:128], in1=bsB[:, ct, :], op=ALU.add),
                                       r=[pb] + b_bsB, w=[b_tA])
                                kb.dve(lambda e, ct=ct, tt=tt: e.tensor_tensor(out=gu[:, ct, tt * 128:(tt + 1) * 128], in0=tA[:, 0:128], in1=gu[:, ct, tt * 128:(tt + 1) * 128], op=ALU.mult),
                                       r=[b_tA, b_gu], w=[b_gu])
                        fm_norm(gu, b_gu, gnT[:, l, 6:8], cat, b_cat, 256, ones_bf, False, mes)
                        wout_add(l, 3, cat, b_cat)
                        S.barrier()

                def mixer_fnet(l):
                    TT = T // 128
                    TBK = 256
                    dft_d = dft_s if T == 2048 else dft_p
                    with ExitStack() as mes:
                        cat = kb.sb(mes, "catc", [128, 2, NT], BF16); b_cat = Buf("catc")
                        xfr = kb.sb(mes, "xfr", [128, 2, NT], BF16); b_xfr = Buf("xfr")
                        bd = kb.sb(mes, "bd", [128, 2, 512], BF16); b_bd = Buf("bd")
                        kb.dma(bd[:], bdcs_d.rearrange("(k p) c -> p k c", p=128), w=[b_bd])
                        with ExitStack() as m2:
                            xcs = kb.sb(m2, "xcs", [128, NTT, 512], BF16); b_xcs = Buf("xcs")
                            with ExitStack() as m3:
                                xc = kb.sb(m3, "xc", [128, 2, NT], BF16); b_xc = Buf("xc")

                                def cb_xc(ct, blk, pt, pb):
                                    kb.act(lambda e: e.activation(out=xc[:, ct, blk * 512:(blk + 1) * 512], in_=pt[:], func=AF.Copy), r=[pb], w=[b_xc])
                                proj_fm(l, 6 * 256, 2, cb_xc)
                                for tt in range(NTT):
                                    pt, pb = kb.ps()
                                    for k in range(2):
                                        kb.pe(lambda e, pt=pt, k=k, tt=tt: e.matmul(pt[:], lhsT=xc[:, k, tt * 128:(tt + 1) * 128], rhs=bd[:, k, :], start=(k == 0), stop=(k == 1)),
                                              r=[b_xc, b_bd], w=[pb])
                                    kb.act(lambda e, pt=pt, tt=tt: e.activation(out=xcs[:, tt, :], in_=pt[:], func=AF.Copy), r=[pb], w=[b_xcs])
                                S.barrier()
                            with ExitStack() as m3:
                                dCs = [kb.sb(m3, "dC%d" % i, [128, TT, TBK], BF16) for i in range(2)]; b_dCs = [Buf("dC0"), Buf("dC1")]
                                dSs = [kb.sb(m3, "dS%d" % i, [128, TT, TBK], BF16) for i in range(2)]; b_dSs = [Buf("dS0"), Buf("dS1")]
                                dctr = 0
                                for sq_ in range(nseq):
                                    for tb in range(T // TBK):
                                        dC, b_dC, dS, b_dS = dCs[dctr % 2], b_dCs[dctr % 2], dSs[dctr % 2], b_dSs[dctr % 2]
                                        dctr += 1
                                        kb.dma(dC[:], dft_d[0].rearrange("(t p) c -> p t c", p=128)[:, :, tb * TBK:(tb + 1) * TBK], w=[b_dC])
                                        kb.dma(dS[:], dft_d[1].rearrange("(t p) c -> p t c", p=128)[:, :, tb * TBK:(tb + 1) * TBK], w=[b_dS])
                                        for ct in range(2):
                                            pt, pb = kb.ps()
                                            for tt in range(TT):
                                                kb.pe(lambda e, pt=pt, tt=tt, ct=ct, sq_=sq_: e.matmul(pt[:, 0:TBK], lhsT=xcs[:, sq_ * TT + tt, ct * 128:(ct + 1) * 128], rhs=dC[:, tt, :],
                                                                                             start=(tt == 0), stop=False), r=[b_xcs, b_dC], w=[pb])
                                            for tt in range(TT):
                                                kb.pe(lambda e, pt=pt, tt=tt, ct=ct, sq_=sq_: e.matmul(pt[:, 0:TBK], lhsT=xcs[:, sq_ * TT + tt, 256 + ct * 128:256 + (ct + 1) * 128], rhs=dS[:, tt, :],
                                                                                             start=False, stop=(tt == TT - 1)), r=[b_xcs, b_dS], w=[pb])
                                            c0 = sq_ * T + tb * TBK
                                            kb.act(lambda e, pt=pt, ct=ct, c0=c0: e.activation(out=xfr[:, ct, c0:c0 + TBK], in_=pt[:, 0:TBK], func=AF.Copy), r=[pb], w=[b_xfr])
                                S.barrier()
                        with ExitStack() as m2:
                            cpre = kb.sb(m2, "cpre", [128, 2, NT]); b_cpre = Buf("cpre")
                            wv, bw = wload(fn_w[l].rearrange("(k p) c -> p k c", p=128), 2, 256)
                            for ct in range(2):
                                for blk in range(NB):
                                    pt, pb = kb.ps()
                                    for k in range(2):
                                        kb.pe(lambda e, pt=pt, k=k, ct=ct, blk=blk, wv=wv: e.matmul(pt[:], lhsT=wv[:, k, ct * 128:(ct + 1) * 128], rhs=xfr[:, k, blk * 512:(blk + 1) * 512],
                                                                                          start=(k == 0), stop=(k == 1)), r=[bw, b_xfr], w=[pb])
                                    kb.act(lambda e, pt=pt, ct=ct, blk=blk: e.activation(out=cpre[:, ct, blk * 512:(blk + 1) * 512], in_=pt[:], func=AF.Copy), r=[pb], w=[b_cpre])
                            fm_norm(cpre, b_cpre, gnT[:, l, 4:6], cat, b_cat, 256, ones_bf, False, m2)
                            wout_add(l, 2, cat, b_cat)
                            S.barrier()

                def mixer_hgrn(l):
                    TT = T // 128
                    NCH = NT // 32
                    with ExitStack() as mes:
                        cat = kb.sb(mes, "catb", [128, 2, NT], BF16); b_cat = Buf("catb")
                        mk = kb.sb(mes, "hmask", [128, 3, 128]); b_mk = Buf("hmask")
                        onesbd = kb.sb(mes, "onesbd", [128, 128], BF16)
                        smask = kb.sb(mes, "smask", [128, NT], BF16); b_smask = Buf("smask")
                        kb.dma(mk[:], hmask_d.rearrange("m p c -> p m c"), w=[b_mk])
                        kb.dma(smask[:], smask_d[:, 0:NT], w=[b_smask])
                        kb.dve(lambda e: e.tensor_copy(out=onesbd[:], in_=mk[:, 2, :]), r=[b_mk], w=[b_mk])
                        stmp = kb.sb(mes, "stmp", [128, 512]); b_stmp = Buf("stmp")
                        stmp2 = kb.sb(mes, "stmp2", [128, 512]); b_stmp2 = Buf("stmp2")
                        Sst = [kb.sb(mes, "Sst%d" % i, [128, 128]) for i in range(2)]
                        b_S = [Buf("Sst0"), Buf("Sst1")]
                        Sbd = [kb.sb(mes, "Sbd%d" % i, [128, 8, 128], BF16) for i in range(2)]
                        b_Sbd = [[Buf("Sbd") for i in range(8)] for j in range(2)]
                        scT = [[kb.sb(mes, "scT%d%d" % (j, i), [128, 128], BF16) for i in range(2)] for j in range(2)]
                        b_scT = [[Buf("scT") for i in range(2)] for j in range(2)]
                        vblk = [kb.sb(mes, "vblk%d" % i, [128, 4, 128], BF16) for i in range(2)]
                        b_vblk = [Buf("vblk%d" % i) for i in range(2)]
                        cmk = kb.sb(mes, "cmk", [128, 4]); b_cmk = Buf("cmk")
                        kb.dma(cmk[:], cmask_d, w=[b_cmk])
                        vctr = [0]
                        for ct in range(2):
                            with ExitStack() as ces:
                                vtok = kb.sb(ces, "vtok", [128, NTT, 128], BF16); b_vtok = Buf("vtok")
                                oacc = kb.sb(ces, "oacc", [128, NT]); b_oacc = Buf("oacc")
                                Qt = [kb.sb(ces, "Qt%d" % i, [128, NT], BF16) for i in range(2)]; b_Qt = [Buf("Qt0"), Buf("Qt1")]
                                Kt = [kb.sb(ces, "Kt%d" % i, [128, NT], BF16) for i in range(2)]; b_Kt = [Buf("Kt0"), Buf("Kt1")]
                                Khtok = [kb.sb(ces, "Khtok%d" % i, [128, NTT, 128], BF16) for i in range(2)]; b_Khtok = [Buf("Kh0"), Buf("Kh1")]
                                glast = [kb.sb(ces, "glast%d" % i, [128, NCH]) for i in range(2)]; b_gl = [Buf("gl0"), Buf("gl1")]

                                def cb_v(tt, pt, pb):
                                    kb.act(lambda e: e.activation(out=vtok[:, tt, :], in_=pt[:, 0:128], func=AF.Copy), r=[pb], w=[b_vtok])
                                proj_tok(l, 4 * 256 + ct * 128, 128, cb_v)
                                Kh_sh = kb.sb(ces, "Kh", [128, NT], BF16); b_Kh_sh = Buf("Kh")
                                tmpA = [(kb.sb(ces, "bc%d" % i, [128, NT]), Buf("bc"), kb.sb(ces, "kk%d" % i, [128, NT], BF16), Buf("kk"), Kh_sh, b_Kh_sh) for i in range(2)]
                                for d in range(2):
                                    with ExitStack() as des:
                                        bc, b_bc, kk, b_kk, Kh, b_Kh = tmpA[d]
                                        lbA = lbT[:, l, d, ct:ct + 1]
                                        omlA = omlT[:, l, d, ct:ct + 1]
                                        nomlA = nomlT[:, l, d, ct:ct + 1]

                                        def cb_z(ti, blk, pt, pb):
                                            sl = slice(blk * 512, (blk + 1) * 512)
                                            kb.act(lambda e: e.activation(out=stmp[:], in_=pt[:], func=AF.Sigmoid), r=[pb], w=[b_stmp])
                                            kb.act(lambda e: e.activation(out=bc[:, sl], in_=stmp[:], func=AF.Ln, scale=omlA, bias=lbA), r=[b_stmp, b_lb], w=[b_bc])
                                            kb.dve(lambda e: e.tensor_scalar(out=kk[:, sl], in0=stmp[:], scalar1=nomlA, scalar2=omlA, op0=ALU.mult, op1=ALU.add),
                                                   r=[b_stmp, b_lb], w=[b_kk])
                                        proj_fm(l, (2 + d) * 256 + ct * 128, 1, cb_z)
                                        if d == 0:
                                            kb.dve(lambda e: e.tensor_tensor_scan(out=bc[:], data0=smask[:], data1=bc[:], initial=0.0, op0=ALU.mult, op1=ALU.add),
                                                   r=[b_bc, b_smask], w=[b_bc])
                                            kb.act(lambda e: e.activation(out=glast[d][:], in_=bc[:].rearrange("p (c t) -> p c t", t=32)[:, :, 31], func=AF.Exp), r=[b_bc], w=[b_gl[d]])
                                        else:
                                            kb.dve(lambda e: e.tensor_tensor_scan(out=bc[:, ::-1], data0=smask[:], data1=bc[:, ::-1], initial=0.0, op0=ALU.mult, op1=ALU.add),
                                                   r=[b_bc, b_smask], w=[b_bc])
                                            kb.act(lambda e: e.activation(out=glast[d][:], in_=bc[:].rearrange("p (c t) -> p c t", t=32)[:, :, 0], func=AF.Exp), r=[b_bc], w=[b_gl[d]])
                                        for blk in range(NB):
                                            sl = slice(blk * 512, (blk + 1) * 512)
                                            kb.act(lambda e, sl=sl: e.activation(out=stmp[:], in_=bc[:, sl], func=AF.Exp, scale=-1.0), r=[b_bc], w=[b_stmp])
                                            kb.dve(lambda e, sl=sl: e.tensor_tensor(out=Kt[d][:, sl], in0=kk[:, sl], in1=stmp[:], op=ALU.mult), r=[b_kk, b_stmp], w=[b_Kt[d]])
                                        kb.dve(lambda e: e.tensor_tensor(out=Kh[:].rearrange("p (c t) -> p c t", t=32), in0=Kt[d][:].rearrange("p (c t) -> p c t", t=32),
                                                                         in1=glast[d][:].unsqueeze(2).broadcast_to([128, NCH, 32]), op=ALU.mult), r=[b_Kt[d], b_gl[d]], w=[b_Kh])

                                        def cb_q(ti, blk, pt, pb):
                                            sl = slice(blk * 512, (blk + 1) * 512)
                                            kb.act(lambda e: e.activation(out=stmp2[:], in_=bc[:, sl], func=AF.Exp), r=[b_bc], w=[b_stmp2])
                                            kb.dve(lambda e: e.tensor_tensor(out=Qt[d][:, sl], in0=pt[:], in1=stmp2[:], op=ALU.mult), r=[pb, b_stmp2], w=[b_Qt[d]])
                                        proj_fm(l, 1 * 256 + ct * 128, 1, cb_q)
                                        for tt in range(NTT):
                                            ptb, pb = kb.ps()
                                            ptv = ptb.bitcast(BF16)
                                            kb.pe(lambda e, tt=tt, ptv=ptv: e.transpose(out=ptv[:, 0:128], in_=Kh[:, tt * 128:(tt + 1) * 128], identity=identb[:]), r=[b_Kh, b_ident], w=[pb])
                                            kb.act(lambda e, tt=tt, ptv=ptv: e.activation(out=Khtok[d][:, tt, :], in_=ptv[:, 0:128], func=AF.Copy), r=[pb], w=[b_Khtok[d]])
                                for sq_ in range(nseq):
                                    for d in range(2):
                                        kb.dve(lambda e, d=d: e.memset(Sst[d][:], 0.0), w=[b_S[d]])
                                        if is_sample:
                                            for hh in range(2):
                                                kb.dma(Sst[d][hh * 64:(hh + 1) * 64, hh * 64:(hh + 1) * 64], hgst_d[l, d, ct * 2 + hh], w=[b_S[d]])
                                        kb.dve(lambda e, d=d: e.tensor_tensor(out=Sbd[d][:, 0, :], in0=Sst[d][:], in1=mk[:, 2, :], op=ALU.mult), r=[b_S[d], b_mk], w=[b_Sbd[d][0]])
                                    for step in range(TT):
                                        par = step % 2
                                        info = []
                                        for d in range(2):
                                            ttl = step if d == 0 else TT - 1 - step
                                            tt = sq_ * TT + ttl
                                            tsl = slice(tt * 128, (tt + 1) * 128)
                                            for hh in range(2):
                                                ps_, pbs = kb.ps()
                                                kb.pe(lambda e, ps_=ps_, hh=hh, tsl=tsl, d=d: e.matmul(ps_[:, 0:128], lhsT=Kt[d][hh * 64:(hh + 1) * 64, tsl], rhs=Qt[d][hh * 64:(hh + 1) * 64, tsl],
                                                                                               start=True, stop=True), r=[b_Kt[d], b_Qt[d]], w=[pbs])
                                                kb.dve(lambda e, ps_=ps_, hh=hh, d=d: e.tensor_tensor(out=scT[d][hh][:], in0=ps_[:, 0:128], in1=mk[:, d, :], op=ALU.mult),
                                                       r=[pbs, b_mk], w=[b_scT[d][hh]])
                                            vs = vctr[0] % 2
                                            vctr[0] += 1
                                            kb.pool(lambda e, vs=vs, tt=tt: e.tensor_tensor(out=vblk[vs][:], in0=vtok[:, tt, :].unsqueeze(1).broadcast_to([128, 4, 128]),
                                                                                     in1=cmk[:].unsqueeze(2).broadcast_to([128, 4, 128]), op=ALU.mult),
                                                    r=[b_vtok, b_cmk], w=[b_vblk[vs]])
                                            pd, pbd = kb.ps()
                                            kb.pe(lambda e, pd=pd, vs=vs, tt=tt, d=d: e.matmul(pd[:], lhsT=Khtok[d][:, tt, :], rhs=vblk[vs][:].rearrange("p c v -> p (c v)"), start=True, stop=True),
                                                  r=[b_Khtok[d], b_vblk[vs]], w=[pbd])
                                            po, pbo = kb.ps()
                                            for hh in range(2):
                                                kb.pe(lambda e, po=po, hh=hh, tt=tt, d=d: e.matmul(po[hh * 64:(hh + 1) * 64, 0:128], lhsT=vtok[:, tt, hh * 64:(hh + 1) * 64], rhs=scT[d][hh][:],
                                                                                           start=True, stop=False), r=[b_vtok, b_scT[d][hh]], w=[pbo])
                                            info.append((tt, tsl, pd, pbd, po, pbo))
                                        for ci in range(4):
                                            for d in range(2):
                                                tt, tsl, pd, pbd, po, pbo = info[d]
                                                cl = ci if d == 0 else 3 - ci
                                                cg = tt * 4 + cl
                                                csl = slice(tt * 128 + cl * 32, tt * 128 + (cl + 1) * 32)
                                                scur = par * 4 + ci
                                                snxt = par * 4 + ci + 1 if ci < 3 else (1 - par) * 4
                                                kb.pe(lambda e, po=po, cl=cl, csl=csl, ci=ci, scur=scur, d=d: e.matmul(po[:, cl * 32:(cl + 1) * 32], lhsT=Sbd[d][:, scur, :], rhs=Qt[d][:, csl], start=False, stop=(ci == 3)),
                                                      r=[b_Sbd[d][scur], b_Qt[d]], w=[pbo])
                                                kb.dve(lambda e, pd=pd, cg=cg, cl=cl, d=d: e.scalar_tensor_tensor(out=Sst[d][:], in0=Sst[d][:], scalar=glast[d][:, cg:cg + 1], in1=pd[:, cl * 128:(cl + 1) * 128],
                                                                                                  op0=ALU.mult, op1=ALU.add), r=[b_S[d], b_gl[d], pbd], w=[b_S[d]])
                                                (kb.pool if d == 0 else kb.dve)(lambda e, snxt=snxt, d=d: e.tensor_tensor(out=Sbd[d][:, snxt, :], in0=Sst[d][:], in1=mk[:, 2, :], op=ALU.mult), r=[b_S[d], b_mk], w=[b_Sbd[d][snxt]])
                                        for d in range(2):
                                            tt, tsl, pd, pbd, po, pbo = info[d]
                                            ttl = tt - sq_ * TT
                                            first = (ttl < TT // 2) if d == 0 else (ttl >= TT // 2)
                                            if first:
                                                kb.act(lambda e, po=po, tsl=tsl: e.activation(out=oacc[:, tsl], in_=po[:, 0:128], func=AF.Copy), r=[pbo], w=[b_oacc])
                                            else:
                                                kb.dve(lambda e, po=po, tsl=tsl: e.tensor_tensor(out=oacc[:, tsl], in0=po[:, 0:128], in1=oacc[:, tsl], op=ALU.add), r=[pbo, b_oacc], w=[b_oacc])
                                    if not is_sample:
                                        for d in range(2):
                                            for hh in range(2):
                                                kb.dma(sthg_o[sq_, l, d, ct * 2 + hh], Sst[d][hh * 64:(hh + 1) * 64, hh * 64:(hh + 1) * 64], r=[b_S[d]], sembuf=b_S[d])
                                S.barrier()
                                with ExitStack() as des:
                                    sqb = kb.sb(des, "hsq", [128, 512], BF16); b_sqb = Buf("hsq")
                                    rs = stmp2; b_rs = b_stmp2

                                    def cb_g(ti, blk, pt, pb):
                                        sl = slice(blk * 512, (blk + 1) * 512)
                                        kb.act(lambda e: e.activation(out=sqb[:], in_=oacc[:, sl], func=AF.Square), r=[b_oacc], w=[b_sqb])
                                        p2, pb2 = kb.ps()
                                        kb.pe(lambda e: e.matmul(p2[:], lhsT=onesbd[:], rhs=sqb[:], start=True, stop=True), r=[b_sqb, b_mk], w=[pb2])
                                        kb.act(lambda e: e.activation(out=rs[:], in_=p2[:], func=AF.Sqrt, scale=1.0 / 64, bias=EPS), r=[pb2], w=[b_rs])
                                        kb.dve(lambda e: e.reciprocal(out=rs[:], in_=rs[:]), r=[b_rs], w=[b_rs])
                                        kb.dve(lambda e: e.scalar_tensor_tensor(out=rs[:], in0=oacc[:, sl], scalar=gnT[:, l, 2 + ct:3 + ct], in1=rs[:], op0=ALU.mult, op1=ALU.mult),
                                               r=[b_oacc, b_rs, b_cv], w=[b_rs])
                                        kb.act(lambda e: e.activation(out=stmp[:], in_=pt[:], func=AF.Silu), r=[pb], w=[b_stmp])
                                        kb.dve(lambda e: e.tensor_tensor(out=cat[:, ct, sl], in0=rs[:], in1=stmp[:], op=ALU.mult), r=[b_rs, b_stmp], w=[b_cat])
                                    proj_fm(l, 5 * 256 + ct * 128, 1, cb_g)
                                    S.barrier()
                        wout_add(l, 1, cat, b_cat)
                        S.barrier()

                def mixer_s5(l):
                    NCHT = NT // 8
                    CW = min(128, NCHT)
                    NCT = NCHT // CW
                    NST = T // 8
                    TWO_PI = 6.283185307179586
                    with ExitStack() as mes:
                        PWr = kb.sb(mes, "PWr", [128, 16, 9]); PWi = kb.sb(mes, "PWi", [128, 16, 9])
                        NPr = kb.sb(mes, "NPr", [128, 16, 8]); NPi = kb.sb(mes, "NPi", [128, 16, 8])
                        bbr = kb.sb(mes, "bbr", [128, 16, 16]); bbi = kb.sb(mes, "bbi", [128, 16, 16])
                        Cr = kb.sb(mes, "Cr", [128, 16, 16]); Ci = kb.sb(mes, "Ci", [128, 16, 16])
                        Dg = kb.sb(mes, "Dg", [128, 16])
                        smk = kb.sb(mes, "s5mk", [128, 2, 128])
                        b_P = Buf("s5params")
                        U = kb.sb(mes, "U", [128, 16, NCHT], BF16); b_U = Buf("U")
                        Yx = kb.sb(mes, "Yx", [128, NCT, 8, 256], BF16); b_Yx = Buf("Yx")
                        cnt = [0]

                        def vop(fn, r=(), w=()):
                            cnt[0] += 1
                            return kb.dve(fn, r, w)

                        for pes in ([ExitStack()] if is_sample else []):
                            def t16(nm):
                                return kb.sb(pes, nm, [128, 16])
                            stg = kb.sb(pes, "lamst", [16, 2, 128]); b_stg = [Buf("st0"), Buf("st1"), Buf("st2"), Buf("st3")]
                            lre, lim, dtv = t16("lre"), t16("lim"), t16("dtv")
                            for ri, src in enumerate((lamre_d, lamim_d)):
                                for d in range(2):
                                    kb.dma(stg[:, ri, d * 64:(d + 1) * 64], src[l, d], w=[b_stg[ri * 2 + d]])
                                pt, pb = kb.ps()
                                kb.pe(lambda e, pt=pt, ri=ri: e.transpose(out=pt[:, 0:16], in_=stg[:, ri, :], identity=ident[0:16, 0:16]), r=b_stg + [b_ident], w=[pb])
                                dst = lre if ri == 0 else lim
                                kb.dve(lambda e, pt=pt, dst=dst: e.tensor_copy(out=dst[:], in_=pt[:, 0:16]), r=[pb], w=[b_P])
                            b_dt = [Buf("dt0"), Buf("dt1")]
                            for d in range(2):
                                kb.dma(dtv[d * 64:(d + 1) * 64, :], logdt_d[l, d].rearrange("(o n) -> o n", o=1).broadcast_to([64, 16]), w=[b_dt[d]])
                            kb.act(lambda e: e.activation(out=dtv[:], in_=dtv[:], func=AF.Exp), r=b_dt, w=[b_P])
                            lr, mag, ang, kf, r1, r2, s1, c1 = [t16("p%d" % i) for i in range(8)]
                            ki = kb.sb(pes, "ki", [128, 16], I32)
                            abre, abim, den, xr, zre, zim, inre, inim, ta, tb2 = [t16("q%d" % i) for i in range(10)]
                            P = [b_P]
                            vop(lambda e: e.tensor_scalar(out=lr[:], in0=lre[:], scalar1=-1e-4, scalar2=None, op0=ALU.min), P, P)
                            vop(lambda e: e.tensor_tensor(out=mag[:], in0=lr[:], in1=dtv[:], op=ALU.mult), P, P)
                            kb.act(lambda e: e.activation(out=mag[:], in_=mag[:], func=AF.Exp), P, P)
                            vop(lambda e: e.tensor_tensor(out=ang[:], in0=lim[:], in1=dtv[:], op=ALU.mult), P, P)
                            vop(lambda e: e.tensor_scalar(out=kf[:], in0=ang[:], scalar1=1.0 / TWO_PI, scalar2=None, op0=ALU.mult), P, P)
                            vop(lambda e: e.tensor_copy(out=ki[:], in_=kf[:]), P, P)
                            vop(lambda e: e.tensor_copy(out=kf[:], in_=ki[:]), P, P)
                            vop(lambda e: e.scalar_tensor_tensor(out=r1[:], in0=kf[:], scalar=-6.28125, in1=ang[:], op0=ALU.mult, op1=ALU.add), P, P)
                            vop(lambda e: e.scalar_tensor_tensor(out=r1[:], in0=kf[:], scalar=-0.0019353071795864769, in1=r1[:], op0=ALU.mult, op1=ALU.add), P, P)
                            vop(lambda e: e.tensor_scalar(out=r1[:], in0=r1[:], scalar1=-3.1415925, scalar2=3.1415925, op0=ALU.max, op1=ALU.min), P, P)
                            kb.act(lambda e: e.activation(out=s1[:], in_=r1[:], func=AF.Sin), P, P)
                            vop(lambda e: e.tensor_scalar(out=r2[:], in0=r1[:], scalar1=1.5707963267948966, scalar2=None, op0=ALU.add), P, P)
                            vop(lambda e: e.tensor_scalar(out=ta[:], in0=r2[:], scalar1=3.141592653589793, scalar2=-TWO_PI, op0=ALU.is_gt, op1=ALU.mult), P, P)
                            vop(lambda e: e.tensor_tensor(out=r2[:], in0=r2[:], in1=ta[:], op=ALU.add), P, P)
                            vop(lambda e: e.tensor_scalar(out=r2[:], in0=r2[:], scalar1=-3.1415925, scalar2=3.1415925, op0=ALU.max, op1=ALU.min), P, P)
                            kb.act(lambda e: e.activation(out=c1[:], in_=r2[:], func=AF.Sin), P, P)
                            vop(lambda e: e.tensor_tensor(out=abre[:], in0=mag[:], in1=c1[:], op=ALU.mult), P, P)
                            vop(lambda e: e.tensor_tensor(out=abim[:], in0=mag[:], in1=s1[:], op=ALU.mult), P, P)
                            vop(lambda e: e.tensor_tensor(out=den[:], in0=lr[:], in1=lr[:], op=ALU.mult), P, P)
                            vop(lambda e: e.tensor_tensor(out=ta[:], in0=lim[:], in1=lim[:], op=ALU.mult), P, P)
                            vop(lambda e: e.tensor_tensor(out=den[:], in0=den[:], in1=ta[:], op=ALU.add), P, P)
                            vop(lambda e: e.reciprocal(out=den[:], in_=den[:]), P, P)
                            vop(lambda e: e.tensor_scalar(out=xr[:], in0=abre[:], scalar1=-1.0, scalar2=None, op0=ALU.add), P, P)
                            vop(lambda e: e.tensor_tensor(out=ta[:], in0=xr[:], in1=lr[:], op=ALU.mult), P, P)
                            vop(lambda e: e.tensor_tensor(out=tb2[:], in0=abim[:], in1=lim[:], op=ALU.mult), P, P)
                            vop(lambda e: e.tensor_tensor(out=ta[:], in0=ta[:], in1=tb2[:], op=ALU.add), P, P)
                            vop(lambda e: e.tensor_tensor(out=zre[:], in0=ta[:], in1=den[:], op=ALU.mult), P, P)
                            vop(lambda e: e.tensor_tensor(out=ta[:], in0=abim[:], in1=lr[:], op=ALU.mult), P, P)
                            vop(lambda e: e.tensor_tensor(out=tb2[:], in0=xr[:], in1=lim[:], op=ALU.mult), P, P)
                            vop(lambda e: e.tensor_tensor(out=ta[:], in0=ta[:], in1=tb2[:], op=ALU.subtract), P, P)
                            vop(lambda e: e.tensor_tensor(out=zim[:], in0=ta[:], in1=den[:], op=ALU.mult), P, P)
                            vop(lambda e: e.tensor_tensor(out=ta[:], in0=mag[:], in1=mag[:], op=ALU.mult), P, P)
                            vop(lambda e: e.reciprocal(out=ta[:], in_=ta[:]), P, P)
                            vop(lambda e: e.tensor_tensor(out=inre[:], in0=abre[:], in1=ta[:], op=ALU.mult), P, P)
                            vop(lambda e: e.scalar_tensor_tensor(out=inim[:], in0=abim[:], scalar=-1.0, in1=ta[:], op0=ALU.mult, op1=ALU.mult), P, P)
                            for (Pr, Pi, br_, bi_, nmax) in ((PWr, PWi, abre, abim, 9), (NPr, NPi, inre, inim, 8)):
                                vop(lambda e, Pr=Pr: e.memset(Pr[:, :, 0], 1.0), (), P)
                                vop(lambda e, Pi=Pi: e.memset(Pi[:, :, 0], 0.0), (), P)
                                for m in range(1, nmax):
                                    vop(lambda e, Pr=Pr, br_=br_, m=m: e.tensor_tensor(out=ta[:], in0=Pr[:, :, m - 1], in1=br_[:], op=ALU.mult), P, P)
                                    vop(lambda e, Pi=Pi, bi_=bi_, m=m: e.tensor_tensor(out=tb2[:], in0=Pi[:, :, m - 1], in1=bi_[:], op=ALU.mult), P, P)
                                    vop(lambda e, Pr=Pr, m=m: e.tensor_tensor(out=Pr[:, :, m], in0=ta[:], in1=tb2[:], op=ALU.subtract), P, P)
                                    vop(lambda e, Pr=Pr, bi_=bi_, m=m: e.tensor_tensor(out=ta[:], in0=Pr[:, :, m - 1], in1=bi_[:], op=ALU.mult), P, P)
                                    vop(lambda e, Pi=Pi, br_=br_, m=m: e.tensor_tensor(out=tb2[:], in0=Pi[:, :, m - 1], in1=br_[:], op=ALU.mult), P, P)
                                    vop(lambda e, Pi=Pi, m=m: e.tensor_tensor(out=Pi[:, :, m], in0=ta[:], in1=tb2[:], op=ALU.add), P, P)
                            braw = [kb.sb(pes, "braw%d" % i, [128, 16, 16]) for i in range(2)]
                            b_br = [Buf("br%d" % i) for i in range(4)]
                            for ri, src in enumerate((s5b_re_d, s5b_im_d)):
                                for d in range(2):
                                    kb.dma(braw[ri][d * 64:(d + 1) * 64, :, :], src[l, d].rearrange("g n p -> n g p"), w=[b_br[ri * 2 + d]])
                            t3a = kb.sb(pes, "t3a", [128, 16, 16]); t3b = kb.sb(pes, "t3b", [128, 16, 16])
                            zrb = zre[:].unsqueeze(2).broadcast_to([128, 16, 16])
                            zib = zim[:].unsqueeze(2).broadcast_to([128, 16, 16])
                            vop(lambda e: e.tensor_tensor(out=t3a[:], in0=braw[0][:], in1=zrb, op=ALU.mult), P + b_br, P)
                            vop(lambda e: e.tensor_tensor(out=t3b[:], in0=braw[1][:], in1=zib, op=ALU.mult), P + b_br, P)
                            vop(lambda e: e.tensor_tensor(out=bbr[:], in0=t3a[:], in1=t3b[:], op=ALU.subtract), P, P)
                            vop(lambda e: e.tensor_tensor(out=t3a[:], in0=braw[1][:], in1=zrb, op=ALU.mult), P, P)
                            vop(lambda e: e.tensor_tensor(out=t3b[:], in0=braw[0][:], in1=zib, op=ALU.mult), P, P)
                            vop(lambda e: e.tensor_tensor(out=bbi[:], in0=t3a[:], in1=t3b[:], op=ALU.add), P, P)
                            cst = kb.sb(pes, "cst", [128, 128]); b_cst = [Buf("cst0"), Buf("cst1")]
                            for ri, src in enumerate((s5c_re_d, s5c_im_d)):
                                dstC = Cr if ri == 0 else Ci
                                for half in range(2):
                                    for d in range(2):
                                        kb.dma(cst[:, d * 64:(d + 1) * 64], src[l, d, half * 8:(half + 1) * 8].rearrange("g p n -> (g p) n"), w=[b_cst[d]])
                                    pt, pb = kb.ps()
                                    kb.pe(lambda e, pt=pt: e.transpose(out=pt[:, 0:128], in_=cst[:], identity=ident[:]), r=b_cst + [b_ident], w=[pb])
                                    kb.dve(lambda e, pt=pt, dstC=dstC, half=half: e.tensor_copy(out=dstC[:, half * 8:(half + 1) * 8, :], in_=pt[:, 0:128].rearrange("p (g q) -> p g q", q=16)),
                                           r=[pb], w=[b_P])
                            b_dg = [Buf("dg%d" % i) for i in range(8)]
                            for i in range(8):
                                kb.dma(Dg[i * 16:(i + 1) * 16, :], s5d_d[l].rearrange("(g p) -> p g", p=16), w=[b_dg[i]], slow=True)
                            b_smk = Buf("smk")
                            kb.dma(smk[:], s5mask_d.rearrange("m p c -> p m c"), w=[b_smk])
                            vop(lambda e: e.tensor_copy(out=Dg[:], in_=Dg[:]), b_dg + [b_smk], P)
                            S.barrier()
                            pes.close()
                        if stage == 51:
                            return

                        with ExitStack() as xes:
                            Xx = kb.sb(xes, "Xx", [128, NCT, 16, 8, 16]); b_Xx = Buf("Xx")
                            wv, bw = wload(w_in[l].rearrange("(k p) c -> p k c", p=128)[:, :, 0:256], 8, 256)
                            for ctile in range(NCT):
                                for i in range(8):
                                    pt, pb = kb.ps()
                                    t0_ = ctile * CW * 8 + i
                                    for k in range(8):
                                        kb.pe(lambda e, pt=pt, k=k, t0_=t0_, wv=wv: e.matmul(pt[0:CW, 0:256], lhsT=h[:, k, t0_:t0_ + (CW - 1) * 8 + 1:8], rhs=wv[:, k, :], start=(k == 0), stop=(k == 7)),
                                              r=[bw, b_h], w=[pb])
                                    kb.act(lambda e, pt=pt, ctile=ctile, i=i: e.activation(out=Xx[0:CW, ctile, :, i, :], in_=pt[0:CW, 0:256].rearrange("p (g q) -> p g q", q=16), func=AF.Copy), r=[pb], w=[b_Xx])
                            for ctile in range(NCT):
                                for g4 in range(4):
                                    pt, pb = kb.ps()
                                    for gg in range(4):
                                        g = g4 * 4 + gg
                                        kb.pe(lambda e, pt=pt, gg=gg, g=g, ctile=ctile: e.transpose(out=pt[:, gg * CW:(gg + 1) * CW], in_=Xx[0:CW, ctile, g, :, :], identity=ident[0:CW, 0:CW]),
                                              r=[b_Xx, b_ident], w=[pb])
                                    kb.act(lambda e, pt=pt, g4=g4, ctile=ctile: e.activation(out=U[:, g4 * 4:(g4 + 1) * 4, ctile * CW:(ctile + 1) * CW],
                                                                                     in_=pt[:, 0:4 * CW].rearrange("p (g c) -> p g c", g=4), func=AF.Copy), r=[pb], w=[b_U])
                            S.barrier()
                        if stage == 52:
                            return

                        for hf in range(2):
                            G0 = hf * 8
                            GS = slice(G0, G0 + 8)
                            with ExitStack() as hes:
                                WD = kb.sb(hes, "WD", [128, 8, 2, 128], BF16); b_WD = Buf("WD")
                                KT = kb.sb(hes, "KT", [128, 8, 128], BF16); b_KT = Buf("KT")
                                Vr = kb.sb(hes, "Vr", [128, 8, 8, 16]); Vi = kb.sb(hes, "Vi", [128, 8, 8, 16]); b_V = Buf("V")
                                AA = kb.sb(hes, "AA", [128, 2, 8]); BB = kb.sb(hes, "BB", [128, 2, 8]); b_AB = Buf("AB")
                                P = [b_P]
                                if is_sample:
                                    for ri in range(2):
                                        vop(lambda e, ri=ri: e.tensor_copy(out=AA[:, ri, :], in_=PWr[:, GS, 8]), P, [b_AB])
                                    vop(lambda e: e.tensor_scalar(out=BB[:, 0, :], in0=PWi[:, GS, 8], scalar1=-1.0, scalar2=None, op0=ALU.mult), P, [b_AB])
                                    vop(lambda e: e.tensor_copy(out=BB[:, 1, :], in_=PWi[:, GS, 8]), P, [b_AB])
                                for wes in ([ExitStack()] if is_sample else []):
                                    W = [kb.sb(wes, "wk%d" % i, [128, 8, 8, 16]) for i in range(4)]
                                    b_W = [Buf("wk%d" % i) for i in range(4)]
                                    t4a = kb.sb(wes, "t4a", [128, 8, 8, 16]); t4b = kb.sb(wes, "t4b", [128, 8, 8, 16]); b_t4 = Buf("t4")

                                    def cmul(outr, outi, b_out, ar, ai, br_, bi_, rows, neg_im=False):
                                        A_r = ar.unsqueeze(3).broadcast_to([64, 8, 8, 16]); A_i = ai.unsqueeze(3).broadcast_to([64, 8, 8, 16])
                                        B_r = br_.unsqueeze(2).broadcast_to([64, 8, 8, 16]); B_i = bi_.unsqueeze(2).broadcast_to([64, 8, 8, 16])
                                        ta_, tb_ = t4a[rows], t4b[rows]
                                        vop(lambda e: e.tensor_tensor(out=ta_, in0=A_r, in1=B_r, op=ALU.mult), P, [b_t4])
                                        vop(lambda e: e.tensor_tensor(out=tb_, in0=A_i, in1=B_i, op=ALU.mult), P, [b_t4])
                                        vop(lambda e: e.tensor_tensor(out=outr[rows], in0=ta_, in1=tb_, op=ALU.subtract), [b_t4], [b_out])
                                        vop(lambda e: e.tensor_tensor(out=ta_, in0=A_r, in1=B_i, op=ALU.mult), P, [b_t4])
                                        vop(lambda e: e.tensor_tensor(out=tb_, in0=A_i, in1=B_r, op=ALU.mult), P, [b_t4])
                                        if neg_im:
                                            vop(lambda e: e.scalar_tensor_tensor(out=outi[rows], in0=ta_, scalar=-1.0, in1=tb_, op0=ALU.mult, op1=ALU.subtract), [b_t4], [b_out])
                                        else:
                                            vop(lambda e: e.tensor_tensor(out=outi[rows], in0=ta_, in1=tb_, op=ALU.add), [b_t4], [b_out])
                                    F_, B_ = slice(0, 64), slice(64, 128)
                                    cmul(W[0], W[1], b_W[0], PWr[F_, GS, 7::-1], PWi[F_, GS, 7::-1], bbr[F_, GS, :], bbi[F_, GS, :], F_)
                                    cmul(W[0], W[1], b_W[0], PWr[B_, GS, 0:8], PWi[B_, GS, 0:8], bbr[B_, GS, :], bbi[B_, GS, :], B_)
                                    for g in range(8):
                                        pt, pb = kb.ps()
                                        for ri in range(2):
                                            kb.pe(lambda e, pt=pt, g=g, ri=ri: e.transpose(out=pt[:, ri * 128:(ri + 1) * 128], in_=W[ri][:, g, :, :], identity=ident[:]), r=[b_W[0], b_ident], w=[pb])
                                        kb.act(lambda e, pt=pt, g=g: e.activation(out=WD[:, g, :, :], in_=pt[:, 0:256].rearrange("p (r c) -> p r c", r=2), func=AF.Copy), r=[pb], w=[b_WD])
                                    cmul(W[2], W[3], b_W[2], NPr[F_, GS, 0:8], NPi[F_, GS, 0:8], bbr[F_, GS, :], bbi[F_, GS, :], F_)
                                    cmul(W[2], W[3], b_W[2], PWr[B_, GS, 0:8], PWi[B_, GS, 0:8], bbr[B_, GS, :], bbi[B_, GS, :], B_)
                                    cmul(W[0], W[1], b_W[0], PWr[F_, GS, 0:8], PWi[F_, GS, 0:8], Cr[F_, GS, :], Ci[F_, GS, :], F_, neg_im=True)
                                    cmul(W[0], W[1], b_W[0], NPr[B_, GS, 0:8], NPi[B_, GS, 0:8], Cr[B_, GS, :], Ci[B_, GS, :], B_, neg_im=True)
                                    ktmp = kb.sb(wes, "ktmp", [128, 2, 128]); b_ktmp = Buf("ktmp")
                                    for g in range(8):
                                        pts = []
                                        for d in range(2):
                                            rows = slice(d * 64, (d + 1) * 64)
                                            pt, pb = kb.ps()
                                            pts.append((pt, pb))
                                            kb.pe(lambda e, pt=pt, g=g, rows=rows: e.matmul(pt[:, 0:128], lhsT=W[2][rows, g, :, :], rhs=W[0][rows, g, :, :], start=True, stop=False),
                                                  r=[b_W[2], b_W[0]], w=[pb])
                                            kb.pe(lambda e, pt=pt, g=g, rows=rows: e.matmul(pt[:, 0:128], lhsT=W[3][rows, g, :, :], rhs=W[1][rows, g, :, :], start=False, stop=True),
                                                  r=[b_W[2], b_W[0]], w=[pb])
                                            kb.dve(lambda e, pt=pt, d=d: e.tensor_tensor(out=ktmp[:, d, :], in0=pt[:, 0:128], in1=smk[:, d, :], op=ALU.mult), r=[pb, b_P], w=[b_ktmp])
                                        kb.pool(lambda e: e.tensor_tensor(out=ktmp[:, 0, :], in0=ktmp[:, 0, :], in1=ktmp[:, 1, :], op=ALU.add), r=[b_ktmp], w=[b_ktmp])
                                        kb.dve(lambda e, g=g: e.scalar_tensor_tensor(out=KT[:, g, :], in0=ident[:], scalar=Dg[:, G0 + g:G0 + g + 1], in1=ktmp[:, 0, :], op0=ALU.mult, op1=ALU.add),
                                               r=[b_ktmp, b_P, b_ident], w=[b_KT])
                                    cmul(Vr, Vi, b_V, PWr[F_, GS, 1:9], PWi[F_, GS, 1:9], Cr[F_, GS, :], Ci[F_, GS, :], F_, neg_im=True)
                                    cmul(Vr, Vi, b_V, PWr[B_, GS, 8:0:-1], PWi[B_, GS, 8:0:-1], Cr[B_, GS, :], Ci[B_, GS, :], B_, neg_im=True)
                                    S.barrier()
                                    wes.close()
                                ck = (l, hf)
                                if ck not in s5cache:
                                    s5cache[ck] = {
                                        "KT": (nc.dram_tensor("s5c_KT_%d_%d" % ck, [128, 8, 128], BF16, kind="Internal").ap(), Buf("cKT")),
                                        "WD": (nc.dram_tensor("s5c_WD_%d_%d" % ck, [128, 8, 2, 128], BF16, kind="Internal").ap(), Buf("cWD")),
                                        "Vr": (nc.dram_tensor("s5c_Vr_%d_%d" % ck, [128, 8, 8, 16], F32, kind="Internal").ap(), Buf("cVr")),
                                        "Vi": (nc.dram_tensor("s5c_Vi_%d_%d" % ck, [128, 8, 8, 16], F32, kind="Internal").ap(), Buf("cVi")),
                                        "AA": (nc.dram_tensor("s5c_AA_%d_%d" % ck, [128, 2, 8], F32, kind="Internal").ap(), Buf("cAA")),
                                        "BB": (nc.dram_tensor("s5c_BB_%d_%d" % ck, [128, 2, 8], F32, kind="Internal").ap(), Buf("cBB")),
                                    }
                                cc_ = s5cache[ck]
                                if is_sample:
                                    for nm_, (t_, tb_) in (("KT", (KT, b_KT)), ("WD", (WD, b_WD)), ("Vr", (Vr, b_V)), ("Vi", (Vi, b_V)), ("AA", (AA, b_AB)), ("BB", (BB, b_AB))):
                                        kb.dma(cc_[nm_][0], t_[:], r=[tb_], w=[cc_[nm_][1]])
                                else:
                                    b_Vi2 = Buf("Vi2"); b_BB2 = Buf("BB2")
                                    kb.dma(KT[:], cc_["KT"][0], r=[cc_["KT"][1]], w=[b_KT])
                                    kb.dma(WD[:], cc_["WD"][0], r=[cc_["WD"][1]], w=[b_WD])
                                    kb.dma(Vr[:], cc_["Vr"][0], r=[cc_["Vr"][1]], w=[b_V])
                                    kb.dma(Vi[:], cc_["Vi"][0], r=[cc_["Vi"][1]], w=[b_Vi2])
                                    kb.dma(AA[:], cc_["AA"][0], r=[cc_["AA"][1]], w=[b_AB])
                                    kb.dma(BB[:], cc_["BB"][0], r=[cc_["BB"][1]], w=[b_BB2])
                                    vop(lambda e: e.tensor_copy(out=AA[:, 0, 0:1], in_=AA[:, 0, 0:1]), [b_AB, b_BB2, b_Vi2, b_V], [b_AB, b_V])
                                if stage == 53:
                                    return
                                SS = kb.sb(hes, "SS", [128, NST + 1 + (16 if NST >= 64 else 0), nseq, 2, 8]); b_SS = Buf("SS")
                                vop(lambda e: e.memset(SS[:, 0, :, :, :], 0.0), (), [b_SS])
                                if is_sample:
                                    b_si = [Buf("si%d" % i) for i in range(4)]
                                    for ri, src in enumerate((s5st_re_d, s5st_im_d)):
                                        for d in range(2):
                                            kb.dma(SS[d * 64:(d + 1) * 64, 0, 0, ri, :], src[l, d, G0:G0 + 8].rearrange("g n -> n g"), r=[b_SS], w=[b_si[ri * 2 + d]], slow=True)
                                    vop(lambda e: e.tensor_copy(out=SS[:, 0, 0, 0, 0:1], in_=SS[:, 0, 0, 0, 0:1]), b_si, [b_SS])
                                for g in range(8):
                                    for ri in range(2):
                                        pt, pb = kb.ps()
                                        kb.pe(lambda e, pt=pt, g=g, ri=ri: e.matmul(pt[:, 0:NCHT], lhsT=WD[:, g, ri, :], rhs=U[:, G0 + g, :], start=True, stop=True), r=[b_WD, b_U], w=[pb])
                                        for sq_ in range(nseq):
                                            kb.act(lambda e, pt=pt, g=g, ri=ri, sq_=sq_: e.activation(out=SS[0:64, 1:NST + 1, sq_, ri, g], in_=pt[0:64, sq_ * NST:(sq_ + 1) * NST], func=AF.Copy),
                                                   r=[pb], w=[b_SS])
                                            kb.dve(lambda e, pt=pt, g=g, ri=ri, sq_=sq_: e.tensor_copy(out=SS[64:128, NST:0:-1, sq_, ri, g], in_=pt[64:128, sq_ * NST:(sq_ + 1) * NST]),
                                                   r=[pb], w=[b_SS])
                                if stage == 54:
                                    S.barrier()
                                    return
                                if NST < 64:
                                    ts_ = kb.sb(hes, "ts_", [128, nseq, 2, 8]); tu_ = kb.sb(hes, "tu_", [128, nseq, 2, 8]); b_ts = Buf("ts")
                                    AAb = AA[:].unsqueeze(1).broadcast_to([128, nseq, 2, 8])
                                    BBb = BB[:].unsqueeze(1).broadcast_to([128, nseq, 2, 8])
                                    for s_ in range(NST):
                                        vop(lambda e, s_=s_: e.tensor_tensor(out=ts_[:], in0=SS[:, s_, :, :, :], in1=AAb, op=ALU.mult), [b_SS, b_AB], [b_ts])
                                        vop(lambda e, s_=s_: e.tensor_tensor(out=tu_[:], in0=SS[:, s_, :, ::-1, :], in1=BBb, op=ALU.mult), [b_SS, b_AB], [b_ts])
                                        vop(lambda e: e.tensor_tensor(out=ts_[:], in0=ts_[:], in1=tu_[:], op=ALU.add), [b_ts], [b_ts])
                                        vop(lambda e, s_=s_: e.tensor_tensor(out=SS[:, s_ + 1, :, :, :], in0=SS[:, s_ + 1, :, :, :], in1=ts_[:], op=ALU.add), [b_ts, b_SS], [b_SS])
                                else:
                                    R_ = 16
                                    NBk = NST // R_
                                    SSb = SS[:, 1:1 + (NBk + 1) * R_, 0, :, :].rearrange("p (b r) i g -> p b r i g", r=R_)
                                    tl = kb.sb(hes, "tl", [128, NBk + 1, 2, 8]); tl2 = kb.sb(hes, "tl2", [128, NBk + 1, 2, 8]); b_tl = Buf("tl")
                                    CC = kb.sb(hes, "CC", [128, NBk + 1, 2, 8]); b_CC = Buf("CC")
                                    PA = kb.sb(hes, "PA", [128, R_, 2, 8]); PB = kb.sb(hes, "PB", [128, R_, 2, 8]); b_PAB = Buf("PAB")
                                    tf = [kb.sb(hes, "tf%d" % i, [128, R_, 2, 8]) for i in range(4)]
                                    b_tf = [Buf("tf0"), Buf("tf1")]
                                    AAl = AA[:].unsqueeze(1).broadcast_to([128, NBk + 1, 2, 8])
                                    BBl = BB[:].unsqueeze(1).broadcast_to([128, NBk + 1, 2, 8])
                                    vop(lambda e: e.memset(SSb[:, NBk, :, :, :], 0.0), [b_SS], [b_SS])
                                    vop(lambda e: e.tensor_copy(out=SSb[:, NBk, 0, 0, :], in_=AA[:, 0, :]), [b_AB, b_SS], [b_SS])
                                    vop(lambda e: e.tensor_copy(out=SSb[:, NBk, 0, 1, :], in_=BB[:, 1, :]), [b_AB, b_SS], [b_SS])
                                    for r in range(1, R_):
                                        vop(lambda e, r=r: e.tensor_tensor(out=tl[:], in0=SSb[:, :, r - 1, :, :], in1=AAl, op=ALU.mult), [b_SS, b_AB], [b_tl])
                                        vop(lambda e, r=r: e.tensor_tensor(out=tl2[:], in0=SSb[:, :, r - 1, ::-1, :], in1=BBl, op=ALU.mult), [b_SS, b_AB], [b_tl])
                                        vop(lambda e: e.tensor_tensor(out=tl[:], in0=tl[:], in1=tl2[:], op=ALU.add), [b_tl], [b_tl])
                                        vop(lambda e, r=r: e.tensor_tensor(out=SSb[:, :, r, :, :], in0=SSb[:, :, r, :, :], in1=tl[:], op=ALU.add), [b_tl, b_SS], [b_SS])
                                    vop(lambda e: e.tensor_copy(out=PA[:, :, 0, :], in_=SSb[:, NBk, :, 0, :]), [b_SS], [b_PAB])
                                    vop(lambda e: e.tensor_copy(out=PA[:, :, 1, :], in_=SSb[:, NBk, :, 0, :]), [b_SS], [b_PAB])
                                    vop(lambda e: e.tensor_scalar(out=PB[:, :, 0, :], in0=SSb[:, NBk, :, 1, :], scalar1=-1.0, scalar2=None, op0=ALU.mult), [b_SS], [b_PAB])
                                    vop(lambda e: e.tensor_copy(out=PB[:, :, 1, :], in_=SSb[:, NBk, :, 1, :]), [b_SS], [b_PAB])
                                    vop(lambda e: e.tensor_copy(out=CC[:, 0, :, :], in_=SS[:, 0, 0, :, :]), [b_SS], [b_CC])
                                    for b_ in range(NBk):
                                        vop(lambda e, b_=b_: e.tensor_tensor(out=tf[0][:, 0, :, :], in0=CC[:, b_, :, :], in1=PA[:, R_ - 1, :, :], op=ALU.mult), [b_CC, b_PAB], [b_tf[0]])
                                        vop(lambda e, b_=b_: e.tensor_tensor(out=tf[1][:, 0, :, :], in0=CC[:, b_, ::-1, :], in1=PB[:, R_ - 1, :, :], op=ALU.mult), [b_CC, b_PAB], [b_tf[0]])
                                        vop(lambda e: e.tensor_tensor(out=tf[0][:, 0, :, :], in0=tf[0][:, 0, :, :], in1=tf[1][:, 0, :, :], op=ALU.add), [b_tf[0]], [b_tf[0]])
                                        vop(lambda e, b_=b_: e.tensor_tensor(out=CC[:, b_ + 1, :, :], in0=SSb[:, b_, R_ - 1, :, :], in1=tf[0][:, 0, :, :], op=ALU.add), [b_tf[0], b_SS], [b_CC])
                                    for b_ in range(NBk):
                                        k2 = b_ % 2
                                        ta_, tb_ = tf[2 * k2], tf[2 * k2 + 1]
                                        Cb = CC[:, b_, :, :].unsqueeze(1).broadcast_to([128, R_, 2, 8])
                                        Cs = CC[:, b_, ::-1, :].unsqueeze(1).broadcast_to([128, R_, 2, 8])
                                        vop(lambda e, ta_=ta_, Cb=Cb: e.tensor_tensor(out=ta_[:], in0=PA[:], in1=Cb, op=ALU.mult), [b_CC, b_PAB], [b_tf[k2]])
                                        vop(lambda e, tb_=tb_, Cs=Cs: e.tensor_tensor(out=tb_[:], in0=PB[:], in1=Cs, op=ALU.mult), [b_CC, b_PAB], [b_tf[k2]])
                                        vop(lambda e, ta_=ta_, tb_=tb_: e.tensor_tensor(out=ta_[:], in0=ta_[:], in1=tb_[:], op=ALU.add), [b_tf[k2]], [b_tf[k2]])
                                        vop(lambda e, ta_=ta_, b_=b_: e.tensor_tensor(out=SSb[:, b_, :, :, :], in0=SSb[:, b_, :, :, :], in1=ta_[:], op=ALU.add), [b_tf[k2], b_SS], [b_SS])
                                if not is_sample:
                                    for sq_ in range(nseq):
                                        for ri, dst in enumerate((s5o_re, s5o_im)):
                                            for d in range(2):
                                                kb.dma(dst[sq_, l, d, G0:G0 + 8].rearrange("g n -> n g"), SS[d * 64:(d + 1) * 64, NST, sq_, ri, :], r=[b_SS], is_out=True, sembuf=b_SS, slow=True)
                                if stage == 55:
                                    S.barrier()
                                    return
                                yin = [kb.sb(hes, "yin%d" % i, [128, NCHT]) for i in range(2)]
                                b_yin = [Buf("yin0"), Buf("yin1")]
                                gA = kb.sb(hes, "gA", [128, 512]); b_gA = Buf("gA")
                                gB = kb.sb(hes, "gB", [128, 512]); b_gB = Buf("gB")
                                yg4 = kb.sb(hes, "yg4", [128, 4, NCHT]); b_yg4 = Buf("yg4")
                                SSc = kb.sb(hes, "SSc", [128, nseq, 2, 4, NST]); b_SSc = Buf("SSc")
                                for g4 in range(2):
                                    for sq_ in range(nseq):
                                        kb.act(lambda e, g4=g4, sq_=sq_: e.activation(out=SSc[0:64, sq_, :, :, :], in_=SS[0:64, 0:NST, sq_, :, g4 * 4:(g4 + 1) * 4].rearrange("p s r g -> p r g s"),
                                                                                func=AF.Copy), r=[b_SS], w=[b_SSc])
                                        kb.dve(lambda e, g4=g4, sq_=sq_: e.tensor_copy(out=SSc[64:128, sq_, :, :, :], in_=SS[64:128, NST - 1::-1, sq_, :, g4 * 4:(g4 + 1) * 4].rearrange("p s r g -> p r g s")),
                                               r=[b_SS], w=[b_SSc])
                                    for gg in range(4):
                                        g = g4 * 4 + gg
                                        yi = g % 2
                                        p1, pb1 = kb.ps()
                                        kb.pe(lambda e, p1=p1, g=g: e.matmul(p1[:, 0:NCHT], lhsT=KT[:, g, :], rhs=U[:, G0 + g, :], start=True, stop=True), r=[b_KT, b_U], w=[pb1])
                                        p2, pb2 = kb.ps()
                                        p3, pb3 = kb.ps()
                                        for sq_ in range(nseq):
                                            for d in range(2):
                                                rows = slice(d * 64, (d + 1) * 64)
                                                pp, ppb = (p2, pb2) if d == 0 else (p3, pb3)
                                                for ri, Vt in enumerate((Vr, Vi)):
                                                    rhs = SSc[rows, sq_, ri, gg, :]
                                                    kb.pe(lambda e, pp=pp, g=g, rows=rows, Vt=Vt, rhs=rhs, sq_=sq_, ri=ri: e.matmul(pp[:, sq_ * NST:(sq_ + 1) * NST], lhsT=Vt[rows, g, :, :], rhs=rhs,
                                                                                                                   start=(ri == 0), stop=(ri == 1)), r=[b_V, b_SSc], w=[ppb])
                                        kb.act(lambda e, p2=p2, yi=yi: e.activation(out=yin[yi][:], in_=p2[:, 0:NCHT], func=AF.Copy), r=[pb2], w=[b_yin[yi]])
                                        kb.dve(lambda e, p3=p3, yi=yi: e.tensor_tensor(out=yin[yi][:], in0=p3[:, 0:NCHT], in1=yin[yi][:], op=ALU.add), r=[pb3, b_yin[yi]], w=[b_yin[yi]])
                                        kb.dve(lambda e, p1=p1, yi=yi, gg=gg: e.tensor_tensor(out=yg4[:, gg, :], in0=p1[:, 0:NCHT], in1=yin[yi][:], op=ALU.add), r=[pb1, b_yin[yi]], w=[b_yg4])
                                    if stage == 56:
                                        continue
                                    for ctile in range(NCT):
                                        pt, pb = kb.ps()
                                        for gg in range(4):
                                            kb.pe(lambda e, pt=pt, gg=gg, ctile=ctile: e.transpose(out=pt[0:CW, gg * 128:(gg + 1) * 128], in_=yg4[:, gg, ctile * CW:(ctile + 1) * CW], identity=ident[:]),
                                                  r=[b_yg4, b_ident], w=[pb])
                                        gcol = (G0 + g4 * 4) * 16
                                        src = pt[0:CW, :].rearrange("p (g j q) -> p g j q", g=4, j=8)
                                        dsty = Yx[0:CW, ctile, :, gcol:gcol + 64].rearrange("p j (g q) -> p g j q", g=4)
                                        gelu_psum(src, pb, dsty, b_Yx, gA[0:CW, :].rearrange("p (g j q) -> p g j q", g=4, j=8), b_gA,
                                                  gB[0:CW, :].rearrange("p (g j q) -> p g j q", g=4, j=8), b_gB)
                                S.barrier()
                        if stage in (56, 57):
                            return

                        with ExitStack() as qes:
                            y5 = kb.sb(qes, "y5", [128, 2, NT], BF16); b_y5 = Buf("y5")
                            prod = kb.sb(qes, "prod", [128, 2, NT]); b_prod = Buf("prod")
                            cat = kb.sb(qes, "cata", [128, 2, NT], BF16); b_cat = Buf("cata")
                            sgt = kb.sb(qes, "sgt", [128, 512]); b_sgt = Buf("sgt")
                            for ctile in range(NCT):
                                for ct in range(2):
                                    ptb, pb = kb.ps()
                                    ptv = ptb.bitcast(BF16)
                                    for j in range(8):
                                        kb.pe(lambda e, ptv=ptv, j=j, ct=ct, ctile=ctile: e.transpose(out=ptv[:, j * CW:(j + 1) * CW], in_=Yx[0:CW, ctile, j, ct * 128:(ct + 1) * 128], identity=identb[0:CW, 0:CW]),
                                              r=[b_Yx, b_ident], w=[pb])
                                    kb.act(lambda e, ptv=ptv, ct=ct, ctile=ctile: e.activation(out=y5[:, ct, ctile * 8 * CW:(ctile + 1) * 8 * CW], in_=ptv[:, 0:8 * CW], func=AF.Copy), r=[pb], w=[b_y5])
                            wv, bw = wload(wglu_d[l].rearrange("(k p) c -> p k c", p=128), 2, 256)
                            for ct in range(2):
                                for blk in range(NB):
                                    sl = slice(blk * 512, (blk + 1) * 512)
                                    pt, pb = kb.ps()
                                    for k in range(2):
                                        kb.pe(lambda e, pt=pt, k=k, ct=ct, sl=sl, wv=wv: e.matmul(pt[:], lhsT=wv[:, k, ct * 128:(ct + 1) * 128], rhs=y5[:, k, sl], start=(k == 0), stop=(k == 1)),
                                              r=[bw, b_y5], w=[pb])
                                    kb.act(lambda e, pt=pt: e.activation(out=sgt[:], in_=pt[:], func=AF.Sigmoid), r=[pb], w=[b_sgt])
                                    kb.dve(lambda e, ct=ct, sl=sl: e.tensor_tensor(out=prod[:, ct, sl], in0=y5[:, ct, sl], in1=sgt[:], op=ALU.mult), r=[b_y5, b_sgt], w=[b_prod])
                            fm_norm(prod, b_prod, gnT[:, l, 0:2], cat, b_cat, 256, ones_bf, False, qes)

                            def xv(dm, blk):
                                if NCT == 2:
                                    ctile, jb = blk // 2, blk % 2
                                    v = x[:, dm, ctile * 1024:(ctile + 1) * 1024].rearrange("p (c j) -> p j c", j=8)
                                    return v[:, jb * 4:(jb + 1) * 4, :]
                                return x[:, dm, :].rearrange("p (c j) -> p j c", j=8)
                            wout_add(l, 0, cat, b_cat, xv=xv)
                            S.barrier()

                for l in range(DEPTH if stage < 40 else 1):
                    norm_to_h(l, 0)
                    if stage == 40:
                        mixer_hgrn(l)
                        continue
                    if stage >= 50:
                        mixer_s5(l)
                        continue
                    if stage >= 2:
                        mixer_gmlp(l)
                    if stage >= 3:
                        mixer_fnet(l)
                    if stage >= 4:
                        mixer_hgrn(l)
                    if stage >= 5:
                        mixer_s5(l)
                    norm_to_h(l, 1)
                    ffn(l)

                with ExitStack() as oes:
                    gB = kb.sb(oes, "gB", [128, D]); b_gB = Buf("gB")
                    kb.dma(gB[:], fng.rearrange("(o n) -> o n", o=1).broadcast_to([128, D]), w=[b_gB])
                    yo = [kb.sb(oes, "yo%d" % i, [128, D]) for i in range(2)]
                    b_yo = [Buf("yo0"), Buf("yo1")]
                    junk = kb.sb(oes, "junk", [128, 512]); b_junk = Buf("junk")
                    ssq = [kb.sb(oes, "ssq%d" % i, [128, 2]) for i in range(2)]
                    b_ssq = [Buf("ssq0"), Buf("ssq1")]
                    for tt in range(NT // 128):
                        s = tt % 2
                        pts = []
                        for kq in range(2):
                            pt, pb = kb.ps()
                            pts.append((pt, pb))
                            for kk in range(4):
                                k = kq * 4 + kk
                                kb.pe(lambda e, k=k, kk=kk, pt=pt, tt=tt: e.transpose(out=pt[:, kk * 128:(kk + 1) * 128], in_=x[:, k, tt * 128:(tt + 1) * 128], identity=ident[:]),
                                      r=[b_x, b_ident], w=[pb])
                            kb.act(lambda e, pt=pt, s=s, kq=kq: e.activation(out=junk[:], in_=pt[:], func=AF.Square, accum_out=ssq[s][:, kq:kq + 1]), r=[pb], w=[b_junk, b_ssq[s]])
                        kb.dve(lambda e, s=s: e.tensor_tensor(out=ssq[s][:, 0:1], in0=ssq[s][:, 0:1], in1=ssq[s][:, 1:2], op=ALU.add), r=[b_ssq[s]], w=[b_ssq[s]])
                        kb.act(lambda e, s=s: e.activation(out=ssq[s][:, 0:1], in_=ssq[s][:, 0:1], func=AF.Sqrt, scale=1.0 / D, bias=EPS), r=[b_ssq[s]], w=[b_ssq[s]])
                        kb.dve(lambda e, s=s: e.reciprocal(out=ssq[s][:, 0:1], in_=ssq[s][:, 0:1]), r=[b_ssq[s]], w=[b_ssq[s]])
                        for kq in range(2):
                            pt, pb = pts[kq]
                            kb.dve(lambda e, pt=pt, s=s, kq=kq: e.scalar_tensor_tensor(out=yo[s][:, kq * 512:(kq + 1) * 512], in0=pt[:], scalar=ssq[s][:, 0:1],
                                                                                in1=gB[:, kq * 512:(kq + 1) * 512], op0=ALU.mult, op1=ALU.mult),
                                   r=[pb, b_ssq[s], b_gB], w=[b_yo[s]])
                        kb.dma(y_dram[tt * 128:(tt + 1) * 128, :], yo[s][:], r=[b_yo[s]], is_out=True, sembuf=b_yo[s])
                    S.barrier()

        run_group("s", xs_d, ys_d, 2048, 2048, 1, True)
        if stage < 40:
            run_group("p", xp_d, yp_d, 512, 256, 0, False)
        S.finish()
        S.emit()
    return nc


def make_consts():
    c = {"ident": np.eye(128, dtype=np.float32)}
    n = np.arange(64)
    ang = 2 * np.pi * np.outer(n, n) / 64
    C64 = np.cos(ang) / 8.0
    S64 = np.sin(ang) / 8.0
    bd = np.zeros((256, 512), np.float64)
    for hd in range(4):
        bd[hd * 64:(hd + 1) * 64, hd * 64:(hd + 1) * 64] = C64
        bd[hd * 64:(hd + 1) * 64, 256 + hd * 64:256 + (hd + 1) * 64] = S64
    c["bdcs"] = bd.astype(ml_dtypes.bfloat16)
    i = np.arange(128)
    same = (i[:, None] // 32) == (i[None, :] // 32)
    mF = (same & (i[:, None] <= i[None, :])).astype(np.float32)
    mB = (same & (i[:, None] >= i[None, :])).astype(np.float32)
    bdm = ((i[:, None] // 64) == (i[None, :] // 64)).astype(np.float32)
    c["hmask"] = np.stack([mF, mB, bdm], axis=0)
    jj = i // 16
    c["s5mask"] = np.stack([(jj[None, :] >= jj[:, None]), (jj[:, None] >= jj[None, :])], axis=0).astype(np.float32)
    c["cmask"] = ((i[:, None] // 32) == np.arange(4)[None, :]).astype(np.float32)
    sm = np.ones((128, 2048), np.float32)
    sm[:, ::32] = 0.0
    c["smask"] = sm.astype(ml_dtypes.bfloat16)
    for nm, T in (("dft_s", 2048), ("dft_p", 256)):
        t = np.arange(T)
        a = 2 * np.pi * (np.outer(t, t) % T) / T
        m = np.stack([np.cos(a), -np.sin(a)], axis=0) / np.sqrt(T)
        c[nm] = m.astype(ml_dtypes.bfloat16)
    return c


_CACHE = {}
_DBG = {}


def kernel(**inputs):
    inp = {k: np.ascontiguousarray(np.asarray(v)) for k, v in inputs.items()}
    if "nc" not in _CACHE:
        _CACHE["nc"] = build_program()
    nc = _CACHE["nc"]
    consts = make_consts()
    shared = {k: inp[k] for k in ["w_ada", "b_ada", "norm1_g", "norm2_g", "w_in", "w_out", "ffn_w_up", "ffn_conv_w", "ffn_conv_b",
                                  "ffn_w_down", "final_norm_g", "grp_norm_g", "gm_norm_g", "gm_ws", "gm_bs", "fn_w", "hg_lb_logits", "s5_lam_re", "s5_lam_im", "s5_log_dt", "s5_b_re", "s5_b_im", "s5_c_re", "s5_c_im", "s5_d", "s5_w_glu"]}
    in_maps = []
    for c in range(NCORES):
        b = c // 4
        m = dict(shared)
        m.update(consts)
        m["xs"] = inp["x_sample"][b]
        m["xp"] = inp["x_prompt"][2 * c:2 * c + 2].reshape(512, D)
        m["cvec"] = np.stack([inp["c_ctx"], inp["c"][b]], axis=0)
        m["hg_state"] = inp["state_hgrn"][b]
        m["s5st_re"] = inp["state_s5_re"][b]
        m["s5st_im"] = inp["state_s5_im"][b]
        in_maps.append(m)
    res = run_bass_kernel_spmd(nc, in_maps, core_ids=list(range(NCORES)))
    r = res.results
    y_prompt = np.concatenate([r[c]["yp"].reshape(2, 256, D) for c in range(NCORES)], axis=0)
    y_sample = np.stack([r[0]["ys"], r[4]["ys"]], axis=0)
    st_re = np.concatenate([r[c]["s5o_re"] for c in range(NCORES)], axis=0)
    st_im = np.concatenate([r[c]["s5o_im"] for c in range(NCORES)], axis=0)
    st_hg = np.concatenate([r[c]["st_hg"] for c in range(NCORES)], axis=0)
    return y_prompt, y_sample, st_re, st_im, st_hg
```
